# Optimizing a Trainium2 kernel written in Bass

```python
import math
import jax, jax.numpy as jnp
from jax import lax
import numpy as np

D_MODEL = 1024
BATCH = 16
SEQ = 256
DEPTH = 4
DEC_BATCH = 8
DEC_SEQ = 4096
PAST_LEN = 512

GRID_W = 64
N_MIXERS = 2
N_ATTN_LAYERS = (DEPTH + 1) // 2
N_HYENA_LAYERS = DEPTH // 2
N_HEADS = 8
HEAD_DIM = D_MODEL // N_HEADS // 2
V_DIM = 2 * HEAD_DIM
ROPE_PAIRS = HEAD_DIM // 4
ROPE_BASE = 10000.0
Q_BLOCK = 128
FILT_BANDS = 16
FILT_EMB = 1 + 2 * FILT_BANDS
FILT_ORDER = 64
FILT_TARGET = 1e-2
FILT_FAST_PCT = 0.3
FILT_SLOW_PCT = 1.5
FILT_EPS = 1e-6
D_FF = -(-8 * D_MODEL // (3 * 256)) * 256
EPS = 1e-6
SUBLN_EPS = 1e-5

kernel_name = "diffattn_hyena_hybrid_dit_step"


def rmsnorm(x, g, eps=EPS):
    xf = x.astype(jnp.float32)
    y = xf * lax.rsqrt(jnp.mean(xf * xf, axis=-1, keepdims=True) + eps)
    return (y * g.astype(jnp.float32)).astype(x.dtype)


def adaln(cvec, w, b):
    m = jax.nn.silu(cvec) @ w + b
    return jnp.split(m, 6, axis=-1)


def modulate(h, shift, scale):
    return h * (1.0 + scale) + shift


def swiglu(h, w_gu, w_down):
    g, u = jnp.split(h @ w_gu, 2, axis=-1)
    return (jax.nn.silu(g) * u) @ w_down


def axial_rope(L):
    rows = L // GRID_W
    r, col = jnp.meshgrid(jnp.arange(rows, dtype=jnp.float32), jnp.arange(GRID_W, dtype=jnp.float32), indexing="ij")
    inv = ROPE_BASE ** (-jnp.arange(ROPE_PAIRS, dtype=jnp.float32) / ROPE_PAIRS)
    ang = jnp.stack([r.reshape(-1)[:, None] * inv, col.reshape(-1)[:, None] * inv], axis=1)
    return jnp.cos(ang), jnp.sin(ang)


def apply_rope(x, cos, sin):
    xs = x.reshape(x.shape[:-1] + (2, 2, ROPE_PAIRS))
    x1, x2 = xs[..., 0, :], xs[..., 1, :]
    c = cos[:, None, None].astype(x.dtype)
    s = sin[:, None, None].astype(x.dtype)
    out = jnp.stack([x1 * c - x2 * s, x2 * c + x1 * s], axis=-2)
    return out.reshape(x.shape)


def diff_qkv(h, w_qkv):
    B, L, _ = h.shape
    q, k, v = jnp.split(h @ w_qkv, 3, axis=-1)
    return (q.reshape(B, L, N_HEADS, 2, HEAD_DIM),
            k.reshape(B, L, N_HEADS, 2, HEAD_DIM),
            v.reshape(B, L, N_HEADS, V_DIM))


def diff_lambda(lam_params, layer_idx):
    lp = lam_params.astype(jnp.float32)
    lam_init = 0.8 - 0.6 * math.exp(-0.3 * layer_idx)
    lam = jnp.exp(jnp.sum(lp[0] * lp[1])) - jnp.exp(jnp.sum(lp[2] * lp[3])) + lam_init
    return lam, lam_init


def diff_attend(q, k, v, lam):
    B, Lq = q.shape[:2]
    nblk = Lq // Q_BLOCK
    scale = HEAD_DIM ** -0.5
    qb = q.reshape(B, nblk, Q_BLOCK, N_HEADS, 2, HEAD_DIM).transpose(1, 0, 2, 3, 4, 5)

    def block(qi):
        s = jnp.einsum("bqhpd,bkhpd->bhpqk", qi, k, preferred_element_type=jnp.float32) * scale
        p = jax.nn.softmax(s, axis=-1)
        a = p[:, :, 0] - lam * p[:, :, 1]
        return jnp.einsum("bhqk,bkhe->bqhe", a.astype(v.dtype), v)

    o = lax.map(block, qb)
    return o.transpose(1, 0, 2, 3, 4).reshape(B, Lq, N_HEADS, V_DIM)


def diff_out(o, subln_g, lam_init, w_o):
    o = rmsnorm(o, subln_g, SUBLN_EPS) * (1.0 - lam_init)
    B, L = o.shape[:2]
    return o.reshape(B, L, N_HEADS * V_DIM) @ w_o


def implicit_filter(L, w1, b1, w2, b2, w3, b3, freq):
    f32 = jnp.float32
    pos = jnp.arange(L, dtype=f32)
    t = pos / max(L - 1, 1)
    w = 2.0 * math.pi * pos / L
    bands = jnp.linspace(1e-4, FILT_BANDS - 1, FILT_BANDS, dtype=f32)
    z = jnp.concatenate([t[:, None], jnp.cos(w[:, None] * bands), -jnp.sin(w[:, None] * bands)], axis=-1)
    fr = freq.astype(f32)
    hdn = jnp.sin(fr * (z @ w1.astype(f32) + b1.astype(f32)))
    hdn = jnp.sin(fr * (hdn @ w2.astype(f32) + b2.astype(f32)))
    h = (hdn @ w3.astype(f32) + b3.astype(f32)).reshape(L, 2, D_MODEL)
    deltas = jnp.abs(jnp.linspace(math.log(FILT_TARGET) / FILT_SLOW_PCT,
                                  math.log(FILT_TARGET) / FILT_FAST_PCT, D_MODEL, dtype=f32))
    h = h * jnp.exp(-t[:, None, None] * deltas)
    return h / (jnp.sum(jnp.abs(h), axis=(0, 1), keepdims=True) + FILT_EPS)


def bidir_fftconv(v, filt):
    B, L, D = v.shape
    n = 2 * L
    h_f, h_b = filt[:, 0], filt[:, 1]
    kk = jnp.concatenate([h_f, jnp.zeros((1, D), filt.dtype), h_b[:0:-1]], axis=0)
    vf = jnp.fft.rfft(v.astype(jnp.float32), n=n, axis=1)
    kf = jnp.fft.rfft(kk, n=n, axis=0)
    return jnp.fft.irfft(vf * kf[None], n=n, axis=1)[:, :L]


def hyena(h, w_in, b_in, conv_w, conv_b, fw1, fb1, fw2, fb2, fw3, fb3, freq, skip, w_out, b_out):
    B, L, D = h.shape
    u = h @ w_in + b_in
    up = jnp.pad(u, ((0, 0), (1, 1), (0, 0)))
    u = up[:, :-2] * conv_w[0] + up[:, 1:-1] * conv_w[1] + up[:, 2:] * conv_w[2] + conv_b
    x0, x1, v = jnp.split(u, 3, axis=-1)
    filt = implicit_filter(L, fw1, fb1, fw2, fb2, fw3, fb3, freq)
    v = v * x1
    y = bidir_fftconv(v, filt).astype(h.dtype) + v * skip
    return (y * x0) @ w_out + b_out


def setup_inputs(seed: int = 0) -> dict:
    key = jax.random.key(seed)
    ks = jax.random.split(key, 40)
    f32 = jnp.float32

    def nrm(i, shape, scale=1.0):
        return jax.random.normal(ks[i], shape, f32) * scale

    D = D_MODEL
    NA, NH = N_ATTN_LAYERS, N_HYENA_LAYERS
    return {
        "x_prompt": nrm(0, (BATCH, SEQ, D)),
        "x_sample": nrm(1, (DEC_BATCH, DEC_SEQ, D)),
        "cache_k": nrm(2, (DEC_BATCH, NA, PAST_LEN, N_HEADS, 2, HEAD_DIM)),
        "cache_v": nrm(3, (DEC_BATCH, NA, PAST_LEN, N_HEADS, V_DIM)),
        "c": nrm(4, (DEC_BATCH, D)),
        "c_ctx": nrm(5, (D,)),
        "ada_w": nrm(6, (DEPTH, D, 6 * D), 0.5 * D ** -0.5),
        "ada_b": nrm(7, (DEPTH, 6 * D), 0.01),
        "norm1_g": 1.0 + nrm(8, (DEPTH, D), 0.01),
        "norm2_g": 1.0 + nrm(9, (DEPTH, D), 0.01),
        "attn_w_qkv": nrm(10, (NA, D, 3 * D), D ** -0.5),
        "attn_lambda": nrm(11, (NA, 4, HEAD_DIM), 0.1),
        "attn_subln_g": 1.0 + nrm(12, (NA, V_DIM), 0.01),
        "attn_w_o": nrm(13, (NA, D, D), D ** -0.5),
        "hy_w_in": nrm(14, (NH, D, 3 * D), D ** -0.5),
        "hy_b_in": nrm(15, (NH, 3 * D), 0.01),
        "hy_conv_w": nrm(16, (NH, 3, 3 * D), 3 ** -0.5),
        "hy_conv_b": nrm(17, (NH, 3 * D), 0.01),
        "filt_w1": nrm(18, (NH, FILT_EMB, FILT_ORDER), FILT_EMB ** -0.5),
        "filt_b1": nrm(19, (NH, FILT_ORDER), 0.1),
        "filt_w2": nrm(20, (NH, FILT_ORDER, FILT_ORDER), FILT_ORDER ** -0.5),
        "filt_b2": nrm(21, (NH, FILT_ORDER), 0.1),
        "filt_w3": nrm(22, (NH, FILT_ORDER, 2 * D), FILT_ORDER ** -0.5),
        "filt_b3": nrm(23, (NH, 2 * D), 0.1),
        "filt_freq": 1.0 + nrm(24, (NH, FILT_ORDER), 0.01),
        "hy_skip": nrm(25, (NH, D)),
        "hy_w_out": nrm(26, (NH, D, D), D ** -0.5),
        "hy_b_out": nrm(27, (NH, D), 0.01),
        "ffn_w_gu": nrm(28, (DEPTH, D, 2 * D_FF), D ** -0.5),
        "ffn_w_down": nrm(29, (DEPTH, D_FF, D), D_FF ** -0.5),
        "final_g": 1.0 + nrm(30, (D,), 0.01),
    }


def reference(x_prompt, x_sample, cache_k, cache_v, c, c_ctx, ada_w, ada_b, norm1_g, norm2_g,
              attn_w_qkv, attn_lambda, attn_subln_g, attn_w_o,
              hy_w_in, hy_b_in, hy_conv_w, hy_conv_b, filt_w1, filt_b1, filt_w2, filt_b2,
              filt_w3, filt_b3, filt_freq, hy_skip, hy_w_out, hy_b_out,
              ffn_w_gu, ffn_w_down, final_g):
    def hy_params(j):
        return (hy_w_in[j], hy_b_in[j], hy_conv_w[j], hy_conv_b[j], filt_w1[j], filt_b1[j],
                filt_w2[j], filt_b2[j], filt_w3[j], filt_b3[j], filt_freq[j], hy_skip[j],
                hy_w_out[j], hy_b_out[j])

    xp = x_prompt
    ctx_k, ctx_v = [], []
    for i in range(DEPTH):
        j = i // N_MIXERS
        sh1, sc1, g1, sh2, sc2, g2 = adaln(c_ctx, ada_w[i], ada_b[i])
        h = modulate(rmsnorm(xp, norm1_g[i]), sh1, sc1)
        if i % N_MIXERS == 0:
            lam, lam_init = diff_lambda(attn_lambda[j], i)
            q, k, v = diff_qkv(h, attn_w_qkv[j])
            out = diff_out(diff_attend(q, k, v, lam), attn_subln_g[j], lam_init, attn_w_o[j])
            ctx_k.append(k)
            ctx_v.append(v)
        else:
            out = hyena(h, *hy_params(j))
        xp = xp + g1 * out
        h = modulate(rmsnorm(xp, norm2_g[i]), sh2, sc2)
        xp = xp + g2 * swiglu(h, ffn_w_gu[i], ffn_w_down[i])
    y_prompt = rmsnorm(xp, final_g)
    new_cache_k = jnp.stack(ctx_k, axis=1)
    new_cache_v = jnp.stack(ctx_v, axis=1)

    xs = x_sample
    cos, sin = axial_rope(xs.shape[1])
    cmod = c[:, None, :]
    for i in range(DEPTH):
        j = i // N_MIXERS
        sh1, sc1, g1, sh2, sc2, g2 = adaln(cmod, ada_w[i], ada_b[i])
        h = modulate(rmsnorm(xs, norm1_g[i]), sh1, sc1)
        if i % N_MIXERS == 0:
            lam, lam_init = diff_lambda(attn_lambda[j], i)
            q, k, v = diff_qkv(h, attn_w_qkv[j])
            q = apply_rope(q, cos, sin)
            k = apply_rope(k, cos, sin)
            k_all = jnp.concatenate([cache_k[:, j].astype(k.dtype), k], axis=1)
            v_all = jnp.concatenate([cache_v[:, j].astype(v.dtype), v], axis=1)
            out = diff_out(diff_attend(q, k_all, v_all, lam), attn_subln_g[j], lam_init, attn_w_o[j])
        else:
            out = hyena(h, *hy_params(j))
        xs = xs + g1 * out
        h = modulate(rmsnorm(xs, norm2_g[i]), sh2, sc2)
        xs = xs + g2 * swiglu(h, ffn_w_gu[i], ffn_w_down[i])
    y_sample = rmsnorm(xs, final_g)

    return (y_prompt, y_sample, new_cache_k, new_cache_v)
```

```python
import contextlib
import os
import math
import numpy as np
import ml_dtypes
import concourse.bass as bass
import concourse.mybir as mybir
from concourse.bass_utils import run_bass_kernel_spmd

F32 = mybir.dt.float32
BF16 = mybir.dt.bfloat16
AF = mybir.ActivationFunctionType
ALU = mybir.AluOpType
AX = mybir.AxisListType

ENGS = ("pe", "act", "dve", "pool", "sp")
D = 1024
TS, TP, T = 4096, 512, 4608
NT = 9
NTT = 36
DFF = 2816
NCH_FF = 22
EPS = 1e-6
SUBLN_EPS = 1e-5
NKEY = 5120


class Res:
    __slots__ = ("w", "r", "name", "excl")

    def __init__(self, name="", excl=False):
        self.w = {}
        self.r = {}
        self.name = name
        self.excl = excl


class Prog:
    def __init__(self, nc, arena_words):
        self.nc = nc
        self.q = {e: [] for e in ENGS}
        self.known = {e: {} for e in ENGS}
        self.dma_cnt = {}
        self.stack = contextlib.ExitStack()
        self.arena = self.stack.enter_context(nc.sbuf_tensor("arena", [128, arena_words], F32))
        self.arena_words = arena_words
        self.top = 0
        self.live = []
        self.retired = []
        self.nsem = 0
        self.keymap = {}
        self.keyres = {}

    def alloc(self, shape, dt, name=""):
        esz = 4 if dt == F32 else 2
        free = 1
        for s in shape[1:]:
            free *= s
        words = (free * esz + 3) // 4
        words = (words + 7) // 8 * 8
        off = self.top
        assert off + words <= self.arena_words, f"SBUF arena overflow {name} {off + words}"
        self.top += words
        v = self.arena[0:shape[0], off:off + (free * esz) // 4]
        if dt != F32:
            v = v.bitcast(dt)
        if len(shape) == 3:
            v = v.rearrange("p (a b) -> p a b", b=shape[2])
        elif len(shape) == 4:
            v = v.rearrange("p (a b c) -> p a b c", b=shape[2], c=shape[3])
        r = Res(name)
        keep = []
        for (a, b, rr) in self.retired:
            if a < off + words and off < b:
                for k, val in rr.w.items():
                    if r.r.get(k, -1) < val:
                        r.r[k] = val
                for k, val in rr.r.items():
                    if r.r.get(k, -1) < val:
                        r.r[k] = val
                if a >= off and b <= off + words:
                    continue
            keep.append((a, b, rr))
        self.retired = keep
        self.live.append((off, off + words, r))
        return v, r

    def mark(self):
        return (self.top, len(self.live))

    def release(self, m):
        top, n = m
        self.retired.extend(self.live[n:])
        del self.live[n:]
        self.top = top

    def _add(self, eng, fn, reads, writes, acc_writes, own):
        deps = {}

        def upd(d):
            for k, v in d.items():
                if deps.get(k, -1) < v:
                    deps[k] = v
        for r in reads:
            upd(r.w)
            if r.excl:
                upd({k: v for k, v in r.r.items() if k != ("c", eng)})
        for w in writes:
            upd(w.w)
            upd(w.r)
        for w in acc_writes:
            upd(w.r)
            upd({k: v for k, v in w.w.items() if k != own})
        q = self.q[eng]
        idx = len(q)
        waits = []
        kn = self.known[eng]
        for k, v in deps.items():
            if k == ("c", eng):
                if eng == "pe":
                    continue
                vv = -1
                for r in reads:
                    x = r.w.get(k, -1)
                    if x > vv:
                        vv = x
                if vv < 0:
                    continue
                v = vv
            if k[0] == "d":
                v = self.dma_cnt[k[1]]
            if kn.get(k, -1) >= v:
                continue
            kn[k] = v
            waits.append((k, v))
            if k[0] == "c":
                self.q[k[1]][v][2] = True
        op = [fn, waits, False, None]
        q.append(op)
        return op, idx

    def op(self, eng, fn, reads=(), writes=(), acc_writes=()):
        op, idx = self._add(eng, fn, reads, writes, acc_writes, ("c", eng))
        k = ("c", eng)
        for r in reads:
            r.r[k] = idx
        for w in writes:
            w.w = {k: idx}
            w.r = {}
        for w in acc_writes:
            w.w[k] = idx
        return op

    def new_phase(self):
        self.keymap = {}

    def dma(self, eng, semkey, fn, reads=(), writes=(), acc_writes=()):
        km = self.keymap
        if semkey not in km:
            km[semkey] = f"g{len(km)}"
        semkey = km[semkey]
        kr = self.keyres.get(semkey)
        if kr is None:
            kr = self.keyres[semkey] = Res(semkey)
        writes = list(writes) + [kr]
        op, idx = self._add(eng, fn, reads, writes, acc_writes, ("d", semkey))
        c = self.dma_cnt.get(semkey, 0) + 1
        self.dma_cnt[semkey] = c
        op[3] = semkey
        k = ("d", semkey)
        for r in reads:
            r.r[k] = c
        for w in writes:
            w.w = {k: c}
            w.r = {}
        for w in acc_writes:
            w.w[k] = c
        return op

    def emit(self, final_keys):
        nc = self.nc
        st = self.stack
        csem = {e: st.enter_context(nc.semaphore(f"c_{e}")) for e in ENGS if e != "sp"}
        dsem = {k: st.enter_context(nc.semaphore(f"d_{i}")) for i, k in enumerate(self.dma_cnt)}
        cum = {}
        for e in ENGS:
            c = 0
            arr = []
            for o in self.q[e]:
                if o[2]:
                    c += 1
                arr.append(c)
            cum[e] = arr
        engobj = {"pe": "tensor", "act": "scalar", "dve": "vector", "pool": "gpsimd", "sp": "sync"}
        with nc.Block() as block:
            for e in ENGS:
                ops = self.q[e]

                def body(eng, e=e, ops=ops):
                    for fn, waits, sig, semkey in ops:
                        for k, v in waits:
                            if k[0] == "c":
                                eng.wait_ge(csem[k[1]], cum[k[1]][v])
                            else:
                                eng.wait_ge(dsem[k[1]], 16 * v)
                        ins = fn(eng)
                        if semkey is not None:
                            ins.then_inc(dsem[semkey], 16)
                        elif sig:
                            ins.then_inc(csem[e], 1)
                    if e == "sp":
                        for k in final_keys:
                            eng.wait_ge(dsem[k], 16 * self.dma_cnt[k])
                getattr(block, engobj[e])(body)


def _bf(a):
    return np.ascontiguousarray(a.astype(ml_dtypes.bfloat16))


def _dft_tables(L):
    N = 2 * L
    nf = L + 1
    nch = (nf + 127) // 128
    S = nch * 128
    a = np.arange(S, dtype=np.int64)
    m = (a[:, None] * a[None, :]) % N
    ang = 2.0 * np.pi * m.astype(np.float64) / N
    valid = (a[:, None] <= L) & (a[None, :] <= L)
    q = np.where(valid, np.cos(ang), 0.0)
    r = np.where(valid, np.sin(ang), 0.0)
    wf = np.where(a <= L, 2.0 / N, 0.0)
    wf[0] = 1.0 / N
    wf[L] = 1.0 / N
    wfc = wf.reshape(nch, 128).T.astype(np.float32)
    return (_bf(q.reshape(nch, 128, S)), _bf(r.reshape(nch, 128, S)), np.ascontiguousarray(wfc), nch, S)


def _filter_consts(L):
    pos = np.arange(L, dtype=np.float32)
    t = pos / np.float32(max(L - 1, 1))
    w = (np.float32(2.0 * math.pi) * pos / np.float32(L)).astype(np.float32)
    bands = np.linspace(1e-4, 15, 16, dtype=np.float32)
    z = np.concatenate([t[:, None], np.cos(w[:, None] * bands), -np.sin(w[:, None] * bands)], axis=-1)
    zT = np.ascontiguousarray(z.T.astype(np.float32))
    negt = np.ascontiguousarray((-t).reshape(L // 128, 128).T.astype(np.float32))
    return zT, negt


def _host_consts():
    c = {}
    tpos = np.arange(TS)
    rowpos = (tpos // 64).astype(np.float32)
    colpos = (tpos % 64).astype(np.float32)
    inv = (10000.0 ** (-np.arange(16, dtype=np.float32) / 16)).astype(np.float32)
    cos = np.zeros((128, TS), np.float32)
    sins = np.zeros((128, TS), np.float32)
    perm = np.zeros((128, 128), np.float32)
    for p in range(2):
        for a in range(2):
            posv = rowpos if a == 0 else colpos
            for hf in range(2):
                for f in range(16):
                    row = p * 64 + a * 32 + hf * 16 + f
                    ang = (posv * inv[f]).astype(np.float32)
                    cos[row] = np.cos(ang)
                    sins[row] = np.sin(ang) * (-1.0 if hf == 0 else 1.0)
                    other = p * 64 + a * 32 + (1 - hf) * 16 + f
                    perm[other, row] = 1.0
    c["rcos"] = cos
    c["rsin"] = sins
    c["perm"] = _bf(perm)
    c["identb"] = _bf(np.eye(128, dtype=np.float32))
    c["identf"] = np.eye(128, dtype=np.float32)
    c["onesf"] = np.ones((128, 128), np.float32)
    for nm, L in (("s", TS), ("p", 256)):
        q, r, wf, nch, S = _dft_tables(L)
        c["q" + nm], c["r" + nm], c["wf" + nm] = q, r, wf
        zT, negt = _filter_consts(L)
        c["z" + nm], c["negt" + nm] = zT, negt
    deltas = np.abs(np.linspace(math.log(1e-2) / 1.5, math.log(1e-2) / 0.3, D, dtype=np.float32))
    c["delta"] = deltas.astype(np.float32)
    m0 = np.ones((128, 1), np.float32)
    m0[0, 0] = 0.0
    c["mask0"] = m0
    return c


W_SPECS = [
    ("ada_w", [4, D, 6 * D]), ("ada_b", [4, 6 * D]), ("norm1_g", [4, D]), ("norm2_g", [4, D]),
    ("attn_w_qkv", [2, D, 3 * D]), ("attn_lambda", [2, 4, 64]), ("attn_subln_g", [2, 128]),
    ("attn_w_o", [2, D, D]), ("hy_w_in", [2, D, 3 * D]), ("hy_b_in", [2, 3 * D]),
    ("hy_conv_w", [2, 3, 3 * D]), ("hy_conv_b", [2, 3 * D]), ("filt_w1", [2, 33, 64]),
    ("filt_b1", [2, 64]), ("filt_w2", [2, 64, 64]), ("filt_b2", [2, 64]), ("filt_w3", [2, 64, 2 * D]),
    ("filt_b3", [2, 2 * D]), ("filt_freq", [2, 64]), ("hy_skip", [2, D]), ("hy_w_out", [2, D, D]),
    ("hy_b_out", [2, D]), ("ffn_w_gu", [4, D, 2 * DFF]), ("ffn_w_down", [4, DFF, D]), ("final_g", [D]),
]
CONST_SPECS = [
    ("rcos", [128, TS], F32), ("rsin", [128, TS], F32), ("perm", [128, 128], BF16),
    ("identb", [128, 128], BF16), ("identf", [128, 128], F32), ("onesf", [128, 128], F32),
    ("qs", [33, 128, 4224], BF16), ("rs", [33, 128, 4224], BF16), ("wfs", [128, 33], F32),
    ("zs", [33, TS], F32), ("negts", [128, 32], F32),
    ("qp", [3, 128, 384], BF16), ("rp", [3, 128, 384], BF16), ("wfp", [128, 3], F32),
    ("zp", [33, 256], F32), ("negtp", [128, 2], F32),
    ("delta", [D], F32), ("mask0", [128, 1], F32),
]

GROUPS = [
    dict(name="s", tok0=0, T=TS, nseq=1, L=TS, rope=True, ncache=512, tiles=list(range(0, 8)), tt=list(range(0, 32))),
    dict(name="p", tok0=TS, T=TP, nseq=2, L=256, rope=False, ncache=0, tiles=[8], tt=list(range(32, 36))),
]


def build_program(stop_after=None, debug_outs=()):
    nc = bass.Bass("TRN2", target_bir_lowering=False)
    I = {}
    I["x"] = nc.dram_tensor("x", [T, D], F32, kind="ExternalInput").ap()
    I["ck"] = nc.dram_tensor("ck", [2, 512, D], F32, kind="ExternalInput").ap()
    I["cv"] = nc.dram_tensor("cv", [2, 512, D], F32, kind="ExternalInput").ap()
    I["cvec"] = nc.dram_tensor("cvec", [2, D], F32, kind="ExternalInput").ap()
    for nm, shp in W_SPECS:
        I[nm] = nc.dram_tensor(nm, shp, F32, kind="ExternalInput").ap()
    for nm, shp, dt in CONST_SPECS:
        I[nm] = nc.dram_tensor(nm, shp, dt, kind="ExternalInput").ap()
    O = {}
    O["y"] = nc.dram_tensor("y", [T, D], F32, kind="ExternalOutput").ap()
    O["nk"] = nc.dram_tensor("nk", [2, 2, 256, D], F32, kind="ExternalOutput").ap()
    O["nv"] = nc.dram_tensor("nv", [2, 2, 256, D], F32, kind="ExternalOutput").ap()

    def scratch(nm, shp, dt):
        kind = "ExternalOutput" if nm in debug_outs else "Internal"
        return nc.dram_tensor(nm, shp, dt, kind=kind).ap()
    S = {}
    S["X"] = scratch("X", [T, D], F32)
    S["MODS"] = scratch("MODS", [2, 128, 6 * D], F32)
    S["KT"] = scratch("KT", [8, 128, NKEY], BF16)
    S["VS"] = scratch("VS", [NKEY, D], BF16)
    S["QT"] = scratch("QT", [8, 128, T], BF16)
    S["MID"] = scratch("MID", [NCH_FF, 128, T], BF16)
    S["VVT"] = scratch("VVT", [8, 128, T], F32)
    S["X0T"] = scratch("X0T", [8, 128, T], F32)
    S["VVTOK"] = scratch("VVTOK", [T, D], BF16)
    S["HSs"] = scratch("HSs", [TS, D], BF16)
    S["HDs"] = scratch("HDs", [TS, D], BF16)
    S["HSp"] = scratch("HSp", [256, D], BF16)
    S["HDp"] = scratch("HDp", [256, D], BF16)
    S["YF"] = scratch("YF", [2, 33, 128, 2, 512], BF16)
    S["HTD"] = scratch("HTD", [128, 8, T], BF16)

    ARENA_WORDS = 49152
    P = Prog(nc, ARENA_WORDS)
    PS = P.stack.enter_context(nc.psum_tensor("ps", [128, 4096], F32))
    psr = [Res(f"bank{b}", excl=True) for b in range(8)]

    def bank(b, n=512):
        return PS[:, b * 512:b * 512 + n]

    def bankb(b):
        return PS[:, b * 512:(b + 1) * 512].bitcast(BF16)

    R = {}
    R["X"] = [[Res(f"X{i}a"), Res(f"X{i}b")] for i in range(NTT)]
    R["MODS"] = [Res(), Res()]
    R["KT"] = Res()
    R["VS"] = Res()
    R["QT"] = Res()
    R["MID"] = [Res() for _ in range(NT)]
    R["VVT"] = Res()
    R["X0T"] = Res()
    R["VVTOK"] = Res()
    R["HS"] = Res()
    R["YF"] = Res()
    R["OUT"] = Res()
    R["HTD"] = Res()
    semctr = [0]

    def sk(prefix):
        semctr[0] += 1
        return f"{prefix}{semctr[0]}"

    hT, _ = P.alloc([128, 8, T], BF16, "hT")
    hTr = [Res(f"hT{i}") for i in range(NT)]
    identb, identb_r = P.alloc([128, 128], BF16, "identb")
    identf, identf_r = P.alloc([128, 128], F32, "identf")
    onesf, onesf_r = P.alloc([128, 128], F32, "onesf")
    SIL, SIL_r = P.alloc([128, 2, 8, 128], BF16, "SIL")
    P.dma("sp", "c_identb", lambda e: e.dma_start(out=identb, in_=I["identb"]), writes=[identb_r])
    P.dma("sp", "c_identf", lambda e: e.dma_start(out=identf, in_=I["identf"]), writes=[identf_r])
    P.dma("sp", "c_onesf", lambda e: e.dma_start(out=onesf, in_=I["onesf"]), writes=[onesf_r])

    def load_cols(dst, dst_r, col0, vec, nchunks, key):
        P.dma("sp", key, lambda e: e.dma_start(
            out=dst[:, col0:col0 + nchunks], in_=vec.rearrange("(c p) -> p c", p=128), allow_slow_non_contiguous=True),
            acc_writes=[dst_r])

    def bcast_rows(vec1d, n):
        return vec1d.rearrange("(o n) -> o n", o=1).broadcast_to([128, n])

    m0 = P.mark()
    ccol, ccol_r = P.alloc([128, 16], F32, "ccol")
    scol, scol_r = P.alloc([128, 16], F32, "scol")
    for g in range(2):
        load_cols(ccol, ccol_r, g * 8, I["cvec"][g], 8, "ccol")
    P.op("act", lambda e: e.activation(out=scol, in_=ccol, func=AF.Silu), reads=[ccol_r], writes=[scol_r])
    for g in range(2):
        for kc in range(8):
            P.op("dve", lambda e, g=g, kc=kc: e.tensor_scalar(
                out=SIL[:, g, kc, :], in0=onesf, scalar1=scol[:, g * 8 + kc:g * 8 + kc + 1], scalar2=None,
                op0=ALU.mult), reads=[scol_r, onesf_r], acc_writes=[SIL_r])
    P.release(m0)

    for tt in range(NTT):
        P.dma("sp", f"xcopy{tt % 4}", lambda e, tt=tt: e.dma_start(
            out=S["X"][tt * 128:(tt + 1) * 128, :], in_=I["x"][tt * 128:(tt + 1) * 128, :]), writes=R["X"][tt])

    def phase_mod(i):
        P.new_phase()
        m = P.mark()
        adab, adab_r = P.alloc([128, 6 * D], F32, "adab")
        mod = [P.alloc([128, 6 * D], F32, f"mod{g}") for g in range(2)]
        gn = [P.alloc([128, D], F32, f"gn{k}") for k in range(2)]
        wch = [P.alloc([128, 8, 512], BF16, f"adw{k}") for k in range(2)]
        P.dma("sp", "adab", lambda e: e.dma_start(out=adab, in_=bcast_rows(I["ada_b"][i], 6 * D)), writes=[adab_r])
        P.dma("sp", "gn0", lambda e: e.dma_start(out=gn[0][0], in_=bcast_rows(I["norm1_g"][i], D)), writes=[gn[0][1]])
        P.dma("sp", "gn1", lambda e: e.dma_start(out=gn[1][0], in_=bcast_rows(I["norm2_g"][i], D)), writes=[gn[1][1]])
        for n in range(12):
            w, wr = wch[n % 2]
            P.dma("pool", f"adw{n % 2}", lambda e, w=w, n=n: e.dma_start(
                out=w, in_=I["ada_w"][i][:, n * 512:(n + 1) * 512].rearrange("(kc p) n -> p kc n", p=128)), writes=[wr])
            for g in range(2):
                b = (n * 2 + g) % 8
                for kc in range(8):
                    P.op("pe", lambda e, w=w, g=g, kc=kc, b=b: e.matmul(
                        bank(b), lhsT=SIL[:, g, kc, :], rhs=w[:, kc, :], start=(kc == 0), stop=(kc == 7)),
                        reads=[SIL_r, wr], writes=[psr[b]] if kc == 0 else (), acc_writes=[psr[b]] if kc else ())
                P.op("dve", lambda e, g=g, n=n, b=b: e.tensor_tensor(
                    out=mod[g][0][:, n * 512:(n + 1) * 512], in0=bank(b), in1=adab[:, n * 512:(n + 1) * 512], op=ALU.add),
                    reads=[psr[b], adab_r], acc_writes=[mod[g][1]])
        for g in range(2):
            for k, off in ((0, D), (1, 4 * D)):
                P.op("dve", lambda e, g=g, k=k, off=off: e.scalar_tensor_tensor(
                    out=mod[g][0][:, off:off + D], in0=mod[g][0][:, off:off + D], scalar=1.0, in1=gn[k][0],
                    op0=ALU.add, op1=ALU.mult), reads=[mod[g][1], gn[k][1]], acc_writes=[mod[g][1]])
            P.dma("sp", f"mods{g}", lambda e, g=g: e.dma_start(out=S["MODS"][g], in_=mod[g][0]),
                  reads=[mod[g][1]], writes=[R["MODS"][g]])
        P.release(m)

    def load_mod(off, key):
        out = []
        for g in range(2):
            t, r = P.alloc([128, D], F32, f"{key}{g}")
            P.dma("sp", f"ldm_{key}{g}", lambda e, t=t, g=g: e.dma_start(out=t, in_=S["MODS"][g][:, off:off + D]),
                  reads=[R["MODS"][g]], writes=[r])
            out.append((t, r))
        return out

    def rstd_ops(ss, ss_r, rs, rs_r, n, eps):
        P.op("act", lambda e: e.activation(out=rs, in_=ss, func=AF.Ln, scale=1.0 / n, bias=float(eps)),
             reads=[ss_r], writes=[rs_r])
        P.op("act", lambda e: e.activation(out=rs, in_=rs, func=AF.Exp, scale=-0.5),
             reads=[rs_r], writes=[rs_r])

    def phase_norm(offA, offB, final=False):
        P.new_phase()
        m = P.mark()
        if final:
            gt, gr = P.alloc([128, D], F32, "fing")
            P.dma("sp", "fing", lambda e: e.dma_start(out=gt, in_=bcast_rows(I["final_g"], D)), writes=[gr])
            A = [(gt, gr), (gt, gr)]
            B = None
        else:
            A = load_mod(offA, "nA")
            B = load_mod(offB, "nB")
        xt = [P.alloc([128, D], F32, f"nx{k}") for k in range(3)]
        x2 = [P.alloc([128, D], F32, f"nx2{k}") for k in range(2)]
        hb = [P.alloc([128, D], BF16, f"nhb{k}") for k in range(2)]
        junk, junk_r = P.alloc([128, D], BF16, "njunk")
        ss = [P.alloc([128, 1], F32, f"nss{k}") for k in range(2)]
        rs = [P.alloc([128, 1], F32, f"nrs{k}") for k in range(2)]
        for tt in range(NTT):
            g = 0 if tt < 32 else 1
            x, xr = xt[tt % 3]
            y, yr = x2[tt % 2]
            h, hr = hb[tt % 2]
            s_, s_r = ss[tt % 2]
            r_, r_r = rs[tt % 2]
            P.dma("sp", f"nx{tt % 3}", lambda e, x=x, tt=tt: e.dma_start(out=x, in_=S["X"][tt * 128:(tt + 1) * 128, :]),
                  reads=R["X"][tt], writes=[xr])
            P.op("act", lambda e, x=x, s_=s_: e.activation(out=junk, in_=x, func=AF.Square, accum_out=s_),
                 reads=[xr], writes=[junk_r, s_r])
            rstd_ops(s_, s_r, r_, r_r, D, EPS)
            if final:
                P.op("dve", lambda e, x=x, y=y, r_=r_, g=g: e.scalar_tensor_tensor(
                    out=y, in0=x, scalar=r_, in1=A[g][0], op0=ALU.mult, op1=ALU.mult),
                    reads=[xr, r_r, A[g][1]], writes=[yr])
                P.dma("sp", f"fo{tt % 2}", lambda e, y=y, tt=tt: e.dma_start(out=O["y"][tt * 128:(tt + 1) * 128, :], in_=y),
                      reads=[yr], acc_writes=[R["OUT"]])
                continue
            P.op("dve", lambda e, x=x, y=y, r_=r_, g=g: e.scalar_tensor_tensor(
                out=y, in0=x, scalar=r_, in1=A[g][0], op0=ALU.mult, op1=ALU.mult),
                reads=[xr, r_r, A[g][1]], writes=[yr])
            P.op("pool", lambda e, y=y, h=h, g=g: e.tensor_tensor(out=h, in0=y, in1=B[g][0], op=ALU.add),
                 reads=[yr, B[g][1]], writes=[hr])
            b = tt % 2
            for kc in range(8):
                P.op("pe", lambda e, h=h, kc=kc, b=b: e.transpose(bankb(b)[:, kc * 128:(kc + 1) * 128], h[:, kc * 128:(kc + 1) * 128], identb),
                     reads=[hr, identb_r], writes=[psr[b]] if kc == 0 else (), acc_writes=[psr[b]] if kc else ())
            P.op("act", lambda e, b=b, tt=tt: e.copy(out=hT[:, :, tt * 128:(tt + 1) * 128],
                                                     in_=bankb(b).rearrange("p (k t) -> p k t", t=128)),
                 reads=[psr[b]], acc_writes=[hTr[tt // 4]])
        P.release(m)

    def stream_w(dst, dst_r, key, src_ap):
        P.dma("pool", key, lambda e: e.dma_start(out=dst, in_=src_ap.rearrange("(kc p) n -> p kc n", p=128)), writes=[dst_r])

    def mm_wstat(b, w, wr, tile, ncols=512):
        for kc in range(8):
            P.op("pe", lambda e, kc=kc: e.matmul(bank(b, ncols), lhsT=w[:, kc, :], rhs=hT[:, kc, tile * 512:tile * 512 + ncols],
                                                 start=(kc == 0), stop=(kc == 7)),
                 reads=[wr, hTr[tile]], writes=[psr[b]] if kc == 0 else (), acc_writes=[psr[b]] if kc else ())

    def mm_tstat(b, w, wr, tt):
        for kc in range(8):
            P.op("pe", lambda e, kc=kc: e.matmul(bank(b), lhsT=hT[:, kc, tt * 128:(tt + 1) * 128], rhs=w[:, kc, :],
                                                 start=(kc == 0), stop=(kc == 7)),
                 reads=[wr, hTr[tt // 4]], writes=[psr[b]] if kc == 0 else (), acc_writes=[psr[b]] if kc else ())

    def phase_qkv(j):
        P.new_phase()
        m = P.mark()
        rcos, rcos_r = P.alloc([128, TS], F32, "rcos")
        rsin, rsin_r = P.alloc([128, TS], F32, "rsin")
        perm, perm_r = P.alloc([128, 128], BF16, "perm")
        P.dma("sp", "rcos", lambda e: e.dma_start(out=rcos, in_=I["rcos"]), writes=[rcos_r])
        P.dma("sp", "rsin", lambda e: e.dma_start(out=rsin, in_=I["rsin"]), writes=[rsin_r])
        P.dma("sp", "perm", lambda e: e.dma_start(out=perm, in_=I["perm"]), writes=[perm_r])
        ckt = [P.alloc([128, D], BF16, f"ckt{k}") for k in range(2)]
        kst = [P.alloc([128, 8, 128], BF16, f"kst{k}") for k in range(2)]
        cvt = [P.alloc([128, D], BF16, f"cvt{k}") for k in range(2)]
        for tk in range(4):
            c_, c_r = ckt[tk % 2]
            k_, k_r = kst[tk % 2]
            v_, v_r = cvt[tk % 2]
            P.dma("pool", f"ckt{tk % 2}", lambda e, c_=c_, tk=tk: e.dma_start(out=c_, in_=I["ck"][j, tk * 128:(tk + 1) * 128, :]), writes=[c_r])
            b = tk % 2
            for h in range(8):
                P.op("pe", lambda e, c_=c_, h=h, b=b: e.transpose(bankb(b)[:, h * 128:(h + 1) * 128], c_[:, h * 128:(h + 1) * 128], identb),
                     reads=[c_r, identb_r], writes=[psr[b]] if h == 0 else (), acc_writes=[psr[b]] if h else ())
            P.op("act", lambda e, k_=k_, b=b: e.copy(out=k_, in_=bankb(b).rearrange("p (k t) -> p k t", t=128)), reads=[psr[b]], writes=[k_r])
            P.dma("sp", f"kst{tk % 2}", lambda e, k_=k_, tk=tk: e.dma_start(
                out=S["KT"][:, :, tk * 128:(tk + 1) * 128].rearrange("h p c -> p h c"), in_=k_), reads=[k_r], acc_writes=[R["KT"]])
            P.dma("pool", f"cvt{tk % 2}", lambda e, v_=v_, tk=tk: e.dma_start(out=v_, in_=I["cv"][j, tk * 128:(tk + 1) * 128, :]), writes=[v_r])
            P.dma("sp", f"cvo{tk % 2}", lambda e, v_=v_, tk=tk: e.dma_start(out=S["VS"][tk * 128:(tk + 1) * 128, :], in_=v_),
                  reads=[v_r], acc_writes=[R["VS"]])
        if stop_after == f"qkv{2 * j}a":
            raise Stop()
        wq = [P.alloc([128, 8, 128], BF16, f"wq{k}") for k in range(3)]
        qb = [P.alloc([128, 512], BF16, f"qb{k}") for k in range(2)]
        t1 = [P.alloc([128, 512], F32, f"t1{k}") for k in range(2)]
        t2 = [P.alloc([128, 512], F32, f"t2{k}") for k in range(2)]
        qr = [P.alloc([128, 512], BF16, f"qr{k}") for k in range(3)]
        it = 0
        for ch in range(16):
            isk, h = ch // 8, ch % 8
            w, wr = wq[ch % 3]
            stream_w(w, wr, f"wq{ch % 3}", I["attn_w_qkv"][j][:, isk * D + h * 128: isk * D + (h + 1) * 128])
            for tile in range(NT):
                bA = (it % 3) * 2
                bB = bA + 1
                q_, q_r = qr[it % 3]
                mm_wstat(bA, w, wr, tile)
                if tile < 8 and not os.environ.get('NOROPE'):
                    qq, qq_r = qb[it % 2]
                    a1, a1r = t1[it % 2]
                    a2, a2r = t2[it % 2]
                    P.op("act", lambda e, qq=qq, bA=bA: e.copy(out=qq, in_=bank(bA)), reads=[psr[bA]], writes=[qq_r])
                    P.op("pe", lambda e, qq=qq, bB=bB: e.matmul(bank(bB), lhsT=perm, rhs=qq, start=True, stop=True),
                         reads=[perm_r, qq_r], writes=[psr[bB]])
                    P.op("dve", lambda e, a1=a1, bA=bA, tile=tile: e.tensor_tensor(
                        out=a1, in0=bank(bA), in1=rcos[:, tile * 512:(tile + 1) * 512], op=ALU.mult),
                        reads=[psr[bA], rcos_r], writes=[a1r])
                    P.op("dve", lambda e, a2=a2, bB=bB, tile=tile: e.tensor_tensor(
                        out=a2, in0=bank(bB), in1=rsin[:, tile * 512:(tile + 1) * 512], op=ALU.mult),
                        reads=[psr[bB], rsin_r], writes=[a2r])
                    P.op("pool", lambda e, q_=q_, a1=a1, a2=a2: e.tensor_tensor(out=q_, in0=a1, in1=a2, op=ALU.add),
                         reads=[a1r, a2r], writes=[q_r])
                else:
                    P.op("act", lambda e, q_=q_, bA=bA: e.copy(out=q_, in_=bank(bA)), reads=[psr[bA]], writes=[q_r])
                if isk:
                    P.dma("sp", f"qro{it % 3}", lambda e, q_=q_, h=h, tile=tile: e.dma_start(
                        out=S["KT"][h, :, 512 + tile * 512: 512 + (tile + 1) * 512], in_=q_), reads=[q_r], acc_writes=[R["KT"]])
                else:
                    P.dma("sp", f"qro{it % 3}", lambda e, q_=q_, h=h, tile=tile: e.dma_start(
                        out=S["QT"][h, :, tile * 512:(tile + 1) * 512], in_=q_), reads=[q_r], acc_writes=[R["QT"]])
                it += 1
        if stop_after == f"qkv{2 * j}b":
            raise Stop()
        wv = [P.alloc([128, 8, 512], BF16, f"wv{k}") for k in range(2)]
        vb = [P.alloc([128, 512], BF16, f"vb{k}") for k in range(3)]
        vf = [P.alloc([128, 512], F32, f"vf{k}") for k in range(2)]
        it = 0
        for kind in ("v", "k"):
            for half in range(2):
                w, wr = wv[(it // 64) % 2]
                col0 = (2 * D if kind == "v" else D) + half * 512
                w, wr = wv[half]
                stream_w(w, wr, f"wv{half}{kind}", I["attn_w_qkv"][j][:, col0:col0 + 512])
                tts = range(NTT) if kind == "v" else range(32, 36)
                for tt in tts:
                    b = 6 + it % 2
                    mm_tstat(b, w, wr, tt)
                    if kind == "v":
                        v_, v_r = vb[it % 3]
                        P.op("act", lambda e, v_=v_, b=b: e.copy(out=v_, in_=bank(b)), reads=[psr[b]], writes=[v_r])
                        P.dma("sp", f"vbo{it % 3}", lambda e, v_=v_, tt=tt, half=half: e.dma_start(
                            out=S["VS"][512 + tt * 128: 512 + (tt + 1) * 128, half * 512:(half + 1) * 512], in_=v_),
                            reads=[v_r], acc_writes=[R["VS"]])
                    if tt >= 32:
                        f_, f_r = vf[it % 2]
                        P.op("dve", lambda e, f_=f_, b=b: e.tensor_copy(out=f_, in_=bank(b)), reads=[psr[b]], writes=[f_r])
                        s, r0 = (tt - 32) // 2, ((tt - 32) % 2) * 128
                        dst = O["nv"] if kind == "v" else O["nk"]
                        P.dma("sp", f"vfo{it % 2}", lambda e, f_=f_, s=s, r0=r0, half=half, dst=dst: e.dma_start(
                            out=dst[s, j, r0:r0 + 128, half * 512:(half + 1) * 512], in_=f_), reads=[f_r], acc_writes=[R["OUT"]])
                    it += 1
        P.release(m)

    def phase_attn(j, i):
        P.new_phase()
        m = P.mark()
        lam_init = 0.8 - 0.6 * math.exp(-0.3 * i)
        lp, lp_r = P.alloc([128, 4, 64], F32, "lp")
        lpp, lpp_r = P.alloc([128, 2, 64], F32, "lpp")
        lsum, lsum_r = P.alloc([128, 2], F32, "lsum")
        lexp, lexp_r = P.alloc([128, 2], F32, "lexp")
        nlam, nlam_r = P.alloc([128, 1], F32, "nlam")
        gsub, gsub_r = P.alloc([128, 128], F32, "gsub")
        P.dma("sp", "lp", lambda e: e.dma_start(out=lp, in_=I["attn_lambda"][j].rearrange("(o a) b -> o a b", o=1).broadcast_to([128, 4, 64])), writes=[lp_r])
        P.dma("sp", "gsub", lambda e: e.dma_start(out=gsub, in_=bcast_rows(I["attn_subln_g"][j], 128)), writes=[gsub_r])
        P.op("dve", lambda e: e.tensor_scalar(out=gsub, in0=gsub, scalar1=1.0 - lam_init, scalar2=None, op0=ALU.mult),
             reads=[gsub_r], writes=[gsub_r])
        for k in range(2):
            P.op("dve", lambda e, k=k: e.tensor_tensor(out=lpp[:, k, :], in0=lp[:, 2 * k, :], in1=lp[:, 2 * k + 1, :], op=ALU.mult),
                 reads=[lp_r], acc_writes=[lpp_r])
        P.op("dve", lambda e: e.tensor_reduce(out=lsum, in_=lpp, axis=AX.X, op=ALU.add), reads=[lpp_r], writes=[lsum_r])
        P.op("act", lambda e: e.activation(out=lexp, in_=lsum, func=AF.Exp), reads=[lsum_r], writes=[lexp_r])
        P.op("dve", lambda e: e.tensor_tensor(out=nlam, in0=lexp[:, 1:2], in1=lexp[:, 0:1], op=ALU.subtract), reads=[lexp_r], writes=[nlam_r])
        P.op("dve", lambda e: e.tensor_scalar(out=nlam, in0=nlam, scalar1=-lam_init, scalar2=None, op0=ALU.add), reads=[nlam_r], writes=[nlam_r])

        kTb = [P.alloc([128, 4608], BF16, f"kTh{k}") for k in range(2)]
        vhb = [P.alloc([128, 36, 132], BF16, f"vh{k}") for k in range(2)]
        for k in range(2):
            P.op("pool", lambda e, k=k: e.memset(vhb[k][0], 1.0), writes=[vhb[k][1]])
        qtb = [P.alloc([128, 2, 256], BF16, f"qt{k}") for k in range(3)]
        for k in range(3):
            P.op("pool", lambda e, k=k: e.memset(qtb[k][0], 0.0), writes=[qtb[k][1]])
        Pb = [P.alloc([128, 2, 256], BF16, f"P{k}") for k in range(3)]
        osb = [P.alloc([128, 128], F32, f"o{k}") for k in range(2)]
        obb = [P.alloc([128, 128], BF16, f"ob{k}") for k in range(2)]
        rrb = [P.alloc([128, 2], F32, f"rr{k}") for k in range(2)]
        ssb = [P.alloc([128, 1], F32, f"ss{k}") for k in range(2)]
        rsb = [P.alloc([128, 1], F32, f"rs{k}") for k in range(2)]
        junk, junk_r = P.alloc([128, 128], BF16, "ajunk")
        hcount = 0
        qcount = 0
        scount = 0
        ecount = 0
        for G in GROUPS:
            for s in range(G["nseq"]):
                L = G["L"]
                nk = G["ncache"] + L
                nkc = nk // 128
                kcol0 = 0 if G["name"] == "s" else 4608 + s * 256
                qtok0 = G["tok0"] + s * L
                for h in range(8):
                    kT, kT_r = kTb[hcount % 2]
                    vh, vh_r = vhb[hcount % 2]
                    P.dma("sp", f"kTh{hcount % 2}", lambda e, kT=kT, h=h, kcol0=kcol0, nk=nk: e.dma_start(
                        out=kT[:, 0:nk], in_=S["KT"][h, :, kcol0:kcol0 + nk]), reads=[R["KT"]], writes=[kT_r])
                    P.dma("sp", f"vh{hcount % 2}", lambda e, vh=vh, h=h, kcol0=kcol0, nk=nk, nkc=nkc: e.dma_start(
                        out=vh[:, 0:nkc, 0:128], in_=S["VS"][kcol0:kcol0 + nk, h * 128:(h + 1) * 128].rearrange("(c p) e -> p c e", p=128)),
                        reads=[R["VS"]], acc_writes=[vh_r])
                    hcount += 1
                    for qb_ in range(L // 256):
                        qt, qt_r = qtb[qcount % 3]
                        qcount += 1
                        q0 = qtok0 + qb_ * 256
                        for mp in range(2):
                            P.dma("sp", f"qt{qcount % 3}_{mp}", lambda e, qt=qt, h=h, q0=q0, mp=mp: e.dma_start(
                                out=qt[mp * 64:(mp + 1) * 64, mp, :], in_=S["QT"][h, mp * 64:(mp + 1) * 64, q0:q0 + 256]),
                                reads=[R["QT"]], acc_writes=[qt_r])
                        for c in range(nkc):
                            sb_ = scount % 3
                            Pt, Pt_r = Pb[scount % 3]
                            scount += 1
                            for mp in range(2):
                                P.op("pe", lambda e, kT=kT, qt=qt, c=c, mp=mp, sb_=sb_: e.matmul(
                                    bank(sb_)[:, mp * 256:(mp + 1) * 256], lhsT=kT[:, c * 128:(c + 1) * 128],
                                    rhs=qt[:, mp, :], start=True, stop=True),
                                    reads=[kT_r, qt_r], writes=[psr[sb_]] if mp == 0 else (), acc_writes=[psr[sb_]] if mp else ())
                            P.op("act", lambda e, Pt=Pt, sb_=sb_: e.activation(
                                out=Pt, in_=bank(sb_).rearrange("p (m q) -> p m q", q=256), func=AF.Exp, scale=0.125),
                                reads=[psr[sb_]], writes=[Pt_r])
                            for qs in range(2):
                                for mp in range(2):
                                    ab = 4 + qs * 2 + mp
                                    P.op("pe", lambda e, Pt=Pt, vh=vh, c=c, qs=qs, mp=mp, ab=ab, nkc=nkc: e.matmul(
                                        bank(ab, 129), lhsT=Pt[:, mp, qs * 128:(qs + 1) * 128], rhs=vh[:, c, 0:129],
                                        start=(c == 0), stop=(c == nkc - 1)),
                                        reads=[Pt_r, vh_r], writes=[psr[ab]] if c == 0 else (), acc_writes=[psr[ab]] if c else ())
                        for qs in range(2):
                            a0, a1 = 4 + qs * 2, 5 + qs * 2
                            o_, o_r = osb[ecount % 2]
                            ob, ob_r = obb[ecount % 2]
                            rr, rr_r = rrb[ecount % 2]
                            s_, s_r = ssb[ecount % 2]
                            r_, r_r = rsb[ecount % 2]
                            ecount += 1
                            P.op("dve", lambda e, rr=rr, a0=a0: e.reciprocal(out=rr[:, 0:1], in_=bank(a0)[:, 128:129]),
                                 reads=[psr[a0]], writes=[rr_r])
                            P.op("dve", lambda e, rr=rr, a1=a1: e.reciprocal(out=rr[:, 1:2], in_=bank(a1)[:, 128:129]),
                                 reads=[psr[a1]], acc_writes=[rr_r])
                            P.op("dve", lambda e, rr=rr: e.tensor_tensor(out=rr[:, 1:2], in0=rr[:, 1:2], in1=nlam, op=ALU.mult),
                                 reads=[rr_r, nlam_r], writes=[rr_r])
                            P.op("dve", lambda e, o_=o_, rr=rr, a0=a0: e.tensor_scalar(
                                out=o_, in0=bank(a0)[:, 0:128], scalar1=rr[:, 0:1], scalar2=None, op0=ALU.mult),
                                reads=[psr[a0], rr_r], writes=[o_r])
                            P.op("dve", lambda e, o_=o_, rr=rr, a1=a1: e.scalar_tensor_tensor(
                                out=o_, in0=bank(a1)[:, 0:128], scalar=rr[:, 1:2], in1=o_, op0=ALU.mult, op1=ALU.add),
                                reads=[psr[a1], rr_r, o_r], writes=[o_r])
                            P.op("act", lambda e, o_=o_, s_=s_: e.activation(out=junk, in_=o_, func=AF.Square, accum_out=s_),
                                 reads=[o_r], writes=[junk_r, s_r])
                            rstd_ops(s_, s_r, r_, r_r, 128, SUBLN_EPS)
                            P.op("dve", lambda e, ob=ob, o_=o_, r_=r_: e.scalar_tensor_tensor(
                                out=ob, in0=o_, scalar=r_, in1=gsub, op0=ALU.mult, op1=ALU.mult),
                                reads=[o_r, r_r, gsub_r], writes=[ob_r])
                            P.op("pe", lambda e, ob=ob: e.transpose(bankb(3)[:, 0:128], ob, identb), reads=[ob_r, identb_r], writes=[psr[3]])
                            tok = q0 + qs * 128
                            P.op("act", lambda e, h=h, tok=tok: e.copy(out=hT[:, h, tok:tok + 128], in_=bankb(3)[:, 0:128]),
                                 reads=[psr[3]], acc_writes=[hTr[tok // 512]])
        P.release(m)

    def phase_outproj(wsrc, bias_vec, offG):
        P.new_phase()
        m = P.mark()
        G_ = load_mod(offG, "oG")
        if bias_vec is not None:
            brep, brep_r = P.alloc([128, D], F32, "obias")
            P.dma("sp", "obias", lambda e: e.dma_start(out=brep, in_=bcast_rows(bias_vec, D)), writes=[brep_r])
        wo = [P.alloc([128, 8, 512], BF16, f"wo{k}") for k in range(2)]
        xt = [P.alloc([128, 512], F32, f"ox{k}") for k in range(3)]
        tb = [P.alloc([128, 512], F32, f"ot{k}") for k in range(2)]
        it = 0
        for half in range(2):
            w, wr = wo[half]
            stream_w(w, wr, f"wo{half}", wsrc[:, half * 512:(half + 1) * 512])
            for tt in range(NTT):
                g = 0 if tt < 32 else 1
                b = it % 4
                x, xr = xt[it % 3]
                t_, t_r = tb[it % 2]
                P.dma("sp", f"ox{it % 3}", lambda e, x=x, tt=tt, half=half: e.dma_start(
                    out=x, in_=S["X"][tt * 128:(tt + 1) * 128, half * 512:(half + 1) * 512]), reads=[R["X"][tt][half]], writes=[xr])
                mm_tstat(b, w, wr, tt)
                src = bank(b)
                if bias_vec is not None:
                    P.op("dve", lambda e, t_=t_, b=b, half=half: e.tensor_tensor(
                        out=t_, in0=bank(b), in1=brep[:, half * 512:(half + 1) * 512], op=ALU.add),
                        reads=[psr[b], brep_r], writes=[t_r])
                    P.op("dve", lambda e, t_=t_, g=g, half=half: e.tensor_tensor(
                        out=t_, in0=t_, in1=G_[g][0][:, half * 512:(half + 1) * 512], op=ALU.mult),
                        reads=[t_r, G_[g][1]], writes=[t_r])
                else:
                    P.op("dve", lambda e, t_=t_, b=b, g=g, half=half: e.tensor_tensor(
                        out=t_, in0=bank(b), in1=G_[g][0][:, half * 512:(half + 1) * 512], op=ALU.mult),
                        reads=[psr[b], G_[g][1]], writes=[t_r])
                P.op("pool", lambda e, x=x, t_=t_: e.tensor_tensor(out=x, in0=x, in1=t_, op=ALU.add), reads=[xr, t_r], writes=[xr])
                P.dma("sp", f"oxo{it % 3}", lambda e, x=x, tt=tt, half=half: e.dma_start(
                    out=S["X"][tt * 128:(tt + 1) * 128, half * 512:(half + 1) * 512], in_=x), reads=[xr], writes=[R["X"][tt][half]])
                it += 1
        P.release(m)

    def phase_ffn(i):
        P.new_phase()
        m = P.mark()
        wg = [P.alloc([128, 2, 8, 128], BF16, f"wg{k}") for k in range(3)]
        sg = [P.alloc([128, 512], F32, f"sg{k}") for k in range(2)]
        md = [P.alloc([128, 512], BF16, f"md{k}") for k in range(3)]
        it = 0
        for c in range(NCH_FF):
            w, wr = wg[c % 3]
            for u in range(2):
                P.dma("pool", f"wg{c % 3}_{u}", lambda e, w=w, c=c, u=u: e.dma_start(
                    out=w[:, u], in_=I["ffn_w_gu"][i][:, u * DFF + c * 128: u * DFF + (c + 1) * 128].rearrange("(kc p) n -> p kc n", p=128)),
                    writes=[wr] if u == 0 else (), acc_writes=[wr] if u else ())
            for tile in range(NT):
                bG = (it % 4) * 2
                bU = bG + 1
                s_, s_r = sg[it % 2]
                m_, m_r = md[it % 3]
                mm_wstat(bG, w[:, 0], wr, tile)
                mm_wstat(bU, w[:, 1], wr, tile)
                P.op("act", lambda e, s_=s_, bG=bG: e.activation(out=s_, in_=bank(bG), func=AF.Silu), reads=[psr[bG]], writes=[s_r])
                P.op("dve", lambda e, m_=m_, s_=s_, bU=bU: e.tensor_tensor(out=m_, in0=bank(bU), in1=s_, op=ALU.mult),
                     reads=[psr[bU], s_r], writes=[m_r])
                P.dma("sp", f"mdo{it % 3}", lambda e, m_=m_, c=c, tile=tile: e.dma_start(
                    out=S["MID"][c, :, tile * 512:(tile + 1) * 512], in_=m_), reads=[m_r], acc_writes=[R["MID"][tile]])
                it += 1
        P.release(m)
        P.new_phase()
        m = P.mark()
        G_ = load_mod(5 * D, "fG")
        wd, wd_r = P.alloc([128, NCH_FF, D], BF16, "wd")
        P.dma("pool", "wd", lambda e: e.dma_start(out=wd, in_=I["ffn_w_down"][i].rearrange("(c p) n -> p c n", p=128)), writes=[wd_r])
        mt = [P.alloc([128, NCH_FF, 512], BF16, f"mt{k}") for k in range(2)]
        xt = [P.alloc([128, 512], F32, f"fx{k}") for k in range(3)]
        tb = [P.alloc([128, 512], F32, f"ft{k}") for k in range(2)]
        it = 0
        for tile in range(NT):
            g = 0 if tile < 8 else 1
            mt_, mt_r = mt[tile % 2]
            P.dma("sp", f"mt{tile % 2}", lambda e, mt_=mt_, tile=tile: e.dma_start(
                out=mt_, in_=S["MID"][:, :, tile * 512:(tile + 1) * 512].rearrange("c p t -> p c t")), reads=[R["MID"][tile]], writes=[mt_r])
            for ts in range(4):
                tt = tile * 4 + ts
                for half in range(2):
                    b = it % 4
                    x, xr = xt[it % 3]
                    t_, t_r = tb[it % 2]
                    P.dma("sp", f"fx{it % 3}", lambda e, x=x, tt=tt, half=half: e.dma_start(
                        out=x, in_=S["X"][tt * 128:(tt + 1) * 128, half * 512:(half + 1) * 512]), reads=[R["X"][tt][half]], writes=[xr])
                    for c in range(NCH_FF):
                        P.op("pe", lambda e, mt_=mt_, c=c, ts=ts, half=half, b=b: e.matmul(
                            bank(b), lhsT=mt_[:, c, ts * 128:(ts + 1) * 128], rhs=wd[:, c, half * 512:(half + 1) * 512],
                            start=(c == 0), stop=(c == NCH_FF - 1)),
                            reads=[mt_r, wd_r], writes=[psr[b]] if c == 0 else (), acc_writes=[psr[b]] if c else ())
                    P.op("dve", lambda e, t_=t_, b=b, g=g, half=half: e.tensor_tensor(
                        out=t_, in0=bank(b), in1=G_[g][0][:, half * 512:(half + 1) * 512], op=ALU.mult),
                        reads=[psr[b], G_[g][1]], writes=[t_r])
                    P.op("pool", lambda e, x=x, t_=t_: e.tensor_tensor(out=x, in0=x, in1=t_, op=ALU.add), reads=[xr, t_r], writes=[xr])
                    P.dma("sp", f"fxo{it % 3}", lambda e, x=x, tt=tt, half=half: e.dma_start(
                        out=S["X"][tt * 128:(tt + 1) * 128, half * 512:(half + 1) * 512], in_=x), reads=[xr], writes=[R["X"][tt][half]])
                    it += 1
        P.release(m)

    def phase_hy_in(j):
        P.new_phase()
        m = P.mark()
        CV, CV_r = P.alloc([128, 120], F32, "CV")
        load_cols(CV, CV_r, 0, I["hy_b_in"][j], 24, "cvl")
        for k in range(3):
            load_cols(CV, CV_r, 24 + k * 24, I["hy_conv_w"][j, k], 24, "cvl")
        load_cols(CV, CV_r, 96, I["hy_conv_b"][j], 24, "cvl")
        ubS = [P.alloc([128, TS + 2], F32, f"ubS{k}") for k in range(2)]
        ubP = [P.alloc([128, 2, 258], F32, f"ubP{k}") for k in range(2)]
        for k in range(2):
            P.op("pool", lambda e, k=k: e.memset(ubS[k][0], 0.0), writes=[ubS[k][1]])
            P.op("pool", lambda e, k=k: e.memset(ubP[k][0], 0.0), writes=[ubP[k][1]])
        cb1, cb1_r = P.alloc([128, T], F32, "cb1")
        cb2 = [P.alloc([128, T], F32, f"cb2{k}") for k in range(2)]
        vst, vst_r = P.alloc([128, NTT, 128], BF16, "vst")
        wi = [P.alloc([128, 8, 128], BF16, f"wi{k}") for k in range(3)]
        it = 0
        uc = 0
        c2 = 0
        for jd in range(8):
            for part, fc in (("x1", 8 + jd), ("v", 16 + jd), ("x0", jd)):
                w, wr = wi[it % 3]
                it += 1
                stream_w(w, wr, f"wi{it % 3}", I["hy_w_in"][j][:, fc * 128:(fc + 1) * 128])
                uS, uS_r = ubS[uc % 2]
                uP, uP_r = ubP[uc % 2]
                uc += 1
                for tile in range(NT):
                    b = tile % 4
                    mm_wstat(b, w, wr, tile)
                    if tile < 8:
                        P.op("act", lambda e, uS=uS, b=b, tile=tile, fc=fc: e.activation(
                            out=uS[:, 1 + tile * 512: 1 + (tile + 1) * 512], in_=bank(b), func=AF.Identity, bias=CV[:, fc:fc + 1]),
                            reads=[psr[b], CV_r], acc_writes=[uS_r])
                    else:
                        P.op("act", lambda e, uP=uP, b=b, fc=fc: e.activation(
                            out=uP[:, :, 1:257], in_=bank(b).rearrange("p (s t) -> p s t", t=256), func=AF.Identity, bias=CV[:, fc:fc + 1]),
                            reads=[psr[b], CV_r], acc_writes=[uP_r])
                if part == "x1":
                    dst, dst_r = cb1, cb1_r
                else:
                    dst, dst_r = cb2[c2 % 2]
                    c2 += 1
                w0, w1, w2, cbc = (CV[:, 24 + fc:25 + fc], CV[:, 48 + fc:49 + fc], CV[:, 72 + fc:73 + fc], CV[:, 96 + fc:97 + fc])
                dS = dst[:, 0:TS]
                dP = dst[:, TS:T].rearrange("p (s t) -> p s t", t=256)
                for (dd, uu, ur, n) in ((dS, uS, uS_r, TS), (dP, uP, uP_r, 256)):
                    def sl(o, uu=uu, n=n):
                        return uu[:, o:o + n] if len(uu.shape) == 2 else uu[:, :, o:o + n]
                    P.op("dve", lambda e, dd=dd, sl=sl, w0=w0, cbc=cbc: e.tensor_scalar(out=dd, in0=sl(0), scalar1=w0, scalar2=cbc, op0=ALU.mult, op1=ALU.add),
                         reads=[ur, CV_r], acc_writes=[dst_r])
                    P.op("dve", lambda e, dd=dd, sl=sl, w1=w1: e.scalar_tensor_tensor(out=dd, in0=sl(1), scalar=w1, in1=dd, op0=ALU.mult, op1=ALU.add),
                         reads=[ur, CV_r, dst_r], acc_writes=[dst_r])
                    P.op("dve", lambda e, dd=dd, sl=sl, w2=w2: e.scalar_tensor_tensor(out=dd, in0=sl(2), scalar=w2, in1=dd, op0=ALU.mult, op1=ALU.add),
                         reads=[ur, CV_r, dst_r], acc_writes=[dst_r])
                if part == "v":
                    P.op("pool", lambda e, dst=dst: e.tensor_tensor(out=dst, in0=dst, in1=cb1, op=ALU.mult), reads=[dst_r, cb1_r], writes=[dst_r])
                    P.dma("sp", f"vvt{c2 % 2}", lambda e, dst=dst, jd=jd: e.dma_start(out=S["VVT"][jd], in_=dst), reads=[dst_r], acc_writes=[R["VVT"]])
                    for t4 in range(NTT // 4):
                        b = 4 + t4 % 4
                        for q in range(4):
                            tt = t4 * 4 + q
                            P.op("pe", lambda e, dst=dst, tt=tt, q=q, b=b: e.transpose(
                                bank(b)[:, q * 128:(q + 1) * 128], dst[:, tt * 128:(tt + 1) * 128], identf),
                                reads=[dst_r, identf_r], writes=[psr[b]] if q == 0 else (), acc_writes=[psr[b]] if q else ())
                        P.op("act", lambda e, t4=t4, b=b: e.copy(out=vst[:, t4 * 4:(t4 + 1) * 4, :], in_=bank(b).rearrange("p (q d) -> p q d", d=128)),
                             reads=[psr[b]], acc_writes=[vst_r])
                    P.dma("sp", "vsto", lambda e, jd=jd: e.dma_start(
                        out=S["VVTOK"][:, jd * 128:(jd + 1) * 128].rearrange("(c p) d -> p c d", p=128), in_=vst),
                        reads=[vst_r], acc_writes=[R["VVTOK"]])
                    vst_r.w = dict(vst_r.w)
                if part == "x0":
                    P.dma("sp", f"x0t{c2 % 2}", lambda e, dst=dst, jd=jd: e.dma_start(out=S["X0T"][jd], in_=dst), reads=[dst_r], acc_writes=[R["X0T"]])
        P.release(m)

    def phase_hy_filter(j, G):
        L = G["L"]
        nm = G["name"]
        ntc = L // 128
        HS, HD = S["HS" + nm], S["HD" + nm]
        P.new_phase()
        rn, rn_r = P.alloc([128, D], F32, "rn")
        m = P.mark()
        zT, zT_r = P.alloc([33, L], F32, "zT")
        w1, w1_r = P.alloc([33, 64], F32, "fw1")
        w2, w2_r = P.alloc([64, 64], F32, "fw2")
        w3, w3_r = P.alloc([64, 2 * D], F32, "fw3")
        b3, b3_r = P.alloc([128, 2 * D], F32, "fb3")
        fcol, fcol_r = P.alloc([64, 4], F32, "fcol")
        negt, negt_r = P.alloc([128, ntc], F32, "negt")
        drep, drep_r = P.alloc([128, D], F32, "drep")
        mask0, mask0_r = P.alloc([128, 1], F32, "mask0")
        h1, h1_r = P.alloc([64, L], F32, "h1")
        h2, h2_r = P.alloc([64, L], F32, "h2")
        tmp = [P.alloc([64, 512], F32, f"ftmp{k}") for k in range(2)]
        P.dma("sp", "zT", lambda e: e.dma_start(out=zT, in_=I["z" + nm]), writes=[zT_r])
        P.dma("sp", "fw1", lambda e: e.dma_start(out=w1, in_=I["filt_w1"][j]), writes=[w1_r])
        P.dma("sp", "fw2", lambda e: e.dma_start(out=w2, in_=I["filt_w2"][j]), writes=[w2_r])
        P.dma("sp", "fw3", lambda e: e.dma_start(out=w3, in_=I["filt_w3"][j]), writes=[w3_r])
        P.dma("sp", "fb3", lambda e: e.dma_start(out=b3, in_=bcast_rows(I["filt_b3"][j], 2 * D)), writes=[b3_r])
        P.dma("sp", "negt", lambda e: e.dma_start(out=negt, in_=I["negt" + nm]), writes=[negt_r])
        P.dma("sp", "drep", lambda e: e.dma_start(out=drep, in_=bcast_rows(I["delta"], D)), writes=[drep_r])
        P.dma("sp", "mask0", lambda e: e.dma_start(out=mask0, in_=I["mask0"]), writes=[mask0_r])
        for k, v in enumerate((I["filt_b1"][j], I["filt_b2"][j], I["filt_freq"][j])):
            P.dma("sp", "fcol", lambda e, k=k, v=v: e.dma_start(out=fcol[:, k:k + 1], in_=v.rearrange("(p o) -> p o", o=1)), acc_writes=[fcol_r])
        TWO_PI = 2.0 * math.pi
        SC = TWO_PI * (1.0 - 2e-6)
        wr_, wr_r = P.alloc([64, 512], F32, "fwrap")

        def sin_layer(lhsT, lhsT_r, src, src_r, dstb, dstb_r, bcol):
            n = min(512, L)
            for ti in range(L // n):
                b = ti % 2
                t_, t_r = tmp[ti % 2]
                P.op("pe", lambda e, ti=ti, b=b: e.matmul(bank(b)[0:64, 0:n], lhsT=lhsT, rhs=src[:, ti * n:(ti + 1) * n], start=True, stop=True),
                     reads=[lhsT_r, src_r], writes=[psr[b]])
                P.op("dve", lambda e, t_=t_, b=b: e.tensor_scalar(
                    out=t_[:, 0:n], in0=bank(b)[0:64, 0:n], scalar1=fcol[:, bcol:bcol + 1], scalar2=fcol[:, 2:3], op0=ALU.add, op1=ALU.mult),
                    reads=[psr[b], fcol_r], writes=[t_r])
                for rnd in range(2):
                    for (cmp_, thr, sgn) in ((ALU.is_lt, -math.pi, ALU.add), (ALU.is_gt, math.pi, ALU.subtract)):
                        P.op("dve", lambda e, t_=t_, cmp_=cmp_, thr=thr: e.tensor_scalar(
                            out=wr_[:, 0:n], in0=t_[:, 0:n], scalar1=thr, scalar2=TWO_PI, op0=cmp_, op1=ALU.mult),
                            reads=[t_r], writes=[wr_r])
                        P.op("dve", lambda e, t_=t_, sgn=sgn: e.tensor_tensor(out=t_[:, 0:n], in0=t_[:, 0:n], in1=wr_[:, 0:n], op=sgn),
                             reads=[t_r, wr_r], writes=[t_r])
                P.op("act", lambda e, t_=t_, ti=ti: e.activation(out=dstb[:, ti * n:(ti + 1) * n], in_=t_[:, 0:n], func=AF.Sin, scale=1.0 - 2e-6),
                     reads=[t_r], acc_writes=[dstb_r])
        sin_layer(w1, w1_r, zT, zT_r, h1, h1_r, 0)
        sin_layer(w2, w2_r, h1, h1_r, h2, h2_r, 1)
        dec = [P.alloc([128, D], F32, "dec0")] * 2
        hf = [P.alloc([128, D], F32, f"hf{k}") for k in range(2)]
        hb_ = [P.alloc([128, D], F32, f"hbk{k}") for k in range(2)]
        ab = [P.alloc([128, 2 * D], F32, "ab0")] * 2
        hs = [P.alloc([128, D], BF16, f"hs{k}") for k in range(2)]
        hd = [P.alloc([128, D], BF16, f"hd{k}") for k in range(2)]
        for tc in range(ntc):
            k = tc % 2
            for cb in range(4):
                P.op("pe", lambda e, tc=tc, cb=cb: e.matmul(bank(cb), lhsT=h2[:, tc * 128:(tc + 1) * 128], rhs=w3[:, cb * 512:(cb + 1) * 512], start=True, stop=True),
                     reads=[h2_r, w3_r], writes=[psr[cb]])
            P.op("act", lambda e, k=k, tc=tc: e.activation(out=dec[k][0], in_=drep, func=AF.Exp, scale=negt[:, tc:tc + 1]),
                 reads=[drep_r, negt_r], writes=[dec[k][1]])
            for (dst, lo) in ((hf[k], 0), (hb_[k], D)):
                P.op("dve", lambda e, dst=dst, lo=lo: e.tensor_tensor(out=dst[0], in0=PS[:, lo:lo + D], in1=b3[:, lo:lo + D], op=ALU.add),
                     reads=[psr[lo // 512], psr[lo // 512 + 1], b3_r], writes=[dst[1]])
                P.op("dve", lambda e, dst=dst, k=k: e.tensor_tensor(out=dst[0], in0=dst[0], in1=dec[k][0], op=ALU.mult),
                     reads=[dst[1], dec[k][1]], writes=[dst[1]])
                P.op("act", lambda e, dst=dst, lo=lo, k=k: e.activation(out=ab[k][0][:, lo:lo + D], in_=dst[0], func=AF.Abs),
                     reads=[dst[1]], acc_writes=[ab[k][1]])
            for q in range(4):
                bq = 4 + q % 2
                first = (tc == 0 and q < 2)
                last = (tc == ntc - 1 and q >= 2)
                P.op("pe", lambda e, k=k, q=q, bq=bq, first=first, last=last: e.matmul(
                    bank(bq), lhsT=onesf, rhs=ab[k][0][:, q * 512:(q + 1) * 512], start=first, stop=last),
                    reads=[onesf_r, ab[k][1]], writes=[psr[bq]] if first else (), acc_writes=() if first else [psr[bq]])
            if tc == 0:
                P.op("dve", lambda e, k=k: e.tensor_scalar(out=hb_[k][0], in0=hb_[k][0], scalar1=mask0[:, 0:1], scalar2=None, op0=ALU.mult),
                     reads=[hb_[k][1], mask0_r], writes=[hb_[k][1]])
            P.op("pool", lambda e, k=k: e.tensor_tensor(out=hs[k][0], in0=hf[k][0], in1=hb_[k][0], op=ALU.add),
                 reads=[hf[k][1], hb_[k][1]], writes=[hs[k][1]])
            P.op("pool", lambda e, k=k: e.tensor_tensor(out=hd[k][0], in0=hb_[k][0], in1=hf[k][0], op=ALU.subtract),
                 reads=[hf[k][1], hb_[k][1]], writes=[hd[k][1]])
            P.dma("sp", f"hso{k}", lambda e, k=k, tc=tc: e.dma_start(out=HS[tc * 128:(tc + 1) * 128, :], in_=hs[k][0]), reads=[hs[k][1]], acc_writes=[R["HS"]])
            P.dma("sp", f"hdo{k}", lambda e, k=k, tc=tc: e.dma_start(out=HD[tc * 128:(tc + 1) * 128, :], in_=hd[k][0]), reads=[hd[k][1]], acc_writes=[R["HS"]])
        P.op("dve", lambda e: e.tensor_scalar(out=rn, in0=PS[:, 4 * 512:6 * 512], scalar1=1e-6, scalar2=None, op0=ALU.add),
             reads=[psr[4], psr[5]], writes=[rn_r])
        P.op("dve", lambda e: e.reciprocal(out=rn, in_=rn), reads=[rn_r], writes=[rn_r])
        P.release(m)
        return rn, rn_r

    def phase_hy_conv(j, G, s, rn, rn_r):
        L = G["L"]
        nm = G["name"]
        ntc = L // 128
        nfc = 33 if nm == "s" else 3
        SQ = nfc * 128
        Qt, Rt, WFd = I["q" + nm], I["r" + nm], I["wf" + nm]
        HS, HD = S["HS" + nm], S["HD" + nm]
        tok0 = G["tok0"] + s * L
        P.new_phase()
        m = P.mark()
        wf, wf_r = P.alloc([128, nfc], F32, "wf")
        P.dma("sp", "wf", lambda e: e.dma_start(out=wf, in_=WFd), writes=[wf_r])
        vvh, vvh_r = P.alloc([128, ntc, 256], BF16, "vvh")
        hsh, hsh_r = P.alloc([128, ntc, 256], BF16, "hsh")
        hdh, hdh_r = P.alloc([128, ntc, 256], BF16, "hdh")
        qch = [P.alloc([128, ntc, 128], BF16, f"qch{k}") for k in range(2)]
        rch = [P.alloc([128, ntc, 128], BF16, f"rch{k}") for k in range(2)]
        kcs = [P.alloc([128, 256], F32, f"kcs{k}") for k in range(2)]
        kss = [P.alloc([128, 256], F32, f"kss{k}") for k in range(2)]
        ta = [P.alloc([128, 256], F32, f"ta{k}") for k in range(2)]
        tb = [P.alloc([128, 256], F32, f"tb{k}") for k in range(2)]
        yst = [P.alloc([128, 2, 256], BF16, f"yst{k}") for k in range(2)]
        it = 0
        for dq in range(4):
            dh = dq // 2
            P.dma("sp", "vvh", lambda e, dq=dq: e.dma_start(
                out=vvh, in_=S["VVTOK"][tok0:tok0 + L, dq * 256:(dq + 1) * 256].rearrange("(c p) d -> p c d", p=128)), reads=[R["VVTOK"]], writes=[vvh_r])
            P.dma("sp", "hsh", lambda e, dq=dq: e.dma_start(
                out=hsh, in_=HS[:, dq * 256:(dq + 1) * 256].rearrange("(c p) d -> p c d", p=128)), reads=[R["HS"]], writes=[hsh_r])
            P.dma("sp", "hdh", lambda e, dq=dq: e.dma_start(
                out=hdh, in_=HD[:, dq * 256:(dq + 1) * 256].rearrange("(c p) d -> p c d", p=128)), reads=[R["HS"]], writes=[hdh_r])
            for fc in range(nfc):
                qc, qc_r = qch[it % 2]
                rc, rc_r = rch[it % 2]
                P.dma("sp", f"qch{it % 2}", lambda e, qc=qc, fc=fc: e.dma_start(
                    out=qc, in_=Qt[0:ntc, :, fc * 128:(fc + 1) * 128].rearrange("c p f -> p c f")), writes=[qc_r])
                P.dma("sp", f"rch{it % 2}", lambda e, rc=rc, fc=fc: e.dma_start(
                    out=rc, in_=Rt[0:ntc, :, fc * 128:(fc + 1) * 128].rearrange("c p f -> p c f")), writes=[rc_r])
                b0 = (it % 2) * 4
                bVc, bKc, bVs, bKs = b0, b0 + 1, b0 + 2, b0 + 3
                for tc in range(ntc):
                    st, sp_ = (tc == 0), (tc == ntc - 1)
                    for (bb, tab, tab_r, mov, mov_r) in ((bVc, qc, qc_r, vvh, vvh_r), (bKc, qc, qc_r, hsh, hsh_r),
                                                         (bVs, rc, rc_r, vvh, vvh_r), (bKs, rc, rc_r, hdh, hdh_r)):
                        P.op("pe", lambda e, bb=bb, tab=tab, mov=mov, tc=tc, st=st, sp_=sp_: e.matmul(
                            bank(bb, 256), lhsT=tab[:, tc, :], rhs=mov[:, tc, :], start=st, stop=sp_),
                            reads=[tab_r, mov_r], writes=[psr[bb]] if st else (), acc_writes=() if st else [psr[bb]])
                k = it % 2
                wcol = wf[:, fc:fc + 1]
                rsl = rn[:, dq * 256:(dq + 1) * 256]
                P.op("dve", lambda e, k=k, bKc=bKc, wcol=wcol, rsl=rsl: e.scalar_tensor_tensor(
                    out=kcs[k][0], in0=bank(bKc, 256), scalar=wcol, in1=rsl, op0=ALU.mult, op1=ALU.mult),
                    reads=[psr[bKc], wf_r, rn_r], writes=[kcs[k][1]])
                P.op("dve", lambda e, k=k, bKs=bKs, wcol=wcol, rsl=rsl: e.scalar_tensor_tensor(
                    out=kss[k][0], in0=bank(bKs, 256), scalar=wcol, in1=rsl, op0=ALU.mult, op1=ALU.mult),
                    reads=[psr[bKs], wf_r, rn_r], writes=[kss[k][1]])
                P.op("dve", lambda e, k=k, bVc=bVc: e.tensor_tensor(out=ta[k][0], in0=bank(bVc, 256), in1=kcs[k][0], op=ALU.mult),
                     reads=[psr[bVc], kcs[k][1]], writes=[ta[k][1]])
                P.op("dve", lambda e, k=k, bVs=bVs: e.tensor_tensor(out=tb[k][0], in0=bank(bVs, 256), in1=kss[k][0], op=ALU.mult),
                     reads=[psr[bVs], kss[k][1]], writes=[tb[k][1]])
                P.op("pool", lambda e, k=k: e.tensor_tensor(out=yst[k][0][:, 0, :], in0=ta[k][0], in1=tb[k][0], op=ALU.add),
                     reads=[ta[k][1], tb[k][1]], writes=[yst[k][1]])
                P.op("dve", lambda e, k=k, bVs=bVs: e.tensor_tensor(out=ta[k][0], in0=bank(bVs, 256), in1=kcs[k][0], op=ALU.mult),
                     reads=[psr[bVs], kcs[k][1]], writes=[ta[k][1]])
                P.op("dve", lambda e, k=k, bVc=bVc: e.tensor_tensor(out=tb[k][0], in0=bank(bVc, 256), in1=kss[k][0], op=ALU.mult),
                     reads=[psr[bVc], kss[k][1]], writes=[tb[k][1]])
                P.op("pool", lambda e, k=k: e.tensor_tensor(out=yst[k][0][:, 1, :], in0=ta[k][0], in1=tb[k][0], op=ALU.subtract),
                     reads=[ta[k][1], tb[k][1]], acc_writes=[yst[k][1]])
                P.dma("sp", f"yfo{k}", lambda e, k=k, dh=dh, dq=dq, fc=fc: e.dma_start(out=S["YF"][dh, fc][:, :, (dq % 2) * 256:(dq % 2 + 1) * 256], in_=yst[k][0]),
                      reads=[yst[k][1]], acc_writes=[R["YF"]])
                it += 1
        P.release(m)
        P.new_phase()
        m = P.mark()
        skc, skc_r = P.alloc([128, 8], F32, "skc")
        load_cols(skc, skc_r, 0, I["hy_skip"][j], 8, "skc")
        Yg, Yg_r = P.alloc([128, nfc, 2, 512], BF16, "Yg")
        n = min(512, L)
        tq = [P.alloc([128, 2, n], BF16, f"tq{k}") for k in range(6)]
        vvt = [P.alloc([128, n], F32, f"vvt{k}") for k in range(3)]
        x0t = [P.alloc([128, n], F32, f"x0t{k}") for k in range(3)]
        it = 0
        ie = 0
        for dh in range(2):
            P.dma("sp", "Yg", lambda e, dh=dh: e.dma_start(out=Yg, in_=S["YF"][dh, 0:nfc].rearrange("c p s d -> p c s d")),
                  reads=[R["YF"]], writes=[Yg_r])
            for tt in range(L // n):
                b0 = ((dh * (L // n) + tt) % 2) * 4
                for fc in range(nfc):
                    t_, t_r = tq[it % 6]
                    P.dma("sp", f"tq{it % 6}a", lambda e, t_=t_, fc=fc, tt=tt: e.dma_start(out=t_[:, 0, :], in_=Qt[fc, :, tt * n:(tt + 1) * n]), writes=[t_r])
                    P.dma("sp", f"tq{it % 6}b", lambda e, t_=t_, fc=fc, tt=tt: e.dma_start(out=t_[:, 1, :], in_=Rt[fc, :, tt * n:(tt + 1) * n]), acc_writes=[t_r])
                    it += 1
                    for dcl in range(4):
                        for cs in range(2):
                            st = (fc == 0 and cs == 0)
                            sp_ = (fc == nfc - 1 and cs == 1)
                            P.op("pe", lambda e, t_=t_, fc=fc, dcl=dcl, cs=cs, st=st, sp_=sp_, b0=b0: e.matmul(
                                bank(b0 + dcl, n), lhsT=Yg[:, fc, cs, dcl * 128:(dcl + 1) * 128], rhs=t_[:, cs, :], start=st, stop=sp_),
                                reads=[Yg_r, t_r], writes=[psr[b0 + dcl]] if st else (), acc_writes=() if st else [psr[b0 + dcl]])
                for dcl in range(4):
                    dc = dh * 4 + dcl
                    v_, v_r = vvt[ie % 3]
                    x_, x_r = x0t[ie % 3]
                    ie += 1
                    c0 = tok0 + tt * n
                    P.dma("sp", f"vvt{ie % 3}", lambda e, v_=v_, dc=dc, c0=c0: e.dma_start(out=v_, in_=S["VVT"][dc, :, c0:c0 + n]), reads=[R["VVT"]], writes=[v_r])
                    P.dma("sp", f"x0t{ie % 3}", lambda e, x_=x_, dc=dc, c0=c0: e.dma_start(out=x_, in_=S["X0T"][dc, :, c0:c0 + n]), reads=[R["X0T"]], writes=[x_r])
                    P.op("dve", lambda e, v_=v_, dc=dc, dcl=dcl, b0=b0: e.scalar_tensor_tensor(
                        out=v_, in0=v_, scalar=skc[:, dc:dc + 1], in1=bank(b0 + dcl, n), op0=ALU.mult, op1=ALU.add),
                        reads=[v_r, skc_r, psr[b0 + dcl]], writes=[v_r])
                    P.op("pool", lambda e, v_=v_, x_=x_, dc=dc, c0=c0: e.tensor_tensor(out=hT[:, dc, c0:c0 + n], in0=v_, in1=x_, op=ALU.mult),
                         reads=[v_r, x_r], acc_writes=[hTr[c0 // 512]])
        P.release(m)

    class Stop(Exception):
        pass

    def chk(tag):
        if stop_after == tag:
            raise Stop()

    def dump_hT():
        for t in range(NT):
            P.dma("sp", "htd", lambda e, t=t: e.dma_start(out=S["HTD"][:, :, t * 512:(t + 1) * 512], in_=hT[:, :, t * 512:(t + 1) * 512]),
                  reads=[hTr[t]], acc_writes=[R["HTD"]])

    try:
        for i in range(4):
            j = i // 2
            phase_mod(i)
            chk(f"mod{i}")
            phase_norm(D, 0)
            chk(f"norm1_{i}")
            if i % 2 == 0:
                phase_qkv(j)
                chk(f"qkv{i}")
                phase_attn(j, i)
                chk(f"attn{i}")
                phase_outproj(I["attn_w_o"][j], None, 2 * D)
            else:
                phase_hy_in(j)
                chk(f"hyin{i}")
                for G in GROUPS:
                    mk = P.mark()
                    rn, rn_r = phase_hy_filter(j, G)
                    chk(f"hyfilt{i}{G['name']}")
                    for s in range(G["nseq"]):
                        phase_hy_conv(j, G, s, rn, rn_r)
                    P.release(mk)
                chk(f"hyconv{i}")
                phase_outproj(I["hy_w_out"][j], I["hy_b_out"][j], 2 * D)
            chk(f"mix{i}")
            phase_norm(4 * D, 3 * D)
            phase_ffn(i)
            chk(f"ffn{i}")
        phase_norm(0, 0, final=True)
    except Stop:
        if "HTD" in debug_outs:
            dump_hT()

    final_keys = [k for k in P.dma_cnt]
    print("n dma sems", len(final_keys), {e: len(P.q[e]) for e in ENGS})
    P.emit(final_keys)
    return nc, P


_CONSTS = None


def _core_inputs(b, inp, consts):
    m = {}
    m["x"] = np.ascontiguousarray(np.concatenate(
        [inp["x_sample"][b], inp["x_prompt"][2 * b], inp["x_prompt"][2 * b + 1]], axis=0).astype(np.float32))
    m["ck"] = np.ascontiguousarray(inp["cache_k"][b].reshape(2, 512, D).astype(np.float32))
    m["cv"] = np.ascontiguousarray(inp["cache_v"][b].reshape(2, 512, D).astype(np.float32))
    m["cvec"] = np.ascontiguousarray(np.stack([inp["c"][b], inp["c_ctx"]], axis=0).astype(np.float32))
    for nm, shp in W_SPECS:
        m[nm] = np.ascontiguousarray(np.asarray(inp[nm], dtype=np.float32).reshape(shp))
    for nm, shp, dt in CONST_SPECS:
        m[nm] = consts[nm]
    return m


def kernel(**inputs):
    global _CONSTS
    if _CONSTS is None:
        _CONSTS = _host_consts()
    inp = {k: np.asarray(v) for k, v in inputs.items()}
    nc, _ = build_program()
    in_maps = [_core_inputs(b, inp, _CONSTS) for b in range(8)]
    res = run_bass_kernel_spmd(nc, in_maps, core_ids=list(range(8)))
    y_prompt = np.zeros((16, 256, D), np.float32)
    y_sample = np.zeros((8, TS, D), np.float32)
    nk = np.zeros((16, 2, 256, 8, 2, 64), np.float32)
    nv = np.zeros((16, 2, 256, 8, 128), np.float32)
    for b in range(8):
        r = res.results[b]
        y = np.asarray(r["y"], dtype=np.float32)
        y_sample[b] = y[:TS]
        y_prompt[2 * b] = y[TS:TS + 256]
        y_prompt[2 * b + 1] = y[TS + 256:]
        k_ = np.asarray(r["nk"], dtype=np.float32).reshape(2, 2, 256, 8, 2, 64)
        v_ = np.asarray(r["nv"], dtype=np.float32).reshape(2, 2, 256, 8, 128)
        nk[2 * b], nk[2 * b + 1] = k_[0], k_[1]
        nv[2 * b], nv[2 * b + 1] = v_[0], v_[1]
    return (y_prompt, y_sample, nk, nv)
```

```python
import contextlib
import os
import math
import numpy as np
import ml_dtypes
import concourse.bass as bass
import concourse.mybir as mybir
from concourse.bass_utils import run_bass_kernel_spmd

F32 = mybir.dt.float32
BF16 = mybir.dt.bfloat16
AF = mybir.ActivationFunctionType
ALU = mybir.AluOpType
AX = mybir.AxisListType

ENGS = ("pe", "act", "dve", "pool", "sp")
D = 1024
TS, TP, T = 4096, 512, 4608
NT = 9
NTT = 36
DFF = 2816
NCH_FF = 22
EPS = 1e-6
SUBLN_EPS = 1e-5
NKEY = 5120


class Res:
    __slots__ = ("w", "r", "name", "excl")

    def __init__(self, name="", excl=False):
        self.w = {}
        self.r = {}
        self.name = name
        self.excl = excl


class Prog:
    def __init__(self, nc, arena_words):
        self.nc = nc
        self.q = {e: [] for e in ENGS}
        self.known = {e: {} for e in ENGS}
        self.dma_cnt = {}
        self.stack = contextlib.ExitStack()
        self.arena = self.stack.enter_context(nc.sbuf_tensor("arena", [128, arena_words], F32))
        self.arena_words = arena_words
        self.top = 0
        self.live = []
        self.retired = []
        self.nsem = 0
        self.keymap = {}
        self.keyres = {}

    def alloc(self, shape, dt, name=""):
        esz = 4 if dt == F32 else 2
        free = 1
        for s in shape[1:]:
            free *= s
        words = (free * esz + 3) // 4
        words = (words + 7) // 8 * 8
        off = self.top
        assert off + words <= self.arena_words, f"SBUF arena overflow {name} {off + words}"
        self.top += words
        v = self.arena[0:shape[0], off:off + (free * esz) // 4]
        if dt != F32:
            v = v.bitcast(dt)
        if len(shape) == 3:
            v = v.rearrange("p (a b) -> p a b", b=shape[2])
        elif len(shape) == 4:
            v = v.rearrange("p (a b c) -> p a b c", b=shape[2], c=shape[3])
        r = Res(name)
        keep = []
        for (a, b, rr) in self.retired:
            if a < off + words and off < b:
                for k, val in rr.w.items():
                    if r.r.get(k, -1) < val:
                        r.r[k] = val
                for k, val in rr.r.items():
                    if r.r.get(k, -1) < val:
                        r.r[k] = val
                if a >= off and b <= off + words:
                    continue
            keep.append((a, b, rr))
        self.retired = keep
        self.live.append((off, off + words, r))
        return v, r

    def mark(self):
        return (self.top, len(self.live))

    def release(self, m):
        top, n = m
        self.retired.extend(self.live[n:])
        del self.live[n:]
        self.top = top

    def _add(self, eng, fn, reads, writes, acc_writes, own):
        deps = {}

        def upd(d):
            for k, v in d.items():
                if deps.get(k, -1) < v:
                    deps[k] = v
        for r in reads:
            upd(r.w)
            if r.excl:
                upd({k: v for k, v in r.r.items() if k != ("c", eng)})
        for w in writes:
            upd(w.w)
            upd(w.r)
        for w in acc_writes:
            upd(w.r)
            upd({k: v for k, v in w.w.items() if k != own})
        q = self.q[eng]
        idx = len(q)
        waits = []
        kn = self.known[eng]
        for k, v in deps.items():
            if k == ("c", eng):
                if eng == "pe":
                    continue
                vv = -1
                for r in reads:
                    x = r.w.get(k, -1)
                    if x > vv:
                        vv = x
                if vv < 0:
                    continue
                v = vv
            if k[0] == "d":
                v = self.dma_cnt[k[1]]
            if kn.get(k, -1) >= v:
                continue
            kn[k] = v
            waits.append((k, v))
            if k[0] == "c":
                self.q[k[1]][v][2] = True
        op = [fn, waits, False, None]
        q.append(op)
        return op, idx

    def op(self, eng, fn, reads=(), writes=(), acc_writes=()):
        op, idx = self._add(eng, fn, reads, writes, acc_writes, ("c", eng))
        k = ("c", eng)
        for r in reads:
            r.r[k] = idx
        for w in writes:
            w.w = {k: idx}
            w.r = {}
        for w in acc_writes:
            w.w[k] = idx
        return op

    def new_phase(self):
        self.keymap = {}

    def dma(self, eng, semkey, fn, reads=(), writes=(), acc_writes=()):
        km = self.keymap
        if semkey not in km:
            km[semkey] = f"g{len(km)}"
        semkey = km[semkey]
        kr = self.keyres.get(semkey)
        if kr is None:
            kr = self.keyres[semkey] = Res(semkey)
        writes = list(writes) + [kr]
        op, idx = self._add(eng, fn, reads, writes, acc_writes, ("d", semkey))
        c = self.dma_cnt.get(semkey, 0) + 1
        self.dma_cnt[semkey] = c
        op[3] = semkey
        k = ("d", semkey)
        for r in reads:
            r.r[k] = c
        for w in writes:
            w.w = {k: c}
            w.r = {}
        for w in acc_writes:
            w.w[k] = c
        return op

    def emit(self, final_keys):
        nc = self.nc
        st = self.stack
        csem = {e: st.enter_context(nc.semaphore(f"c_{e}")) for e in ENGS if e != "sp"}
        dsem = {k: st.enter_context(nc.semaphore(f"d_{i}")) for i, k in enumerate(self.dma_cnt)}
        cum = {}
        for e in ENGS:
            c = 0
            arr = []
            for o in self.q[e]:
                if o[2]:
                    c += 1
                arr.append(c)
            cum[e] = arr
        engobj = {"pe": "tensor", "act": "scalar", "dve": "vector", "pool": "gpsimd", "sp": "sync"}
        with nc.Block() as block:
            for e in ENGS:
                ops = self.q[e]

                def body(eng, e=e, ops=ops):
                    for fn, waits, sig, semkey in ops:
                        for k, v in waits:
                            if k[0] == "c":
                                eng.wait_ge(csem[k[1]], cum[k[1]][v])
                            else:
                                eng.wait_ge(dsem[k[1]], 16 * v)
                        ins = fn(eng)
                        if semkey is not None:
                            ins.then_inc(dsem[semkey], 16)
                        elif sig:
                            ins.then_inc(csem[e], 1)
                    if e == "sp":
                        for k in final_keys:
                            eng.wait_ge(dsem[k], 16 * self.dma_cnt[k])
                getattr(block, engobj[e])(body)


def _bf(a):
    return np.ascontiguousarray(a.astype(ml_dtypes.bfloat16))


def _dft_tables(L):
    N = 2 * L
    nf = L + 1
    nch = (nf + 127) // 128
    S = nch * 128
    a = np.arange(S, dtype=np.int64)
    m = (a[:, None] * a[None, :]) % N
    ang = 2.0 * np.pi * m.astype(np.float64) / N
    valid = (a[:, None] <= L) & (a[None, :] <= L)
    q = np.where(valid, np.cos(ang), 0.0)
    r = np.where(valid, np.sin(ang), 0.0)
    wf = np.where(a <= L, 2.0 / N, 0.0)
    wf[0] = 1.0 / N
    wf[L] = 1.0 / N
    wfc = wf.reshape(nch, 128).T.astype(np.float32)
    return (_bf(q.reshape(nch, 128, S)), _bf(r.reshape(nch, 128, S)), np.ascontiguousarray(wfc), nch, S)


def _filter_consts(L):
    pos = np.arange(L, dtype=np.float32)
    t = pos / np.float32(max(L - 1, 1))
    w = (np.float32(2.0 * math.pi) * pos / np.float32(L)).astype(np.float32)
    bands = np.linspace(1e-4, 15, 16, dtype=np.float32)
    z = np.concatenate([t[:, None], np.cos(w[:, None] * bands), -np.sin(w[:, None] * bands)], axis=-1)
    zT = np.ascontiguousarray(z.T.astype(np.float32))
    negt = np.ascontiguousarray((-t).reshape(L // 128, 128).T.astype(np.float32))
    return zT, negt


def _host_consts():
    c = {}
    tpos = np.arange(TS)
    rowpos = (tpos // 64).astype(np.float32)
    colpos = (tpos % 64).astype(np.float32)
    inv = (10000.0 ** (-np.arange(16, dtype=np.float32) / 16)).astype(np.float32)
    cos = np.zeros((128, TS), np.float32)
    sins = np.zeros((128, TS), np.float32)
    perm = np.zeros((128, 128), np.float32)
    for p in range(2):
        for a in range(2):
            posv = rowpos if a == 0 else colpos
            for hf in range(2):
                for f in range(16):
                    row = p * 64 + a * 32 + hf * 16 + f
                    ang = (posv * inv[f]).astype(np.float32)
                    cos[row] = np.cos(ang)
                    sins[row] = np.sin(ang) * (-1.0 if hf == 0 else 1.0)
                    other = p * 64 + a * 32 + (1 - hf) * 16 + f
                    perm[other, row] = 1.0
    c["rcos"] = cos
    c["rsin"] = sins
    c["perm"] = _bf(perm)
    c["identb"] = _bf(np.eye(128, dtype=np.float32))
    c["identf"] = np.eye(128, dtype=np.float32)
    c["onesf"] = np.ones((128, 128), np.float32)
    for nm, L in (("s", TS), ("p", 256)):
        q, r, wf, nch, S = _dft_tables(L)
        c["q" + nm], c["r" + nm], c["wf" + nm] = q, r, wf
        zT, negt = _filter_consts(L)
        c["z" + nm], c["negt" + nm] = zT, negt
    deltas = np.abs(np.linspace(math.log(1e-2) / 1.5, math.log(1e-2) / 0.3, D, dtype=np.float32))
    c["delta"] = deltas.astype(np.float32)
    m0 = np.ones((128, 1), np.float32)
    m0[0, 0] = 0.0
    c["mask0"] = m0
    return c


W_SPECS = [
    ("ada_w", [4, D, 6 * D]), ("ada_b", [4, 6 * D]), ("norm1_g", [4, D]), ("norm2_g", [4, D]),
    ("attn_w_qkv", [2, D, 3 * D]), ("attn_lambda", [2, 4, 64]), ("attn_subln_g", [2, 128]),
    ("attn_w_o", [2, D, D]), ("hy_w_in", [2, D, 3 * D]), ("hy_b_in", [2, 3 * D]),
    ("hy_conv_w", [2, 3, 3 * D]), ("hy_conv_b", [2, 3 * D]), ("filt_w1", [2, 33, 64]),
    ("filt_b1", [2, 64]), ("filt_w2", [2, 64, 64]), ("filt_b2", [2, 64]), ("filt_w3", [2, 64, 2 * D]),
    ("filt_b3", [2, 2 * D]), ("filt_freq", [2, 64]), ("hy_skip", [2, D]), ("hy_w_out", [2, D, D]),
    ("hy_b_out", [2, D]), ("ffn_w_gu", [4, D, 2 * DFF]), ("ffn_w_down", [4, DFF, D]), ("final_g", [D]),
]
CONST_SPECS = [
    ("rcos", [128, TS], F32), ("rsin", [128, TS], F32), ("perm", [128, 128], BF16),
    ("identb", [128, 128], BF16), ("identf", [128, 128], F32), ("onesf", [128, 128], F32),
    ("qs", [33, 128, 4224], BF16), ("rs", [33, 128, 4224], BF16), ("wfs", [128, 33], F32),
    ("zs", [33, TS], F32), ("negts", [128, 32], F32),
    ("qp", [3, 128, 384], BF16), ("rp", [3, 128, 384], BF16), ("wfp", [128, 3], F32),
    ("zp", [33, 256], F32), ("negtp", [128, 2], F32),
    ("delta", [D], F32), ("mask0", [128, 1], F32),
]

GROUPS = [
    dict(name="s", tok0=0, T=TS, nseq=1, L=TS, rope=True, ncache=512, tiles=list(range(0, 8)), tt=list(range(0, 32))),
    dict(name="p", tok0=TS, T=TP, nseq=2, L=256, rope=False, ncache=0, tiles=[8], tt=list(range(32, 36))),
]


def build_program(stop_after=None, debug_outs=()):
    nc = bass.Bass("TRN2", target_bir_lowering=False)
    I = {}
    I["x"] = nc.dram_tensor("x", [T, D], F32, kind="ExternalInput").ap()
    I["ck"] = nc.dram_tensor("ck", [2, 512, D], F32, kind="ExternalInput").ap()
    I["cv"] = nc.dram_tensor("cv", [2, 512, D], F32, kind="ExternalInput").ap()
    I["cvec"] = nc.dram_tensor("cvec", [2, D], F32, kind="ExternalInput").ap()
    for nm, shp in W_SPECS:
        I[nm] = nc.dram_tensor(nm, shp, F32, kind="ExternalInput").ap()
    for nm, shp, dt in CONST_SPECS:
        I[nm] = nc.dram_tensor(nm, shp, dt, kind="ExternalInput").ap()
    O = {}
    O["y"] = nc.dram_tensor("y", [T, D], F32, kind="ExternalOutput").ap()
    O["nk"] = nc.dram_tensor("nk", [2, 2, 256, D], F32, kind="ExternalOutput").ap()
    O["nv"] = nc.dram_tensor("nv", [2, 2, 256, D], F32, kind="ExternalOutput").ap()

    def scratch(nm, shp, dt):
        kind = "ExternalOutput" if nm in debug_outs else "Internal"
        return nc.dram_tensor(nm, shp, dt, kind=kind).ap()
    S = {}
    S["X"] = scratch("X", [T, D], F32)
    S["MODS"] = scratch("MODS", [2, 128, 6 * D], F32)
    S["KT"] = scratch("KT", [8, 128, NKEY], BF16)
    S["VS"] = scratch("VS", [NKEY, D], BF16)
    S["QT"] = scratch("QT", [8, 128, T], BF16)
    S["MID"] = scratch("MID", [NCH_FF, 128, T], BF16)
    S["VVT"] = scratch("VVT", [8, 128, T], F32)
    S["X0T"] = scratch("X0T", [8, 128, T], F32)
    S["VVTOK"] = scratch("VVTOK", [T, D], BF16)
    S["HSs"] = scratch("HSs", [TS, D], BF16)
    S["HDs"] = scratch("HDs", [TS, D], BF16)
    S["HSp"] = scratch("HSp", [256, D], BF16)
    S["HDp"] = scratch("HDp", [256, D], BF16)
    S["YF"] = scratch("YF", [2, 33, 128, 2, 512], BF16)
    S["HTD"] = scratch("HTD", [128, 8, T], BF16)

    ARENA_WORDS = 49152
    P = Prog(nc, ARENA_WORDS)
    PS = P.stack.enter_context(nc.psum_tensor("ps", [128, 4096], F32))
    psr = [Res(f"bank{b}", excl=True) for b in range(8)]

    def bank(b, n=512):
        return PS[:, b * 512:b * 512 + n]

    def bankb(b):
        return PS[:, b * 512:(b + 1) * 512].bitcast(BF16)

    R = {}
    R["X"] = [[Res(f"X{i}a"), Res(f"X{i}b")] for i in range(NTT)]
    R["MODS"] = [Res(), Res()]
    R["KT"] = Res()
    R["VS"] = Res()
    R["QT"] = Res()
    R["MID"] = [Res() for _ in range(NT)]
    R["VVT"] = Res()
    R["X0T"] = Res()
    R["VVTOK"] = Res()
    R["HS"] = Res()
    R["YF"] = Res()
    R["OUT"] = Res()
    R["HTD"] = Res()
    semctr = [0]

    def sk(prefix):
        semctr[0] += 1
        return f"{prefix}{semctr[0]}"

    hT, _ = P.alloc([128, 8, T], BF16, "hT")
    hTr = [Res(f"hT{i}") for i in range(NT)]
    identb, identb_r = P.alloc([128, 128], BF16, "identb")
    identf, identf_r = P.alloc([128, 128], F32, "identf")
    onesf, onesf_r = P.alloc([128, 128], F32, "onesf")
    SIL, SIL_r = P.alloc([128, 2, 8, 128], BF16, "SIL")
    P.dma("sp", "c_identb", lambda e: e.dma_start(out=identb, in_=I["identb"]), writes=[identb_r])
    P.dma("sp", "c_identf", lambda e: e.dma_start(out=identf, in_=I["identf"]), writes=[identf_r])
    P.dma("sp", "c_onesf", lambda e: e.dma_start(out=onesf, in_=I["onesf"]), writes=[onesf_r])

    def load_cols(dst, dst_r, col0, vec, nchunks, key):
        P.dma("sp", key, lambda e: e.dma_start(
            out=dst[:, col0:col0 + nchunks], in_=vec.rearrange("(c p) -> p c", p=128), allow_slow_non_contiguous=True),
            acc_writes=[dst_r])

    def bcast_rows(vec1d, n):
        return vec1d.rearrange("(o n) -> o n", o=1).broadcast_to([128, n])

    m0 = P.mark()
    ccol, ccol_r = P.alloc([128, 16], F32, "ccol")
    scol, scol_r = P.alloc([128, 16], F32, "scol")
    for g in range(2):
        load_cols(ccol, ccol_r, g * 8, I["cvec"][g], 8, "ccol")
    P.op("act", lambda e: e.activation(out=scol, in_=ccol, func=AF.Silu), reads=[ccol_r], writes=[scol_r])
    for g in range(2):
        for kc in range(8):
            P.op("dve", lambda e, g=g, kc=kc: e.tensor_scalar(
                out=SIL[:, g, kc, :], in0=onesf, scalar1=scol[:, g * 8 + kc:g * 8 + kc + 1], scalar2=None,
                op0=ALU.mult), reads=[scol_r, onesf_r], acc_writes=[SIL_r])
    P.release(m0)

    for tt in range(NTT):
        P.dma("sp", f"xcopy{tt % 4}", lambda e, tt=tt: e.dma_start(
            out=S["X"][tt * 128:(tt + 1) * 128, :], in_=I["x"][tt * 128:(tt + 1) * 128, :]), writes=R["X"][tt])

    def phase_mod(i):
        P.new_phase()
        m = P.mark()
        adab, adab_r = P.alloc([128, 6 * D], F32, "adab")
        mod = [P.alloc([128, 6 * D], F32, f"mod{g}") for g in range(2)]
        gn = [P.alloc([128, D], F32, f"gn{k}") for k in range(2)]
        wch = [P.alloc([128, 8, 512], BF16, f"adw{k}") for k in range(2)]
        P.dma("sp", "adab", lambda e: e.dma_start(out=adab, in_=bcast_rows(I["ada_b"][i], 6 * D)), writes=[adab_r])
        P.dma("sp", "gn0", lambda e: e.dma_start(out=gn[0][0], in_=bcast_rows(I["norm1_g"][i], D)), writes=[gn[0][1]])
        P.dma("sp", "gn1", lambda e: e.dma_start(out=gn[1][0], in_=bcast_rows(I["norm2_g"][i], D)), writes=[gn[1][1]])
        for n in range(12):
            w, wr = wch[n % 2]
            P.dma("pool", f"adw{n % 2}", lambda e, w=w, n=n: e.dma_start(
                out=w, in_=I["ada_w"][i][:, n * 512:(n + 1) * 512].rearrange("(kc p) n -> p kc n", p=128)), writes=[wr])
            for g in range(2):
                b = (n * 2 + g) % 8
                for kc in range(8):
                    P.op("pe", lambda e, w=w, g=g, kc=kc, b=b: e.matmul(
                        bank(b), lhsT=SIL[:, g, kc, :], rhs=w[:, kc, :], start=(kc == 0), stop=(kc == 7)),
                        reads=[SIL_r, wr], writes=[psr[b]] if kc == 0 else (), acc_writes=[psr[b]] if kc else ())
                P.op("dve", lambda e, g=g, n=n, b=b: e.tensor_tensor(
                    out=mod[g][0][:, n * 512:(n + 1) * 512], in0=bank(b), in1=adab[:, n * 512:(n + 1) * 512], op=ALU.add),
                    reads=[psr[b], adab_r], acc_writes=[mod[g][1]])
        for g in range(2):
            for k, off in ((0, D), (1, 4 * D)):
                P.op("dve", lambda e, g=g, k=k, off=off: e.scalar_tensor_tensor(
                    out=mod[g][0][:, off:off + D], in0=mod[g][0][:, off:off + D], scalar=1.0, in1=gn[k][0],
                    op0=ALU.add, op1=ALU.mult), reads=[mod[g][1], gn[k][1]], acc_writes=[mod[g][1]])
            P.dma("sp", f"mods{g}", lambda e, g=g: e.dma_start(out=S["MODS"][g], in_=mod[g][0]),
                  reads=[mod[g][1]], writes=[R["MODS"][g]])
        P.release(m)

    def load_mod(off, key):
        out = []
        for g in range(2):
            t, r = P.alloc([128, D], F32, f"{key}{g}")
            P.dma("sp", f"ldm_{key}{g}", lambda e, t=t, g=g: e.dma_start(out=t, in_=S["MODS"][g][:, off:off + D]),
                  reads=[R["MODS"][g]], writes=[r])
            out.append((t, r))
        return out

    def rstd_ops(ss, ss_r, rs, rs_r, n, eps):
        P.op("act", lambda e: e.activation(out=rs, in_=ss, func=AF.Ln, scale=1.0 / n, bias=float(eps)),
             reads=[ss_r], writes=[rs_r])
        P.op("act", lambda e: e.activation(out=rs, in_=rs, func=AF.Exp, scale=-0.5),
             reads=[rs_r], writes=[rs_r])

    def phase_norm(offA, offB, final=False):
        P.new_phase()
        m = P.mark()
        if final:
            gt, gr = P.alloc([128, D], F32, "fing")
            P.dma("sp", "fing", lambda e: e.dma_start(out=gt, in_=bcast_rows(I["final_g"], D)), writes=[gr])
            A = [(gt, gr), (gt, gr)]
            B = None
        else:
            A = load_mod(offA, "nA")
            B = load_mod(offB, "nB")
        xt = [P.alloc([128, D], F32, f"nx{k}") for k in range(3)]
        x2 = [P.alloc([128, D], F32, f"nx2{k}") for k in range(2)]
        hb = [P.alloc([128, D], BF16, f"nhb{k}") for k in range(2)]
        junk, junk_r = P.alloc([128, D], BF16, "njunk")
        ss = [P.alloc([128, 1], F32, f"nss{k}") for k in range(2)]
        rs = [P.alloc([128, 1], F32, f"nrs{k}") for k in range(2)]
        for tt in range(NTT):
            g = 0 if tt < 32 else 1
            x, xr = xt[tt % 3]
            y, yr = x2[tt % 2]
            h, hr = hb[tt % 2]
            s_, s_r = ss[tt % 2]
            r_, r_r = rs[tt % 2]
            P.dma("sp", f"nx{tt % 3}", lambda e, x=x, tt=tt: e.dma_start(out=x, in_=S["X"][tt * 128:(tt + 1) * 128, :]),
                  reads=R["X"][tt], writes=[xr])
            P.op("act", lambda e, x=x, s_=s_: e.activation(out=junk, in_=x, func=AF.Square, accum_out=s_),
                 reads=[xr], writes=[junk_r, s_r])
            rstd_ops(s_, s_r, r_, r_r, D, EPS)
            if final:
                P.op("dve", lambda e, x=x, y=y, r_=r_, g=g: e.scalar_tensor_tensor(
                    out=y, in0=x, scalar=r_, in1=A[g][0], op0=ALU.mult, op1=ALU.mult),
                    reads=[xr, r_r, A[g][1]], writes=[yr])
                P.dma("sp", f"fo{tt % 2}", lambda e, y=y, tt=tt: e.dma_start(out=O["y"][tt * 128:(tt + 1) * 128, :], in_=y),
                      reads=[yr], acc_writes=[R["OUT"]])
                continue
            P.op("dve", lambda e, x=x, y=y, r_=r_, g=g: e.scalar_tensor_tensor(
                out=y, in0=x, scalar=r_, in1=A[g][0], op0=ALU.mult, op1=ALU.mult),
                reads=[xr, r_r, A[g][1]], writes=[yr])
            P.op("pool", lambda e, y=y, h=h, g=g: e.tensor_tensor(out=h, in0=y, in1=B[g][0], op=ALU.add),
                 reads=[yr, B[g][1]], writes=[hr])
            b = tt % 2
            for kc in range(8):
                P.op("pe", lambda e, h=h, kc=kc, b=b: e.transpose(bankb(b)[:, kc * 128:(kc + 1) * 128], h[:, kc * 128:(kc + 1) * 128], identb),
                     reads=[hr, identb_r], writes=[psr[b]] if kc == 0 else (), acc_writes=[psr[b]] if kc else ())
            P.op("act", lambda e, b=b, tt=tt: e.copy(out=hT[:, :, tt * 128:(tt + 1) * 128],
                                                     in_=bankb(b).rearrange("p (k t) -> p k t", t=128)),
                 reads=[psr[b]], acc_writes=[hTr[tt // 4]])
        P.release(m)

    def stream_w(dst, dst_r, key, src_ap):
        P.dma("pool", key, lambda e: e.dma_start(out=dst, in_=src_ap.rearrange("(kc p) n -> p kc n", p=128)), writes=[dst_r])

    def mm_wstat(b, w, wr, tile, ncols=512):
        for kc in range(8):
            P.op("pe", lambda e, kc=kc: e.matmul(bank(b, ncols), lhsT=w[:, kc, :], rhs=hT[:, kc, tile * 512:tile * 512 + ncols],
                                                 start=(kc == 0), stop=(kc == 7)),
                 reads=[wr, hTr[tile]], writes=[psr[b]] if kc == 0 else (), acc_writes=[psr[b]] if kc else ())

    def mm_tstat(b, w, wr, tt):
        for kc in range(8):
            P.op("pe", lambda e, kc=kc: e.matmul(bank(b), lhsT=hT[:, kc, tt * 128:(tt + 1) * 128], rhs=w[:, kc, :],
                                                 start=(kc == 0), stop=(kc == 7)),
                 reads=[wr, hTr[tt // 4]], writes=[psr[b]] if kc == 0 else (), acc_writes=[psr[b]] if kc else ())

    def phase_qkv(j):
        P.new_phase()
        m = P.mark()
        rcos, rcos_r = P.alloc([128, TS], F32, "rcos")
        rsin, rsin_r = P.alloc([128, TS], F32, "rsin")
        perm, perm_r = P.alloc([128, 128], BF16, "perm")
        P.dma("sp", "rcos", lambda e: e.dma_start(out=rcos, in_=I["rcos"]), writes=[rcos_r])
        P.dma("sp", "rsin", lambda e: e.dma_start(out=rsin, in_=I["rsin"]), writes=[rsin_r])
        P.dma("sp", "perm", lambda e: e.dma_start(out=perm, in_=I["perm"]), writes=[perm_r])
        ckt = [P.alloc([128, D], BF16, f"ckt{k}") for k in range(2)]
        kst = [P.alloc([128, 8, 128], BF16, f"kst{k}") for k in range(2)]
        cvt = [P.alloc([128, D], BF16, f"cvt{k}") for k in range(2)]
        for tk in range(4):
            c_, c_r = ckt[tk % 2]
            k_, k_r = kst[tk % 2]
            v_, v_r = cvt[tk % 2]
            P.dma("pool", f"ckt{tk % 2}", lambda e, c_=c_, tk=tk: e.dma_start(out=c_, in_=I["ck"][j, tk * 128:(tk + 1) * 128, :]), writes=[c_r])
            b = tk % 2
            for h in range(8):
                P.op("pe", lambda e, c_=c_, h=h, b=b: e.transpose(bankb(b)[:, h * 128:(h + 1) * 128], c_[:, h * 128:(h + 1) * 128], identb),
                     reads=[c_r, identb_r], writes=[psr[b]] if h == 0 else (), acc_writes=[psr[b]] if h else ())
            P.op("act", lambda e, k_=k_, b=b: e.copy(out=k_, in_=bankb(b).rearrange("p (k t) -> p k t", t=128)), reads=[psr[b]], writes=[k_r])
            P.dma("sp", f"kst{tk % 2}", lambda e, k_=k_, tk=tk: e.dma_start(
                out=S["KT"][:, :, tk * 128:(tk + 1) * 128].rearrange("h p c -> p h c"), in_=k_), reads=[k_r], acc_writes=[R["KT"]])
            P.dma("pool", f"cvt{tk % 2}", lambda e, v_=v_, tk=tk: e.dma_start(out=v_, in_=I["cv"][j, tk * 128:(tk + 1) * 128, :]), writes=[v_r])
            P.dma("sp", f"cvo{tk % 2}", lambda e, v_=v_, tk=tk: e.dma_start(out=S["VS"][tk * 128:(tk + 1) * 128, :], in_=v_),
                  reads=[v_r], acc_writes=[R["VS"]])
        if stop_after == f"qkv{2 * j}a":
            raise Stop()
        wq = [P.alloc([128, 8, 128], BF16, f"wq{k}") for k in range(3)]
        qb = [P.alloc([128, 512], BF16, f"qb{k}") for k in range(2)]
        t1 = [P.alloc([128, 512], F32, f"t1{k}") for k in range(2)]
        t2 = [P.alloc([128, 512], F32, f"t2{k}") for k in range(2)]
        qr = [P.alloc([128, 512], BF16, f"qr{k}") for k in range(3)]
        it = 0
        for ch in range(16):
            isk, h = ch // 8, ch % 8
            w, wr = wq[ch % 3]
            stream_w(w, wr, f"wq{ch % 3}", I["attn_w_qkv"][j][:, isk * D + h * 128: isk * D + (h + 1) * 128])
            for tile in range(NT):
                bA = (it % 3) * 2
                bB = bA + 1
                q_, q_r = qr[it % 3]
                mm_wstat(bA, w, wr, tile)
                if tile < 8 and not os.environ.get('NOROPE'):
                    qq, qq_r = qb[it % 2]
                    a1, a1r = t1[it % 2]
                    a2, a2r = t2[it % 2]
                    P.op("act", lambda e, qq=qq, bA=bA: e.copy(out=qq, in_=bank(bA)), reads=[psr[bA]], writes=[qq_r])
                    P.op("pe", lambda e, qq=qq, bB=bB: e.matmul(bank(bB), lhsT=perm, rhs=qq, start=True, stop=True),
                         reads=[perm_r, qq_r], writes=[psr[bB]])
                    P.op("dve", lambda e, a1=a1, bA=bA, tile=tile: e.tensor_tensor(
                        out=a1, in0=bank(bA), in1=rcos[:, tile * 512:(tile + 1) * 512], op=ALU.mult),
                        reads=[psr[bA], rcos_r], writes=[a1r])
                    P.op("dve", lambda e, a2=a2, bB=bB, tile=tile: e.tensor_tensor(
                        out=a2, in0=bank(bB), in1=rsin[:, tile * 512:(tile + 1) * 512], op=ALU.mult),
                        reads=[psr[bB], rsin_r], writes=[a2r])
                    P.op("pool", lambda e, q_=q_, a1=a1, a2=a2: e.tensor_tensor(out=q_, in0=a1, in1=a2, op=ALU.add),
                         reads=[a1r, a2r], writes=[q_r])
                else:
                    P.op("act", lambda e, q_=q_, bA=bA: e.copy(out=q_, in_=bank(bA)), reads=[psr[bA]], writes=[q_r])
                if isk:
                    P.dma("sp", f"qro{it % 3}", lambda e, q_=q_, h=h, tile=tile: e.dma_start(
                        out=S["KT"][h, :, 512 + tile * 512: 512 + (tile + 1) * 512], in_=q_), reads=[q_r], acc_writes=[R["KT"]])
                else:
                    P.dma("sp", f"qro{it % 3}", lambda e, q_=q_, h=h, tile=tile: e.dma_start(
                        out=S["QT"][h, :, tile * 512:(tile + 1) * 512], in_=q_), reads=[q_r], acc_writes=[R["QT"]])
                it += 1
        if stop_after == f"qkv{2 * j}b":
            raise Stop()
        wv = [P.alloc([128, 8, 512], BF16, f"wv{k}") for k in range(2)]
        vb = [P.alloc([128, 512], BF16, f"vb{k}") for k in range(3)]
        vf = [P.alloc([128, 512], F32, f"vf{k}") for k in range(2)]
        it = 0
        for kind in ("v", "k"):
            for half in range(2):
                w, wr = wv[(it // 64) % 2]
                col0 = (2 * D if kind == "v" else D) + half * 512
                w, wr = wv[half]
                stream_w(w, wr, f"wv{half}{kind}", I["attn_w_qkv"][j][:, col0:col0 + 512])
                tts = range(NTT) if kind == "v" else range(32, 36)
                for tt in tts:
                    b = 6 + it % 2
                    mm_tstat(b, w, wr, tt)
                    if kind == "v":
                        v_, v_r = vb[it % 3]
                        P.op("act", lambda e, v_=v_, b=b: e.copy(out=v_, in_=bank(b)), reads=[psr[b]], writes=[v_r])
                        P.dma("sp", f"vbo{it % 3}", lambda e, v_=v_, tt=tt, half=half: e.dma_start(
                            out=S["VS"][512 + tt * 128: 512 + (tt + 1) * 128, half * 512:(half + 1) * 512], in_=v_),
                            reads=[v_r], acc_writes=[R["VS"]])
                    if tt >= 32:
                        f_, f_r = vf[it % 2]
                        P.op("dve", lambda e, f_=f_, b=b: e.tensor_copy(out=f_, in_=bank(b)), reads=[psr[b]], writes=[f_r])
                        s, r0 = (tt - 32) // 2, ((tt - 32) % 2) * 128
                        dst = O["nv"] if kind == "v" else O["nk"]
                        P.dma("sp", f"vfo{it % 2}", lambda e, f_=f_, s=s, r0=r0, half=half, dst=dst: e.dma_start(
                            out=dst[s, j, r0:r0 + 128, half * 512:(half + 1) * 512], in_=f_), reads=[f_r], acc_writes=[R["OUT"]])
                    it += 1
        P.release(m)

    def phase_attn(j, i):
        P.new_phase()
        m = P.mark()
        lam_init = 0.8 - 0.6 * math.exp(-0.3 * i)
        lp, lp_r = P.alloc([128, 4, 64], F32, "lp")
        lpp, lpp_r = P.alloc([128, 2, 64], F32, "lpp")
        lsum, lsum_r = P.alloc([128, 2], F32, "lsum")
        lexp, lexp_r = P.alloc([128, 2], F32, "lexp")
        nlam, nlam_r = P.alloc([128, 1], F32, "nlam")
        gsub, gsub_r = P.alloc([128, 128], F32, "gsub")
        P.dma("sp", "lp", lambda e: e.dma_start(out=lp, in_=I["attn_lambda"][j].rearrange("(o a) b -> o a b", o=1).broadcast_to([128, 4, 64])), writes=[lp_r])
        P.dma("sp", "gsub", lambda e: e.dma_start(out=gsub, in_=bcast_rows(I["attn_subln_g"][j], 128)), writes=[gsub_r])
        P.op("dve", lambda e: e.tensor_scalar(out=gsub, in0=gsub, scalar1=1.0 - lam_init, scalar2=None, op0=ALU.mult),
             reads=[gsub_r], writes=[gsub_r])
        for k in range(2):
            P.op("dve", lambda e, k=k: e.tensor_tensor(out=lpp[:, k, :], in0=lp[:, 2 * k, :], in1=lp[:, 2 * k + 1, :], op=ALU.mult),
                 reads=[lp_r], acc_writes=[lpp_r])
        P.op("dve", lambda e: e.tensor_reduce(out=lsum, in_=lpp, axis=AX.X, op=ALU.add), reads=[lpp_r], writes=[lsum_r])
        P.op("act", lambda e: e.activation(out=lexp, in_=lsum, func=AF.Exp), reads=[lsum_r], writes=[lexp_r])
        P.op("dve", lambda e: e.tensor_tensor(out=nlam, in0=lexp[:, 1:2], in1=lexp[:, 0:1], op=ALU.subtract), reads=[lexp_r], writes=[nlam_r])
        P.op("dve", lambda e: e.tensor_scalar(out=nlam, in0=nlam, scalar1=-lam_init, scalar2=None, op0=ALU.add), reads=[nlam_r], writes=[nlam_r])

        kTb = [P.alloc([128, 4608], BF16, f"kTh{k}") for k in range(2)]
        vhb = [P.alloc([128, 36, 132], BF16, f"vh{k}") for k in range(2)]
        for k in range(2):
            P.op("pool", lambda e, k=k: e.memset(vhb[k][0], 1.0), writes=[vhb[k][1]])
        qtb = [P.alloc([128, 2, 256], BF16, f"qt{k}") for k in range(3)]
        for k in range(3):
            P.op("pool", lambda e, k=k: e.memset(qtb[k][0], 0.0), writes=[qtb[k][1]])
        Pb = [P.alloc([128, 2, 256], BF16, f"P{k}") for k in range(3)]
        osb = [P.alloc([128, 128], F32, f"o{k}") for k in range(2)]
        obb = [P.alloc([128, 128], BF16, f"ob{k}") for k in range(2)]
        rrb = [P.alloc([128, 2], F32, f"rr{k}") for k in range(2)]
        ssb = [P.alloc([128, 1], F32, f"ss{k}") for k in range(2)]
        rsb = [P.alloc([128, 1], F32, f"rs{k}") for k in range(2)]
        junk, junk_r = P.alloc([128, 128], BF16, "ajunk")
        accs = [P.alloc([128, 4, 132], F32, f"accs{k}") for k in range(2)]
        steps = []
        hcount = 0
        for G in GROUPS:
            for s in range(G["nseq"]):
                L = G["L"]
                nk = G["ncache"] + L
                nkc = nk // 128
                kcol0 = 0 if G["name"] == "s" else 4608 + s * 256
                qtok0 = G["tok0"] + s * L
                for h in range(8):
                    for qb_ in range(L // 256):
                        for c in range(nkc):
                            steps.append(dict(h=h, hid=hcount, q0=qtok0 + qb_ * 256, c=c, nkc=nkc, nk=nk, kcol0=kcol0,
                                              newh=(qb_ == 0 and c == 0), newq=(c == 0)))
                    hcount += 1
        qcount = [0]
        ecount = [0]
        cur = {}

        def stage_a(i, st):
            h = st["h"]
            if st["newh"]:
                kT, kT_r = kTb[st["hid"] % 2]
                vh, vh_r = vhb[st["hid"] % 2]
                nk, nkc, kcol0 = st["nk"], st["nkc"], st["kcol0"]
                P.dma("sp", f"kTh{st['hid'] % 2}", lambda e: e.dma_start(
                    out=kT[:, 0:nk], in_=S["KT"][h, :, kcol0:kcol0 + nk]), reads=[R["KT"]], writes=[kT_r])
                P.dma("sp", f"vh{st['hid'] % 2}", lambda e: e.dma_start(
                    out=vh[:, 0:nkc, 0:128], in_=S["VS"][kcol0:kcol0 + nk, h * 128:(h + 1) * 128].rearrange("(c p) e -> p c e", p=128)),
                    reads=[R["VS"]], acc_writes=[vh_r])
                cur["kT"], cur["vh"] = (kT, kT_r), (vh, vh_r)
            if st["newq"]:
                qt, qt_r = qtb[qcount[0] % 3]
                q0 = st["q0"]
                for mp in range(2):
                    P.dma("sp", f"qt{qcount[0] % 3}_{mp}", lambda e, mp=mp: e.dma_start(
                        out=qt[mp * 64:(mp + 1) * 64, mp, :], in_=S["QT"][h, mp * 64:(mp + 1) * 64, q0:q0 + 256]),
                        reads=[R["QT"]], acc_writes=[qt_r])
                qcount[0] += 1
                cur["qt"] = (qt, qt_r)
            kT, kT_r = cur["kT"]
            qt, qt_r = cur["qt"]
            st["vh"] = cur["vh"]
            c = st["c"]
            sb_ = i % 3
            Pt, Pt_r = Pb[i % 3]
            st["P"] = (Pt, Pt_r)
            for mp in range(2):
                P.op("pe", lambda e, mp=mp: e.matmul(
                    bank(sb_)[:, mp * 256:(mp + 1) * 256], lhsT=kT[:, c * 128:(c + 1) * 128],
                    rhs=qt[:, mp, :], start=True, stop=True),
                    reads=[kT_r, qt_r], writes=[psr[sb_]] if mp == 0 else (), acc_writes=[psr[sb_]] if mp else ())
            P.op("act", lambda e: e.activation(
                out=Pt, in_=bank(sb_).rearrange("p (m q) -> p m q", q=256), func=AF.Exp, scale=0.125),
                reads=[psr[sb_]], writes=[Pt_r])

        def stage_b(st):
            Pt, Pt_r = st["P"]
            vh, vh_r = st["vh"]
            c, nkc, h = st["c"], st["nkc"], st["h"]
            for qs in range(2):
                for mp in range(2):
                    ab = 4 + qs * 2 + mp
                    P.op("pe", lambda e, qs=qs, mp=mp, ab=ab: e.matmul(
                        bank(ab, 129), lhsT=Pt[:, mp, qs * 128:(qs + 1) * 128], rhs=vh[:, c, 0:129],
                        start=(c == 0), stop=(c == nkc - 1)),
                        reads=[Pt_r, vh_r], writes=[psr[ab]] if c == 0 else (), acc_writes=[psr[ab]] if c else ())
            if c != nkc - 1:
                return
            if nkc < 12:
                run_deferred(10 ** 9)
            ac, ac_r = accs[ecount[0] % 2]
            P.op("act", lambda e: e.copy(out=ac[:, :, 0:129], in_=PS[:, 4 * 512:8 * 512].rearrange("p (b c) -> p b c", c=512)[:, :, 0:129]),
                 reads=[psr[4], psr[5], psr[6], psr[7]], writes=[ac_r])
            for qs in range(2):
                a0, a1 = qs * 2, qs * 2 + 1
                o_, o_r = osb[ecount[0] % 2]
                ob, ob_r = obb[ecount[0] % 2]
                rr, rr_r = rrb[ecount[0] % 2]
                s_, s_r = ssb[ecount[0] % 2]
                r_, r_r = rsb[ecount[0] % 2]
                ecount[0] += 1
                tok = st["q0"] + qs * 128
                P.op("dve", lambda e, rr=rr, a0=a0: e.reciprocal(out=rr, in_=ac[:, a0:a0 + 2, 128]),
                     reads=[ac_r], writes=[rr_r])
                P.op("dve", lambda e, rr=rr: e.tensor_tensor(out=rr[:, 1:2], in0=rr[:, 1:2], in1=nlam, op=ALU.mult),
                     reads=[rr_r, nlam_r], writes=[rr_r])
                P.op("dve", lambda e, o_=o_, rr=rr, a0=a0: e.tensor_scalar(
                    out=o_, in0=ac[:, a0, 0:128], scalar1=rr[:, 0:1], scalar2=None, op0=ALU.mult),
                    reads=[ac_r, rr_r], writes=[o_r])
                P.op("dve", lambda e, o_=o_, rr=rr, a1=a1: e.scalar_tensor_tensor(
                    out=o_, in0=ac[:, a1, 0:128], scalar=rr[:, 1:2], in1=o_, op0=ALU.mult, op1=ALU.add),
                    reads=[ac_r, rr_r, o_r], writes=[o_r])
                P.op("dve", lambda e, o_=o_, s_=s_: e.scalar_tensor_tensor(
                    out=junk, in0=o_, scalar=1.0, in1=o_, op0=ALU.mult, op1=ALU.mult, accum_out=s_),
                    reads=[o_r], writes=[junk_r, s_r])

                def e2(o_=o_, o_r=o_r, ob=ob, ob_r=ob_r, s_=s_, s_r=s_r, r_=r_, r_r=r_r):
                    rstd_ops(s_, s_r, r_, r_r, 128, SUBLN_EPS)
                    P.op("dve", lambda e: e.scalar_tensor_tensor(
                        out=ob, in0=o_, scalar=r_, in1=gsub, op0=ALU.mult, op1=ALU.mult),
                        reads=[o_r, r_r, gsub_r], writes=[ob_r])

                def e4(ob=ob, ob_r=ob_r):
                    P.op("pe", lambda e: e.transpose(bankb(3)[:, 0:128], ob, identb), reads=[ob_r, identb_r], writes=[psr[3]])

                def e5(tok=tok, h=h):
                    P.op("act", lambda e: e.copy(out=hT[:, h, tok:tok + 128], in_=bankb(3)[:, 0:128]),
                         reads=[psr[3]], acc_writes=[hTr[tok // 512]])
                if nkc >= 12:
                    base = st["idx"] + LA
                    deferred.append((base + 2 + qs, e2))
                    deferred.append((base + 4 + 3 * qs, e4))
                    deferred.append((base + 6 + 3 * qs, e5))
                else:
                    e2()
                    e4()
                    e5()

        LA = 2
        deferred = []

        def run_deferred(now):
            keep = []
            for due, fn in deferred:
                if due <= now:
                    fn()
                else:
                    keep.append((due, fn))
            deferred[:] = keep
        for i, st in enumerate(steps):
            st["idx"] = i
            stage_a(i, st)
            if i >= LA:
                stage_b(steps[i - LA])
            run_deferred(i)
        for st in steps[-LA:]:
            stage_b(st)
        run_deferred(10 ** 9)
        P.release(m)

    def phase_outproj(wsrc, bias_vec, offG):
        P.new_phase()
        m = P.mark()
        G_ = load_mod(offG, "oG")
        if bias_vec is not None:
            brep, brep_r = P.alloc([128, D], F32, "obias")
            P.dma("sp", "obias", lambda e: e.dma_start(out=brep, in_=bcast_rows(bias_vec, D)), writes=[brep_r])
        wo = [P.alloc([128, 8, 512], BF16, f"wo{k}") for k in range(2)]
        xt = [P.alloc([128, 512], F32, f"ox{k}") for k in range(3)]
        tb = [P.alloc([128, 512], F32, f"ot{k}") for k in range(2)]
        it = 0
        for half in range(2):
            w, wr = wo[half]
            stream_w(w, wr, f"wo{half}", wsrc[:, half * 512:(half + 1) * 512])
            for tt in range(NTT):
                g = 0 if tt < 32 else 1
                b = it % 4
                x, xr = xt[it % 3]
                t_, t_r = tb[it % 2]
                P.dma("sp", f"ox{it % 3}", lambda e, x=x, tt=tt, half=half: e.dma_start(
                    out=x, in_=S["X"][tt * 128:(tt + 1) * 128, half * 512:(half + 1) * 512]), reads=[R["X"][tt][half]], writes=[xr])
                mm_tstat(b, w, wr, tt)
                src = bank(b)
                if bias_vec is not None:
                    P.op("dve", lambda e, t_=t_, b=b, half=half: e.tensor_tensor(
                        out=t_, in0=bank(b), in1=brep[:, half * 512:(half + 1) * 512], op=ALU.add),
                        reads=[psr[b], brep_r], writes=[t_r])
                    P.op("dve", lambda e, t_=t_, g=g, half=half: e.tensor_tensor(
                        out=t_, in0=t_, in1=G_[g][0][:, half * 512:(half + 1) * 512], op=ALU.mult),
                        reads=[t_r, G_[g][1]], writes=[t_r])
                else:
                    P.op("dve", lambda e, t_=t_, b=b, g=g, half=half: e.tensor_tensor(
                        out=t_, in0=bank(b), in1=G_[g][0][:, half * 512:(half + 1) * 512], op=ALU.mult),
                        reads=[psr[b], G_[g][1]], writes=[t_r])
                P.op("pool", lambda e, x=x, t_=t_: e.tensor_tensor(out=x, in0=x, in1=t_, op=ALU.add), reads=[xr, t_r], writes=[xr])
                P.dma("sp", f"oxo{it % 3}", lambda e, x=x, tt=tt, half=half: e.dma_start(
                    out=S["X"][tt * 128:(tt + 1) * 128, half * 512:(half + 1) * 512], in_=x), reads=[xr], writes=[R["X"][tt][half]])
                it += 1
        P.release(m)

    def phase_ffn(i):
        P.new_phase()
        m = P.mark()
        wg = [P.alloc([128, 2, 8, 128], BF16, f"wg{k}") for k in range(3)]
        sg = [P.alloc([128, 512], F32, f"sg{k}") for k in range(2)]
        md = [P.alloc([128, 512], BF16, f"md{k}") for k in range(3)]
        it = 0
        for c in range(NCH_FF):
            w, wr = wg[c % 3]
            for u in range(2):
                P.dma("pool", f"wg{c % 3}_{u}", lambda e, w=w, c=c, u=u: e.dma_start(
                    out=w[:, u], in_=I["ffn_w_gu"][i][:, u * DFF + c * 128: u * DFF + (c + 1) * 128].rearrange("(kc p) n -> p kc n", p=128)),
                    writes=[wr] if u == 0 else (), acc_writes=[wr] if u else ())
            for tile in range(NT):
                bG = (it % 4) * 2
                bU = bG + 1
                s_, s_r = sg[it % 2]
                m_, m_r = md[it % 3]
                mm_wstat(bG, w[:, 0], wr, tile)
                mm_wstat(bU, w[:, 1], wr, tile)
                P.op("act", lambda e, s_=s_, bG=bG: e.activation(out=s_, in_=bank(bG), func=AF.Silu), reads=[psr[bG]], writes=[s_r])
                P.op("dve", lambda e, m_=m_, s_=s_, bU=bU: e.tensor_tensor(out=m_, in0=bank(bU), in1=s_, op=ALU.mult),
                     reads=[psr[bU], s_r], writes=[m_r])
                P.dma("sp", f"mdo{it % 3}", lambda e, m_=m_, c=c, tile=tile: e.dma_start(
                    out=S["MID"][c, :, tile * 512:(tile + 1) * 512], in_=m_), reads=[m_r], acc_writes=[R["MID"][tile]])
                it += 1
        P.release(m)
        P.new_phase()
        m = P.mark()
        G_ = load_mod(5 * D, "fG")
        wd, wd_r = P.alloc([128, NCH_FF, D], BF16, "wd")
        P.dma("pool", "wd", lambda e: e.dma_start(out=wd, in_=I["ffn_w_down"][i].rearrange("(c p) n -> p c n", p=128)), writes=[wd_r])
        mt = [P.alloc([128, NCH_FF, 512], BF16, f"mt{k}") for k in range(2)]
        xt = [P.alloc([128, 512], F32, f"fx{k}") for k in range(3)]
        tb = [P.alloc([128, 512], F32, f"ft{k}") for k in range(2)]
        it = 0
        for tile in range(NT):
            g = 0 if tile < 8 else 1
            mt_, mt_r = mt[tile % 2]
            P.dma("sp", f"mt{tile % 2}", lambda e, mt_=mt_, tile=tile: e.dma_start(
                out=mt_, in_=S["MID"][:, :, tile * 512:(tile + 1) * 512].rearrange("c p t -> p c t")), reads=[R["MID"][tile]], writes=[mt_r])
            for ts in range(4):
                tt = tile * 4 + ts
                for half in range(2):
                    b = it % 4
                    x, xr = xt[it % 3]
                    t_, t_r = tb[it % 2]
                    P.dma("sp", f"fx{it % 3}", lambda e, x=x, tt=tt, half=half: e.dma_start(
                        out=x, in_=S["X"][tt * 128:(tt + 1) * 128, half * 512:(half + 1) * 512]), reads=[R["X"][tt][half]], writes=[xr])
                    for c in range(NCH_FF):
                        P.op("pe", lambda e, mt_=mt_, c=c, ts=ts, half=half, b=b: e.matmul(
                            bank(b), lhsT=mt_[:, c, ts * 128:(ts + 1) * 128], rhs=wd[:, c, half * 512:(half + 1) * 512],
                            start=(c == 0), stop=(c == NCH_FF - 1)),
                            reads=[mt_r, wd_r], writes=[psr[b]] if c == 0 else (), acc_writes=[psr[b]] if c else ())
                    P.op("dve", lambda e, t_=t_, b=b, g=g, half=half: e.tensor_tensor(
                        out=t_, in0=bank(b), in1=G_[g][0][:, half * 512:(half + 1) * 512], op=ALU.mult),
                        reads=[psr[b], G_[g][1]], writes=[t_r])
                    P.op("pool", lambda e, x=x, t_=t_: e.tensor_tensor(out=x, in0=x, in1=t_, op=ALU.add), reads=[xr, t_r], writes=[xr])
                    P.dma("sp", f"fxo{it % 3}", lambda e, x=x, tt=tt, half=half: e.dma_start(
                        out=S["X"][tt * 128:(tt + 1) * 128, half * 512:(half + 1) * 512], in_=x), reads=[xr], writes=[R["X"][tt][half]])
                    it += 1
        P.release(m)

    def phase_hy_in(j):
        P.new_phase()
        m = P.mark()
        CV, CV_r = P.alloc([128, 120], F32, "CV")
        load_cols(CV, CV_r, 0, I["hy_b_in"][j], 24, "cvl")
        for k in range(3):
            load_cols(CV, CV_r, 24 + k * 24, I["hy_conv_w"][j, k], 24, "cvl")
        load_cols(CV, CV_r, 96, I["hy_conv_b"][j], 24, "cvl")
        ubS = [P.alloc([128, TS + 2], F32, f"ubS{k}") for k in range(2)]
        ubP = [P.alloc([128, 2, 258], F32, f"ubP{k}") for k in range(2)]
        for k in range(2):
            P.op("pool", lambda e, k=k: e.memset(ubS[k][0], 0.0), writes=[ubS[k][1]])
            P.op("pool", lambda e, k=k: e.memset(ubP[k][0], 0.0), writes=[ubP[k][1]])
        cb1, cb1_r = P.alloc([128, T], F32, "cb1")
        cb2 = [P.alloc([128, T], F32, f"cb2{k}") for k in range(2)]
        vst, vst_r = P.alloc([128, NTT, 128], BF16, "vst")
        wi = [P.alloc([128, 8, 128], BF16, f"wi{k}") for k in range(3)]
        it = 0
        uc = 0
        c2 = 0
        for jd in range(8):
            for part, fc in (("x1", 8 + jd), ("v", 16 + jd), ("x0", jd)):
                w, wr = wi[it % 3]
                it += 1
                stream_w(w, wr, f"wi{it % 3}", I["hy_w_in"][j][:, fc * 128:(fc + 1) * 128])
                uS, uS_r = ubS[uc % 2]
                uP, uP_r = ubP[uc % 2]
                uc += 1
                for tile in range(NT):
                    b = tile % 4
                    mm_wstat(b, w, wr, tile)
                    if tile < 8:
                        P.op("act", lambda e, uS=uS, b=b, tile=tile, fc=fc: e.activation(
                            out=uS[:, 1 + tile * 512: 1 + (tile + 1) * 512], in_=bank(b), func=AF.Identity, bias=CV[:, fc:fc + 1]),
                            reads=[psr[b], CV_r], acc_writes=[uS_r])
                    else:
                        P.op("act", lambda e, uP=uP, b=b, fc=fc: e.activation(
                            out=uP[:, :, 1:257], in_=bank(b).rearrange("p (s t) -> p s t", t=256), func=AF.Identity, bias=CV[:, fc:fc + 1]),
                            reads=[psr[b], CV_r], acc_writes=[uP_r])
                if part == "x1":
                    dst, dst_r = cb1, cb1_r
                else:
                    dst, dst_r = cb2[c2 % 2]
                    c2 += 1
                w0, w1, w2, cbc = (CV[:, 24 + fc:25 + fc], CV[:, 48 + fc:49 + fc], CV[:, 72 + fc:73 + fc], CV[:, 96 + fc:97 + fc])
                dS = dst[:, 0:TS]
                dP = dst[:, TS:T].rearrange("p (s t) -> p s t", t=256)
                for (dd, uu, ur, n) in ((dS, uS, uS_r, TS), (dP, uP, uP_r, 256)):
                    def sl(o, uu=uu, n=n):
                        return uu[:, o:o + n] if len(uu.shape) == 2 else uu[:, :, o:o + n]
                    P.op("dve", lambda e, dd=dd, sl=sl, w0=w0, cbc=cbc: e.tensor_scalar(out=dd, in0=sl(0), scalar1=w0, scalar2=cbc, op0=ALU.mult, op1=ALU.add),
                         reads=[ur, CV_r], acc_writes=[dst_r])
                    P.op("dve", lambda e, dd=dd, sl=sl, w1=w1: e.scalar_tensor_tensor(out=dd, in0=sl(1), scalar=w1, in1=dd, op0=ALU.mult, op1=ALU.add),
                         reads=[ur, CV_r, dst_r], acc_writes=[dst_r])
                    P.op("dve", lambda e, dd=dd, sl=sl, w2=w2: e.scalar_tensor_tensor(out=dd, in0=sl(2), scalar=w2, in1=dd, op0=ALU.mult, op1=ALU.add),
                         reads=[ur, CV_r, dst_r], acc_writes=[dst_r])
                if part == "v":
                    P.op("pool", lambda e, dst=dst: e.tensor_tensor(out=dst, in0=dst, in1=cb1, op=ALU.mult), reads=[dst_r, cb1_r], writes=[dst_r])
                    P.dma("sp", f"vvt{c2 % 2}", lambda e, dst=dst, jd=jd: e.dma_start(out=S["VVT"][jd], in_=dst), reads=[dst_r], acc_writes=[R["VVT"]])
                    for t4 in range(NTT // 4):
                        b = 4 + t4 % 4
                        for q in range(4):
                            tt = t4 * 4 + q
                            P.op("pe", lambda e, dst=dst, tt=tt, q=q, b=b: e.transpose(
                                bank(b)[:, q * 128:(q + 1) * 128], dst[:, tt * 128:(tt + 1) * 128], identf),
                                reads=[dst_r, identf_r], writes=[psr[b]] if q == 0 else (), acc_writes=[psr[b]] if q else ())
                        P.op("act", lambda e, t4=t4, b=b: e.copy(out=vst[:, t4 * 4:(t4 + 1) * 4, :], in_=bank(b).rearrange("p (q d) -> p q d", d=128)),
                             reads=[psr[b]], acc_writes=[vst_r])
                    P.dma("sp", "vsto", lambda e, jd=jd: e.dma_start(
                        out=S["VVTOK"][:, jd * 128:(jd + 1) * 128].rearrange("(c p) d -> p c d", p=128), in_=vst),
                        reads=[vst_r], acc_writes=[R["VVTOK"]])
                    vst_r.w = dict(vst_r.w)
                if part == "x0":
                    P.dma("sp", f"x0t{c2 % 2}", lambda e, dst=dst, jd=jd: e.dma_start(out=S["X0T"][jd], in_=dst), reads=[dst_r], acc_writes=[R["X0T"]])
        P.release(m)

    def phase_hy_filter(j, G):
        L = G["L"]
        nm = G["name"]
        ntc = L // 128
        HS, HD = S["HS" + nm], S["HD" + nm]
        P.new_phase()
        rn, rn_r = P.alloc([128, D], F32, "rn")
        m = P.mark()
        zT, zT_r = P.alloc([33, L], F32, "zT")
        w1, w1_r = P.alloc([33, 64], F32, "fw1")
        w2, w2_r = P.alloc([64, 64], F32, "fw2")
        w3, w3_r = P.alloc([64, 2 * D], F32, "fw3")
        b3, b3_r = P.alloc([128, 2 * D], F32, "fb3")
        fcol, fcol_r = P.alloc([64, 4], F32, "fcol")
        negt, negt_r = P.alloc([128, ntc], F32, "negt")
        drep, drep_r = P.alloc([128, D], F32, "drep")
        mask0, mask0_r = P.alloc([128, 1], F32, "mask0")
        h1, h1_r = P.alloc([64, L], F32, "h1")
        h2, h2_r = P.alloc([64, L], F32, "h2")
        tmp = [P.alloc([64, 512], F32, f"ftmp{k}") for k in range(2)]
        P.dma("sp", "zT", lambda e: e.dma_start(out=zT, in_=I["z" + nm]), writes=[zT_r])
        P.dma("sp", "fw1", lambda e: e.dma_start(out=w1, in_=I["filt_w1"][j]), writes=[w1_r])
        P.dma("sp", "fw2", lambda e: e.dma_start(out=w2, in_=I["filt_w2"][j]), writes=[w2_r])
        P.dma("sp", "fw3", lambda e: e.dma_start(out=w3, in_=I["filt_w3"][j]), writes=[w3_r])
        P.dma("sp", "fb3", lambda e: e.dma_start(out=b3, in_=bcast_rows(I["filt_b3"][j], 2 * D)), writes=[b3_r])
        P.dma("sp", "negt", lambda e: e.dma_start(out=negt, in_=I["negt" + nm]), writes=[negt_r])
        P.dma("sp", "drep", lambda e: e.dma_start(out=drep, in_=bcast_rows(I["delta"], D)), writes=[drep_r])
        P.dma("sp", "mask0", lambda e: e.dma_start(out=mask0, in_=I["mask0"]), writes=[mask0_r])
        for k, v in enumerate((I["filt_b1"][j], I["filt_b2"][j], I["filt_freq"][j])):
            P.dma("sp", "fcol", lambda e, k=k, v=v: e.dma_start(out=fcol[:, k:k + 1], in_=v.rearrange("(p o) -> p o", o=1)), acc_writes=[fcol_r])
        TWO_PI = 2.0 * math.pi
        SC = TWO_PI * (1.0 - 2e-6)
        wr_, wr_r = P.alloc([64, 512], F32, "fwrap")

        def sin_layer(lhsT, lhsT_r, src, src_r, dstb, dstb_r, bcol):
            n = min(512, L)
            for ti in range(L // n):
                b = ti % 2
                t_, t_r = tmp[ti % 2]
                P.op("pe", lambda e, ti=ti, b=b: e.matmul(bank(b)[0:64, 0:n], lhsT=lhsT, rhs=src[:, ti * n:(ti + 1) * n], start=True, stop=True),
                     reads=[lhsT_r, src_r], writes=[psr[b]])
                P.op("dve", lambda e, t_=t_, b=b: e.tensor_scalar(
                    out=t_[:, 0:n], in0=bank(b)[0:64, 0:n], scalar1=fcol[:, bcol:bcol + 1], scalar2=fcol[:, 2:3], op0=ALU.add, op1=ALU.mult),
                    reads=[psr[b], fcol_r], writes=[t_r])
                for rnd in range(2):
                    for (cmp_, thr, sgn) in ((ALU.is_lt, -math.pi, ALU.add), (ALU.is_gt, math.pi, ALU.subtract)):
                        P.op("dve", lambda e, t_=t_, cmp_=cmp_, thr=thr: e.tensor_scalar(
                            out=wr_[:, 0:n], in0=t_[:, 0:n], scalar1=thr, scalar2=TWO_PI, op0=cmp_, op1=ALU.mult),
                            reads=[t_r], writes=[wr_r])
                        P.op("dve", lambda e, t_=t_, sgn=sgn: e.tensor_tensor(out=t_[:, 0:n], in0=t_[:, 0:n], in1=wr_[:, 0:n], op=sgn),
                             reads=[t_r, wr_r], writes=[t_r])
                P.op("act", lambda e, t_=t_, ti=ti: e.activation(out=dstb[:, ti * n:(ti + 1) * n], in_=t_[:, 0:n], func=AF.Sin, scale=1.0 - 2e-6),
                     reads=[t_r], acc_writes=[dstb_r])
        sin_layer(w1, w1_r, zT, zT_r, h1, h1_r, 0)
        sin_layer(w2, w2_r, h1, h1_r, h2, h2_r, 1)
        dec = [P.alloc([128, D], F32, "dec0")] * 2
        hf = [P.alloc([128, D], F32, f"hf{k}") for k in range(2)]
        hb_ = [P.alloc([128, D], F32, f"hbk{k}") for k in range(2)]
        ab = [P.alloc([128, 2 * D], F32, "ab0")] * 2
        hs = [P.alloc([128, D], BF16, f"hs{k}") for k in range(2)]
        hd = [P.alloc([128, D], BF16, f"hd{k}") for k in range(2)]
        for tc in range(ntc):
            k = tc % 2
            for cb in range(4):
                P.op("pe", lambda e, tc=tc, cb=cb: e.matmul(bank(cb), lhsT=h2[:, tc * 128:(tc + 1) * 128], rhs=w3[:, cb * 512:(cb + 1) * 512], start=True, stop=True),
                     reads=[h2_r, w3_r], writes=[psr[cb]])
            P.op("act", lambda e, k=k, tc=tc: e.activation(out=dec[k][0], in_=drep, func=AF.Exp, scale=negt[:, tc:tc + 1]),
                 reads=[drep_r, negt_r], writes=[dec[k][1]])
            for (dst, lo) in ((hf[k], 0), (hb_[k], D)):
                P.op("dve", lambda e, dst=dst, lo=lo: e.tensor_tensor(out=dst[0], in0=PS[:, lo:lo + D], in1=b3[:, lo:lo + D], op=ALU.add),
                     reads=[psr[lo // 512], psr[lo // 512 + 1], b3_r], writes=[dst[1]])
                P.op("dve", lambda e, dst=dst, k=k: e.tensor_tensor(out=dst[0], in0=dst[0], in1=dec[k][0], op=ALU.mult),
                     reads=[dst[1], dec[k][1]], writes=[dst[1]])
                P.op("act", lambda e, dst=dst, lo=lo, k=k: e.activation(out=ab[k][0][:, lo:lo + D], in_=dst[0], func=AF.Abs),
                     reads=[dst[1]], acc_writes=[ab[k][1]])
            for q in range(4):
                bq = 4 + q % 2
                first = (tc == 0 and q < 2)
                last = (tc == ntc - 1 and q >= 2)
                P.op("pe", lambda e, k=k, q=q, bq=bq, first=first, last=last: e.matmul(
                    bank(bq), lhsT=onesf, rhs=ab[k][0][:, q * 512:(q + 1) * 512], start=first, stop=last),
                    reads=[onesf_r, ab[k][1]], writes=[psr[bq]] if first else (), acc_writes=() if first else [psr[bq]])
            if tc == 0:
                P.op("dve", lambda e, k=k: e.tensor_scalar(out=hb_[k][0], in0=hb_[k][0], scalar1=mask0[:, 0:1], scalar2=None, op0=ALU.mult),
                     reads=[hb_[k][1], mask0_r], writes=[hb_[k][1]])
            P.op("pool", lambda e, k=k: e.tensor_tensor(out=hs[k][0], in0=hf[k][0], in1=hb_[k][0], op=ALU.add),
                 reads=[hf[k][1], hb_[k][1]], writes=[hs[k][1]])
            P.op("pool", lambda e, k=k: e.tensor_tensor(out=hd[k][0], in0=hb_[k][0], in1=hf[k][0], op=ALU.subtract),
                 reads=[hf[k][1], hb_[k][1]], writes=[hd[k][1]])
            P.dma("sp", f"hso{k}", lambda e, k=k, tc=tc: e.dma_start(out=HS[tc * 128:(tc + 1) * 128, :], in_=hs[k][0]), reads=[hs[k][1]], acc_writes=[R["HS"]])
            P.dma("sp", f"hdo{k}", lambda e, k=k, tc=tc: e.dma_start(out=HD[tc * 128:(tc + 1) * 128, :], in_=hd[k][0]), reads=[hd[k][1]], acc_writes=[R["HS"]])
        P.op("dve", lambda e: e.tensor_scalar(out=rn, in0=PS[:, 4 * 512:6 * 512], scalar1=1e-6, scalar2=None, op0=ALU.add),
             reads=[psr[4], psr[5]], writes=[rn_r])
        P.op("dve", lambda e: e.reciprocal(out=rn, in_=rn), reads=[rn_r], writes=[rn_r])
        P.release(m)
        return rn, rn_r

    def phase_hy_conv(j, G, s, rn, rn_r):
        L = G["L"]
        nm = G["name"]
        ntc = L // 128
        nfc = 33 if nm == "s" else 3
        SQ = nfc * 128
        Qt, Rt, WFd = I["q" + nm], I["r" + nm], I["wf" + nm]
        HS, HD = S["HS" + nm], S["HD" + nm]
        tok0 = G["tok0"] + s * L
        P.new_phase()
        m = P.mark()
        wf, wf_r = P.alloc([128, nfc], F32, "wf")
        P.dma("sp", "wf", lambda e: e.dma_start(out=wf, in_=WFd), writes=[wf_r])
        vvhs, vvhs_r = P.alloc([128, ntc, 512], BF16, "vvhs")
        vvhd, vvhd_r = P.alloc([128, ntc, 512], BF16, "vvhd")
        qch = [P.alloc([128, ntc, 128], BF16, f"qch{k}") for k in range(2)]
        rch = [P.alloc([128, ntc, 128], BF16, f"rch{k}") for k in range(2)]
        kcs = [P.alloc([128, 256], F32, f"kcs{k}") for k in range(2)]
        kss = [P.alloc([128, 256], F32, f"kss{k}") for k in range(2)]
        ta = [P.alloc([128, 256], F32, f"ta{k}") for k in range(2)]
        tb = [P.alloc([128, 256], F32, f"tb{k}") for k in range(2)]
        yst = [P.alloc([128, 2, 256], BF16, f"yst{k}") for k in range(2)]
        it = 0
        for dq in range(4):
            dh = dq // 2
            vsrc = S["VVTOK"][tok0:tok0 + L, dq * 256:(dq + 1) * 256].rearrange("(c p) d -> p c d", p=128)
            P.dma("sp", "vva", lambda e, vsrc=vsrc: e.dma_start(out=vvhs[:, :, 0:256], in_=vsrc), reads=[R["VVTOK"]], writes=[vvhs_r])
            P.dma("sp", "vvb", lambda e, vsrc=vsrc: e.dma_start(out=vvhd[:, :, 0:256], in_=vsrc), reads=[R["VVTOK"]], writes=[vvhd_r])
            P.dma("sp", "hsh", lambda e, dq=dq: e.dma_start(
                out=vvhs[:, :, 256:512], in_=HS[:, dq * 256:(dq + 1) * 256].rearrange("(c p) d -> p c d", p=128)), reads=[R["HS"]], acc_writes=[vvhs_r])
            P.dma("sp", "hdh", lambda e, dq=dq: e.dma_start(
                out=vvhd[:, :, 256:512], in_=HD[:, dq * 256:(dq + 1) * 256].rearrange("(c p) d -> p c d", p=128)), reads=[R["HS"]], acc_writes=[vvhd_r])
            for fc in range(nfc):
                qc, qc_r = qch[it % 2]
                rc, rc_r = rch[it % 2]
                P.dma("sp", f"qch{it % 2}", lambda e, qc=qc, fc=fc: e.dma_start(
                    out=qc, in_=Qt[0:ntc, :, fc * 128:(fc + 1) * 128].rearrange("c p f -> p c f")), writes=[qc_r])
                P.dma("sp", f"rch{it % 2}", lambda e, rc=rc, fc=fc: e.dma_start(
                    out=rc, in_=Rt[0:ntc, :, fc * 128:(fc + 1) * 128].rearrange("c p f -> p c f")), writes=[rc_r])
                b0 = (it % 4) * 2
                bC, bS = b0, b0 + 1
                for tc in range(ntc):
                    st, sp_ = (tc == 0), (tc == ntc - 1)
                    for (bb, tab, tab_r, mov, mov_r) in ((bC, qc, qc_r, vvhs, vvhs_r), (bS, rc, rc_r, vvhd, vvhd_r)):
                        P.op("pe", lambda e, bb=bb, tab=tab, mov=mov, tc=tc, st=st, sp_=sp_: e.matmul(
                            bank(bb), lhsT=tab[:, tc, :], rhs=mov[:, tc, :], start=st, stop=sp_),
                            reads=[tab_r, mov_r], writes=[psr[bb]] if st else (), acc_writes=() if st else [psr[bb]])
                bVc = bKc = bC
                bVs = bKs = bS
                Vc_, Kc_ = bank(bC)[:, 0:256], bank(bC)[:, 256:512]
                Vs_, Ks_ = bank(bS)[:, 0:256], bank(bS)[:, 256:512]
                k = it % 2
                wcol = wf[:, fc:fc + 1]
                rsl = rn[:, dq * 256:(dq + 1) * 256]
                P.op("dve", lambda e, k=k, Kc_=Kc_, wcol=wcol, rsl=rsl: e.scalar_tensor_tensor(
                    out=kcs[k][0], in0=Kc_, scalar=wcol, in1=rsl, op0=ALU.mult, op1=ALU.mult),
                    reads=[psr[bKc], wf_r, rn_r], writes=[kcs[k][1]])
                P.op("dve", lambda e, k=k, Ks_=Ks_, wcol=wcol, rsl=rsl: e.scalar_tensor_tensor(
                    out=kss[k][0], in0=Ks_, scalar=wcol, in1=rsl, op0=ALU.mult, op1=ALU.mult),
                    reads=[psr[bKs], wf_r, rn_r], writes=[kss[k][1]])
                P.op("dve", lambda e, k=k, Vc_=Vc_: e.tensor_tensor(out=ta[k][0], in0=Vc_, in1=kcs[k][0], op=ALU.mult),
                     reads=[psr[bVc], kcs[k][1]], writes=[ta[k][1]])
                P.op("dve", lambda e, k=k, Vs_=Vs_: e.tensor_tensor(out=tb[k][0], in0=Vs_, in1=kss[k][0], op=ALU.mult),
                     reads=[psr[bVs], kss[k][1]], writes=[tb[k][1]])
                P.op("pool", lambda e, k=k: e.tensor_tensor(out=yst[k][0][:, 0, :], in0=ta[k][0], in1=tb[k][0], op=ALU.add),
                     reads=[ta[k][1], tb[k][1]], writes=[yst[k][1]])
                P.op("dve", lambda e, k=k, Vs_=Vs_: e.tensor_tensor(out=ta[k][0], in0=Vs_, in1=kcs[k][0], op=ALU.mult),
                     reads=[psr[bVs], kcs[k][1]], writes=[ta[k][1]])
                P.op("dve", lambda e, k=k, Vc_=Vc_: e.tensor_tensor(out=tb[k][0], in0=Vc_, in1=kss[k][0], op=ALU.mult),
                     reads=[psr[bVc], kss[k][1]], writes=[tb[k][1]])
                P.op("pool", lambda e, k=k: e.tensor_tensor(out=yst[k][0][:, 1, :], in0=ta[k][0], in1=tb[k][0], op=ALU.subtract),
                     reads=[ta[k][1], tb[k][1]], acc_writes=[yst[k][1]])
                P.dma("sp", f"yfo{k}", lambda e, k=k, dh=dh, dq=dq, fc=fc: e.dma_start(out=S["YF"][dh, fc][:, :, (dq % 2) * 256:(dq % 2 + 1) * 256], in_=yst[k][0]),
                      reads=[yst[k][1]], acc_writes=[R["YF"]])
                it += 1
        P.release(m)
        P.new_phase()
        m = P.mark()
        skc, skc_r = P.alloc([128, 8], F32, "skc")
        load_cols(skc, skc_r, 0, I["hy_skip"][j], 8, "skc")
        Yg, Yg_r = P.alloc([128, nfc, 2, 512], BF16, "Yg")
        n = min(512, L)
        tq = [P.alloc([128, 2, n], BF16, f"tq{k}") for k in range(6)]
        vvt = [P.alloc([128, n], F32, f"vvt{k}") for k in range(3)]
        x0t = [P.alloc([128, n], F32, f"x0t{k}") for k in range(3)]
        it = 0
        ie = 0
        for dh in range(2):
            P.dma("sp", "Yg", lambda e, dh=dh: e.dma_start(out=Yg, in_=S["YF"][dh, 0:nfc].rearrange("c p s d -> p c s d")),
                  reads=[R["YF"]], writes=[Yg_r])
            for tt in range(L // n):
                b0 = ((dh * (L // n) + tt) % 2) * 4
                for fc in range(nfc):
                    t_, t_r = tq[it % 6]
                    P.dma("sp", f"tq{it % 6}a", lambda e, t_=t_, fc=fc, tt=tt: e.dma_start(out=t_[:, 0, :], in_=Qt[fc, :, tt * n:(tt + 1) * n]), writes=[t_r])
                    P.dma("sp", f"tq{it % 6}b", lambda e, t_=t_, fc=fc, tt=tt: e.dma_start(out=t_[:, 1, :], in_=Rt[fc, :, tt * n:(tt + 1) * n]), acc_writes=[t_r])
                    it += 1
                    for dcl in range(4):
                        for cs in range(2):
                            st = (fc == 0 and cs == 0)
                            sp_ = (fc == nfc - 1 and cs == 1)
                            P.op("pe", lambda e, t_=t_, fc=fc, dcl=dcl, cs=cs, st=st, sp_=sp_, b0=b0: e.matmul(
                                bank(b0 + dcl, n), lhsT=Yg[:, fc, cs, dcl * 128:(dcl + 1) * 128], rhs=t_[:, cs, :], start=st, stop=sp_),
                                reads=[Yg_r, t_r], writes=[psr[b0 + dcl]] if st else (), acc_writes=() if st else [psr[b0 + dcl]])
                for dcl in range(4):
                    dc = dh * 4 + dcl
                    v_, v_r = vvt[ie % 3]
                    x_, x_r = x0t[ie % 3]
                    ie += 1
                    c0 = tok0 + tt * n
                    P.dma("sp", f"vvt{ie % 3}", lambda e, v_=v_, dc=dc, c0=c0: e.dma_start(out=v_, in_=S["VVT"][dc, :, c0:c0 + n]), reads=[R["VVT"]], writes=[v_r])
                    P.dma("sp", f"x0t{ie % 3}", lambda e, x_=x_, dc=dc, c0=c0: e.dma_start(out=x_, in_=S["X0T"][dc, :, c0:c0 + n]), reads=[R["X0T"]], writes=[x_r])
                    P.op("dve", lambda e, v_=v_, dc=dc, dcl=dcl, b0=b0: e.scalar_tensor_tensor(
                        out=v_, in0=v_, scalar=skc[:, dc:dc + 1], in1=bank(b0 + dcl, n), op0=ALU.mult, op1=ALU.add),
                        reads=[v_r, skc_r, psr[b0 + dcl]], writes=[v_r])
                    P.op("pool", lambda e, v_=v_, x_=x_, dc=dc, c0=c0: e.tensor_tensor(out=hT[:, dc, c0:c0 + n], in0=v_, in1=x_, op=ALU.mult),
                         reads=[v_r, x_r], acc_writes=[hTr[c0 // 512]])
        P.release(m)

    class Stop(Exception):
        pass

    def chk(tag):
        if stop_after == tag:
            raise Stop()

    def dump_hT():
        for t in range(NT):
            P.dma("sp", "htd", lambda e, t=t: e.dma_start(out=S["HTD"][:, :, t * 512:(t + 1) * 512], in_=hT[:, :, t * 512:(t + 1) * 512]),
                  reads=[hTr[t]], acc_writes=[R["HTD"]])

    try:
        for i in range(4):
            j = i // 2
            phase_mod(i)
            chk(f"mod{i}")
            phase_norm(D, 0)
            chk(f"norm1_{i}")
            if i % 2 == 0:
                phase_qkv(j)
                chk(f"qkv{i}")
                phase_attn(j, i)
                chk(f"attn{i}")
                phase_outproj(I["attn_w_o"][j], None, 2 * D)
            else:
                phase_hy_in(j)
                chk(f"hyin{i}")
                for G in GROUPS:
                    mk = P.mark()
                    rn, rn_r = phase_hy_filter(j, G)
                    chk(f"hyfilt{i}{G['name']}")
                    for s in range(G["nseq"]):
                        phase_hy_conv(j, G, s, rn, rn_r)
                    P.release(mk)
                chk(f"hyconv{i}")
                phase_outproj(I["hy_w_out"][j], I["hy_b_out"][j], 2 * D)
            chk(f"mix{i}")
            phase_norm(4 * D, 3 * D)
            phase_ffn(i)
            chk(f"ffn{i}")
        phase_norm(0, 0, final=True)
    except Stop:
        if "HTD" in debug_outs:
            dump_hT()

    final_keys = [k for k in P.dma_cnt]
    print("n dma sems", len(final_keys), {e: len(P.q[e]) for e in ENGS})
    P.emit(final_keys)
    return nc, P


_CONSTS = None


def _core_inputs(b, inp, consts):
    m = {}
    m["x"] = np.ascontiguousarray(np.concatenate(
        [inp["x_sample"][b], inp["x_prompt"][2 * b], inp["x_prompt"][2 * b + 1]], axis=0).astype(np.float32))
    m["ck"] = np.ascontiguousarray(inp["cache_k"][b].reshape(2, 512, D).astype(np.float32))
    m["cv"] = np.ascontiguousarray(inp["cache_v"][b].reshape(2, 512, D).astype(np.float32))
    m["cvec"] = np.ascontiguousarray(np.stack([inp["c"][b], inp["c_ctx"]], axis=0).astype(np.float32))
    for nm, shp in W_SPECS:
        m[nm] = np.ascontiguousarray(np.asarray(inp[nm], dtype=np.float32).reshape(shp))
    for nm, shp, dt in CONST_SPECS:
        m[nm] = consts[nm]
    return m


def kernel(**inputs):
    global _CONSTS
    if _CONSTS is None:
        _CONSTS = _host_consts()
    inp = {k: np.asarray(v) for k, v in inputs.items()}
    nc, _ = build_program()
    in_maps = [_core_inputs(b, inp, _CONSTS) for b in range(8)]
    res = run_bass_kernel_spmd(nc, in_maps, core_ids=list(range(8)))
    y_prompt = np.zeros((16, 256, D), np.float32)
    y_sample = np.zeros((8, TS, D), np.float32)
    nk = np.zeros((16, 2, 256, 8, 2, 64), np.float32)
    nv = np.zeros((16, 2, 256, 8, 128), np.float32)
    for b in range(8):
        r = res.results[b]
        y = np.asarray(r["y"], dtype=np.float32)
        y_sample[b] = y[:TS]
        y_prompt[2 * b] = y[TS:TS + 256]
        y_prompt[2 * b + 1] = y[TS + 256:]
        k_ = np.asarray(r["nk"], dtype=np.float32).reshape(2, 2, 256, 8, 2, 64)
        v_ = np.asarray(r["nv"], dtype=np.float32).reshape(2, 2, 256, 8, 128)
        nk[2 * b], nk[2 * b + 1] = k_[0], k_[1]
        nv[2 * b], nv[2 * b + 1] = v_[0], v_[1]
    return (y_prompt, y_sample, nk, nv)
```

```python
import contextlib
import os
import math
import numpy as np
import ml_dtypes
import concourse.bass as bass
import concourse.mybir as mybir
from concourse.bass_utils import run_bass_kernel_spmd

F32 = mybir.dt.float32
BF16 = mybir.dt.bfloat16
AF = mybir.ActivationFunctionType
ALU = mybir.AluOpType
AX = mybir.AxisListType

ENGS = ("pe", "act", "dve", "pool", "sp")
D = 1024
TS, TP, T = 4096, 512, 4608
NT = 9
NTT = 36
DFF = 2816
NCH_FF = 22
EPS = 1e-6
SUBLN_EPS = 1e-5
NKEY = 5120


class Res:
    __slots__ = ("w", "r", "name", "excl")

    def __init__(self, name="", excl=False):
        self.w = {}
        self.r = {}
        self.name = name
        self.excl = excl


class Prog:
    def __init__(self, nc, arena_words):
        self.nc = nc
        self.q = {e: [] for e in ENGS}
        self.known = {e: {} for e in ENGS}
        self.dma_cnt = {}
        self.stack = contextlib.ExitStack()
        self.arena = self.stack.enter_context(nc.sbuf_tensor("arena", [128, arena_words], F32))
        self.arena_words = arena_words
        self.top = 0
        self.live = []
        self.retired = []
        self.nsem = 0
        self.keymap = {}
        self.keyres = {}

    def alloc(self, shape, dt, name=""):
        esz = 4 if dt == F32 else 2
        free = 1
        for s in shape[1:]:
            free *= s
        words = (free * esz + 3) // 4
        words = (words + 7) // 8 * 8
        off = self.top
        assert off + words <= self.arena_words, f"SBUF arena overflow {name} {off + words}"
        self.top += words
        v = self.arena[0:shape[0], off:off + (free * esz) // 4]
        if dt != F32:
            v = v.bitcast(dt)
        if len(shape) == 3:
            v = v.rearrange("p (a b) -> p a b", b=shape[2])
        elif len(shape) == 4:
            v = v.rearrange("p (a b c) -> p a b c", b=shape[2], c=shape[3])
        r = Res(name)
        keep = []
        for (a, b, rr) in self.retired:
            if a < off + words and off < b:
                for k, val in rr.w.items():
                    if r.r.get(k, -1) < val:
                        r.r[k] = val
                for k, val in rr.r.items():
                    if r.r.get(k, -1) < val:
                        r.r[k] = val
                if a >= off and b <= off + words:
                    continue
            keep.append((a, b, rr))
        self.retired = keep
        self.live.append((off, off + words, r))
        return v, r

    def mark(self):
        return (self.top, len(self.live))

    def release(self, m):
        top, n = m
        self.retired.extend(self.live[n:])
        del self.live[n:]
        self.top = top

    def _add(self, eng, fn, reads, writes, acc_writes, own):
        deps = {}

        def upd(d):
            for k, v in d.items():
                if deps.get(k, -1) < v:
                    deps[k] = v
        for r in reads:
            upd(r.w)
            if r.excl:
                upd({k: v for k, v in r.r.items() if k != ("c", eng)})
        for w in writes:
            upd(w.w)
            upd(w.r)
        for w in acc_writes:
            upd(w.r)
            upd({k: v for k, v in w.w.items() if k != own})
        q = self.q[eng]
        idx = len(q)
        waits = []
        kn = self.known[eng]
        for k, v in deps.items():
            if k == ("c", eng):
                if eng == "pe":
                    continue
                vv = -1
                for r in reads:
                    x = r.w.get(k, -1)
                    if x > vv:
                        vv = x
                if vv < 0:
                    continue
                v = vv
            if k[0] == "d":
                v = self.dma_cnt[k[1]]
            if kn.get(k, -1) >= v:
                continue
            kn[k] = v
            waits.append((k, v))
            if k[0] == "c":
                self.q[k[1]][v][2] = True
        op = [fn, waits, False, None]
        q.append(op)
        return op, idx

    def op(self, eng, fn, reads=(), writes=(), acc_writes=()):
        op, idx = self._add(eng, fn, reads, writes, acc_writes, ("c", eng))
        k = ("c", eng)
        for r in reads:
            r.r[k] = idx
        for w in writes:
            w.w = {k: idx}
            w.r = {}
        for w in acc_writes:
            w.w[k] = idx
        return op

    def new_phase(self):
        self.keymap = {}

    def dma(self, eng, semkey, fn, reads=(), writes=(), acc_writes=()):
        km = self.keymap
        if semkey not in km:
            km[semkey] = f"g{len(km)}"
        semkey = km[semkey]
        kr = self.keyres.get(semkey)
        if kr is None:
            kr = self.keyres[semkey] = Res(semkey)
        writes = list(writes) + [kr]
        op, idx = self._add(eng, fn, reads, writes, acc_writes, ("d", semkey))
        c = self.dma_cnt.get(semkey, 0) + 1
        self.dma_cnt[semkey] = c
        op[3] = semkey
        k = ("d", semkey)
        for r in reads:
            r.r[k] = c
        for w in writes:
            w.w = {k: c}
            w.r = {}
        for w in acc_writes:
            w.w[k] = c
        return op

    def emit(self, final_keys):
        nc = self.nc
        st = self.stack
        csem = {e: st.enter_context(nc.semaphore(f"c_{e}")) for e in ENGS if e != "sp"}
        dsem = {k: st.enter_context(nc.semaphore(f"d_{i}")) for i, k in enumerate(self.dma_cnt)}
        cum = {}
        for e in ENGS:
            c = 0
            arr = []
            for o in self.q[e]:
                if o[2]:
                    c += 1
                arr.append(c)
            cum[e] = arr
        engobj = {"pe": "tensor", "act": "scalar", "dve": "vector", "pool": "gpsimd", "sp": "sync"}
        with nc.Block() as block:
            for e in ENGS:
                ops = self.q[e]

                def body(eng, e=e, ops=ops):
                    for fn, waits, sig, semkey in ops:
                        for k, v in waits:
                            if k[0] == "c":
                                eng.wait_ge(csem[k[1]], cum[k[1]][v])
                            else:
                                eng.wait_ge(dsem[k[1]], 16 * v)
                        ins = fn(eng)
                        if semkey is not None:
                            ins.then_inc(dsem[semkey], 16)
                        elif sig:
                            ins.then_inc(csem[e], 1)
                    if e == "sp":
                        for k in final_keys:
                            eng.wait_ge(dsem[k], 16 * self.dma_cnt[k])
                getattr(block, engobj[e])(body)


def _bf(a):
    return np.ascontiguousarray(a.astype(ml_dtypes.bfloat16))


def _dft_tables(L):
    N = 2 * L
    nf = L + 1
    nch = (nf + 127) // 128
    S = nch * 128
    a = np.arange(S, dtype=np.int64)
    m = (a[:, None] * a[None, :]) % N
    ang = 2.0 * np.pi * m.astype(np.float64) / N
    valid = (a[:, None] <= L) & (a[None, :] <= L)
    q = np.where(valid, np.cos(ang), 0.0)
    r = np.where(valid, np.sin(ang), 0.0)
    wf = np.where(a <= L, 2.0 / N, 0.0)
    wf[0] = 1.0 / N
    wf[L] = 1.0 / N
    wfc = wf.reshape(nch, 128).T.astype(np.float32)
    ntc = L // 128
    qf = q[:L].reshape(ntc, 128, nch, 128).transpose(2, 1, 0, 3)
    rf = r[:L].reshape(ntc, 128, nch, 128).transpose(2, 1, 0, 3)
    return (_bf(q.reshape(nch, 128, S)), _bf(r.reshape(nch, 128, S)), np.ascontiguousarray(wfc), nch, S, _bf(qf), _bf(rf))


def _filter_consts(L):
    pos = np.arange(L, dtype=np.float32)
    t = pos / np.float32(max(L - 1, 1))
    w = (np.float32(2.0 * math.pi) * pos / np.float32(L)).astype(np.float32)
    bands = np.linspace(1e-4, 15, 16, dtype=np.float32)
    z = np.concatenate([t[:, None], np.cos(w[:, None] * bands), -np.sin(w[:, None] * bands)], axis=-1)
    zT = np.ascontiguousarray(z.T.astype(np.float32))
    negt = np.ascontiguousarray((-t).reshape(L // 128, 128).T.astype(np.float32))
    return zT, negt


def _host_consts():
    c = {}
    tpos = np.arange(TS)
    rowpos = (tpos // 64).astype(np.float32)
    colpos = (tpos % 64).astype(np.float32)
    inv = (10000.0 ** (-np.arange(16, dtype=np.float32) / 16)).astype(np.float32)
    cos = np.zeros((128, TS), np.float32)
    sins = np.zeros((128, TS), np.float32)
    perm = np.zeros((128, 128), np.float32)
    for p in range(2):
        for a in range(2):
            posv = rowpos if a == 0 else colpos
            for hf in range(2):
                for f in range(16):
                    row = p * 64 + a * 32 + hf * 16 + f
                    ang = (posv * inv[f]).astype(np.float32)
                    cos[row] = np.cos(ang)
                    sins[row] = np.sin(ang) * (-1.0 if hf == 0 else 1.0)
                    other = p * 64 + a * 32 + (1 - hf) * 16 + f
                    perm[other, row] = 1.0
    c["rcos"] = cos
    c["rsin"] = sins
    c["perm"] = _bf(perm)
    c["identb"] = _bf(np.eye(128, dtype=np.float32))
    c["identf"] = np.eye(128, dtype=np.float32)
    c["onesf"] = np.ones((128, 128), np.float32)
    for nm, L in (("s", TS), ("p", 256)):
        q, r, wf, nch, S, qf, rf = _dft_tables(L)
        c["q" + nm], c["r" + nm], c["wf" + nm] = q, r, wf
        if nm == "s":
            c["qfs"], c["rfs"] = qf, rf
        zT, negt = _filter_consts(L)
        c["z" + nm], c["negt" + nm] = zT, negt
    deltas = np.abs(np.linspace(math.log(1e-2) / 1.5, math.log(1e-2) / 0.3, D, dtype=np.float32))
    c["delta"] = deltas.astype(np.float32)
    m0 = np.ones((128, 1), np.float32)
    m0[0, 0] = 0.0
    c["mask0"] = m0
    return c


W_SPECS = [
    ("ada_w", [4, D, 6 * D]), ("ada_b", [4, 6 * D]), ("norm1_g", [4, D]), ("norm2_g", [4, D]),
    ("attn_w_qkv", [2, D, 3 * D]), ("attn_lambda", [2, 4, 64]), ("attn_subln_g", [2, 128]),
    ("attn_w_o", [2, D, D]), ("hy_w_in", [2, D, 3 * D]), ("hy_b_in", [2, 3 * D]),
    ("hy_conv_w", [2, 3, 3 * D]), ("hy_conv_b", [2, 3 * D]), ("filt_w1", [2, 33, 64]),
    ("filt_b1", [2, 64]), ("filt_w2", [2, 64, 64]), ("filt_b2", [2, 64]), ("filt_w3", [2, 64, 2 * D]),
    ("filt_b3", [2, 2 * D]), ("filt_freq", [2, 64]), ("hy_skip", [2, D]), ("hy_w_out", [2, D, D]),
    ("hy_b_out", [2, D]), ("ffn_w_gu", [4, D, 2 * DFF]), ("ffn_w_down", [4, DFF, D]), ("final_g", [D]),
]
CONST_SPECS = [
    ("rcos", [128, TS], F32), ("rsin", [128, TS], F32), ("perm", [128, 128], BF16),
    ("identb", [128, 128], BF16), ("identf", [128, 128], F32), ("onesf", [128, 128], F32),
    ("qs", [33, 128, 4224], BF16), ("rs", [33, 128, 4224], BF16), ("wfs", [128, 33], F32),
    ("qfs", [33, 128, 32, 128], BF16), ("rfs", [33, 128, 32, 128], BF16),
    ("zs", [33, TS], F32), ("negts", [128, 32], F32),
    ("qp", [3, 128, 384], BF16), ("rp", [3, 128, 384], BF16), ("wfp", [128, 3], F32),
    ("zp", [33, 256], F32), ("negtp", [128, 2], F32),
    ("delta", [D], F32), ("mask0", [128, 1], F32),
]

GROUPS = [
    dict(name="s", tok0=0, T=TS, nseq=1, L=TS, rope=True, ncache=512, tiles=list(range(0, 8)), tt=list(range(0, 32))),
    dict(name="p", tok0=TS, T=TP, nseq=2, L=256, rope=False, ncache=0, tiles=[8], tt=list(range(32, 36))),
]


def build_program(stop_after=None, debug_outs=()):
    nc = bass.Bass("TRN2", target_bir_lowering=False)
    I = {}
    I["x"] = nc.dram_tensor("x", [T, D], F32, kind="ExternalInput").ap()
    I["ck"] = nc.dram_tensor("ck", [2, 512, D], F32, kind="ExternalInput").ap()
    I["cv"] = nc.dram_tensor("cv", [2, 512, D], F32, kind="ExternalInput").ap()
    I["cvec"] = nc.dram_tensor("cvec", [2, D], F32, kind="ExternalInput").ap()
    for nm, shp in W_SPECS:
        I[nm] = nc.dram_tensor(nm, shp, F32, kind="ExternalInput").ap()
    for nm, shp, dt in CONST_SPECS:
        I[nm] = nc.dram_tensor(nm, shp, dt, kind="ExternalInput").ap()
    O = {}
    O["y"] = nc.dram_tensor("y", [T, D], F32, kind="ExternalOutput").ap()
    O["nk"] = nc.dram_tensor("nk", [2, 2, 256, D], F32, kind="ExternalOutput").ap()
    O["nv"] = nc.dram_tensor("nv", [2, 2, 256, D], F32, kind="ExternalOutput").ap()

    def scratch(nm, shp, dt):
        kind = "ExternalOutput" if nm in debug_outs else "Internal"
        return nc.dram_tensor(nm, shp, dt, kind=kind).ap()
    S = {}
    S["X"] = scratch("X", [T, D], F32)
    S["MODS"] = scratch("MODS", [2, 128, 6 * D], F32)
    S["KT"] = scratch("KT", [8, 128, NKEY], BF16)
    S["VS"] = scratch("VS", [NKEY, D], BF16)
    S["QT"] = scratch("QT", [8, 128, T], BF16)
    S["MID"] = scratch("MID", [NCH_FF, 128, T], BF16)
    S["VVT"] = scratch("VVT", [8, 128, T], F32)
    S["X0T"] = scratch("X0T", [8, 128, T], F32)
    S["VVTOK"] = scratch("VVTOK", [T, D], BF16)
    S["HSs"] = scratch("HSs", [TS, D], BF16)
    S["HDs"] = scratch("HDs", [TS, D], BF16)
    S["HSp"] = scratch("HSp", [256, D], BF16)
    S["HDp"] = scratch("HDp", [256, D], BF16)
    S["YF"] = scratch("YF", [2, 33, 128, 2, 512], BF16)
    S["HTD"] = scratch("HTD", [128, 8, T], BF16)

    ARENA_WORDS = 49152
    P = Prog(nc, ARENA_WORDS)
    PS = P.stack.enter_context(nc.psum_tensor("ps", [128, 4096], F32))
    psr = [Res(f"bank{b}", excl=True) for b in range(8)]

    def bank(b, n=512):
        return PS[:, b * 512:b * 512 + n]

    def bankb(b):
        return PS[:, b * 512:(b + 1) * 512].bitcast(BF16)

    R = {}
    R["X"] = [[Res(f"X{i}a"), Res(f"X{i}b")] for i in range(NTT)]
    R["MODS"] = [Res(), Res()]
    R["KT"] = Res()
    R["VS"] = Res()
    R["QT"] = Res()
    R["MID"] = [Res() for _ in range(NT)]
    R["VVT"] = Res()
    R["X0T"] = Res()
    R["VVTOK"] = Res()
    R["HS"] = Res()
    R["YF"] = Res()
    R["OUT"] = Res()
    R["HTD"] = Res()
    semctr = [0]

    def sk(prefix):
        semctr[0] += 1
        return f"{prefix}{semctr[0]}"

    hT, _ = P.alloc([128, 8, T], BF16, "hT")
    hTr = [Res(f"hT{i}") for i in range(NT)]
    identb, identb_r = P.alloc([128, 128], BF16, "identb")
    identf, identf_r = P.alloc([128, 128], F32, "identf")
    onesf, onesf_r = P.alloc([128, 128], F32, "onesf")
    SIL, SIL_r = P.alloc([128, 2, 8, 128], BF16, "SIL")
    P.dma("sp", "c_identb", lambda e: e.dma_start(out=identb, in_=I["identb"]), writes=[identb_r])
    P.dma("sp", "c_identf", lambda e: e.dma_start(out=identf, in_=I["identf"]), writes=[identf_r])
    P.dma("sp", "c_onesf", lambda e: e.dma_start(out=onesf, in_=I["onesf"]), writes=[onesf_r])

    def load_cols(dst, dst_r, col0, vec, nchunks, key):
        P.dma("sp", key, lambda e: e.dma_start(
            out=dst[:, col0:col0 + nchunks], in_=vec.rearrange("(c p) -> p c", p=128), allow_slow_non_contiguous=True),
            acc_writes=[dst_r])

    def bcast_rows(vec1d, n):
        return vec1d.rearrange("(o n) -> o n", o=1).broadcast_to([128, n])

    m0 = P.mark()
    ccol, ccol_r = P.alloc([128, 16], F32, "ccol")
    scol, scol_r = P.alloc([128, 16], F32, "scol")
    for g in range(2):
        load_cols(ccol, ccol_r, g * 8, I["cvec"][g], 8, "ccol")
    P.op("act", lambda e: e.activation(out=scol, in_=ccol, func=AF.Silu), reads=[ccol_r], writes=[scol_r])
    for g in range(2):
        for kc in range(8):
            P.op("dve", lambda e, g=g, kc=kc: e.tensor_scalar(
                out=SIL[:, g, kc, :], in0=onesf, scalar1=scol[:, g * 8 + kc:g * 8 + kc + 1], scalar2=None,
                op0=ALU.mult), reads=[scol_r, onesf_r], acc_writes=[SIL_r])
    P.release(m0)

    for tt in range(NTT):
        P.dma("sp", f"xcopy{tt % 4}", lambda e, tt=tt: e.dma_start(
            out=S["X"][tt * 128:(tt + 1) * 128, :], in_=I["x"][tt * 128:(tt + 1) * 128, :]), writes=R["X"][tt])

    def phase_mod(i):
        P.new_phase()
        m = P.mark()
        adab, adab_r = P.alloc([128, 6 * D], F32, "adab")
        mod = [P.alloc([128, 6 * D], F32, f"mod{g}") for g in range(2)]
        gn = [P.alloc([128, D], F32, f"gn{k}") for k in range(2)]
        wch = [P.alloc([128, 8, 512], BF16, f"adw{k}") for k in range(2)]
        P.dma("sp", "adab", lambda e: e.dma_start(out=adab, in_=bcast_rows(I["ada_b"][i], 6 * D)), writes=[adab_r])
        P.dma("sp", "gn0", lambda e: e.dma_start(out=gn[0][0], in_=bcast_rows(I["norm1_g"][i], D)), writes=[gn[0][1]])
        P.dma("sp", "gn1", lambda e: e.dma_start(out=gn[1][0], in_=bcast_rows(I["norm2_g"][i], D)), writes=[gn[1][1]])
        for n in range(12):
            w, wr = wch[n % 2]
            P.dma("pool", f"adw{n % 2}", lambda e, w=w, n=n: e.dma_start(
                out=w, in_=I["ada_w"][i][:, n * 512:(n + 1) * 512].rearrange("(kc p) n -> p kc n", p=128)), writes=[wr])
            for g in range(2):
                b = (n * 2 + g) % 8
                for kc in range(8):
                    P.op("pe", lambda e, w=w, g=g, kc=kc, b=b: e.matmul(
                        bank(b), lhsT=SIL[:, g, kc, :], rhs=w[:, kc, :], start=(kc == 0), stop=(kc == 7)),
                        reads=[SIL_r, wr], writes=[psr[b]] if kc == 0 else (), acc_writes=[psr[b]] if kc else ())
                P.op("dve", lambda e, g=g, n=n, b=b: e.tensor_tensor(
                    out=mod[g][0][:, n * 512:(n + 1) * 512], in0=bank(b), in1=adab[:, n * 512:(n + 1) * 512], op=ALU.add),
                    reads=[psr[b], adab_r], acc_writes=[mod[g][1]])
        for g in range(2):
            for k, off in ((0, D), (1, 4 * D)):
                P.op("dve", lambda e, g=g, k=k, off=off: e.scalar_tensor_tensor(
                    out=mod[g][0][:, off:off + D], in0=mod[g][0][:, off:off + D], scalar=1.0, in1=gn[k][0],
                    op0=ALU.add, op1=ALU.mult), reads=[mod[g][1], gn[k][1]], acc_writes=[mod[g][1]])
            P.dma("sp", f"mods{g}", lambda e, g=g: e.dma_start(out=S["MODS"][g], in_=mod[g][0]),
                  reads=[mod[g][1]], writes=[R["MODS"][g]])
        P.release(m)

    def load_mod(off, key):
        out = []
        for g in range(2):
            t, r = P.alloc([128, D], F32, f"{key}{g}")
            P.dma("sp", f"ldm_{key}{g}", lambda e, t=t, g=g: e.dma_start(out=t, in_=S["MODS"][g][:, off:off + D]),
                  reads=[R["MODS"][g]], writes=[r])
            out.append((t, r))
        return out

    def rstd_ops(ss, ss_r, rs, rs_r, n, eps):
        P.op("act", lambda e: e.activation(out=rs, in_=ss, func=AF.Ln, scale=1.0 / n, bias=float(eps)),
             reads=[ss_r], writes=[rs_r])
        P.op("act", lambda e: e.activation(out=rs, in_=rs, func=AF.Exp, scale=-0.5),
             reads=[rs_r], writes=[rs_r])

    def phase_norm(offA, offB, final=False):
        P.new_phase()
        m = P.mark()
        if final:
            gt, gr = P.alloc([128, D], F32, "fing")
            P.dma("sp", "fing", lambda e: e.dma_start(out=gt, in_=bcast_rows(I["final_g"], D)), writes=[gr])
            A = [(gt, gr), (gt, gr)]
            B = None
        else:
            A = load_mod(offA, "nA")
            B = load_mod(offB, "nB")
        xt = [P.alloc([128, D], F32, f"nx{k}") for k in range(3)]
        x2 = [P.alloc([128, D], F32, f"nx2{k}") for k in range(2)]
        hb = [P.alloc([128, D], BF16, f"nhb{k}") for k in range(2)]
        junk, junk_r = P.alloc([128, D], BF16, "njunk")
        ss = [P.alloc([128, 1], F32, f"nss{k}") for k in range(2)]
        rs = [P.alloc([128, 1], F32, f"nrs{k}") for k in range(2)]
        for tt in range(NTT):
            g = 0 if tt < 32 else 1
            x, xr = xt[tt % 3]
            y, yr = x2[tt % 2]
            h, hr = hb[tt % 2]
            s_, s_r = ss[tt % 2]
            r_, r_r = rs[tt % 2]
            P.dma("sp", f"nx{tt % 3}", lambda e, x=x, tt=tt: e.dma_start(out=x, in_=S["X"][tt * 128:(tt + 1) * 128, :]),
                  reads=R["X"][tt], writes=[xr])
            P.op("act", lambda e, x=x, s_=s_: e.activation(out=junk, in_=x, func=AF.Square, accum_out=s_),
                 reads=[xr], writes=[junk_r, s_r])
            rstd_ops(s_, s_r, r_, r_r, D, EPS)
            if final:
                P.op("dve", lambda e, x=x, y=y, r_=r_, g=g: e.scalar_tensor_tensor(
                    out=y, in0=x, scalar=r_, in1=A[g][0], op0=ALU.mult, op1=ALU.mult),
                    reads=[xr, r_r, A[g][1]], writes=[yr])
                P.dma("sp", f"fo{tt % 2}", lambda e, y=y, tt=tt: e.dma_start(out=O["y"][tt * 128:(tt + 1) * 128, :], in_=y),
                      reads=[yr], acc_writes=[R["OUT"]])
                continue
            P.op("dve", lambda e, x=x, y=y, r_=r_, g=g: e.scalar_tensor_tensor(
                out=y, in0=x, scalar=r_, in1=A[g][0], op0=ALU.mult, op1=ALU.mult),
                reads=[xr, r_r, A[g][1]], writes=[yr])
            P.op("pool", lambda e, y=y, h=h, g=g: e.tensor_tensor(out=h, in0=y, in1=B[g][0], op=ALU.add),
                 reads=[yr, B[g][1]], writes=[hr])
            b = tt % 2
            for kc in range(8):
                P.op("pe", lambda e, h=h, kc=kc, b=b: e.transpose(bankb(b)[:, kc * 128:(kc + 1) * 128], h[:, kc * 128:(kc + 1) * 128], identb),
                     reads=[hr, identb_r], writes=[psr[b]] if kc == 0 else (), acc_writes=[psr[b]] if kc else ())
            P.op("act", lambda e, b=b, tt=tt: e.copy(out=hT[:, :, tt * 128:(tt + 1) * 128],
                                                     in_=bankb(b).rearrange("p (k t) -> p k t", t=128)),
                 reads=[psr[b]], acc_writes=[hTr[tt // 4]])
        P.release(m)

    def stream_w(dst, dst_r, key, src_ap):
        P.dma("pool", key, lambda e: e.dma_start(out=dst, in_=src_ap.rearrange("(kc p) n -> p kc n", p=128)), writes=[dst_r])

    def mm_wstat(b, w, wr, tile, ncols=512):
        for kc in range(8):
            P.op("pe", lambda e, kc=kc: e.matmul(bank(b, ncols), lhsT=w[:, kc, :], rhs=hT[:, kc, tile * 512:tile * 512 + ncols],
                                                 start=(kc == 0), stop=(kc == 7)),
                 reads=[wr, hTr[tile]], writes=[psr[b]] if kc == 0 else (), acc_writes=[psr[b]] if kc else ())

    def mm_tstat(b, w, wr, tt):
        for kc in range(8):
            P.op("pe", lambda e, kc=kc: e.matmul(bank(b), lhsT=hT[:, kc, tt * 128:(tt + 1) * 128], rhs=w[:, kc, :],
                                                 start=(kc == 0), stop=(kc == 7)),
                 reads=[wr, hTr[tt // 4]], writes=[psr[b]] if kc == 0 else (), acc_writes=[psr[b]] if kc else ())

    def phase_qkv(j):
        P.new_phase()
        m = P.mark()
        rcos, rcos_r = P.alloc([128, TS], F32, "rcos")
        rsin, rsin_r = P.alloc([128, TS], F32, "rsin")
        perm, perm_r = P.alloc([128, 128], BF16, "perm")
        P.dma("sp", "rcos", lambda e: e.dma_start(out=rcos, in_=I["rcos"]), writes=[rcos_r])
        P.dma("sp", "rsin", lambda e: e.dma_start(out=rsin, in_=I["rsin"]), writes=[rsin_r])
        P.dma("sp", "perm", lambda e: e.dma_start(out=perm, in_=I["perm"]), writes=[perm_r])
        ckt = [P.alloc([128, D], BF16, f"ckt{k}") for k in range(2)]
        kst = [P.alloc([128, 8, 128], BF16, f"kst{k}") for k in range(2)]
        cvt = [P.alloc([128, D], BF16, f"cvt{k}") for k in range(2)]
        for tk in range(4):
            c_, c_r = ckt[tk % 2]
            k_, k_r = kst[tk % 2]
            v_, v_r = cvt[tk % 2]
            P.dma("pool", f"ckt{tk % 2}", lambda e, c_=c_, tk=tk: e.dma_start(out=c_, in_=I["ck"][j, tk * 128:(tk + 1) * 128, :]), writes=[c_r])
            b = tk % 2
            for h in range(8):
                P.op("pe", lambda e, c_=c_, h=h, b=b: e.transpose(bankb(b)[:, h * 128:(h + 1) * 128], c_[:, h * 128:(h + 1) * 128], identb),
                     reads=[c_r, identb_r], writes=[psr[b]] if h == 0 else (), acc_writes=[psr[b]] if h else ())
            P.op("act", lambda e, k_=k_, b=b: e.copy(out=k_, in_=bankb(b).rearrange("p (k t) -> p k t", t=128)), reads=[psr[b]], writes=[k_r])
            P.dma("sp", f"kst{tk % 2}", lambda e, k_=k_, tk=tk: e.dma_start(
                out=S["KT"][:, :, tk * 128:(tk + 1) * 128].rearrange("h p c -> p h c"), in_=k_), reads=[k_r], acc_writes=[R["KT"]])
            P.dma("pool", f"cvt{tk % 2}", lambda e, v_=v_, tk=tk: e.dma_start(out=v_, in_=I["cv"][j, tk * 128:(tk + 1) * 128, :]), writes=[v_r])
            P.dma("sp", f"cvo{tk % 2}", lambda e, v_=v_, tk=tk: e.dma_start(out=S["VS"][tk * 128:(tk + 1) * 128, :], in_=v_),
                  reads=[v_r], acc_writes=[R["VS"]])
        if stop_after == f"qkv{2 * j}a":
            raise Stop()
        wq = [P.alloc([128, 8, 128], BF16, f"wq{k}") for k in range(3)]
        qb = [P.alloc([128, 512], BF16, f"qb{k}") for k in range(2)]
        t1 = [P.alloc([128, 512], F32, f"t1{k}") for k in range(2)]
        t2 = [P.alloc([128, 512], F32, f"t2{k}") for k in range(2)]
        qr = [P.alloc([128, 512], BF16, f"qr{k}") for k in range(3)]
        it = 0
        for ch in range(16):
            isk, h = ch // 8, ch % 8
            w, wr = wq[ch % 3]
            stream_w(w, wr, f"wq{ch % 3}", I["attn_w_qkv"][j][:, isk * D + h * 128: isk * D + (h + 1) * 128])
            for tile in range(NT):
                bA = (it % 3) * 2
                bB = bA + 1
                q_, q_r = qr[it % 3]
                mm_wstat(bA, w, wr, tile)
                if tile < 8 and not os.environ.get('NOROPE'):
                    qq, qq_r = qb[it % 2]
                    a1, a1r = t1[it % 2]
                    a2, a2r = t2[it % 2]
                    P.op("act", lambda e, qq=qq, bA=bA: e.copy(out=qq, in_=bank(bA)), reads=[psr[bA]], writes=[qq_r])
                    P.op("pe", lambda e, qq=qq, bB=bB: e.matmul(bank(bB), lhsT=perm, rhs=qq, start=True, stop=True),
                         reads=[perm_r, qq_r], writes=[psr[bB]])
                    P.op("dve", lambda e, a1=a1, bA=bA, tile=tile: e.tensor_tensor(
                        out=a1, in0=bank(bA), in1=rcos[:, tile * 512:(tile + 1) * 512], op=ALU.mult),
                        reads=[psr[bA], rcos_r], writes=[a1r])
                    P.op("dve", lambda e, a2=a2, bB=bB, tile=tile: e.tensor_tensor(
                        out=a2, in0=bank(bB), in1=rsin[:, tile * 512:(tile + 1) * 512], op=ALU.mult),
                        reads=[psr[bB], rsin_r], writes=[a2r])
                    P.op("pool", lambda e, q_=q_, a1=a1, a2=a2: e.tensor_tensor(out=q_, in0=a1, in1=a2, op=ALU.add),
                         reads=[a1r, a2r], writes=[q_r])
                else:
                    P.op("act", lambda e, q_=q_, bA=bA: e.copy(out=q_, in_=bank(bA)), reads=[psr[bA]], writes=[q_r])
                if isk:
                    P.dma("sp", f"qro{it % 3}", lambda e, q_=q_, h=h, tile=tile: e.dma_start(
                        out=S["KT"][h, :, 512 + tile * 512: 512 + (tile + 1) * 512], in_=q_), reads=[q_r], acc_writes=[R["KT"]])
                else:
                    P.dma("sp", f"qro{it % 3}", lambda e, q_=q_, h=h, tile=tile: e.dma_start(
                        out=S["QT"][h, :, tile * 512:(tile + 1) * 512], in_=q_), reads=[q_r], acc_writes=[R["QT"]])
                it += 1
        if stop_after == f"qkv{2 * j}b":
            raise Stop()
        wv = [P.alloc([128, 8, 512], BF16, f"wv{k}") for k in range(2)]
        vb = [P.alloc([128, 512], BF16, f"vb{k}") for k in range(3)]
        vf = [P.alloc([128, 512], F32, f"vf{k}") for k in range(2)]
        it = 0
        for kind in ("v", "k"):
            for half in range(2):
                w, wr = wv[(it // 64) % 2]
                col0 = (2 * D if kind == "v" else D) + half * 512
                w, wr = wv[half]
                stream_w(w, wr, f"wv{half}{kind}", I["attn_w_qkv"][j][:, col0:col0 + 512])
                tts = range(NTT) if kind == "v" else range(32, 36)
                for tt in tts:
                    b = 6 + it % 2
                    mm_tstat(b, w, wr, tt)
                    if kind == "v":
                        v_, v_r = vb[it % 3]
                        P.op("act", lambda e, v_=v_, b=b: e.copy(out=v_, in_=bank(b)), reads=[psr[b]], writes=[v_r])
                        P.dma("sp", f"vbo{it % 3}", lambda e, v_=v_, tt=tt, half=half: e.dma_start(
                            out=S["VS"][512 + tt * 128: 512 + (tt + 1) * 128, half * 512:(half + 1) * 512], in_=v_),
                            reads=[v_r], acc_writes=[R["VS"]])
                    if tt >= 32:
                        f_, f_r = vf[it % 2]
                        P.op("dve", lambda e, f_=f_, b=b: e.tensor_copy(out=f_, in_=bank(b)), reads=[psr[b]], writes=[f_r])
                        s, r0 = (tt - 32) // 2, ((tt - 32) % 2) * 128
                        dst = O["nv"] if kind == "v" else O["nk"]
                        P.dma("sp", f"vfo{it % 2}", lambda e, f_=f_, s=s, r0=r0, half=half, dst=dst: e.dma_start(
                            out=dst[s, j, r0:r0 + 128, half * 512:(half + 1) * 512], in_=f_), reads=[f_r], acc_writes=[R["OUT"]])
                    it += 1
        P.release(m)

    def phase_attn(j, i):
        P.new_phase()
        m = P.mark()
        lam_init = 0.8 - 0.6 * math.exp(-0.3 * i)
        lp, lp_r = P.alloc([128, 4, 64], F32, "lp")
        lpp, lpp_r = P.alloc([128, 2, 64], F32, "lpp")
        lsum, lsum_r = P.alloc([128, 2], F32, "lsum")
        lexp, lexp_r = P.alloc([128, 2], F32, "lexp")
        nlam, nlam_r = P.alloc([128, 1], F32, "nlam")
        gsub, gsub_r = P.alloc([128, 128], F32, "gsub")
        P.dma("sp", "lp", lambda e: e.dma_start(out=lp, in_=I["attn_lambda"][j].rearrange("(o a) b -> o a b", o=1).broadcast_to([128, 4, 64])), writes=[lp_r])
        P.dma("sp", "gsub", lambda e: e.dma_start(out=gsub, in_=bcast_rows(I["attn_subln_g"][j], 128)), writes=[gsub_r])
        P.op("dve", lambda e: e.tensor_scalar(out=gsub, in0=gsub, scalar1=1.0 - lam_init, scalar2=None, op0=ALU.mult),
             reads=[gsub_r], writes=[gsub_r])
        for k in range(2):
            P.op("dve", lambda e, k=k: e.tensor_tensor(out=lpp[:, k, :], in0=lp[:, 2 * k, :], in1=lp[:, 2 * k + 1, :], op=ALU.mult),
                 reads=[lp_r], acc_writes=[lpp_r])
        P.op("dve", lambda e: e.tensor_reduce(out=lsum, in_=lpp, axis=AX.X, op=ALU.add), reads=[lpp_r], writes=[lsum_r])
        P.op("act", lambda e: e.activation(out=lexp, in_=lsum, func=AF.Exp), reads=[lsum_r], writes=[lexp_r])
        P.op("dve", lambda e: e.tensor_tensor(out=nlam, in0=lexp[:, 1:2], in1=lexp[:, 0:1], op=ALU.subtract), reads=[lexp_r], writes=[nlam_r])
        P.op("dve", lambda e: e.tensor_scalar(out=nlam, in0=nlam, scalar1=-lam_init, scalar2=None, op0=ALU.add), reads=[nlam_r], writes=[nlam_r])

        kTb = [P.alloc([128, 4608], BF16, f"kTh{k}") for k in range(2)]
        vhb = [P.alloc([128, 36, 132], BF16, f"vh{k}") for k in range(2)]
        for k in range(2):
            P.op("pool", lambda e, k=k: e.memset(vhb[k][0], 1.0), writes=[vhb[k][1]])
        qtb = [P.alloc([128, 2, 256], BF16, f"qt{k}") for k in range(3)]
        for k in range(3):
            P.op("pool", lambda e, k=k: e.memset(qtb[k][0], 0.0), writes=[qtb[k][1]])
        Pb = [P.alloc([128, 2, 256], BF16, f"P{k}") for k in range(3)]
        osb = [P.alloc([128, 128], F32, f"o{k}") for k in range(2)]
        obb = [P.alloc([128, 128], BF16, f"ob{k}") for k in range(2)]
        rrb = [P.alloc([128, 2], F32, f"rr{k}") for k in range(2)]
        ssb = [P.alloc([128, 1], F32, f"ss{k}") for k in range(2)]
        rsb = [P.alloc([128, 1], F32, f"rs{k}") for k in range(2)]
        junk, junk_r = P.alloc([128, 128], BF16, "ajunk")
        accs = [P.alloc([128, 4, 132], F32, f"accs{k}") for k in range(2)]
        steps = []
        hcount = 0
        for G in GROUPS:
            for s in range(G["nseq"]):
                L = G["L"]
                nk = G["ncache"] + L
                nkc = nk // 128
                kcol0 = 0 if G["name"] == "s" else 4608 + s * 256
                qtok0 = G["tok0"] + s * L
                for h in range(8):
                    for qb_ in range(L // 256):
                        for c in range(nkc):
                            steps.append(dict(h=h, hid=hcount, q0=qtok0 + qb_ * 256, c=c, nkc=nkc, nk=nk, kcol0=kcol0,
                                              newh=(qb_ == 0 and c == 0), newq=(c == 0)))
                    hcount += 1
        qcount = [0]
        ecount = [0]
        cur = {}

        def stage_a(i, st):
            h = st["h"]
            if st["newh"]:
                kT, kT_r = kTb[st["hid"] % 2]
                vh, vh_r = vhb[st["hid"] % 2]
                nk, nkc, kcol0 = st["nk"], st["nkc"], st["kcol0"]
                P.dma("sp", f"kTh{st['hid'] % 2}", lambda e: e.dma_start(
                    out=kT[:, 0:nk], in_=S["KT"][h, :, kcol0:kcol0 + nk]), reads=[R["KT"]], writes=[kT_r])
                P.dma("sp", f"vh{st['hid'] % 2}", lambda e: e.dma_start(
                    out=vh[:, 0:nkc, 0:128], in_=S["VS"][kcol0:kcol0 + nk, h * 128:(h + 1) * 128].rearrange("(c p) e -> p c e", p=128)),
                    reads=[R["VS"]], acc_writes=[vh_r])
                cur["kT"], cur["vh"] = (kT, kT_r), (vh, vh_r)
            if st["newq"]:
                qt, qt_r = qtb[qcount[0] % 3]
                q0 = st["q0"]
                for mp in range(2):
                    P.dma("sp", f"qt{qcount[0] % 3}_{mp}", lambda e, mp=mp: e.dma_start(
                        out=qt[mp * 64:(mp + 1) * 64, mp, :], in_=S["QT"][h, mp * 64:(mp + 1) * 64, q0:q0 + 256]),
                        reads=[R["QT"]], acc_writes=[qt_r])
                qcount[0] += 1
                cur["qt"] = (qt, qt_r)
            kT, kT_r = cur["kT"]
            qt, qt_r = cur["qt"]
            st["vh"] = cur["vh"]
            c = st["c"]
            sb_ = i % 3
            Pt, Pt_r = Pb[i % 3]
            st["P"] = (Pt, Pt_r)
            for mp in range(2):
                P.op("pe", lambda e, mp=mp: e.matmul(
                    bank(sb_)[:, mp * 256:(mp + 1) * 256], lhsT=kT[:, c * 128:(c + 1) * 128],
                    rhs=qt[:, mp, :], start=True, stop=True),
                    reads=[kT_r, qt_r], writes=[psr[sb_]] if mp == 0 else (), acc_writes=[psr[sb_]] if mp else ())
            P.op("act", lambda e: e.activation(
                out=Pt, in_=bank(sb_).rearrange("p (m q) -> p m q", q=256), func=AF.Exp, scale=0.125),
                reads=[psr[sb_]], writes=[Pt_r])

        def stage_b(st):
            Pt, Pt_r = st["P"]
            vh, vh_r = st["vh"]
            c, nkc, h = st["c"], st["nkc"], st["h"]
            for qs in range(2):
                for mp in range(2):
                    ab = 4 + qs * 2 + mp
                    P.op("pe", lambda e, qs=qs, mp=mp, ab=ab: e.matmul(
                        bank(ab, 129), lhsT=Pt[:, mp, qs * 128:(qs + 1) * 128], rhs=vh[:, c, 0:129],
                        start=(c == 0), stop=(c == nkc - 1)),
                        reads=[Pt_r, vh_r], writes=[psr[ab]] if c == 0 else (), acc_writes=[psr[ab]] if c else ())
            if c != nkc - 1:
                return
            if nkc < 12:
                run_deferred(10 ** 9)
            ac, ac_r = accs[ecount[0] % 2]
            P.op("act", lambda e: e.copy(out=ac[:, :, 0:129], in_=PS[:, 4 * 512:8 * 512].rearrange("p (b c) -> p b c", c=512)[:, :, 0:129]),
                 reads=[psr[4], psr[5], psr[6], psr[7]], writes=[ac_r])
            for qs in range(2):
                a0, a1 = qs * 2, qs * 2 + 1
                o_, o_r = osb[ecount[0] % 2]
                ob, ob_r = obb[ecount[0] % 2]
                rr, rr_r = rrb[ecount[0] % 2]
                s_, s_r = ssb[ecount[0] % 2]
                r_, r_r = rsb[ecount[0] % 2]
                ecount[0] += 1
                tok = st["q0"] + qs * 128
                P.op("dve", lambda e, rr=rr, a0=a0: e.reciprocal(out=rr, in_=ac[:, a0:a0 + 2, 128]),
                     reads=[ac_r], writes=[rr_r])
                P.op("dve", lambda e, rr=rr: e.tensor_tensor(out=rr[:, 1:2], in0=rr[:, 1:2], in1=nlam, op=ALU.mult),
                     reads=[rr_r, nlam_r], writes=[rr_r])
                P.op("dve", lambda e, o_=o_, rr=rr, a0=a0: e.tensor_scalar(
                    out=o_, in0=ac[:, a0, 0:128], scalar1=rr[:, 0:1], scalar2=None, op0=ALU.mult),
                    reads=[ac_r, rr_r], writes=[o_r])
                P.op("dve", lambda e, o_=o_, rr=rr, a1=a1: e.scalar_tensor_tensor(
                    out=o_, in0=ac[:, a1, 0:128], scalar=rr[:, 1:2], in1=o_, op0=ALU.mult, op1=ALU.add),
                    reads=[ac_r, rr_r, o_r], writes=[o_r])
                P.op("dve", lambda e, o_=o_, s_=s_: e.scalar_tensor_tensor(
                    out=junk, in0=o_, scalar=1.0, in1=o_, op0=ALU.mult, op1=ALU.mult, accum_out=s_),
                    reads=[o_r], writes=[junk_r, s_r])

                def e2(o_=o_, o_r=o_r, ob=ob, ob_r=ob_r, s_=s_, s_r=s_r, r_=r_, r_r=r_r):
                    rstd_ops(s_, s_r, r_, r_r, 128, SUBLN_EPS)
                    P.op("dve", lambda e: e.scalar_tensor_tensor(
                        out=ob, in0=o_, scalar=r_, in1=gsub, op0=ALU.mult, op1=ALU.mult),
                        reads=[o_r, r_r, gsub_r], writes=[ob_r])

                def e4(ob=ob, ob_r=ob_r):
                    P.op("pe", lambda e: e.transpose(bankb(3)[:, 0:128], ob, identb), reads=[ob_r, identb_r], writes=[psr[3]])

                def e5(tok=tok, h=h):
                    P.op("act", lambda e: e.copy(out=hT[:, h, tok:tok + 128], in_=bankb(3)[:, 0:128]),
                         reads=[psr[3]], acc_writes=[hTr[tok // 512]])
                if nkc >= 12:
                    base = st["idx"] + LA
                    deferred.append((base + 2 + qs, e2))
                    deferred.append((base + 4 + 3 * qs, e4))
                    deferred.append((base + 6 + 3 * qs, e5))
                else:
                    e2()
                    e4()
                    e5()

        LA = 2
        deferred = []

        def run_deferred(now):
            keep = []
            for due, fn in deferred:
                if due <= now:
                    fn()
                else:
                    keep.append((due, fn))
            deferred[:] = keep
        for i, st in enumerate(steps):
            st["idx"] = i
            stage_a(i, st)
            if i >= LA:
                stage_b(steps[i - LA])
            run_deferred(i)
        for st in steps[-LA:]:
            stage_b(st)
        run_deferred(10 ** 9)
        P.release(m)

    def phase_outproj(wsrc, bias_vec, offG):
        P.new_phase()
        m = P.mark()
        G_ = load_mod(offG, "oG")
        if bias_vec is not None:
            brep, brep_r = P.alloc([128, D], F32, "obias")
            P.dma("sp", "obias", lambda e: e.dma_start(out=brep, in_=bcast_rows(bias_vec, D)), writes=[brep_r])
        wo = [P.alloc([128, 8, 512], BF16, f"wo{k}") for k in range(2)]
        xt = [P.alloc([128, 512], F32, f"ox{k}") for k in range(3)]
        tb = [P.alloc([128, 512], F32, f"ot{k}") for k in range(2)]
        it = 0
        for half in range(2):
            w, wr = wo[half]
            stream_w(w, wr, f"wo{half}", wsrc[:, half * 512:(half + 1) * 512])
            for tt in range(NTT):
                g = 0 if tt < 32 else 1
                b = it % 4
                x, xr = xt[it % 3]
                t_, t_r = tb[it % 2]
                P.dma("sp", f"ox{it % 3}", lambda e, x=x, tt=tt, half=half: e.dma_start(
                    out=x, in_=S["X"][tt * 128:(tt + 1) * 128, half * 512:(half + 1) * 512]), reads=[R["X"][tt][half]], writes=[xr])
                mm_tstat(b, w, wr, tt)
                src = bank(b)
                if bias_vec is not None:
                    P.op("dve", lambda e, t_=t_, b=b, half=half: e.tensor_tensor(
                        out=t_, in0=bank(b), in1=brep[:, half * 512:(half + 1) * 512], op=ALU.add),
                        reads=[psr[b], brep_r], writes=[t_r])
                    P.op("dve", lambda e, t_=t_, g=g, half=half: e.tensor_tensor(
                        out=t_, in0=t_, in1=G_[g][0][:, half * 512:(half + 1) * 512], op=ALU.mult),
                        reads=[t_r, G_[g][1]], writes=[t_r])
                else:
                    P.op("dve", lambda e, t_=t_, b=b, g=g, half=half: e.tensor_tensor(
                        out=t_, in0=bank(b), in1=G_[g][0][:, half * 512:(half + 1) * 512], op=ALU.mult),
                        reads=[psr[b], G_[g][1]], writes=[t_r])
                P.op("pool", lambda e, x=x, t_=t_: e.tensor_tensor(out=x, in0=x, in1=t_, op=ALU.add), reads=[xr, t_r], writes=[xr])
                P.dma("sp", f"oxo{it % 3}", lambda e, x=x, tt=tt, half=half: e.dma_start(
                    out=S["X"][tt * 128:(tt + 1) * 128, half * 512:(half + 1) * 512], in_=x), reads=[xr], writes=[R["X"][tt][half]])
                it += 1
        P.release(m)

    def phase_ffn(i):
        P.new_phase()
        m = P.mark()
        wg = [P.alloc([128, 2, 8, 128], BF16, f"wg{k}") for k in range(3)]
        sg = [P.alloc([128, 512], F32, f"sg{k}") for k in range(2)]
        md = [P.alloc([128, 512], BF16, f"md{k}") for k in range(3)]
        it = 0
        for c in range(NCH_FF):
            w, wr = wg[c % 3]
            for u in range(2):
                P.dma("pool", f"wg{c % 3}_{u}", lambda e, w=w, c=c, u=u: e.dma_start(
                    out=w[:, u], in_=I["ffn_w_gu"][i][:, u * DFF + c * 128: u * DFF + (c + 1) * 128].rearrange("(kc p) n -> p kc n", p=128)),
                    writes=[wr] if u == 0 else (), acc_writes=[wr] if u else ())
            for tile in range(NT):
                bG = (it % 4) * 2
                bU = bG + 1
                s_, s_r = sg[it % 2]
                m_, m_r = md[it % 3]
                mm_wstat(bG, w[:, 0], wr, tile)
                mm_wstat(bU, w[:, 1], wr, tile)
                P.op("act", lambda e, s_=s_, bG=bG: e.activation(out=s_, in_=bank(bG), func=AF.Silu), reads=[psr[bG]], writes=[s_r])
                P.op("dve", lambda e, m_=m_, s_=s_, bU=bU: e.tensor_tensor(out=m_, in0=bank(bU), in1=s_, op=ALU.mult),
                     reads=[psr[bU], s_r], writes=[m_r])
                P.dma("sp", f"mdo{it % 3}", lambda e, m_=m_, c=c, tile=tile: e.dma_start(
                    out=S["MID"][c, :, tile * 512:(tile + 1) * 512], in_=m_), reads=[m_r], acc_writes=[R["MID"][tile]])
                it += 1
        P.release(m)
        P.new_phase()
        m = P.mark()
        G_ = load_mod(5 * D, "fG")
        wd, wd_r = P.alloc([128, NCH_FF, D], BF16, "wd")
        P.dma("pool", "wd", lambda e: e.dma_start(out=wd, in_=I["ffn_w_down"][i].rearrange("(c p) n -> p c n", p=128)), writes=[wd_r])
        mt = [P.alloc([128, NCH_FF, 512], BF16, f"mt{k}") for k in range(2)]
        xt = [P.alloc([128, 512], F32, f"fx{k}") for k in range(3)]
        tb = [P.alloc([128, 512], F32, f"ft{k}") for k in range(2)]
        it = 0
        for tile in range(NT):
            g = 0 if tile < 8 else 1
            mt_, mt_r = mt[tile % 2]
            P.dma("sp", f"mt{tile % 2}", lambda e, mt_=mt_, tile=tile: e.dma_start(
                out=mt_, in_=S["MID"][:, :, tile * 512:(tile + 1) * 512].rearrange("c p t -> p c t")), reads=[R["MID"][tile]], writes=[mt_r])
            for ts in range(4):
                tt = tile * 4 + ts
                for half in range(2):
                    b = it % 4
                    x, xr = xt[it % 3]
                    t_, t_r = tb[it % 2]
                    P.dma("sp", f"fx{it % 3}", lambda e, x=x, tt=tt, half=half: e.dma_start(
                        out=x, in_=S["X"][tt * 128:(tt + 1) * 128, half * 512:(half + 1) * 512]), reads=[R["X"][tt][half]], writes=[xr])
                    for c in range(NCH_FF):
                        P.op("pe", lambda e, mt_=mt_, c=c, ts=ts, half=half, b=b: e.matmul(
                            bank(b), lhsT=mt_[:, c, ts * 128:(ts + 1) * 128], rhs=wd[:, c, half * 512:(half + 1) * 512],
                            start=(c == 0), stop=(c == NCH_FF - 1)),
                            reads=[mt_r, wd_r], writes=[psr[b]] if c == 0 else (), acc_writes=[psr[b]] if c else ())
                    P.op("dve", lambda e, t_=t_, b=b, g=g, half=half: e.tensor_tensor(
                        out=t_, in0=bank(b), in1=G_[g][0][:, half * 512:(half + 1) * 512], op=ALU.mult),
                        reads=[psr[b], G_[g][1]], writes=[t_r])
                    P.op("pool", lambda e, x=x, t_=t_: e.tensor_tensor(out=x, in0=x, in1=t_, op=ALU.add), reads=[xr, t_r], writes=[xr])
                    P.dma("sp", f"fxo{it % 3}", lambda e, x=x, tt=tt, half=half: e.dma_start(
                        out=S["X"][tt * 128:(tt + 1) * 128, half * 512:(half + 1) * 512], in_=x), reads=[xr], writes=[R["X"][tt][half]])
                    it += 1
        P.release(m)

    def phase_hy_in(j):
        P.new_phase()
        m = P.mark()
        CV, CV_r = P.alloc([128, 120], F32, "CV")
        load_cols(CV, CV_r, 0, I["hy_b_in"][j], 24, "cvl")
        for k in range(3):
            load_cols(CV, CV_r, 24 + k * 24, I["hy_conv_w"][j, k], 24, "cvl")
        load_cols(CV, CV_r, 96, I["hy_conv_b"][j], 24, "cvl")
        ubS = [P.alloc([128, TS + 2], F32, f"ubS{k}") for k in range(2)]
        ubP = [P.alloc([128, 2, 258], F32, f"ubP{k}") for k in range(2)]
        for k in range(2):
            P.op("pool", lambda e, k=k: e.memset(ubS[k][0], 0.0), writes=[ubS[k][1]])
            P.op("pool", lambda e, k=k: e.memset(ubP[k][0], 0.0), writes=[ubP[k][1]])
        cb1, cb1_r = P.alloc([128, T], F32, "cb1")
        cb2 = [P.alloc([128, T], F32, f"cb2{k}") for k in range(2)]
        vst, vst_r = P.alloc([128, NTT, 128], BF16, "vst")
        wi = [P.alloc([128, 8, 128], BF16, f"wi{k}") for k in range(3)]
        it = 0
        uc = 0
        c2 = 0
        for jd in range(8):
            for part, fc in (("x1", 8 + jd), ("v", 16 + jd), ("x0", jd)):
                w, wr = wi[it % 3]
                it += 1
                stream_w(w, wr, f"wi{it % 3}", I["hy_w_in"][j][:, fc * 128:(fc + 1) * 128])
                uS, uS_r = ubS[uc % 2]
                uP, uP_r = ubP[uc % 2]
                uc += 1
                for tile in range(NT):
                    b = tile % 4
                    mm_wstat(b, w, wr, tile)
                    if tile < 8:
                        P.op("act", lambda e, uS=uS, b=b, tile=tile, fc=fc: e.activation(
                            out=uS[:, 1 + tile * 512: 1 + (tile + 1) * 512], in_=bank(b), func=AF.Identity, bias=CV[:, fc:fc + 1]),
                            reads=[psr[b], CV_r], acc_writes=[uS_r])
                    else:
                        P.op("act", lambda e, uP=uP, b=b, fc=fc: e.activation(
                            out=uP[:, :, 1:257], in_=bank(b).rearrange("p (s t) -> p s t", t=256), func=AF.Identity, bias=CV[:, fc:fc + 1]),
                            reads=[psr[b], CV_r], acc_writes=[uP_r])
                if part == "x1":
                    dst, dst_r = cb1, cb1_r
                else:
                    dst, dst_r = cb2[c2 % 2]
                    c2 += 1
                w0, w1, w2, cbc = (CV[:, 24 + fc:25 + fc], CV[:, 48 + fc:49 + fc], CV[:, 72 + fc:73 + fc], CV[:, 96 + fc:97 + fc])
                dS = dst[:, 0:TS]
                dP = dst[:, TS:T].rearrange("p (s t) -> p s t", t=256)
                for (dd, uu, ur, n) in ((dS, uS, uS_r, TS), (dP, uP, uP_r, 256)):
                    def sl(o, uu=uu, n=n):
                        return uu[:, o:o + n] if len(uu.shape) == 2 else uu[:, :, o:o + n]
                    P.op("dve", lambda e, dd=dd, sl=sl, w0=w0, cbc=cbc: e.tensor_scalar(out=dd, in0=sl(0), scalar1=w0, scalar2=cbc, op0=ALU.mult, op1=ALU.add),
                         reads=[ur, CV_r], acc_writes=[dst_r])
                    P.op("dve", lambda e, dd=dd, sl=sl, w1=w1: e.scalar_tensor_tensor(out=dd, in0=sl(1), scalar=w1, in1=dd, op0=ALU.mult, op1=ALU.add),
                         reads=[ur, CV_r, dst_r], acc_writes=[dst_r])
                    P.op("dve", lambda e, dd=dd, sl=sl, w2=w2: e.scalar_tensor_tensor(out=dd, in0=sl(2), scalar=w2, in1=dd, op0=ALU.mult, op1=ALU.add),
                         reads=[ur, CV_r, dst_r], acc_writes=[dst_r])
                if part == "v":
                    P.op("pool", lambda e, dst=dst: e.tensor_tensor(out=dst, in0=dst, in1=cb1, op=ALU.mult), reads=[dst_r, cb1_r], writes=[dst_r])
                    P.dma("sp", f"vvt{c2 % 2}", lambda e, dst=dst, jd=jd: e.dma_start(out=S["VVT"][jd], in_=dst), reads=[dst_r], acc_writes=[R["VVT"]])
                    for t4 in range(NTT // 4):
                        b = 4 + t4 % 4
                        for q in range(4):
                            tt = t4 * 4 + q
                            P.op("pe", lambda e, dst=dst, tt=tt, q=q, b=b: e.transpose(
                                bank(b)[:, q * 128:(q + 1) * 128], dst[:, tt * 128:(tt + 1) * 128], identf),
                                reads=[dst_r, identf_r], writes=[psr[b]] if q == 0 else (), acc_writes=[psr[b]] if q else ())
                        P.op("act", lambda e, t4=t4, b=b: e.copy(out=vst[:, t4 * 4:(t4 + 1) * 4, :], in_=bank(b).rearrange("p (q d) -> p q d", d=128)),
                             reads=[psr[b]], acc_writes=[vst_r])
                    P.dma("sp", "vsto", lambda e, jd=jd: e.dma_start(
                        out=S["VVTOK"][:, jd * 128:(jd + 1) * 128].rearrange("(c p) d -> p c d", p=128), in_=vst),
                        reads=[vst_r], acc_writes=[R["VVTOK"]])
                    vst_r.w = dict(vst_r.w)
                if part == "x0":
                    P.dma("sp", f"x0t{c2 % 2}", lambda e, dst=dst, jd=jd: e.dma_start(out=S["X0T"][jd], in_=dst), reads=[dst_r], acc_writes=[R["X0T"]])
        P.release(m)

    def phase_hy_filter(j, G):
        L = G["L"]
        nm = G["name"]
        ntc = L // 128
        HS, HD = S["HS" + nm], S["HD" + nm]
        P.new_phase()
        rn, rn_r = P.alloc([128, D], F32, "rn")
        m = P.mark()
        zT, zT_r = P.alloc([33, L], F32, "zT")
        w1, w1_r = P.alloc([33, 64], F32, "fw1")
        w2, w2_r = P.alloc([64, 64], F32, "fw2")
        w3, w3_r = P.alloc([64, 2 * D], F32, "fw3")
        b3, b3_r = P.alloc([128, 2 * D], F32, "fb3")
        fcol, fcol_r = P.alloc([64, 4], F32, "fcol")
        negt, negt_r = P.alloc([128, ntc], F32, "negt")
        drep, drep_r = P.alloc([128, D], F32, "drep")
        mask0, mask0_r = P.alloc([128, 1], F32, "mask0")
        h1, h1_r = P.alloc([64, L], F32, "h1")
        h2, h2_r = P.alloc([64, L], F32, "h2")
        tmp = [P.alloc([64, 512], F32, f"ftmp{k}") for k in range(2)]
        P.dma("sp", "zT", lambda e: e.dma_start(out=zT, in_=I["z" + nm]), writes=[zT_r])
        P.dma("sp", "fw1", lambda e: e.dma_start(out=w1, in_=I["filt_w1"][j]), writes=[w1_r])
        P.dma("sp", "fw2", lambda e: e.dma_start(out=w2, in_=I["filt_w2"][j]), writes=[w2_r])
        P.dma("sp", "fw3", lambda e: e.dma_start(out=w3, in_=I["filt_w3"][j]), writes=[w3_r])
        P.dma("sp", "fb3", lambda e: e.dma_start(out=b3, in_=bcast_rows(I["filt_b3"][j], 2 * D)), writes=[b3_r])
        P.dma("sp", "negt", lambda e: e.dma_start(out=negt, in_=I["negt" + nm]), writes=[negt_r])
        P.dma("sp", "drep", lambda e: e.dma_start(out=drep, in_=bcast_rows(I["delta"], D)), writes=[drep_r])
        P.dma("sp", "mask0", lambda e: e.dma_start(out=mask0, in_=I["mask0"]), writes=[mask0_r])
        for k, v in enumerate((I["filt_b1"][j], I["filt_b2"][j], I["filt_freq"][j])):
            P.dma("sp", "fcol", lambda e, k=k, v=v: e.dma_start(out=fcol[:, k:k + 1], in_=v.rearrange("(p o) -> p o", o=1)), acc_writes=[fcol_r])
        TWO_PI = 2.0 * math.pi
        SC = TWO_PI * (1.0 - 2e-6)
        wr_, wr_r = P.alloc([64, 512], F32, "fwrap")

        def sin_layer(lhsT, lhsT_r, src, src_r, dstb, dstb_r, bcol):
            n = min(512, L)
            for ti in range(L // n):
                b = ti % 2
                t_, t_r = tmp[ti % 2]
                P.op("pe", lambda e, ti=ti, b=b: e.matmul(bank(b)[0:64, 0:n], lhsT=lhsT, rhs=src[:, ti * n:(ti + 1) * n], start=True, stop=True),
                     reads=[lhsT_r, src_r], writes=[psr[b]])
                P.op("dve", lambda e, t_=t_, b=b: e.tensor_scalar(
                    out=t_[:, 0:n], in0=bank(b)[0:64, 0:n], scalar1=fcol[:, bcol:bcol + 1], scalar2=fcol[:, 2:3], op0=ALU.add, op1=ALU.mult),
                    reads=[psr[b], fcol_r], writes=[t_r])
                for rnd in range(2):
                    for (cmp_, thr, sgn) in ((ALU.is_lt, -math.pi, ALU.add), (ALU.is_gt, math.pi, ALU.subtract)):
                        P.op("dve", lambda e, t_=t_, cmp_=cmp_, thr=thr: e.tensor_scalar(
                            out=wr_[:, 0:n], in0=t_[:, 0:n], scalar1=thr, scalar2=TWO_PI, op0=cmp_, op1=ALU.mult),
                            reads=[t_r], writes=[wr_r])
                        P.op("dve", lambda e, t_=t_, sgn=sgn: e.tensor_tensor(out=t_[:, 0:n], in0=t_[:, 0:n], in1=wr_[:, 0:n], op=sgn),
                             reads=[t_r, wr_r], writes=[t_r])
                P.op("act", lambda e, t_=t_, ti=ti: e.activation(out=dstb[:, ti * n:(ti + 1) * n], in_=t_[:, 0:n], func=AF.Sin, scale=1.0 - 2e-6),
                     reads=[t_r], acc_writes=[dstb_r])
        sin_layer(w1, w1_r, zT, zT_r, h1, h1_r, 0)
        sin_layer(w2, w2_r, h1, h1_r, h2, h2_r, 1)
        dec = [P.alloc([128, D], F32, "dec0")] * 2
        hf = [P.alloc([128, D], F32, f"hf{k}") for k in range(2)]
        hb_ = [P.alloc([128, D], F32, f"hbk{k}") for k in range(2)]
        ab = [P.alloc([128, 2 * D], F32, "ab0")] * 2
        hs = [P.alloc([128, D], BF16, f"hs{k}") for k in range(2)]
        hd = [P.alloc([128, D], BF16, f"hd{k}") for k in range(2)]
        for tc in range(ntc):
            k = tc % 2
            for cb in range(4):
                P.op("pe", lambda e, tc=tc, cb=cb: e.matmul(bank(cb), lhsT=h2[:, tc * 128:(tc + 1) * 128], rhs=w3[:, cb * 512:(cb + 1) * 512], start=True, stop=True),
                     reads=[h2_r, w3_r], writes=[psr[cb]])
            P.op("act", lambda e, k=k, tc=tc: e.activation(out=dec[k][0], in_=drep, func=AF.Exp, scale=negt[:, tc:tc + 1]),
                 reads=[drep_r, negt_r], writes=[dec[k][1]])
            for (dst, lo) in ((hf[k], 0), (hb_[k], D)):
                P.op("dve", lambda e, dst=dst, lo=lo: e.tensor_tensor(out=dst[0], in0=PS[:, lo:lo + D], in1=b3[:, lo:lo + D], op=ALU.add),
                     reads=[psr[lo // 512], psr[lo // 512 + 1], b3_r], writes=[dst[1]])
                P.op("dve", lambda e, dst=dst, k=k: e.tensor_tensor(out=dst[0], in0=dst[0], in1=dec[k][0], op=ALU.mult),
                     reads=[dst[1], dec[k][1]], writes=[dst[1]])
                P.op("act", lambda e, dst=dst, lo=lo, k=k: e.activation(out=ab[k][0][:, lo:lo + D], in_=dst[0], func=AF.Abs),
                     reads=[dst[1]], acc_writes=[ab[k][1]])
            for q in range(4):
                bq = 4 + q % 2
                first = (tc == 0 and q < 2)
                last = (tc == ntc - 1 and q >= 2)
                P.op("pe", lambda e, k=k, q=q, bq=bq, first=first, last=last: e.matmul(
                    bank(bq), lhsT=onesf, rhs=ab[k][0][:, q * 512:(q + 1) * 512], start=first, stop=last),
                    reads=[onesf_r, ab[k][1]], writes=[psr[bq]] if first else (), acc_writes=() if first else [psr[bq]])
            if tc == 0:
                P.op("dve", lambda e, k=k: e.tensor_scalar(out=hb_[k][0], in0=hb_[k][0], scalar1=mask0[:, 0:1], scalar2=None, op0=ALU.mult),
                     reads=[hb_[k][1], mask0_r], writes=[hb_[k][1]])
            P.op("pool", lambda e, k=k: e.tensor_tensor(out=hs[k][0], in0=hf[k][0], in1=hb_[k][0], op=ALU.add),
                 reads=[hf[k][1], hb_[k][1]], writes=[hs[k][1]])
            P.op("pool", lambda e, k=k: e.tensor_tensor(out=hd[k][0], in0=hb_[k][0], in1=hf[k][0], op=ALU.subtract),
                 reads=[hf[k][1], hb_[k][1]], writes=[hd[k][1]])
            P.dma("sp", f"hso{k}", lambda e, k=k, tc=tc: e.dma_start(out=HS[tc * 128:(tc + 1) * 128, :], in_=hs[k][0]), reads=[hs[k][1]], acc_writes=[R["HS"]])
            P.dma("sp", f"hdo{k}", lambda e, k=k, tc=tc: e.dma_start(out=HD[tc * 128:(tc + 1) * 128, :], in_=hd[k][0]), reads=[hd[k][1]], acc_writes=[R["HS"]])
        P.op("dve", lambda e: e.tensor_scalar(out=rn, in0=PS[:, 4 * 512:6 * 512], scalar1=1e-6, scalar2=None, op0=ALU.add),
             reads=[psr[4], psr[5]], writes=[rn_r])
        P.op("dve", lambda e: e.reciprocal(out=rn, in_=rn), reads=[rn_r], writes=[rn_r])
        P.release(m)
        return rn, rn_r

    def phase_hy_conv(j, G, s, rn, rn_r):
        L = G["L"]
        nm = G["name"]
        ntc = L // 128
        nfc = 33 if nm == "s" else 3
        SQ = nfc * 128
        Qt, Rt, WFd = I["q" + nm], I["r" + nm], I["wf" + nm]
        HS, HD = S["HS" + nm], S["HD" + nm]
        tok0 = G["tok0"] + s * L
        P.new_phase()
        m = P.mark()
        wf, wf_r = P.alloc([128, nfc], F32, "wf")
        P.dma("sp", "wf", lambda e: e.dma_start(out=wf, in_=WFd), writes=[wf_r])
        vvhs, vvhs_r = P.alloc([128, ntc, 512], BF16, "vvhs")
        vvhd, vvhd_r = P.alloc([128, ntc, 512], BF16, "vvhd")
        qch = [P.alloc([128, ntc, 128], BF16, f"qch{k}") for k in range(2)]
        rch = [P.alloc([128, ntc, 128], BF16, f"rch{k}") for k in range(2)]
        kcs = [P.alloc([128, 256], F32, f"kcs{k}") for k in range(2)]
        kss = [P.alloc([128, 256], F32, f"kss{k}") for k in range(2)]
        ta = [P.alloc([128, 256], F32, f"ta{k}") for k in range(2)]
        tb = [P.alloc([128, 256], F32, f"tb{k}") for k in range(2)]
        yst = [P.alloc([128, 2, 256], BF16, f"yst{k}") for k in range(2)]
        it = 0
        for dq in range(4):
            dh = dq // 2
            vsrc = S["VVTOK"][tok0:tok0 + L, dq * 256:(dq + 1) * 256].rearrange("(c p) d -> p c d", p=128)
            P.dma("sp", "vva", lambda e, vsrc=vsrc: e.dma_start(out=vvhs[:, :, 0:256], in_=vsrc), reads=[R["VVTOK"]], writes=[vvhs_r])
            P.dma("sp", "vvb", lambda e, vsrc=vsrc: e.dma_start(out=vvhd[:, :, 0:256], in_=vsrc), reads=[R["VVTOK"]], writes=[vvhd_r])
            P.dma("sp", "hsh", lambda e, dq=dq: e.dma_start(
                out=vvhs[:, :, 256:512], in_=HS[:, dq * 256:(dq + 1) * 256].rearrange("(c p) d -> p c d", p=128)), reads=[R["HS"]], acc_writes=[vvhs_r])
            P.dma("sp", "hdh", lambda e, dq=dq: e.dma_start(
                out=vvhd[:, :, 256:512], in_=HD[:, dq * 256:(dq + 1) * 256].rearrange("(c p) d -> p c d", p=128)), reads=[R["HS"]], acc_writes=[vvhd_r])
            for fc in range(nfc):
                qc, qc_r = qch[it % 2]
                rc, rc_r = rch[it % 2]
                if nm == "s":
                    qsrc, rsrc = I["qfs"][fc], I["rfs"][fc]
                else:
                    qsrc = Qt[0:ntc, :, fc * 128:(fc + 1) * 128].rearrange("c p f -> p c f")
                    rsrc = Rt[0:ntc, :, fc * 128:(fc + 1) * 128].rearrange("c p f -> p c f")
                P.dma("sp", f"qch{it % 2}", lambda e, qc=qc, qsrc=qsrc: e.dma_start(out=qc, in_=qsrc), writes=[qc_r])
                P.dma("sp", f"rch{it % 2}", lambda e, rc=rc, rsrc=rsrc: e.dma_start(out=rc, in_=rsrc), writes=[rc_r])
                b0 = (it % 4) * 2
                bC, bS = b0, b0 + 1
                for tc in range(ntc):
                    st, sp_ = (tc == 0), (tc == ntc - 1)
                    for (bb, tab, tab_r, mov, mov_r) in ((bC, qc, qc_r, vvhs, vvhs_r), (bS, rc, rc_r, vvhd, vvhd_r)):
                        P.op("pe", lambda e, bb=bb, tab=tab, mov=mov, tc=tc, st=st, sp_=sp_: e.matmul(
                            bank(bb), lhsT=tab[:, tc, :], rhs=mov[:, tc, :], start=st, stop=sp_),
                            reads=[tab_r, mov_r], writes=[psr[bb]] if st else (), acc_writes=() if st else [psr[bb]])
                bVc = bKc = bC
                bVs = bKs = bS
                Vc_, Kc_ = bank(bC)[:, 0:256], bank(bC)[:, 256:512]
                Vs_, Ks_ = bank(bS)[:, 0:256], bank(bS)[:, 256:512]
                k = it % 2
                wcol = wf[:, fc:fc + 1]
                rsl = rn[:, dq * 256:(dq + 1) * 256]
                P.op("dve", lambda e, k=k, Kc_=Kc_, wcol=wcol, rsl=rsl: e.scalar_tensor_tensor(
                    out=kcs[k][0], in0=Kc_, scalar=wcol, in1=rsl, op0=ALU.mult, op1=ALU.mult),
                    reads=[psr[bKc], wf_r, rn_r], writes=[kcs[k][1]])
                P.op("dve", lambda e, k=k, Ks_=Ks_, wcol=wcol, rsl=rsl: e.scalar_tensor_tensor(
                    out=kss[k][0], in0=Ks_, scalar=wcol, in1=rsl, op0=ALU.mult, op1=ALU.mult),
                    reads=[psr[bKs], wf_r, rn_r], writes=[kss[k][1]])
                P.op("dve", lambda e, k=k, Vc_=Vc_: e.tensor_tensor(out=ta[k][0], in0=Vc_, in1=kcs[k][0], op=ALU.mult),
                     reads=[psr[bVc], kcs[k][1]], writes=[ta[k][1]])
                P.op("dve", lambda e, k=k, Vs_=Vs_: e.tensor_tensor(out=tb[k][0], in0=Vs_, in1=kss[k][0], op=ALU.mult),
                     reads=[psr[bVs], kss[k][1]], writes=[tb[k][1]])
                P.op("pool", lambda e, k=k: e.tensor_tensor(out=yst[k][0][:, 0, :], in0=ta[k][0], in1=tb[k][0], op=ALU.add),
                     reads=[ta[k][1], tb[k][1]], writes=[yst[k][1]])
                P.op("dve", lambda e, k=k, Vs_=Vs_: e.tensor_tensor(out=ta[k][0], in0=Vs_, in1=kcs[k][0], op=ALU.mult),
                     reads=[psr[bVs], kcs[k][1]], writes=[ta[k][1]])
                P.op("dve", lambda e, k=k, Vc_=Vc_: e.tensor_tensor(out=tb[k][0], in0=Vc_, in1=kss[k][0], op=ALU.mult),
                     reads=[psr[bVc], kss[k][1]], writes=[tb[k][1]])
                P.op("pool", lambda e, k=k: e.tensor_tensor(out=yst[k][0][:, 1, :], in0=ta[k][0], in1=tb[k][0], op=ALU.subtract),
                     reads=[ta[k][1], tb[k][1]], acc_writes=[yst[k][1]])
                P.dma("sp", f"yfo{k}", lambda e, k=k, dh=dh, dq=dq, fc=fc: e.dma_start(out=S["YF"][dh, fc][:, :, (dq % 2) * 256:(dq % 2 + 1) * 256], in_=yst[k][0]),
                      reads=[yst[k][1]], acc_writes=[R["YF"]])
                it += 1
        P.release(m)
        P.new_phase()
        m = P.mark()
        skc, skc_r = P.alloc([128, 8], F32, "skc")
        load_cols(skc, skc_r, 0, I["hy_skip"][j], 8, "skc")
        Yg, Yg_r = P.alloc([128, nfc, 2, 512], BF16, "Yg")
        n = min(512, L)
        tq = [P.alloc([128, 2, n], BF16, f"tq{k}") for k in range(6)]
        vvt = [P.alloc([128, n], F32, f"vvt{k}") for k in range(3)]
        x0t = [P.alloc([128, n], F32, f"x0t{k}") for k in range(3)]
        it = 0
        ie = 0
        for dh in range(2):
            P.dma("sp", "Yg", lambda e, dh=dh: e.dma_start(out=Yg, in_=S["YF"][dh, 0:nfc].rearrange("c p s d -> p c s d")),
                  reads=[R["YF"]], writes=[Yg_r])
            for tt in range(L // n):
                b0 = ((dh * (L // n) + tt) % 2) * 4
                for fc in range(nfc):
                    t_, t_r = tq[it % 6]
                    P.dma("sp", f"tq{it % 6}a", lambda e, t_=t_, fc=fc, tt=tt: e.dma_start(out=t_[:, 0, :], in_=Qt[fc, :, tt * n:(tt + 1) * n]), writes=[t_r])
                    P.dma("sp", f"tq{it % 6}b", lambda e, t_=t_, fc=fc, tt=tt: e.dma_start(out=t_[:, 1, :], in_=Rt[fc, :, tt * n:(tt + 1) * n]), acc_writes=[t_r])
                    it += 1
                    for dcl in range(4):
                        for cs in range(2):
                            st = (fc == 0 and cs == 0)
                            sp_ = (fc == nfc - 1 and cs == 1)
                            P.op("pe", lambda e, t_=t_, fc=fc, dcl=dcl, cs=cs, st=st, sp_=sp_, b0=b0: e.matmul(
                                bank(b0 + dcl, n), lhsT=Yg[:, fc, cs, dcl * 128:(dcl + 1) * 128], rhs=t_[:, cs, :], start=st, stop=sp_),
                                reads=[Yg_r, t_r], writes=[psr[b0 + dcl]] if st else (), acc_writes=() if st else [psr[b0 + dcl]])
                for dcl in range(4):
                    dc = dh * 4 + dcl
                    v_, v_r = vvt[ie % 3]
                    x_, x_r = x0t[ie % 3]
                    ie += 1
                    c0 = tok0 + tt * n
                    P.dma("sp", f"vvt{ie % 3}", lambda e, v_=v_, dc=dc, c0=c0: e.dma_start(out=v_, in_=S["VVT"][dc, :, c0:c0 + n]), reads=[R["VVT"]], writes=[v_r])
                    P.dma("sp", f"x0t{ie % 3}", lambda e, x_=x_, dc=dc, c0=c0: e.dma_start(out=x_, in_=S["X0T"][dc, :, c0:c0 + n]), reads=[R["X0T"]], writes=[x_r])
                    P.op("dve", lambda e, v_=v_, dc=dc, dcl=dcl, b0=b0: e.scalar_tensor_tensor(
                        out=v_, in0=v_, scalar=skc[:, dc:dc + 1], in1=bank(b0 + dcl, n), op0=ALU.mult, op1=ALU.add),
                        reads=[v_r, skc_r, psr[b0 + dcl]], writes=[v_r])
                    P.op("pool", lambda e, v_=v_, x_=x_, dc=dc, c0=c0: e.tensor_tensor(out=hT[:, dc, c0:c0 + n], in0=v_, in1=x_, op=ALU.mult),
                         reads=[v_r, x_r], acc_writes=[hTr[c0 // 512]])
        P.release(m)

    class Stop(Exception):
        pass

    def chk(tag):
        if stop_after == tag:
            raise Stop()

    def dump_hT():
        for t in range(NT):
            P.dma("sp", "htd", lambda e, t=t: e.dma_start(out=S["HTD"][:, :, t * 512:(t + 1) * 512], in_=hT[:, :, t * 512:(t + 1) * 512]),
                  reads=[hTr[t]], acc_writes=[R["HTD"]])

    try:
        for i in range(4):
            j = i // 2
            phase_mod(i)
            chk(f"mod{i}")
            phase_norm(D, 0)
            chk(f"norm1_{i}")
            if i % 2 == 0:
                phase_qkv(j)
                chk(f"qkv{i}")
                phase_attn(j, i)
                chk(f"attn{i}")
                phase_outproj(I["attn_w_o"][j], None, 2 * D)
            else:
                phase_hy_in(j)
                chk(f"hyin{i}")
                for G in GROUPS:
                    mk = P.mark()
                    rn, rn_r = phase_hy_filter(j, G)
                    chk(f"hyfilt{i}{G['name']}")
                    for s in range(G["nseq"]):
                        phase_hy_conv(j, G, s, rn, rn_r)
                    P.release(mk)
                chk(f"hyconv{i}")
                phase_outproj(I["hy_w_out"][j], I["hy_b_out"][j], 2 * D)
            chk(f"mix{i}")
            phase_norm(4 * D, 3 * D)
            phase_ffn(i)
            chk(f"ffn{i}")
        phase_norm(0, 0, final=True)
    except Stop:
        if "HTD" in debug_outs:
            dump_hT()

    final_keys = [k for k in P.dma_cnt]
    print("n dma sems", len(final_keys), {e: len(P.q[e]) for e in ENGS})
    P.emit(final_keys)
    return nc, P


_CONSTS = None


def _core_inputs(b, inp, consts):
    m = {}
    m["x"] = np.ascontiguousarray(np.concatenate(
        [inp["x_sample"][b], inp["x_prompt"][2 * b], inp["x_prompt"][2 * b + 1]], axis=0).astype(np.float32))
    m["ck"] = np.ascontiguousarray(inp["cache_k"][b].reshape(2, 512, D).astype(np.float32))
    m["cv"] = np.ascontiguousarray(inp["cache_v"][b].reshape(2, 512, D).astype(np.float32))
    m["cvec"] = np.ascontiguousarray(np.stack([inp["c"][b], inp["c_ctx"]], axis=0).astype(np.float32))
    for nm, shp in W_SPECS:
        m[nm] = np.ascontiguousarray(np.asarray(inp[nm], dtype=np.float32).reshape(shp))
    for nm, shp, dt in CONST_SPECS:
        m[nm] = consts[nm]
    return m


def kernel(**inputs):
    global _CONSTS
    if _CONSTS is None:
        _CONSTS = _host_consts()
    inp = {k: np.asarray(v) for k, v in inputs.items()}
    nc, _ = build_program()
    in_maps = [_core_inputs(b, inp, _CONSTS) for b in range(8)]
    res = run_bass_kernel_spmd(nc, in_maps, core_ids=list(range(8)))
    y_prompt = np.zeros((16, 256, D), np.float32)
    y_sample = np.zeros((8, TS, D), np.float32)
    nk = np.zeros((16, 2, 256, 8, 2, 64), np.float32)
    nv = np.zeros((16, 2, 256, 8, 128), np.float32)
    for b in range(8):
        r = res.results[b]
        y = np.asarray(r["y"], dtype=np.float32)
        y_sample[b] = y[:TS]
        y_prompt[2 * b] = y[TS:TS + 256]
        y_prompt[2 * b + 1] = y[TS + 256:]
        k_ = np.asarray(r["nk"], dtype=np.float32).reshape(2, 2, 256, 8, 2, 64)
        v_ = np.asarray(r["nv"], dtype=np.float32).reshape(2, 2, 256, 8, 128)
        nk[2 * b], nk[2 * b + 1] = k_[0], k_[1]
        nv[2 * b], nv[2 * b + 1] = v_[0], v_[1]
    return (y_prompt, y_sample, nk, nv)
```

```python
import contextlib
import os
import math
import numpy as np
import ml_dtypes
import concourse.bass as bass
import concourse.mybir as mybir
from concourse.bass_utils import run_bass_kernel_spmd

F32 = mybir.dt.float32
BF16 = mybir.dt.bfloat16
AF = mybir.ActivationFunctionType
ALU = mybir.AluOpType
AX = mybir.AxisListType

ENGS = ("pe", "act", "dve", "pool", "sp")
D = 1024
TS, TP, T = 4096, 512, 4608
NT = 9
NTT = 36
DFF = 2816
NCH_FF = 22
EPS = 1e-6
SUBLN_EPS = 1e-5
NKEY = 5120


class Res:
    __slots__ = ("w", "r", "name", "excl")

    def __init__(self, name="", excl=False):
        self.w = {}
        self.r = {}
        self.name = name
        self.excl = excl


class Prog:
    def __init__(self, nc, arena_words):
        self.nc = nc
        self.q = {e: [] for e in ENGS}
        self.known = {e: {} for e in ENGS}
        self.dma_cnt = {}
        self.stack = contextlib.ExitStack()
        self.arena = self.stack.enter_context(nc.sbuf_tensor("arena", [128, arena_words], F32))
        self.arena_words = arena_words
        self.top = 0
        self.live = []
        self.retired = []
        self.nsem = 0
        self.keymap = {}
        self.keyres = {}

    def alloc(self, shape, dt, name=""):
        esz = 4 if dt == F32 else 2
        free = 1
        for s in shape[1:]:
            free *= s
        words = (free * esz + 3) // 4
        words = (words + 7) // 8 * 8
        off = self.top
        assert off + words <= self.arena_words, f"SBUF arena overflow {name} {off + words}"
        self.top += words
        v = self.arena[0:shape[0], off:off + (free * esz) // 4]
        if dt != F32:
            v = v.bitcast(dt)
        if len(shape) == 3:
            v = v.rearrange("p (a b) -> p a b", b=shape[2])
        elif len(shape) == 4:
            v = v.rearrange("p (a b c) -> p a b c", b=shape[2], c=shape[3])
        r = Res(name)
        keep = []
        for (a, b, rr) in self.retired:
            if a < off + words and off < b:
                for k, val in rr.w.items():
                    if r.r.get(k, -1) < val:
                        r.r[k] = val
                for k, val in rr.r.items():
                    if r.r.get(k, -1) < val:
                        r.r[k] = val
                if a >= off and b <= off + words:
                    continue
            keep.append((a, b, rr))
        self.retired = keep
        self.live.append((off, off + words, r))
        return v, r

    def mark(self):
        return (self.top, len(self.live))

    def release(self, m):
        top, n = m
        self.retired.extend(self.live[n:])
        del self.live[n:]
        self.top = top

    def _add(self, eng, fn, reads, writes, acc_writes, own):
        deps = {}

        def upd(d):
            for k, v in d.items():
                if deps.get(k, -1) < v:
                    deps[k] = v
        for r in reads:
            upd(r.w)
            if r.excl:
                upd({k: v for k, v in r.r.items() if k != ("c", eng)})
        for w in writes:
            upd(w.w)
            upd(w.r)
        for w in acc_writes:
            upd(w.r)
            upd({k: v for k, v in w.w.items() if k != own})
        q = self.q[eng]
        idx = len(q)
        waits = []
        kn = self.known[eng]
        for k, v in deps.items():
            if k == ("c", eng):
                if eng == "pe":
                    continue
                vv = -1
                for r in reads:
                    x = r.w.get(k, -1)
                    if x > vv:
                        vv = x
                if vv < 0:
                    continue
                v = vv
            if k[0] == "d":
                v = self.dma_cnt[k[1]]
            if kn.get(k, -1) >= v:
                continue
            kn[k] = v
            waits.append((k, v))
            if k[0] == "c":
                self.q[k[1]][v][2] = True
        op = [fn, waits, False, None]
        q.append(op)
        return op, idx

    def op(self, eng, fn, reads=(), writes=(), acc_writes=()):
        op, idx = self._add(eng, fn, reads, writes, acc_writes, ("c", eng))
        k = ("c", eng)
        for r in reads:
            r.r[k] = idx
        for w in writes:
            w.w = {k: idx}
            w.r = {}
        for w in acc_writes:
            w.w[k] = idx
        return op

    def new_phase(self):
        self.keymap = {}

    def dma(self, eng, semkey, fn, reads=(), writes=(), acc_writes=()):
        km = self.keymap
        if semkey not in km:
            km[semkey] = f"g{len(km)}"
        semkey = km[semkey]
        kr = self.keyres.get(semkey)
        if kr is None:
            kr = self.keyres[semkey] = Res(semkey)
        writes = list(writes) + [kr]
        op, idx = self._add(eng, fn, reads, writes, acc_writes, ("d", semkey))
        c = self.dma_cnt.get(semkey, 0) + 1
        self.dma_cnt[semkey] = c
        op[3] = semkey
        k = ("d", semkey)
        for r in reads:
            r.r[k] = c
        for w in writes:
            w.w = {k: c}
            w.r = {}
        for w in acc_writes:
            w.w[k] = c
        return op

    def emit(self, final_keys):
        nc = self.nc
        st = self.stack
        csem = {e: st.enter_context(nc.semaphore(f"c_{e}")) for e in ENGS if e != "sp"}
        dsem = {k: st.enter_context(nc.semaphore(f"d_{i}")) for i, k in enumerate(self.dma_cnt)}
        cum = {}
        for e in ENGS:
            c = 0
            arr = []
            for o in self.q[e]:
                if o[2]:
                    c += 1
                arr.append(c)
            cum[e] = arr
        engobj = {"pe": "tensor", "act": "scalar", "dve": "vector", "pool": "gpsimd", "sp": "sync"}
        with nc.Block() as block:
            for e in ENGS:
                ops = self.q[e]

                def body(eng, e=e, ops=ops):
                    for fn, waits, sig, semkey in ops:
                        for k, v in waits:
                            if k[0] == "c":
                                eng.wait_ge(csem[k[1]], cum[k[1]][v])
                            else:
                                eng.wait_ge(dsem[k[1]], 16 * v)
                        ins = fn(eng)
                        if semkey is not None:
                            ins.then_inc(dsem[semkey], 16)
                        elif sig:
                            ins.then_inc(csem[e], 1)
                    if e == "sp":
                        for k in final_keys:
                            eng.wait_ge(dsem[k], 16 * self.dma_cnt[k])
                getattr(block, engobj[e])(body)


def _bf(a):
    return np.ascontiguousarray(a.astype(ml_dtypes.bfloat16))


def _dft_tables(L):
    N = 2 * L
    nf = L + 1
    nch = (nf + 127) // 128
    S = nch * 128
    a = np.arange(S, dtype=np.int64)
    m = (a[:, None] * a[None, :]) % N
    ang = 2.0 * np.pi * m.astype(np.float64) / N
    valid = (a[:, None] <= L) & (a[None, :] <= L)
    q = np.where(valid, np.cos(ang), 0.0)
    r = np.where(valid, np.sin(ang), 0.0)
    wf = np.where(a <= L, 2.0 / N, 0.0)
    wf[0] = 1.0 / N
    wf[L] = 1.0 / N
    wfc = wf.reshape(nch, 128).T.astype(np.float32)
    ntc = L // 128
    qf = q[:L].reshape(ntc, 128, nch, 128).transpose(2, 1, 0, 3)
    rf = r[:L].reshape(ntc, 128, nch, 128).transpose(2, 1, 0, 3)
    return (_bf(q.reshape(nch, 128, S)), _bf(r.reshape(nch, 128, S)), np.ascontiguousarray(wfc), nch, S, _bf(qf), _bf(rf))


def _filter_consts(L):
    pos = np.arange(L, dtype=np.float32)
    t = pos / np.float32(max(L - 1, 1))
    w = (np.float32(2.0 * math.pi) * pos / np.float32(L)).astype(np.float32)
    bands = np.linspace(1e-4, 15, 16, dtype=np.float32)
    z = np.concatenate([t[:, None], np.cos(w[:, None] * bands), -np.sin(w[:, None] * bands)], axis=-1)
    zT = np.ascontiguousarray(z.T.astype(np.float32))
    negt = np.ascontiguousarray((-t).reshape(L // 128, 128).T.astype(np.float32))
    return zT, negt


def _host_consts():
    c = {}
    tpos = np.arange(TS)
    rowpos = (tpos // 64).astype(np.float32)
    colpos = (tpos % 64).astype(np.float32)
    inv = (10000.0 ** (-np.arange(16, dtype=np.float32) / 16)).astype(np.float32)
    cos = np.zeros((128, TS), np.float32)
    sins = np.zeros((128, TS), np.float32)
    perm = np.zeros((128, 128), np.float32)
    for p in range(2):
        for a in range(2):
            posv = rowpos if a == 0 else colpos
            for hf in range(2):
                for f in range(16):
                    row = p * 64 + a * 32 + hf * 16 + f
                    ang = (posv * inv[f]).astype(np.float32)
                    cos[row] = np.cos(ang)
                    sins[row] = np.sin(ang) * (-1.0 if hf == 0 else 1.0)
                    other = p * 64 + a * 32 + (1 - hf) * 16 + f
                    perm[other, row] = 1.0
    c["rcos"] = cos
    c["rsin"] = sins
    c["perm"] = _bf(perm)
    c["identb"] = _bf(np.eye(128, dtype=np.float32))
    c["identf"] = np.eye(128, dtype=np.float32)
    c["onesf"] = np.ones((128, 128), np.float32)
    for nm, L in (("s", TS), ("p", 256)):
        q, r, wf, nch, S, qf, rf = _dft_tables(L)
        c["q" + nm], c["r" + nm], c["wf" + nm] = q, r, wf
        if nm == "s":
            c["qfs"], c["rfs"] = qf, rf
        zT, negt = _filter_consts(L)
        c["z" + nm], c["negt" + nm] = zT, negt
    deltas = np.abs(np.linspace(math.log(1e-2) / 1.5, math.log(1e-2) / 0.3, D, dtype=np.float32))
    c["delta"] = deltas.astype(np.float32)
    m0 = np.ones((128, 1), np.float32)
    m0[0, 0] = 0.0
    c["mask0"] = m0
    return c


W_SPECS = [
    ("ada_w", [4, D, 6 * D]), ("ada_b", [4, 6 * D]), ("norm1_g", [4, D]), ("norm2_g", [4, D]),
    ("attn_w_qkv", [2, D, 3 * D]), ("attn_lambda", [2, 4, 64]), ("attn_subln_g", [2, 128]),
    ("attn_w_o", [2, D, D]), ("hy_w_in", [2, D, 3 * D]), ("hy_b_in", [2, 3 * D]),
    ("hy_conv_w", [2, 3, 3 * D]), ("hy_conv_b", [2, 3 * D]), ("filt_w1", [2, 33, 64]),
    ("filt_b1", [2, 64]), ("filt_w2", [2, 64, 64]), ("filt_b2", [2, 64]), ("filt_w3", [2, 64, 2 * D]),
    ("filt_b3", [2, 2 * D]), ("filt_freq", [2, 64]), ("hy_skip", [2, D]), ("hy_w_out", [2, D, D]),
    ("hy_b_out", [2, D]), ("ffn_w_gu", [4, D, 2 * DFF]), ("ffn_w_down", [4, DFF, D]), ("final_g", [D]),
]
CONST_SPECS = [
    ("rcos", [128, TS], F32), ("rsin", [128, TS], F32), ("perm", [128, 128], BF16),
    ("identb", [128, 128], BF16), ("identf", [128, 128], F32), ("onesf", [128, 128], F32),
    ("qs", [33, 128, 4224], BF16), ("rs", [33, 128, 4224], BF16), ("wfs", [128, 33], F32),
    ("qfs", [33, 128, 32, 128], BF16), ("rfs", [33, 128, 32, 128], BF16),
    ("zs", [33, TS], F32), ("negts", [128, 32], F32),
    ("qp", [3, 128, 384], BF16), ("rp", [3, 128, 384], BF16), ("wfp", [128, 3], F32),
    ("zp", [33, 256], F32), ("negtp", [128, 2], F32),
    ("delta", [D], F32), ("mask0", [128, 1], F32),
]

GROUPS = [
    dict(name="s", tok0=0, T=TS, nseq=1, L=TS, rope=True, ncache=512, tiles=list(range(0, 8)), tt=list(range(0, 32))),
    dict(name="p", tok0=TS, T=TP, nseq=2, L=256, rope=False, ncache=0, tiles=[8], tt=list(range(32, 36))),
]


def build_program(stop_after=None, debug_outs=()):
    nc = bass.Bass("TRN2", target_bir_lowering=False)
    I = {}
    I["x"] = nc.dram_tensor("x", [T, D], F32, kind="ExternalInput").ap()
    I["ck"] = nc.dram_tensor("ck", [2, 512, D], F32, kind="ExternalInput").ap()
    I["cv"] = nc.dram_tensor("cv", [2, 512, D], F32, kind="ExternalInput").ap()
    I["cvec"] = nc.dram_tensor("cvec", [2, D], F32, kind="ExternalInput").ap()
    for nm, shp in W_SPECS:
        I[nm] = nc.dram_tensor(nm, shp, F32, kind="ExternalInput").ap()
    for nm, shp, dt in CONST_SPECS:
        I[nm] = nc.dram_tensor(nm, shp, dt, kind="ExternalInput").ap()
    O = {}
    O["y"] = nc.dram_tensor("y", [T, D], F32, kind="ExternalOutput").ap()
    O["nk"] = nc.dram_tensor("nk", [2, 2, 256, D], F32, kind="ExternalOutput").ap()
    O["nv"] = nc.dram_tensor("nv", [2, 2, 256, D], F32, kind="ExternalOutput").ap()

    def scratch(nm, shp, dt):
        kind = "ExternalOutput" if nm in debug_outs else "Internal"
        return nc.dram_tensor(nm, shp, dt, kind=kind).ap()
    S = {}
    S["X"] = scratch("X", [T, D], F32)
    S["MODS"] = scratch("MODS", [2, 128, 6 * D], F32)
    S["KT"] = scratch("KT", [8, 128, NKEY], BF16)
    S["VS"] = scratch("VS", [NKEY, D], BF16)
    S["QT"] = scratch("QT", [8, 128, T], BF16)
    S["MID"] = scratch("MID", [NCH_FF, 128, T], BF16)
    S["VVT"] = scratch("VVT", [8, 128, T], F32)
    S["X0T"] = scratch("X0T", [8, 128, T], F32)
    S["VVTOK"] = scratch("VVTOK", [T, D], BF16)
    S["HSs"] = scratch("HSs", [TS, D], BF16)
    S["HDs"] = scratch("HDs", [TS, D], BF16)
    S["HSp"] = scratch("HSp", [256, D], BF16)
    S["HDp"] = scratch("HDp", [256, D], BF16)
    S["YF"] = scratch("YF", [2, 33, 128, 2, 512], BF16)
    S["HTD"] = scratch("HTD", [128, 8, T], BF16)

    ARENA_WORDS = 49152
    P = Prog(nc, ARENA_WORDS)
    PS = P.stack.enter_context(nc.psum_tensor("ps", [128, 4096], F32))
    psr = [Res(f"bank{b}", excl=True) for b in range(8)]

    def bank(b, n=512):
        return PS[:, b * 512:b * 512 + n]

    def bankb(b):
        return PS[:, b * 512:(b + 1) * 512].bitcast(BF16)

    R = {}
    R["X"] = [[Res(f"X{i}a"), Res(f"X{i}b")] for i in range(NTT)]
    R["MODS"] = [Res(), Res()]
    R["KT"] = Res()
    R["VS"] = Res()
    R["QT"] = Res()
    R["MID"] = [Res() for _ in range(NT)]
    R["VVT"] = Res()
    R["X0T"] = Res()
    R["VVTOK"] = Res()
    R["HS"] = Res()
    R["YF"] = Res()
    R["OUT"] = Res()
    R["HTD"] = Res()
    semctr = [0]

    def sk(prefix):
        semctr[0] += 1
        return f"{prefix}{semctr[0]}"

    hT, _ = P.alloc([128, 8, T], BF16, "hT")
    hTr = [Res(f"hT{i}") for i in range(NT)]
    identb, identb_r = P.alloc([128, 128], BF16, "identb")
    identf, identf_r = P.alloc([128, 128], F32, "identf")
    onesf, onesf_r = P.alloc([128, 128], F32, "onesf")
    SIL, SIL_r = P.alloc([128, 2, 8, 128], BF16, "SIL")
    P.dma("sp", "c_identb", lambda e: e.dma_start(out=identb, in_=I["identb"]), writes=[identb_r])
    P.dma("sp", "c_identf", lambda e: e.dma_start(out=identf, in_=I["identf"]), writes=[identf_r])
    P.dma("sp", "c_onesf", lambda e: e.dma_start(out=onesf, in_=I["onesf"]), writes=[onesf_r])

    def load_cols(dst, dst_r, col0, vec, nchunks, key):
        P.dma("sp", key, lambda e: e.dma_start(
            out=dst[:, col0:col0 + nchunks], in_=vec.rearrange("(c p) -> p c", p=128), allow_slow_non_contiguous=True),
            acc_writes=[dst_r])

    def bcast_rows(vec1d, n):
        return vec1d.rearrange("(o n) -> o n", o=1).broadcast_to([128, n])

    m0 = P.mark()
    ccol, ccol_r = P.alloc([128, 16], F32, "ccol")
    scol, scol_r = P.alloc([128, 16], F32, "scol")
    for g in range(2):
        load_cols(ccol, ccol_r, g * 8, I["cvec"][g], 8, "ccol")
    P.op("act", lambda e: e.activation(out=scol, in_=ccol, func=AF.Silu), reads=[ccol_r], writes=[scol_r])
    for g in range(2):
        for kc in range(8):
            P.op("dve", lambda e, g=g, kc=kc: e.tensor_scalar(
                out=SIL[:, g, kc, :], in0=onesf, scalar1=scol[:, g * 8 + kc:g * 8 + kc + 1], scalar2=None,
                op0=ALU.mult), reads=[scol_r, onesf_r], acc_writes=[SIL_r])
    P.release(m0)

    for tt in range(NTT):
        P.dma("sp", f"xcopy{tt % 4}", lambda e, tt=tt: e.dma_start(
            out=S["X"][tt * 128:(tt + 1) * 128, :], in_=I["x"][tt * 128:(tt + 1) * 128, :]), writes=R["X"][tt])

    def phase_mod(i):
        P.new_phase()
        m = P.mark()
        adab, adab_r = P.alloc([128, 6 * D], F32, "adab")
        mod = [P.alloc([128, 6 * D], F32, f"mod{g}") for g in range(2)]
        gn = [P.alloc([128, D], F32, f"gn{k}") for k in range(2)]
        wch = [P.alloc([128, 8, 512], BF16, f"adw{k}") for k in range(2)]
        P.dma("sp", "adab", lambda e: e.dma_start(out=adab, in_=bcast_rows(I["ada_b"][i], 6 * D)), writes=[adab_r])
        P.dma("sp", "gn0", lambda e: e.dma_start(out=gn[0][0], in_=bcast_rows(I["norm1_g"][i], D)), writes=[gn[0][1]])
        P.dma("sp", "gn1", lambda e: e.dma_start(out=gn[1][0], in_=bcast_rows(I["norm2_g"][i], D)), writes=[gn[1][1]])
        for n in range(12):
            w, wr = wch[n % 2]
            P.dma("pool", f"adw{n % 2}", lambda e, w=w, n=n: e.dma_start(
                out=w, in_=I["ada_w"][i][:, n * 512:(n + 1) * 512].rearrange("(kc p) n -> p kc n", p=128)), writes=[wr])
            for g in range(2):
                b = (n * 2 + g) % 8
                for kc in range(8):
                    P.op("pe", lambda e, w=w, g=g, kc=kc, b=b: e.matmul(
                        bank(b), lhsT=SIL[:, g, kc, :], rhs=w[:, kc, :], start=(kc == 0), stop=(kc == 7)),
                        reads=[SIL_r, wr], writes=[psr[b]] if kc == 0 else (), acc_writes=[psr[b]] if kc else ())
                P.op("dve", lambda e, g=g, n=n, b=b: e.tensor_tensor(
                    out=mod[g][0][:, n * 512:(n + 1) * 512], in0=bank(b), in1=adab[:, n * 512:(n + 1) * 512], op=ALU.add),
                    reads=[psr[b], adab_r], acc_writes=[mod[g][1]])
        for g in range(2):
            for k, off in ((0, D), (1, 4 * D)):
                P.op("dve", lambda e, g=g, k=k, off=off: e.scalar_tensor_tensor(
                    out=mod[g][0][:, off:off + D], in0=mod[g][0][:, off:off + D], scalar=1.0, in1=gn[k][0],
                    op0=ALU.add, op1=ALU.mult), reads=[mod[g][1], gn[k][1]], acc_writes=[mod[g][1]])
            P.dma("sp", f"mods{g}", lambda e, g=g: e.dma_start(out=S["MODS"][g], in_=mod[g][0]),
                  reads=[mod[g][1]], writes=[R["MODS"][g]])
        P.release(m)

    def load_mod(off, key):
        out = []
        for g in range(2):
            t, r = P.alloc([128, D], F32, f"{key}{g}")
            P.dma("sp", f"ldm_{key}{g}", lambda e, t=t, g=g: e.dma_start(out=t, in_=S["MODS"][g][:, off:off + D]),
                  reads=[R["MODS"][g]], writes=[r])
            out.append((t, r))
        return out

    def rstd_ops(ss, ss_r, rs, rs_r, n, eps):
        P.op("act", lambda e: e.activation(out=rs, in_=ss, func=AF.Ln, scale=1.0 / n, bias=float(eps)),
             reads=[ss_r], writes=[rs_r])
        P.op("act", lambda e: e.activation(out=rs, in_=rs, func=AF.Exp, scale=-0.5),
             reads=[rs_r], writes=[rs_r])

    def phase_norm(offA, offB, final=False):
        P.new_phase()
        m = P.mark()
        if final:
            gt, gr = P.alloc([128, D], F32, "fing")
            P.dma("sp", "fing", lambda e: e.dma_start(out=gt, in_=bcast_rows(I["final_g"], D)), writes=[gr])
            A = [(gt, gr), (gt, gr)]
            B = None
        else:
            A = load_mod(offA, "nA")
            B = load_mod(offB, "nB")
        xt = [P.alloc([128, D], F32, f"nx{k}") for k in range(3)]
        x2 = [P.alloc([128, D], F32, f"nx2{k}") for k in range(2)]
        hb = [P.alloc([128, D], BF16, f"nhb{k}") for k in range(2)]
        junk, junk_r = P.alloc([128, D], BF16, "njunk")
        ss = [P.alloc([128, 1], F32, f"nss{k}") for k in range(2)]
        rs = [P.alloc([128, 1], F32, f"nrs{k}") for k in range(2)]
        for tt in range(NTT):
            g = 0 if tt < 32 else 1
            x, xr = xt[tt % 3]
            y, yr = x2[tt % 2]
            h, hr = hb[tt % 2]
            s_, s_r = ss[tt % 2]
            r_, r_r = rs[tt % 2]
            P.dma("sp", f"nx{tt % 3}", lambda e, x=x, tt=tt: e.dma_start(out=x, in_=S["X"][tt * 128:(tt + 1) * 128, :]),
                  reads=R["X"][tt], writes=[xr])
            P.op("act", lambda e, x=x, s_=s_: e.activation(out=junk, in_=x, func=AF.Square, accum_out=s_),
                 reads=[xr], writes=[junk_r, s_r])
            rstd_ops(s_, s_r, r_, r_r, D, EPS)
            if final:
                P.op("dve", lambda e, x=x, y=y, r_=r_, g=g: e.scalar_tensor_tensor(
                    out=y, in0=x, scalar=r_, in1=A[g][0], op0=ALU.mult, op1=ALU.mult),
                    reads=[xr, r_r, A[g][1]], writes=[yr])
                P.dma("sp", f"fo{tt % 2}", lambda e, y=y, tt=tt: e.dma_start(out=O["y"][tt * 128:(tt + 1) * 128, :], in_=y),
                      reads=[yr], acc_writes=[R["OUT"]])
                continue
            P.op("dve", lambda e, x=x, y=y, r_=r_, g=g: e.scalar_tensor_tensor(
                out=y, in0=x, scalar=r_, in1=A[g][0], op0=ALU.mult, op1=ALU.mult),
                reads=[xr, r_r, A[g][1]], writes=[yr])
            P.op("pool", lambda e, y=y, h=h, g=g: e.tensor_tensor(out=h, in0=y, in1=B[g][0], op=ALU.add),
                 reads=[yr, B[g][1]], writes=[hr])
            b = tt % 2
            for kc in range(8):
                P.op("pe", lambda e, h=h, kc=kc, b=b: e.transpose(bankb(b)[:, kc * 128:(kc + 1) * 128], h[:, kc * 128:(kc + 1) * 128], identb),
                     reads=[hr, identb_r], writes=[psr[b]] if kc == 0 else (), acc_writes=[psr[b]] if kc else ())
            P.op("act", lambda e, b=b, tt=tt: e.copy(out=hT[:, :, tt * 128:(tt + 1) * 128],
                                                     in_=bankb(b).rearrange("p (k t) -> p k t", t=128)),
                 reads=[psr[b]], acc_writes=[hTr[tt // 4]])
        P.release(m)

    def stream_w(dst, dst_r, key, src_ap):
        P.dma("pool", key, lambda e: e.dma_start(out=dst, in_=src_ap.rearrange("(kc p) n -> p kc n", p=128)), writes=[dst_r])

    def mm_wstat(b, w, wr, tile, ncols=512):
        for kc in range(8):
            P.op("pe", lambda e, kc=kc: e.matmul(bank(b, ncols), lhsT=w[:, kc, :], rhs=hT[:, kc, tile * 512:tile * 512 + ncols],
                                                 start=(kc == 0), stop=(kc == 7)),
                 reads=[wr, hTr[tile]], writes=[psr[b]] if kc == 0 else (), acc_writes=[psr[b]] if kc else ())

    def mm_tstat(b, w, wr, tt):
        for kc in range(8):
            P.op("pe", lambda e, kc=kc: e.matmul(bank(b), lhsT=hT[:, kc, tt * 128:(tt + 1) * 128], rhs=w[:, kc, :],
                                                 start=(kc == 0), stop=(kc == 7)),
                 reads=[wr, hTr[tt // 4]], writes=[psr[b]] if kc == 0 else (), acc_writes=[psr[b]] if kc else ())

    def phase_qkv(j):
        P.new_phase()
        m = P.mark()
        rcos, rcos_r = P.alloc([128, TS], F32, "rcos")
        rsin, rsin_r = P.alloc([128, TS], F32, "rsin")
        perm, perm_r = P.alloc([128, 128], BF16, "perm")
        P.dma("sp", "rcos", lambda e: e.dma_start(out=rcos, in_=I["rcos"]), writes=[rcos_r])
        P.dma("sp", "rsin", lambda e: e.dma_start(out=rsin, in_=I["rsin"]), writes=[rsin_r])
        P.dma("sp", "perm", lambda e: e.dma_start(out=perm, in_=I["perm"]), writes=[perm_r])
        ckt = [P.alloc([128, D], BF16, f"ckt{k}") for k in range(2)]
        kst = [P.alloc([128, 8, 128], BF16, f"kst{k}") for k in range(2)]
        cvt = [P.alloc([128, D], BF16, f"cvt{k}") for k in range(2)]
        for tk in range(4):
            c_, c_r = ckt[tk % 2]
            k_, k_r = kst[tk % 2]
            v_, v_r = cvt[tk % 2]
            P.dma("pool", f"ckt{tk % 2}", lambda e, c_=c_, tk=tk: e.dma_start(out=c_, in_=I["ck"][j, tk * 128:(tk + 1) * 128, :]), writes=[c_r])
            b = tk % 2
            for h in range(8):
                P.op("pe", lambda e, c_=c_, h=h, b=b: e.transpose(bankb(b)[:, h * 128:(h + 1) * 128], c_[:, h * 128:(h + 1) * 128], identb),
                     reads=[c_r, identb_r], writes=[psr[b]] if h == 0 else (), acc_writes=[psr[b]] if h else ())
            P.op("act", lambda e, k_=k_, b=b: e.copy(out=k_, in_=bankb(b).rearrange("p (k t) -> p k t", t=128)), reads=[psr[b]], writes=[k_r])
            P.dma("sp", f"kst{tk % 2}", lambda e, k_=k_, tk=tk: e.dma_start(
                out=S["KT"][:, :, tk * 128:(tk + 1) * 128].rearrange("h p c -> p h c"), in_=k_), reads=[k_r], acc_writes=[R["KT"]])
            P.dma("pool", f"cvt{tk % 2}", lambda e, v_=v_, tk=tk: e.dma_start(out=v_, in_=I["cv"][j, tk * 128:(tk + 1) * 128, :]), writes=[v_r])
            P.dma("sp", f"cvo{tk % 2}", lambda e, v_=v_, tk=tk: e.dma_start(out=S["VS"][tk * 128:(tk + 1) * 128, :], in_=v_),
                  reads=[v_r], acc_writes=[R["VS"]])
        if stop_after == f"qkv{2 * j}a":
            raise Stop()
        wq = [P.alloc([128, 8, 128], BF16, f"wq{k}") for k in range(3)]
        qb = [P.alloc([128, 512], BF16, f"qb{k}") for k in range(2)]
        t1 = [P.alloc([128, 512], F32, f"t1{k}") for k in range(2)]
        t2 = [P.alloc([128, 512], F32, f"t2{k}") for k in range(2)]
        qr = [P.alloc([128, 512], BF16, f"qr{k}") for k in range(3)]
        it = 0
        for ch in range(16):
            isk, h = ch // 8, ch % 8
            w, wr = wq[ch % 3]
            stream_w(w, wr, f"wq{ch % 3}", I["attn_w_qkv"][j][:, isk * D + h * 128: isk * D + (h + 1) * 128])
            for tile in range(NT):
                bA = (it % 3) * 2
                bB = bA + 1
                q_, q_r = qr[it % 3]
                mm_wstat(bA, w, wr, tile)
                if tile < 8 and not os.environ.get('NOROPE'):
                    qq, qq_r = qb[it % 2]
                    a1, a1r = t1[it % 2]
                    a2, a2r = t2[it % 2]
                    P.op("act", lambda e, qq=qq, bA=bA: e.copy(out=qq, in_=bank(bA)), reads=[psr[bA]], writes=[qq_r])
                    P.op("pe", lambda e, qq=qq, bB=bB: e.matmul(bank(bB), lhsT=perm, rhs=qq, start=True, stop=True),
                         reads=[perm_r, qq_r], writes=[psr[bB]])
                    P.op("dve", lambda e, a1=a1, bA=bA, tile=tile: e.tensor_tensor(
                        out=a1, in0=bank(bA), in1=rcos[:, tile * 512:(tile + 1) * 512], op=ALU.mult),
                        reads=[psr[bA], rcos_r], writes=[a1r])
                    P.op("dve", lambda e, a2=a2, bB=bB, tile=tile: e.tensor_tensor(
                        out=a2, in0=bank(bB), in1=rsin[:, tile * 512:(tile + 1) * 512], op=ALU.mult),
                        reads=[psr[bB], rsin_r], writes=[a2r])
                    P.op("pool", lambda e, q_=q_, a1=a1, a2=a2: e.tensor_tensor(out=q_, in0=a1, in1=a2, op=ALU.add),
                         reads=[a1r, a2r], writes=[q_r])
                else:
                    P.op("act", lambda e, q_=q_, bA=bA: e.copy(out=q_, in_=bank(bA)), reads=[psr[bA]], writes=[q_r])
                if isk:
                    P.dma("sp", f"qro{it % 3}", lambda e, q_=q_, h=h, tile=tile: e.dma_start(
                        out=S["KT"][h, :, 512 + tile * 512: 512 + (tile + 1) * 512], in_=q_), reads=[q_r], acc_writes=[R["KT"]])
                else:
                    P.dma("sp", f"qro{it % 3}", lambda e, q_=q_, h=h, tile=tile: e.dma_start(
                        out=S["QT"][h, :, tile * 512:(tile + 1) * 512], in_=q_), reads=[q_r], acc_writes=[R["QT"]])
                it += 1
        if stop_after == f"qkv{2 * j}b":
            raise Stop()
        wv = [P.alloc([128, 8, 512], BF16, f"wv{k}") for k in range(2)]
        vb = [P.alloc([128, 512], BF16, f"vb{k}") for k in range(3)]
        vf = [P.alloc([128, 512], F32, f"vf{k}") for k in range(2)]
        it = 0
        for kind in ("v", "k"):
            for half in range(2):
                w, wr = wv[(it // 64) % 2]
                col0 = (2 * D if kind == "v" else D) + half * 512
                w, wr = wv[half]
                stream_w(w, wr, f"wv{half}{kind}", I["attn_w_qkv"][j][:, col0:col0 + 512])
                tts = range(NTT) if kind == "v" else range(32, 36)
                for tt in tts:
                    b = 6 + it % 2
                    mm_tstat(b, w, wr, tt)
                    if kind == "v":
                        v_, v_r = vb[it % 3]
                        P.op("act", lambda e, v_=v_, b=b: e.copy(out=v_, in_=bank(b)), reads=[psr[b]], writes=[v_r])
                        P.dma("sp", f"vbo{it % 3}", lambda e, v_=v_, tt=tt, half=half: e.dma_start(
                            out=S["VS"][512 + tt * 128: 512 + (tt + 1) * 128, half * 512:(half + 1) * 512], in_=v_),
                            reads=[v_r], acc_writes=[R["VS"]])
                    if tt >= 32:
                        f_, f_r = vf[it % 2]
                        P.op("dve", lambda e, f_=f_, b=b: e.tensor_copy(out=f_, in_=bank(b)), reads=[psr[b]], writes=[f_r])
                        s, r0 = (tt - 32) // 2, ((tt - 32) % 2) * 128
                        dst = O["nv"] if kind == "v" else O["nk"]
                        P.dma("sp", f"vfo{it % 2}", lambda e, f_=f_, s=s, r0=r0, half=half, dst=dst: e.dma_start(
                            out=dst[s, j, r0:r0 + 128, half * 512:(half + 1) * 512], in_=f_), reads=[f_r], acc_writes=[R["OUT"]])
                    it += 1
        P.release(m)

    def phase_attn(j, i):
        P.new_phase()
        m = P.mark()
        lam_init = 0.8 - 0.6 * math.exp(-0.3 * i)
        lp, lp_r = P.alloc([128, 4, 64], F32, "lp")
        lpp, lpp_r = P.alloc([128, 2, 64], F32, "lpp")
        lsum, lsum_r = P.alloc([128, 2], F32, "lsum")
        lexp, lexp_r = P.alloc([128, 2], F32, "lexp")
        nlam, nlam_r = P.alloc([128, 1], F32, "nlam")
        gsub, gsub_r = P.alloc([128, 128], F32, "gsub")
        P.dma("sp", "lp", lambda e: e.dma_start(out=lp, in_=I["attn_lambda"][j].rearrange("(o a) b -> o a b", o=1).broadcast_to([128, 4, 64])), writes=[lp_r])
        P.dma("sp", "gsub", lambda e: e.dma_start(out=gsub, in_=bcast_rows(I["attn_subln_g"][j], 128)), writes=[gsub_r])
        P.op("dve", lambda e: e.tensor_scalar(out=gsub, in0=gsub, scalar1=1.0 - lam_init, scalar2=None, op0=ALU.mult),
             reads=[gsub_r], writes=[gsub_r])
        for k in range(2):
            P.op("dve", lambda e, k=k: e.tensor_tensor(out=lpp[:, k, :], in0=lp[:, 2 * k, :], in1=lp[:, 2 * k + 1, :], op=ALU.mult),
                 reads=[lp_r], acc_writes=[lpp_r])
        P.op("dve", lambda e: e.tensor_reduce(out=lsum, in_=lpp, axis=AX.X, op=ALU.add), reads=[lpp_r], writes=[lsum_r])
        P.op("act", lambda e: e.activation(out=lexp, in_=lsum, func=AF.Exp), reads=[lsum_r], writes=[lexp_r])
        P.op("dve", lambda e: e.tensor_tensor(out=nlam, in0=lexp[:, 1:2], in1=lexp[:, 0:1], op=ALU.subtract), reads=[lexp_r], writes=[nlam_r])
        P.op("dve", lambda e: e.tensor_scalar(out=nlam, in0=nlam, scalar1=-lam_init, scalar2=None, op0=ALU.add), reads=[nlam_r], writes=[nlam_r])

        kTb = [P.alloc([128, 4608], BF16, f"kTh{k}") for k in range(2)]
        vhb = [P.alloc([128, 36, 132], BF16, f"vh{k}") for k in range(2)]
        for k in range(2):
            P.op("pool", lambda e, k=k: e.memset(vhb[k][0], 1.0), writes=[vhb[k][1]])
        qtb = [P.alloc([128, 2, 256], BF16, f"qt{k}") for k in range(3)]
        for k in range(3):
            P.op("pool", lambda e, k=k: e.memset(qtb[k][0], 0.0), writes=[qtb[k][1]])
        Pb = [P.alloc([128, 2, 256], BF16, f"P{k}") for k in range(3)]
        osb = [P.alloc([128, 128], F32, f"o{k}") for k in range(2)]
        obb = [P.alloc([128, 128], BF16, f"ob{k}") for k in range(2)]
        rrb = [P.alloc([128, 2], F32, f"rr{k}") for k in range(2)]
        ssb = [P.alloc([128, 1], F32, f"ss{k}") for k in range(2)]
        rsb = [P.alloc([128, 1], F32, f"rs{k}") for k in range(2)]
        junk, junk_r = P.alloc([128, 128], BF16, "ajunk")
        accs = [P.alloc([128, 4, 132], F32, f"accs{k}") for k in range(2)]
        steps = []
        hcount = 0
        for G in GROUPS:
            for s in range(G["nseq"]):
                L = G["L"]
                nk = G["ncache"] + L
                nkc = nk // 128
                kcol0 = 0 if G["name"] == "s" else 4608 + s * 256
                qtok0 = G["tok0"] + s * L
                for h in range(8):
                    for qb_ in range(L // 256):
                        for c in range(nkc):
                            steps.append(dict(h=h, hid=hcount, q0=qtok0 + qb_ * 256, c=c, nkc=nkc, nk=nk, kcol0=kcol0,
                                              newh=(qb_ == 0 and c == 0), newq=(c == 0)))
                    hcount += 1
        qcount = [0]
        ecount = [0]
        cur = {}

        heads = {st["hid"]: st for st in steps if st["newh"]}
        loaded = set()

        def load_head(hs_):
            kT, kT_r = kTb[hs_["hid"] % 2]
            vh, vh_r = vhb[hs_["hid"] % 2]
            nk, nkc, kcol0, hh = hs_["nk"], hs_["nkc"], hs_["kcol0"], hs_["h"]
            P.dma("sp", f"kTh{hs_['hid'] % 2}", lambda e: e.dma_start(
                out=kT[:, 0:nk], in_=S["KT"][hh, :, kcol0:kcol0 + nk]), reads=[R["KT"]], writes=[kT_r])
            P.dma("sp", f"vh{hs_['hid'] % 2}", lambda e: e.dma_start(
                out=vh[:, 0:nkc, 0:128], in_=S["VS"][kcol0:kcol0 + nk, hh * 128:(hh + 1) * 128].rearrange("(c p) e -> p c e", p=128)),
                reads=[R["VS"]], acc_writes=[vh_r])

        def stage_a(i, st):
            h = st["h"]
            if st["newh"]:
                if st["hid"] not in loaded:
                    loaded.add(st["hid"])
                    load_head(st)
                nxt = heads.get(st["hid"] + 1)
                if nxt is not None and st["nkc"] >= 12:
                    def pf(nxt=nxt):
                        if nxt["hid"] not in loaded:
                            loaded.add(nxt["hid"])
                            load_head(nxt)
                    deferred.append((i + LA, pf))
                cur["kT"], cur["vh"] = kTb[st["hid"] % 2], vhb[st["hid"] % 2]
            if st["newq"]:
                qt, qt_r = qtb[qcount[0] % 3]
                q0 = st["q0"]
                for mp in range(2):
                    P.dma("sp", f"qt{qcount[0] % 3}_{mp}", lambda e, mp=mp: e.dma_start(
                        out=qt[mp * 64:(mp + 1) * 64, mp, :], in_=S["QT"][h, mp * 64:(mp + 1) * 64, q0:q0 + 256]),
                        reads=[R["QT"]], acc_writes=[qt_r])
                qcount[0] += 1
                cur["qt"] = (qt, qt_r)
            kT, kT_r = cur["kT"]
            qt, qt_r = cur["qt"]
            st["vh"] = cur["vh"]
            c = st["c"]
            sb_ = i % 3
            Pt, Pt_r = Pb[i % 3]
            st["P"] = (Pt, Pt_r)
            for mp in range(2):
                P.op("pe", lambda e, mp=mp: e.matmul(
                    bank(sb_)[:, mp * 256:(mp + 1) * 256], lhsT=kT[:, c * 128:(c + 1) * 128],
                    rhs=qt[:, mp, :], start=True, stop=True),
                    reads=[kT_r, qt_r], writes=[psr[sb_]] if mp == 0 else (), acc_writes=[psr[sb_]] if mp else ())
            P.op("act", lambda e: e.activation(
                out=Pt, in_=bank(sb_).rearrange("p (m q) -> p m q", q=256), func=AF.Exp, scale=0.125),
                reads=[psr[sb_]], writes=[Pt_r])

        def stage_b(st):
            Pt, Pt_r = st["P"]
            vh, vh_r = st["vh"]
            c, nkc, h = st["c"], st["nkc"], st["h"]
            for qs in range(2):
                for mp in range(2):
                    ab = 4 + qs * 2 + mp
                    P.op("pe", lambda e, qs=qs, mp=mp, ab=ab: e.matmul(
                        bank(ab, 129), lhsT=Pt[:, mp, qs * 128:(qs + 1) * 128], rhs=vh[:, c, 0:129],
                        start=(c == 0), stop=(c == nkc - 1)),
                        reads=[Pt_r, vh_r], writes=[psr[ab]] if c == 0 else (), acc_writes=[psr[ab]] if c else ())
            if c != nkc - 1:
                return
            if nkc < 12:
                run_deferred(10 ** 9)
            ac, ac_r = accs[ecount[0] % 2]
            P.op("act", lambda e: e.copy(out=ac[:, :, 0:129], in_=PS[:, 4 * 512:8 * 512].rearrange("p (b c) -> p b c", c=512)[:, :, 0:129]),
                 reads=[psr[4], psr[5], psr[6], psr[7]], writes=[ac_r])
            for qs in range(2):
                a0, a1 = qs * 2, qs * 2 + 1
                o_, o_r = osb[ecount[0] % 2]
                ob, ob_r = obb[ecount[0] % 2]
                rr, rr_r = rrb[ecount[0] % 2]
                s_, s_r = ssb[ecount[0] % 2]
                r_, r_r = rsb[ecount[0] % 2]
                ecount[0] += 1
                tok = st["q0"] + qs * 128
                P.op("dve", lambda e, rr=rr, a0=a0: e.reciprocal(out=rr, in_=ac[:, a0:a0 + 2, 128]),
                     reads=[ac_r], writes=[rr_r])
                P.op("dve", lambda e, rr=rr: e.tensor_tensor(out=rr[:, 1:2], in0=rr[:, 1:2], in1=nlam, op=ALU.mult),
                     reads=[rr_r, nlam_r], writes=[rr_r])
                P.op("dve", lambda e, o_=o_, rr=rr, a0=a0: e.tensor_scalar(
                    out=o_, in0=ac[:, a0, 0:128], scalar1=rr[:, 0:1], scalar2=None, op0=ALU.mult),
                    reads=[ac_r, rr_r], writes=[o_r])
                P.op("dve", lambda e, o_=o_, rr=rr, a1=a1: e.scalar_tensor_tensor(
                    out=o_, in0=ac[:, a1, 0:128], scalar=rr[:, 1:2], in1=o_, op0=ALU.mult, op1=ALU.add),
                    reads=[ac_r, rr_r, o_r], writes=[o_r])
                P.op("dve", lambda e, o_=o_, s_=s_: e.scalar_tensor_tensor(
                    out=junk, in0=o_, scalar=1.0, in1=o_, op0=ALU.mult, op1=ALU.mult, accum_out=s_),
                    reads=[o_r], writes=[junk_r, s_r])

                def e2(o_=o_, o_r=o_r, ob=ob, ob_r=ob_r, s_=s_, s_r=s_r, r_=r_, r_r=r_r):
                    rstd_ops(s_, s_r, r_, r_r, 128, SUBLN_EPS)
                    P.op("dve", lambda e: e.scalar_tensor_tensor(
                        out=ob, in0=o_, scalar=r_, in1=gsub, op0=ALU.mult, op1=ALU.mult),
                        reads=[o_r, r_r, gsub_r], writes=[ob_r])

                def e4(ob=ob, ob_r=ob_r):
                    P.op("pe", lambda e: e.transpose(bankb(3)[:, 0:128], ob, identb), reads=[ob_r, identb_r], writes=[psr[3]])

                def e5(tok=tok, h=h):
                    P.op("act", lambda e: e.copy(out=hT[:, h, tok:tok + 128], in_=bankb(3)[:, 0:128]),
                         reads=[psr[3]], acc_writes=[hTr[tok // 512]])
                if nkc >= 12:
                    base = st["idx"] + LA
                    deferred.append((base + 2 + qs, e2))
                    deferred.append((base + 4 + 3 * qs, e4))
                    deferred.append((base + 6 + 3 * qs, e5))
                else:
                    e2()
                    e4()
                    e5()

        LA = 2
        deferred = []

        def run_deferred(now):
            keep = []
            for due, fn in deferred:
                if due <= now:
                    fn()
                else:
                    keep.append((due, fn))
            deferred[:] = keep
        for i, st in enumerate(steps):
            st["idx"] = i
            stage_a(i, st)
            if i >= LA:
                stage_b(steps[i - LA])
            run_deferred(i)
        for st in steps[-LA:]:
            stage_b(st)
        run_deferred(10 ** 9)
        P.release(m)

    def phase_outproj(wsrc, bias_vec, offG):
        P.new_phase()
        m = P.mark()
        G_ = load_mod(offG, "oG")
        if bias_vec is not None:
            brep, brep_r = P.alloc([128, D], F32, "obias")
            P.dma("sp", "obias", lambda e: e.dma_start(out=brep, in_=bcast_rows(bias_vec, D)), writes=[brep_r])
        wo = [P.alloc([128, 8, 512], BF16, f"wo{k}") for k in range(2)]
        xt = [P.alloc([128, 512], F32, f"ox{k}") for k in range(3)]
        tb = [P.alloc([128, 512], F32, f"ot{k}") for k in range(2)]
        it = 0
        for half in range(2):
            w, wr = wo[half]
            stream_w(w, wr, f"wo{half}", wsrc[:, half * 512:(half + 1) * 512])
            for tt in range(NTT):
                g = 0 if tt < 32 else 1
                b = it % 4
                x, xr = xt[it % 3]
                t_, t_r = tb[it % 2]
                P.dma("sp", f"ox{it % 3}", lambda e, x=x, tt=tt, half=half: e.dma_start(
                    out=x, in_=S["X"][tt * 128:(tt + 1) * 128, half * 512:(half + 1) * 512]), reads=[R["X"][tt][half]], writes=[xr])
                mm_tstat(b, w, wr, tt)
                src = bank(b)
                if bias_vec is not None:
                    P.op("dve", lambda e, t_=t_, b=b, half=half: e.tensor_tensor(
                        out=t_, in0=bank(b), in1=brep[:, half * 512:(half + 1) * 512], op=ALU.add),
                        reads=[psr[b], brep_r], writes=[t_r])
                    P.op("dve", lambda e, t_=t_, g=g, half=half: e.tensor_tensor(
                        out=t_, in0=t_, in1=G_[g][0][:, half * 512:(half + 1) * 512], op=ALU.mult),
                        reads=[t_r, G_[g][1]], writes=[t_r])
                else:
                    P.op("dve", lambda e, t_=t_, b=b, g=g, half=half: e.tensor_tensor(
                        out=t_, in0=bank(b), in1=G_[g][0][:, half * 512:(half + 1) * 512], op=ALU.mult),
                        reads=[psr[b], G_[g][1]], writes=[t_r])
                P.op("pool", lambda e, x=x, t_=t_: e.tensor_tensor(out=x, in0=x, in1=t_, op=ALU.add), reads=[xr, t_r], writes=[xr])
                P.dma("sp", f"oxo{it % 3}", lambda e, x=x, tt=tt, half=half: e.dma_start(
                    out=S["X"][tt * 128:(tt + 1) * 128, half * 512:(half + 1) * 512], in_=x), reads=[xr], writes=[R["X"][tt][half]])
                it += 1
        P.release(m)

    def phase_ffn(i):
        P.new_phase()
        m = P.mark()
        wg = [P.alloc([128, 2, 8, 128], BF16, f"wg{k}") for k in range(3)]
        sg = [P.alloc([128, 512], F32, f"sg{k}") for k in range(2)]
        md = [P.alloc([128, 512], BF16, f"md{k}") for k in range(3)]
        it = 0
        for c in range(NCH_FF):
            w, wr = wg[c % 3]
            for u in range(2):
                P.dma("pool", f"wg{c % 3}_{u}", lambda e, w=w, c=c, u=u: e.dma_start(
                    out=w[:, u], in_=I["ffn_w_gu"][i][:, u * DFF + c * 128: u * DFF + (c + 1) * 128].rearrange("(kc p) n -> p kc n", p=128)),
                    writes=[wr] if u == 0 else (), acc_writes=[wr] if u else ())
            for tile in range(NT):
                bG = (it % 4) * 2
                bU = bG + 1
                s_, s_r = sg[it % 2]
                m_, m_r = md[it % 3]
                mm_wstat(bG, w[:, 0], wr, tile)
                mm_wstat(bU, w[:, 1], wr, tile)
                P.op("act", lambda e, s_=s_, bG=bG: e.activation(out=s_, in_=bank(bG), func=AF.Silu), reads=[psr[bG]], writes=[s_r])
                P.op("dve", lambda e, m_=m_, s_=s_, bU=bU: e.tensor_tensor(out=m_, in0=bank(bU), in1=s_, op=ALU.mult),
                     reads=[psr[bU], s_r], writes=[m_r])
                P.dma("sp", f"mdo{it % 3}", lambda e, m_=m_, c=c, tile=tile: e.dma_start(
                    out=S["MID"][c, :, tile * 512:(tile + 1) * 512], in_=m_), reads=[m_r], acc_writes=[R["MID"][tile]])
                it += 1
        P.release(m)
        P.new_phase()
        m = P.mark()
        G_ = load_mod(5 * D, "fG")
        wd, wd_r = P.alloc([128, NCH_FF, D], BF16, "wd")
        P.dma("pool", "wd", lambda e: e.dma_start(out=wd, in_=I["ffn_w_down"][i].rearrange("(c p) n -> p c n", p=128)), writes=[wd_r])
        mt = [P.alloc([128, NCH_FF, 512], BF16, f"mt{k}") for k in range(2)]
        xt = [P.alloc([128, 512], F32, f"fx{k}") for k in range(3)]
        tb = [P.alloc([128, 512], F32, f"ft{k}") for k in range(2)]
        it = 0
        for tile in range(NT):
            g = 0 if tile < 8 else 1
            mt_, mt_r = mt[tile % 2]
            P.dma("sp", f"mt{tile % 2}", lambda e, mt_=mt_, tile=tile: e.dma_start(
                out=mt_, in_=S["MID"][:, :, tile * 512:(tile + 1) * 512].rearrange("c p t -> p c t")), reads=[R["MID"][tile]], writes=[mt_r])
            for ts in range(4):
                tt = tile * 4 + ts
                for half in range(2):
                    b = it % 4
                    x, xr = xt[it % 3]
                    t_, t_r = tb[it % 2]
                    P.dma("sp", f"fx{it % 3}", lambda e, x=x, tt=tt, half=half: e.dma_start(
                        out=x, in_=S["X"][tt * 128:(tt + 1) * 128, half * 512:(half + 1) * 512]), reads=[R["X"][tt][half]], writes=[xr])
                    for c in range(NCH_FF):
                        P.op("pe", lambda e, mt_=mt_, c=c, ts=ts, half=half, b=b: e.matmul(
                            bank(b), lhsT=mt_[:, c, ts * 128:(ts + 1) * 128], rhs=wd[:, c, half * 512:(half + 1) * 512],
                            start=(c == 0), stop=(c == NCH_FF - 1)),
                            reads=[mt_r, wd_r], writes=[psr[b]] if c == 0 else (), acc_writes=[psr[b]] if c else ())
                    P.op("dve", lambda e, t_=t_, b=b, g=g, half=half: e.tensor_tensor(
                        out=t_, in0=bank(b), in1=G_[g][0][:, half * 512:(half + 1) * 512], op=ALU.mult),
                        reads=[psr[b], G_[g][1]], writes=[t_r])
                    P.op("pool", lambda e, x=x, t_=t_: e.tensor_tensor(out=x, in0=x, in1=t_, op=ALU.add), reads=[xr, t_r], writes=[xr])
                    P.dma("sp", f"fxo{it % 3}", lambda e, x=x, tt=tt, half=half: e.dma_start(
                        out=S["X"][tt * 128:(tt + 1) * 128, half * 512:(half + 1) * 512], in_=x), reads=[xr], writes=[R["X"][tt][half]])
                    it += 1
        P.release(m)

    def phase_hy_in(j):
        P.new_phase()
        m = P.mark()
        CV, CV_r = P.alloc([128, 120], F32, "CV")
        load_cols(CV, CV_r, 0, I["hy_b_in"][j], 24, "cvl")
        for k in range(3):
            load_cols(CV, CV_r, 24 + k * 24, I["hy_conv_w"][j, k], 24, "cvl")
        load_cols(CV, CV_r, 96, I["hy_conv_b"][j], 24, "cvl")
        ubS = [P.alloc([128, TS + 2], F32, f"ubS{k}") for k in range(2)]
        ubP = [P.alloc([128, 2, 258], F32, f"ubP{k}") for k in range(2)]
        for k in range(2):
            P.op("pool", lambda e, k=k: e.memset(ubS[k][0], 0.0), writes=[ubS[k][1]])
            P.op("pool", lambda e, k=k: e.memset(ubP[k][0], 0.0), writes=[ubP[k][1]])
        cb1, cb1_r = P.alloc([128, T], F32, "cb1")
        cb2 = [P.alloc([128, T], F32, f"cb2{k}") for k in range(2)]
        vst, vst_r = P.alloc([128, NTT, 128], BF16, "vst")
        wi = [P.alloc([128, 8, 128], BF16, f"wi{k}") for k in range(3)]
        it = 0
        uc = 0
        c2 = 0
        for jd in range(8):
            for part, fc in (("x1", 8 + jd), ("v", 16 + jd), ("x0", jd)):
                w, wr = wi[it % 3]
                it += 1
                stream_w(w, wr, f"wi{it % 3}", I["hy_w_in"][j][:, fc * 128:(fc + 1) * 128])
                uS, uS_r = ubS[uc % 2]
                uP, uP_r = ubP[uc % 2]
                uc += 1
                for tile in range(NT):
                    b = tile % 4
                    mm_wstat(b, w, wr, tile)
                    if tile < 8:
                        P.op("act", lambda e, uS=uS, b=b, tile=tile, fc=fc: e.activation(
                            out=uS[:, 1 + tile * 512: 1 + (tile + 1) * 512], in_=bank(b), func=AF.Identity, bias=CV[:, fc:fc + 1]),
                            reads=[psr[b], CV_r], acc_writes=[uS_r])
                    else:
                        P.op("act", lambda e, uP=uP, b=b, fc=fc: e.activation(
                            out=uP[:, :, 1:257], in_=bank(b).rearrange("p (s t) -> p s t", t=256), func=AF.Identity, bias=CV[:, fc:fc + 1]),
                            reads=[psr[b], CV_r], acc_writes=[uP_r])
                if part == "x1":
                    dst, dst_r = cb1, cb1_r
                else:
                    dst, dst_r = cb2[c2 % 2]
                    c2 += 1
                w0, w1, w2, cbc = (CV[:, 24 + fc:25 + fc], CV[:, 48 + fc:49 + fc], CV[:, 72 + fc:73 + fc], CV[:, 96 + fc:97 + fc])
                dS = dst[:, 0:TS]
                dP = dst[:, TS:T].rearrange("p (s t) -> p s t", t=256)
                for (dd, uu, ur, n) in ((dS, uS, uS_r, TS), (dP, uP, uP_r, 256)):
                    def sl(o, uu=uu, n=n):
                        return uu[:, o:o + n] if len(uu.shape) == 2 else uu[:, :, o:o + n]
                    P.op("dve", lambda e, dd=dd, sl=sl, w0=w0, cbc=cbc: e.tensor_scalar(out=dd, in0=sl(0), scalar1=w0, scalar2=cbc, op0=ALU.mult, op1=ALU.add),
                         reads=[ur, CV_r], acc_writes=[dst_r])
                    P.op("dve", lambda e, dd=dd, sl=sl, w1=w1: e.scalar_tensor_tensor(out=dd, in0=sl(1), scalar=w1, in1=dd, op0=ALU.mult, op1=ALU.add),
                         reads=[ur, CV_r, dst_r], acc_writes=[dst_r])
                    P.op("dve", lambda e, dd=dd, sl=sl, w2=w2: e.scalar_tensor_tensor(out=dd, in0=sl(2), scalar=w2, in1=dd, op0=ALU.mult, op1=ALU.add),
                         reads=[ur, CV_r, dst_r], acc_writes=[dst_r])
                if part == "v":
                    P.op("pool", lambda e, dst=dst: e.tensor_tensor(out=dst, in0=dst, in1=cb1, op=ALU.mult), reads=[dst_r, cb1_r], writes=[dst_r])
                    P.dma("sp", f"vvt{c2 % 2}", lambda e, dst=dst, jd=jd: e.dma_start(out=S["VVT"][jd], in_=dst), reads=[dst_r], acc_writes=[R["VVT"]])
                    for t4 in range(NTT // 4):
                        b = 4 + t4 % 4
                        for q in range(4):
                            tt = t4 * 4 + q
                            P.op("pe", lambda e, dst=dst, tt=tt, q=q, b=b: e.transpose(
                                bank(b)[:, q * 128:(q + 1) * 128], dst[:, tt * 128:(tt + 1) * 128], identf),
                                reads=[dst_r, identf_r], writes=[psr[b]] if q == 0 else (), acc_writes=[psr[b]] if q else ())
                        P.op("act", lambda e, t4=t4, b=b: e.copy(out=vst[:, t4 * 4:(t4 + 1) * 4, :], in_=bank(b).rearrange("p (q d) -> p q d", d=128)),
                             reads=[psr[b]], acc_writes=[vst_r])
                    P.dma("sp", "vsto", lambda e, jd=jd: e.dma_start(
                        out=S["VVTOK"][:, jd * 128:(jd + 1) * 128].rearrange("(c p) d -> p c d", p=128), in_=vst),
                        reads=[vst_r], acc_writes=[R["VVTOK"]])
                    vst_r.w = dict(vst_r.w)
                if part == "x0":
                    P.dma("sp", f"x0t{c2 % 2}", lambda e, dst=dst, jd=jd: e.dma_start(out=S["X0T"][jd], in_=dst), reads=[dst_r], acc_writes=[R["X0T"]])
        P.release(m)

    def phase_hy_filter(j, G):
        L = G["L"]
        nm = G["name"]
        ntc = L // 128
        HS, HD = S["HS" + nm], S["HD" + nm]
        P.new_phase()
        rn, rn_r = P.alloc([128, D], F32, "rn")
        m = P.mark()
        zT, zT_r = P.alloc([33, L], F32, "zT")
        w1, w1_r = P.alloc([33, 64], F32, "fw1")
        w2, w2_r = P.alloc([64, 64], F32, "fw2")
        w3, w3_r = P.alloc([64, 2 * D], F32, "fw3")
        b3, b3_r = P.alloc([128, 2 * D], F32, "fb3")
        fcol, fcol_r = P.alloc([64, 4], F32, "fcol")
        negt, negt_r = P.alloc([128, ntc], F32, "negt")
        drep, drep_r = P.alloc([128, D], F32, "drep")
        mask0, mask0_r = P.alloc([128, 1], F32, "mask0")
        h1, h1_r = P.alloc([64, L], F32, "h1")
        h2, h2_r = P.alloc([64, L], F32, "h2")
        tmp = [P.alloc([64, 512], F32, f"ftmp{k}") for k in range(2)]
        P.dma("sp", "zT", lambda e: e.dma_start(out=zT, in_=I["z" + nm]), writes=[zT_r])
        P.dma("sp", "fw1", lambda e: e.dma_start(out=w1, in_=I["filt_w1"][j]), writes=[w1_r])
        P.dma("sp", "fw2", lambda e: e.dma_start(out=w2, in_=I["filt_w2"][j]), writes=[w2_r])
        P.dma("sp", "fw3", lambda e: e.dma_start(out=w3, in_=I["filt_w3"][j]), writes=[w3_r])
        P.dma("sp", "fb3", lambda e: e.dma_start(out=b3, in_=bcast_rows(I["filt_b3"][j], 2 * D)), writes=[b3_r])
        P.dma("sp", "negt", lambda e: e.dma_start(out=negt, in_=I["negt" + nm]), writes=[negt_r])
        P.dma("sp", "drep", lambda e: e.dma_start(out=drep, in_=bcast_rows(I["delta"], D)), writes=[drep_r])
        P.dma("sp", "mask0", lambda e: e.dma_start(out=mask0, in_=I["mask0"]), writes=[mask0_r])
        for k, v in enumerate((I["filt_b1"][j], I["filt_b2"][j], I["filt_freq"][j])):
            P.dma("sp", "fcol", lambda e, k=k, v=v: e.dma_start(out=fcol[:, k:k + 1], in_=v.rearrange("(p o) -> p o", o=1)), acc_writes=[fcol_r])
        TWO_PI = 2.0 * math.pi
        SC = TWO_PI * (1.0 - 2e-6)
        wr_, wr_r = P.alloc([64, 512], F32, "fwrap")

        def sin_layer(lhsT, lhsT_r, src, src_r, dstb, dstb_r, bcol):
            n = min(512, L)
            for ti in range(L // n):
                b = ti % 2
                t_, t_r = tmp[ti % 2]
                P.op("pe", lambda e, ti=ti, b=b: e.matmul(bank(b)[0:64, 0:n], lhsT=lhsT, rhs=src[:, ti * n:(ti + 1) * n], start=True, stop=True),
                     reads=[lhsT_r, src_r], writes=[psr[b]])
                P.op("dve", lambda e, t_=t_, b=b: e.tensor_scalar(
                    out=t_[:, 0:n], in0=bank(b)[0:64, 0:n], scalar1=fcol[:, bcol:bcol + 1], scalar2=fcol[:, 2:3], op0=ALU.add, op1=ALU.mult),
                    reads=[psr[b], fcol_r], writes=[t_r])
                for rnd in range(2):
                    for (cmp_, thr, sgn) in ((ALU.is_lt, -math.pi, ALU.add), (ALU.is_gt, math.pi, ALU.subtract)):
                        P.op("dve", lambda e, t_=t_, cmp_=cmp_, thr=thr: e.tensor_scalar(
                            out=wr_[:, 0:n], in0=t_[:, 0:n], scalar1=thr, scalar2=TWO_PI, op0=cmp_, op1=ALU.mult),
                            reads=[t_r], writes=[wr_r])
                        P.op("dve", lambda e, t_=t_, sgn=sgn: e.tensor_tensor(out=t_[:, 0:n], in0=t_[:, 0:n], in1=wr_[:, 0:n], op=sgn),
                             reads=[t_r, wr_r], writes=[t_r])
                P.op("act", lambda e, t_=t_, ti=ti: e.activation(out=dstb[:, ti * n:(ti + 1) * n], in_=t_[:, 0:n], func=AF.Sin, scale=1.0 - 2e-6),
                     reads=[t_r], acc_writes=[dstb_r])
        sin_layer(w1, w1_r, zT, zT_r, h1, h1_r, 0)
        sin_layer(w2, w2_r, h1, h1_r, h2, h2_r, 1)
        dec = [P.alloc([128, D], F32, "dec0")] * 2
        hf = [P.alloc([128, D], F32, f"hf{k}") for k in range(2)]
        hb_ = [P.alloc([128, D], F32, f"hbk{k}") for k in range(2)]
        ab = [P.alloc([128, 2 * D], F32, "ab0")] * 2
        hs = [P.alloc([128, D], BF16, f"hs{k}") for k in range(2)]
        hd = [P.alloc([128, D], BF16, f"hd{k}") for k in range(2)]
        for tc in range(ntc):
            k = tc % 2
            for cb in range(4):
                P.op("pe", lambda e, tc=tc, cb=cb: e.matmul(bank(cb), lhsT=h2[:, tc * 128:(tc + 1) * 128], rhs=w3[:, cb * 512:(cb + 1) * 512], start=True, stop=True),
                     reads=[h2_r, w3_r], writes=[psr[cb]])
            P.op("act", lambda e, k=k, tc=tc: e.activation(out=dec[k][0], in_=drep, func=AF.Exp, scale=negt[:, tc:tc + 1]),
                 reads=[drep_r, negt_r], writes=[dec[k][1]])
            for (dst, lo) in ((hf[k], 0), (hb_[k], D)):
                P.op("dve", lambda e, dst=dst, lo=lo: e.tensor_tensor(out=dst[0], in0=PS[:, lo:lo + D], in1=b3[:, lo:lo + D], op=ALU.add),
                     reads=[psr[lo // 512], psr[lo // 512 + 1], b3_r], writes=[dst[1]])
                P.op("dve", lambda e, dst=dst, k=k: e.tensor_tensor(out=dst[0], in0=dst[0], in1=dec[k][0], op=ALU.mult),
                     reads=[dst[1], dec[k][1]], writes=[dst[1]])
                P.op("act", lambda e, dst=dst, lo=lo, k=k: e.activation(out=ab[k][0][:, lo:lo + D], in_=dst[0], func=AF.Abs),
                     reads=[dst[1]], acc_writes=[ab[k][1]])
            for q in range(4):
                bq = 4 + q % 2
                first = (tc == 0 and q < 2)
                last = (tc == ntc - 1 and q >= 2)
                P.op("pe", lambda e, k=k, q=q, bq=bq, first=first, last=last: e.matmul(
                    bank(bq), lhsT=onesf, rhs=ab[k][0][:, q * 512:(q + 1) * 512], start=first, stop=last),
                    reads=[onesf_r, ab[k][1]], writes=[psr[bq]] if first else (), acc_writes=() if first else [psr[bq]])
            if tc == 0:
                P.op("dve", lambda e, k=k: e.tensor_scalar(out=hb_[k][0], in0=hb_[k][0], scalar1=mask0[:, 0:1], scalar2=None, op0=ALU.mult),
                     reads=[hb_[k][1], mask0_r], writes=[hb_[k][1]])
            P.op("pool", lambda e, k=k: e.tensor_tensor(out=hs[k][0], in0=hf[k][0], in1=hb_[k][0], op=ALU.add),
                 reads=[hf[k][1], hb_[k][1]], writes=[hs[k][1]])
            P.op("pool", lambda e, k=k: e.tensor_tensor(out=hd[k][0], in0=hb_[k][0], in1=hf[k][0], op=ALU.subtract),
                 reads=[hf[k][1], hb_[k][1]], writes=[hd[k][1]])
            P.dma("sp", f"hso{k}", lambda e, k=k, tc=tc: e.dma_start(out=HS[tc * 128:(tc + 1) * 128, :], in_=hs[k][0]), reads=[hs[k][1]], acc_writes=[R["HS"]])
            P.dma("sp", f"hdo{k}", lambda e, k=k, tc=tc: e.dma_start(out=HD[tc * 128:(tc + 1) * 128, :], in_=hd[k][0]), reads=[hd[k][1]], acc_writes=[R["HS"]])
        P.op("dve", lambda e: e.tensor_scalar(out=rn, in0=PS[:, 4 * 512:6 * 512], scalar1=1e-6, scalar2=None, op0=ALU.add),
             reads=[psr[4], psr[5]], writes=[rn_r])
        P.op("dve", lambda e: e.reciprocal(out=rn, in_=rn), reads=[rn_r], writes=[rn_r])
        P.release(m)
        return rn, rn_r

    def phase_hy_conv(j, G, s, rn, rn_r):
        L = G["L"]
        nm = G["name"]
        ntc = L // 128
        nfc = 33 if nm == "s" else 3
        SQ = nfc * 128
        Qt, Rt, WFd = I["q" + nm], I["r" + nm], I["wf" + nm]
        HS, HD = S["HS" + nm], S["HD" + nm]
        tok0 = G["tok0"] + s * L
        P.new_phase()
        m = P.mark()
        wf, wf_r = P.alloc([128, nfc], F32, "wf")
        P.dma("sp", "wf", lambda e: e.dma_start(out=wf, in_=WFd), writes=[wf_r])
        vvhs, vvhs_r = P.alloc([128, ntc, 512], BF16, "vvhs")
        vvhd, vvhd_r = P.alloc([128, ntc, 512], BF16, "vvhd")
        qch = [P.alloc([128, ntc, 128], BF16, f"qch{k}") for k in range(2)]
        rch = [P.alloc([128, ntc, 128], BF16, f"rch{k}") for k in range(2)]
        kcs = [P.alloc([128, 256], F32, f"kcs{k}") for k in range(2)]
        kss = [P.alloc([128, 256], F32, f"kss{k}") for k in range(2)]
        ta = [P.alloc([128, 256], F32, f"ta{k}") for k in range(2)]
        tb = [P.alloc([128, 256], F32, f"tb{k}") for k in range(2)]
        yst = [P.alloc([128, 2, 256], BF16, f"yst{k}") for k in range(2)]
        def load_tab(k):
            fc_ = k % nfc
            qc_, qc_r_ = qch[k % 2]
            rc_, rc_r_ = rch[k % 2]
            if nm == "s":
                qsrc, rsrc = I["qfs"][fc_], I["rfs"][fc_]
            else:
                qsrc = Qt[0:ntc, :, fc_ * 128:(fc_ + 1) * 128].rearrange("c p f -> p c f")
                rsrc = Rt[0:ntc, :, fc_ * 128:(fc_ + 1) * 128].rearrange("c p f -> p c f")
            P.dma("sp", f"qch{k % 2}", lambda e: e.dma_start(out=qc_, in_=qsrc), writes=[qc_r_])
            P.dma("sp", f"rch{k % 2}", lambda e: e.dma_start(out=rc_, in_=rsrc), writes=[rc_r_])
        it = 0
        for dq in range(4):
            dh = dq // 2
            vsrc = S["VVTOK"][tok0:tok0 + L, dq * 256:(dq + 1) * 256].rearrange("(c p) d -> p c d", p=128)
            P.dma("sp", "vva", lambda e, vsrc=vsrc: e.dma_start(out=vvhs[:, :, 0:256], in_=vsrc), reads=[R["VVTOK"]], writes=[vvhs_r])
            P.dma("sp", "vvb", lambda e, vsrc=vsrc: e.dma_start(out=vvhd[:, :, 0:256], in_=vsrc), reads=[R["VVTOK"]], writes=[vvhd_r])
            P.dma("sp", "hsh", lambda e, dq=dq: e.dma_start(
                out=vvhs[:, :, 256:512], in_=HS[:, dq * 256:(dq + 1) * 256].rearrange("(c p) d -> p c d", p=128)), reads=[R["HS"]], acc_writes=[vvhs_r])
            P.dma("sp", "hdh", lambda e, dq=dq: e.dma_start(
                out=vvhd[:, :, 256:512], in_=HD[:, dq * 256:(dq + 1) * 256].rearrange("(c p) d -> p c d", p=128)), reads=[R["HS"]], acc_writes=[vvhd_r])
            for fc in range(nfc):
                qc, qc_r = qch[it % 2]
                rc, rc_r = rch[it % 2]
                if it == 0:
                    load_tab(0)
                if it + 1 < 4 * nfc:
                    load_tab(it + 1)
                b0 = (it % 4) * 2
                bC, bS = b0, b0 + 1
                for tc in range(ntc):
                    st, sp_ = (tc == 0), (tc == ntc - 1)
                    for (bb, tab, tab_r, mov, mov_r) in ((bC, qc, qc_r, vvhs, vvhs_r), (bS, rc, rc_r, vvhd, vvhd_r)):
                        P.op("pe", lambda e, bb=bb, tab=tab, mov=mov, tc=tc, st=st, sp_=sp_: e.matmul(
                            bank(bb), lhsT=tab[:, tc, :], rhs=mov[:, tc, :], start=st, stop=sp_),
                            reads=[tab_r, mov_r], writes=[psr[bb]] if st else (), acc_writes=() if st else [psr[bb]])
                bVc = bKc = bC
                bVs = bKs = bS
                Vc_, Kc_ = bank(bC)[:, 0:256], bank(bC)[:, 256:512]
                Vs_, Ks_ = bank(bS)[:, 0:256], bank(bS)[:, 256:512]
                k = it % 2
                wcol = wf[:, fc:fc + 1]
                rsl = rn[:, dq * 256:(dq + 1) * 256]
                P.op("dve", lambda e, k=k, Kc_=Kc_, wcol=wcol, rsl=rsl: e.scalar_tensor_tensor(
                    out=kcs[k][0], in0=Kc_, scalar=wcol, in1=rsl, op0=ALU.mult, op1=ALU.mult),
                    reads=[psr[bKc], wf_r, rn_r], writes=[kcs[k][1]])
                P.op("dve", lambda e, k=k, Ks_=Ks_, wcol=wcol, rsl=rsl: e.scalar_tensor_tensor(
                    out=kss[k][0], in0=Ks_, scalar=wcol, in1=rsl, op0=ALU.mult, op1=ALU.mult),
                    reads=[psr[bKs], wf_r, rn_r], writes=[kss[k][1]])
                P.op("dve", lambda e, k=k, Vc_=Vc_: e.tensor_tensor(out=ta[k][0], in0=Vc_, in1=kcs[k][0], op=ALU.mult),
                     reads=[psr[bVc], kcs[k][1]], writes=[ta[k][1]])
                P.op("dve", lambda e, k=k, Vs_=Vs_: e.tensor_tensor(out=tb[k][0], in0=Vs_, in1=kss[k][0], op=ALU.mult),
                     reads=[psr[bVs], kss[k][1]], writes=[tb[k][1]])
                P.op("pool", lambda e, k=k: e.tensor_tensor(out=yst[k][0][:, 0, :], in0=ta[k][0], in1=tb[k][0], op=ALU.add),
                     reads=[ta[k][1], tb[k][1]], writes=[yst[k][1]])
                P.op("dve", lambda e, k=k, Vs_=Vs_: e.tensor_tensor(out=ta[k][0], in0=Vs_, in1=kcs[k][0], op=ALU.mult),
                     reads=[psr[bVs], kcs[k][1]], writes=[ta[k][1]])
                P.op("dve", lambda e, k=k, Vc_=Vc_: e.tensor_tensor(out=tb[k][0], in0=Vc_, in1=kss[k][0], op=ALU.mult),
                     reads=[psr[bVc], kss[k][1]], writes=[tb[k][1]])
                P.op("pool", lambda e, k=k: e.tensor_tensor(out=yst[k][0][:, 1, :], in0=ta[k][0], in1=tb[k][0], op=ALU.subtract),
                     reads=[ta[k][1], tb[k][1]], acc_writes=[yst[k][1]])
                P.dma("sp", f"yfo{k}", lambda e, k=k, dh=dh, dq=dq, fc=fc: e.dma_start(out=S["YF"][dh, fc][:, :, (dq % 2) * 256:(dq % 2 + 1) * 256], in_=yst[k][0]),
                      reads=[yst[k][1]], acc_writes=[R["YF"]])
                it += 1
        P.release(m)
        P.new_phase()
        m = P.mark()
        skc, skc_r = P.alloc([128, 8], F32, "skc")
        load_cols(skc, skc_r, 0, I["hy_skip"][j], 8, "skc")
        Yg, Yg_r = P.alloc([128, nfc, 2, 512], BF16, "Yg")
        n = min(512, L)
        tq = [P.alloc([128, 2, n], BF16, f"tq{k}") for k in range(6)]
        vvt = [P.alloc([128, n], F32, f"vvt{k}") for k in range(3)]
        x0t = [P.alloc([128, n], F32, f"x0t{k}") for k in range(3)]
        it = 0
        ie = 0
        for dh in range(2):
            P.dma("sp", "Yg", lambda e, dh=dh: e.dma_start(out=Yg, in_=S["YF"][dh, 0:nfc].rearrange("c p s d -> p c s d")),
                  reads=[R["YF"]], writes=[Yg_r])
            for tt in range(L // n):
                b0 = ((dh * (L // n) + tt) % 2) * 4
                for fc in range(nfc):
                    t_, t_r = tq[it % 6]
                    P.dma("sp", f"tq{it % 6}a", lambda e, t_=t_, fc=fc, tt=tt: e.dma_start(out=t_[:, 0, :], in_=Qt[fc, :, tt * n:(tt + 1) * n]), writes=[t_r])
                    P.dma("sp", f"tq{it % 6}b", lambda e, t_=t_, fc=fc, tt=tt: e.dma_start(out=t_[:, 1, :], in_=Rt[fc, :, tt * n:(tt + 1) * n]), acc_writes=[t_r])
                    it += 1
                    for dcl in range(4):
                        for cs in range(2):
                            st = (fc == 0 and cs == 0)
                            sp_ = (fc == nfc - 1 and cs == 1)
                            P.op("pe", lambda e, t_=t_, fc=fc, dcl=dcl, cs=cs, st=st, sp_=sp_, b0=b0: e.matmul(
                                bank(b0 + dcl, n), lhsT=Yg[:, fc, cs, dcl * 128:(dcl + 1) * 128], rhs=t_[:, cs, :], start=st, stop=sp_),
                                reads=[Yg_r, t_r], writes=[psr[b0 + dcl]] if st else (), acc_writes=() if st else [psr[b0 + dcl]])
                for dcl in range(4):
                    dc = dh * 4 + dcl
                    v_, v_r = vvt[ie % 3]
                    x_, x_r = x0t[ie % 3]
                    ie += 1
                    c0 = tok0 + tt * n
                    P.dma("sp", f"vvt{ie % 3}", lambda e, v_=v_, dc=dc, c0=c0: e.dma_start(out=v_, in_=S["VVT"][dc, :, c0:c0 + n]), reads=[R["VVT"]], writes=[v_r])
                    P.dma("sp", f"x0t{ie % 3}", lambda e, x_=x_, dc=dc, c0=c0: e.dma_start(out=x_, in_=S["X0T"][dc, :, c0:c0 + n]), reads=[R["X0T"]], writes=[x_r])
                    P.op("dve", lambda e, v_=v_, dc=dc, dcl=dcl, b0=b0: e.scalar_tensor_tensor(
                        out=v_, in0=v_, scalar=skc[:, dc:dc + 1], in1=bank(b0 + dcl, n), op0=ALU.mult, op1=ALU.add),
                        reads=[v_r, skc_r, psr[b0 + dcl]], writes=[v_r])
                    P.op("pool", lambda e, v_=v_, x_=x_, dc=dc, c0=c0: e.tensor_tensor(out=hT[:, dc, c0:c0 + n], in0=v_, in1=x_, op=ALU.mult),
                         reads=[v_r, x_r], acc_writes=[hTr[c0 // 512]])
        P.release(m)

    class Stop(Exception):
        pass

    def chk(tag):
        if stop_after == tag:
            raise Stop()

    def dump_hT():
        for t in range(NT):
            P.dma("sp", "htd", lambda e, t=t: e.dma_start(out=S["HTD"][:, :, t * 512:(t + 1) * 512], in_=hT[:, :, t * 512:(t + 1) * 512]),
                  reads=[hTr[t]], acc_writes=[R["HTD"]])

    try:
        for i in range(4):
            j = i // 2
            phase_mod(i)
            chk(f"mod{i}")
            phase_norm(D, 0)
            chk(f"norm1_{i}")
            if i % 2 == 0:
                phase_qkv(j)
                chk(f"qkv{i}")
                phase_attn(j, i)
                chk(f"attn{i}")
                phase_outproj(I["attn_w_o"][j], None, 2 * D)
            else:
                phase_hy_in(j)
                chk(f"hyin{i}")
                for G in GROUPS:
                    mk = P.mark()
                    rn, rn_r = phase_hy_filter(j, G)
                    chk(f"hyfilt{i}{G['name']}")
                    for s in range(G["nseq"]):
                        phase_hy_conv(j, G, s, rn, rn_r)
                    P.release(mk)
                chk(f"hyconv{i}")
                phase_outproj(I["hy_w_out"][j], I["hy_b_out"][j], 2 * D)
            chk(f"mix{i}")
            phase_norm(4 * D, 3 * D)
            phase_ffn(i)
            chk(f"ffn{i}")
        phase_norm(0, 0, final=True)
    except Stop:
        if "HTD" in debug_outs:
            dump_hT()

    final_keys = [k for k in P.dma_cnt]
    print("n dma sems", len(final_keys), {e: len(P.q[e]) for e in ENGS})
    P.emit(final_keys)
    return nc, P


_CONSTS = None


def _core_inputs(b, inp, consts):
    m = {}
    m["x"] = np.ascontiguousarray(np.concatenate(
        [inp["x_sample"][b], inp["x_prompt"][2 * b], inp["x_prompt"][2 * b + 1]], axis=0).astype(np.float32))
    m["ck"] = np.ascontiguousarray(inp["cache_k"][b].reshape(2, 512, D).astype(np.float32))
    m["cv"] = np.ascontiguousarray(inp["cache_v"][b].reshape(2, 512, D).astype(np.float32))
    m["cvec"] = np.ascontiguousarray(np.stack([inp["c"][b], inp["c_ctx"]], axis=0).astype(np.float32))
    for nm, shp in W_SPECS:
        m[nm] = np.ascontiguousarray(np.asarray(inp[nm], dtype=np.float32).reshape(shp))
    for nm, shp, dt in CONST_SPECS:
        m[nm] = consts[nm]
    return m


def kernel(**inputs):
    global _CONSTS
    if _CONSTS is None:
        _CONSTS = _host_consts()
    inp = {k: np.asarray(v) for k, v in inputs.items()}
    nc, _ = build_program()
    in_maps = [_core_inputs(b, inp, _CONSTS) for b in range(8)]
    res = run_bass_kernel_spmd(nc, in_maps, core_ids=list(range(8)))
    y_prompt = np.zeros((16, 256, D), np.float32)
    y_sample = np.zeros((8, TS, D), np.float32)
    nk = np.zeros((16, 2, 256, 8, 2, 64), np.float32)
    nv = np.zeros((16, 2, 256, 8, 128), np.float32)
    for b in range(8):
        r = res.results[b]
        y = np.asarray(r["y"], dtype=np.float32)
        y_sample[b] = y[:TS]
        y_prompt[2 * b] = y[TS:TS + 256]
        y_prompt[2 * b + 1] = y[TS + 256:]
        k_ = np.asarray(r["nk"], dtype=np.float32).reshape(2, 2, 256, 8, 2, 64)
        v_ = np.asarray(r["nv"], dtype=np.float32).reshape(2, 2, 256, 8, 128)
        nk[2 * b], nk[2 * b + 1] = k_[0], k_[1]
        nv[2 * b], nv[2 * b + 1] = v_[0], v_[1]
    return (y_prompt, y_sample, nk, nv)
```

```python
import contextlib
import os
import math
import numpy as np
import ml_dtypes
import concourse.bass as bass
import concourse.mybir as mybir
from concourse.bass_utils import run_bass_kernel_spmd

F32 = mybir.dt.float32
BF16 = mybir.dt.bfloat16
AF = mybir.ActivationFunctionType
ALU = mybir.AluOpType
AX = mybir.AxisListType

ENGS = ("pe", "act", "dve", "pool", "sp")
D = 1024
TS, TP, T = 4096, 512, 4608
NT = 9
NTT = 36
DFF = 2816
NCH_FF = 22
EPS = 1e-6
SUBLN_EPS = 1e-5
NKEY = 5120


class Res:
    __slots__ = ("w", "r", "name", "excl")

    def __init__(self, name="", excl=False):
        self.w = {}
        self.r = {}
        self.name = name
        self.excl = excl


class Prog:
    def __init__(self, nc, arena_words):
        self.nc = nc
        self.q = {e: [] for e in ENGS}
        self.known = {e: {} for e in ENGS}
        self.dma_cnt = {}
        self.stack = contextlib.ExitStack()
        self.arena = self.stack.enter_context(nc.sbuf_tensor("arena", [128, arena_words], F32))
        self.arena_words = arena_words
        self.top = 0
        self.live = []
        self.retired = []
        self.nsem = 0
        self.keymap = {}
        self.keyres = {}

    def alloc(self, shape, dt, name=""):
        esz = 4 if dt == F32 else 2
        free = 1
        for s in shape[1:]:
            free *= s
        words = (free * esz + 3) // 4
        words = (words + 7) // 8 * 8
        off = self.top
        assert off + words <= self.arena_words, f"SBUF arena overflow {name} {off + words}"
        self.top += words
        v = self.arena[0:shape[0], off:off + (free * esz) // 4]
        if dt != F32:
            v = v.bitcast(dt)
        if len(shape) == 3:
            v = v.rearrange("p (a b) -> p a b", b=shape[2])
        elif len(shape) == 4:
            v = v.rearrange("p (a b c) -> p a b c", b=shape[2], c=shape[3])
        r = Res(name)
        keep = []
        for (a, b, rr) in self.retired:
            if a < off + words and off < b:
                for k, val in rr.w.items():
                    if r.r.get(k, -1) < val:
                        r.r[k] = val
                for k, val in rr.r.items():
                    if r.r.get(k, -1) < val:
                        r.r[k] = val
                if a >= off and b <= off + words:
                    continue
            keep.append((a, b, rr))
        self.retired = keep
        self.live.append((off, off + words, r))
        return v, r

    def mark(self):
        return (self.top, len(self.live))

    def release(self, m):
        top, n = m
        self.retired.extend(self.live[n:])
        del self.live[n:]
        self.top = top

    def _add(self, eng, fn, reads, writes, acc_writes, own):
        deps = {}

        def upd(d):
            for k, v in d.items():
                if deps.get(k, -1) < v:
                    deps[k] = v
        for r in reads:
            upd(r.w)
            if r.excl:
                upd({k: v for k, v in r.r.items() if k != ("c", eng)})
        for w in writes:
            upd(w.w)
            upd(w.r)
        for w in acc_writes:
            upd(w.r)
            upd({k: v for k, v in w.w.items() if k != own})
        q = self.q[eng]
        idx = len(q)
        waits = []
        kn = self.known[eng]
        for k, v in deps.items():
            if k == ("c", eng):
                if eng == "pe":
                    continue
                vv = -1
                for r in reads:
                    x = r.w.get(k, -1)
                    if x > vv:
                        vv = x
                if vv < 0:
                    continue
                v = vv
            if k[0] == "d":
                v = self.dma_cnt[k[1]]
            if kn.get(k, -1) >= v:
                continue
            kn[k] = v
            waits.append((k, v))
            if k[0] == "c":
                self.q[k[1]][v][2] = True
        op = [fn, waits, False, None]
        q.append(op)
        return op, idx

    def op(self, eng, fn, reads=(), writes=(), acc_writes=()):
        op, idx = self._add(eng, fn, reads, writes, acc_writes, ("c", eng))
        k = ("c", eng)
        for r in reads:
            r.r[k] = idx
        for w in writes:
            w.w = {k: idx}
            w.r = {}
        for w in acc_writes:
            w.w[k] = idx
        return op

    def new_phase(self):
        self.keymap = {}

    def dma(self, eng, semkey, fn, reads=(), writes=(), acc_writes=()):
        km = self.keymap
        if semkey not in km:
            km[semkey] = f"g{len(km)}"
        semkey = km[semkey]
        kr = self.keyres.get(semkey)
        if kr is None:
            kr = self.keyres[semkey] = Res(semkey)
        writes = list(writes) + [kr]
        op, idx = self._add(eng, fn, reads, writes, acc_writes, ("d", semkey))
        c = self.dma_cnt.get(semkey, 0) + 1
        self.dma_cnt[semkey] = c
        op[3] = semkey
        k = ("d", semkey)
        for r in reads:
            r.r[k] = c
        for w in writes:
            w.w = {k: c}
            w.r = {}
        for w in acc_writes:
            w.w[k] = c
        return op

    def emit(self, final_keys):
        nc = self.nc
        st = self.stack
        csem = {e: st.enter_context(nc.semaphore(f"c_{e}")) for e in ENGS if e != "sp"}
        dsem = {k: st.enter_context(nc.semaphore(f"d_{i}")) for i, k in enumerate(self.dma_cnt)}
        cum = {}
        for e in ENGS:
            c = 0
            arr = []
            for o in self.q[e]:
                if o[2]:
                    c += 1
                arr.append(c)
            cum[e] = arr
        engobj = {"pe": "tensor", "act": "scalar", "dve": "vector", "pool": "gpsimd", "sp": "sync"}
        with nc.Block() as block:
            for e in ENGS:
                ops = self.q[e]

                def body(eng, e=e, ops=ops):
                    for fn, waits, sig, semkey in ops:
                        for k, v in waits:
                            if k[0] == "c":
                                eng.wait_ge(csem[k[1]], cum[k[1]][v])
                            else:
                                eng.wait_ge(dsem[k[1]], 16 * v)
                        ins = fn(eng)
                        if semkey is not None:
                            ins.then_inc(dsem[semkey], 16)
                        elif sig:
                            ins.then_inc(csem[e], 1)
                    if e == "sp":
                        for k in final_keys:
                            eng.wait_ge(dsem[k], 16 * self.dma_cnt[k])
                getattr(block, engobj[e])(body)


def _bf(a):
    return np.ascontiguousarray(a.astype(ml_dtypes.bfloat16))


def _dft_tables(L):
    N = 2 * L
    nf = L + 1
    nch = (nf + 127) // 128
    S = nch * 128
    a = np.arange(S, dtype=np.int64)
    m = (a[:, None] * a[None, :]) % N
    ang = 2.0 * np.pi * m.astype(np.float64) / N
    valid = (a[:, None] <= L) & (a[None, :] <= L)
    q = np.where(valid, np.cos(ang), 0.0)
    r = np.where(valid, np.sin(ang), 0.0)
    wf = np.where(a <= L, 2.0 / N, 0.0)
    wf[0] = 1.0 / N
    wf[L] = 1.0 / N
    wfc = wf.reshape(nch, 128).T.astype(np.float32)
    ntc = L // 128
    qf = q[:L].reshape(ntc, 128, nch, 128).transpose(2, 1, 0, 3)
    rf = r[:L].reshape(ntc, 128, nch, 128).transpose(2, 1, 0, 3)
    return (_bf(q.reshape(nch, 128, S)), _bf(r.reshape(nch, 128, S)), np.ascontiguousarray(wfc), nch, S, _bf(qf), _bf(rf))


def _filter_consts(L):
    pos = np.arange(L, dtype=np.float32)
    t = pos / np.float32(max(L - 1, 1))
    w = (np.float32(2.0 * math.pi) * pos / np.float32(L)).astype(np.float32)
    bands = np.linspace(1e-4, 15, 16, dtype=np.float32)
    z = np.concatenate([t[:, None], np.cos(w[:, None] * bands), -np.sin(w[:, None] * bands)], axis=-1)
    zT = np.ascontiguousarray(z.T.astype(np.float32))
    negt = np.ascontiguousarray((-t).reshape(L // 128, 128).T.astype(np.float32))
    return zT, negt


def _host_consts():
    c = {}
    tpos = np.arange(TS)
    rowpos = (tpos // 64).astype(np.float32)
    colpos = (tpos % 64).astype(np.float32)
    inv = (10000.0 ** (-np.arange(16, dtype=np.float32) / 16)).astype(np.float32)
    cos = np.zeros((128, TS), np.float32)
    sins = np.zeros((128, TS), np.float32)
    perm = np.zeros((128, 128), np.float32)
    for p in range(2):
        for a in range(2):
            posv = rowpos if a == 0 else colpos
            for hf in range(2):
                for f in range(16):
                    row = p * 64 + a * 32 + hf * 16 + f
                    ang = (posv * inv[f]).astype(np.float32)
                    cos[row] = np.cos(ang)
                    sins[row] = np.sin(ang) * (-1.0 if hf == 0 else 1.0)
                    other = p * 64 + a * 32 + (1 - hf) * 16 + f
                    perm[other, row] = 1.0
    c["rcos"] = cos
    c["rsin"] = sins
    c["perm"] = _bf(perm)
    c["identb"] = _bf(np.eye(128, dtype=np.float32))
    c["identf"] = np.eye(128, dtype=np.float32)
    c["onesf"] = np.ones((128, 128), np.float32)
    for nm, L in (("s", TS), ("p", 256)):
        q, r, wf, nch, S, qf, rf = _dft_tables(L)
        c["q" + nm], c["r" + nm], c["wf" + nm] = q, r, wf
        if nm == "s":
            c["qfs"], c["rfs"] = qf, rf
        zT, negt = _filter_consts(L)
        c["z" + nm], c["negt" + nm] = zT, negt
    deltas = np.abs(np.linspace(math.log(1e-2) / 1.5, math.log(1e-2) / 0.3, D, dtype=np.float32))
    c["delta"] = deltas.astype(np.float32)
    m0 = np.ones((128, 1), np.float32)
    m0[0, 0] = 0.0
    c["mask0"] = m0
    return c


W_SPECS = [
    ("ada_w", [4, D, 6 * D]), ("ada_b", [4, 6 * D]), ("norm1_g", [4, D]), ("norm2_g", [4, D]),
    ("attn_w_qkv", [2, D, 3 * D]), ("attn_lambda", [2, 4, 64]), ("attn_subln_g", [2, 128]),
    ("attn_w_o", [2, D, D]), ("hy_w_in", [2, D, 3 * D]), ("hy_b_in", [2, 3 * D]),
    ("hy_conv_w", [2, 3, 3 * D]), ("hy_conv_b", [2, 3 * D]), ("filt_w1", [2, 33, 64]),
    ("filt_b1", [2, 64]), ("filt_w2", [2, 64, 64]), ("filt_b2", [2, 64]), ("filt_w3", [2, 64, 2 * D]),
    ("filt_b3", [2, 2 * D]), ("filt_freq", [2, 64]), ("hy_skip", [2, D]), ("hy_w_out", [2, D, D]),
    ("hy_b_out", [2, D]), ("ffn_w_gu", [4, D, 2 * DFF]), ("ffn_w_down", [4, DFF, D]), ("final_g", [D]),
]
CONST_SPECS = [
    ("rcos", [128, TS], F32), ("rsin", [128, TS], F32), ("perm", [128, 128], BF16),
    ("identb", [128, 128], BF16), ("identf", [128, 128], F32), ("onesf", [128, 128], F32),
    ("qs", [33, 128, 4224], BF16), ("rs", [33, 128, 4224], BF16), ("wfs", [128, 33], F32),
    ("qfs", [33, 128, 32, 128], BF16), ("rfs", [33, 128, 32, 128], BF16),
    ("zs", [33, TS], F32), ("negts", [128, 32], F32),
    ("qp", [3, 128, 384], BF16), ("rp", [3, 128, 384], BF16), ("wfp", [128, 3], F32),
    ("zp", [33, 256], F32), ("negtp", [128, 2], F32),
    ("delta", [D], F32), ("mask0", [128, 1], F32),
]

GROUPS = [
    dict(name="s", tok0=0, T=TS, nseq=1, L=TS, rope=True, ncache=512, tiles=list(range(0, 8)), tt=list(range(0, 32))),
    dict(name="p", tok0=TS, T=TP, nseq=2, L=256, rope=False, ncache=0, tiles=[8], tt=list(range(32, 36))),
]


def build_program(stop_after=None, debug_outs=()):
    nc = bass.Bass("TRN2", target_bir_lowering=False)
    I = {}
    I["x"] = nc.dram_tensor("x", [T, D], F32, kind="ExternalInput").ap()
    I["ck"] = nc.dram_tensor("ck", [2, 512, D], F32, kind="ExternalInput").ap()
    I["cv"] = nc.dram_tensor("cv", [2, 512, D], F32, kind="ExternalInput").ap()
    I["cvec"] = nc.dram_tensor("cvec", [2, D], F32, kind="ExternalInput").ap()
    for nm, shp in W_SPECS:
        I[nm] = nc.dram_tensor(nm, shp, F32, kind="ExternalInput").ap()
    for nm, shp, dt in CONST_SPECS:
        I[nm] = nc.dram_tensor(nm, shp, dt, kind="ExternalInput").ap()
    O = {}
    O["y"] = nc.dram_tensor("y", [T, D], F32, kind="ExternalOutput").ap()
    O["nk"] = nc.dram_tensor("nk", [2, 2, 256, D], F32, kind="ExternalOutput").ap()
    O["nv"] = nc.dram_tensor("nv", [2, 2, 256, D], F32, kind="ExternalOutput").ap()

    def scratch(nm, shp, dt):
        kind = "ExternalOutput" if nm in debug_outs else "Internal"
        return nc.dram_tensor(nm, shp, dt, kind=kind).ap()
    S = {}
    S["X"] = scratch("X", [T, D], F32)
    S["MODS"] = scratch("MODS", [2, 128, 6 * D], F32)
    S["KT"] = scratch("KT", [8, 128, NKEY], BF16)
    S["VS"] = scratch("VS", [NKEY, D], BF16)
    S["QT"] = scratch("QT", [8, 128, T], BF16)
    S["MID"] = scratch("MID", [NCH_FF, 128, T], BF16)
    S["VVT"] = scratch("VVT", [8, 128, T], F32)
    S["X0T"] = scratch("X0T", [8, 128, T], F32)
    S["VVTOK"] = scratch("VVTOK", [T, D], BF16)
    S["HSs"] = scratch("HSs", [TS, D], BF16)
    S["HDs"] = scratch("HDs", [TS, D], BF16)
    S["HSp"] = scratch("HSp", [256, D], BF16)
    S["HDp"] = scratch("HDp", [256, D], BF16)
    S["YF"] = scratch("YF", [2, 33, 128, 2, 512], BF16)
    S["HTD"] = scratch("HTD", [128, 8, T], BF16)

    ARENA_WORDS = 49152
    P = Prog(nc, ARENA_WORDS)
    PS = P.stack.enter_context(nc.psum_tensor("ps", [128, 4096], F32))
    psr = [Res(f"bank{b}", excl=True) for b in range(8)]

    def bank(b, n=512):
        return PS[:, b * 512:b * 512 + n]

    def bankb(b):
        return PS[:, b * 512:(b + 1) * 512].bitcast(BF16)

    R = {}
    R["X"] = [[Res(f"X{i}a"), Res(f"X{i}b")] for i in range(NTT)]
    R["MODS"] = [Res(), Res()]
    R["KT"] = Res()
    R["VS"] = Res()
    R["QT"] = Res()
    R["MID"] = [Res() for _ in range(NT)]
    R["VVT"] = Res()
    R["X0T"] = Res()
    R["VVTOK"] = Res()
    R["HS"] = Res()
    R["YF"] = Res()
    R["OUT"] = Res()
    R["HTD"] = Res()
    semctr = [0]

    def sk(prefix):
        semctr[0] += 1
        return f"{prefix}{semctr[0]}"

    hT, _ = P.alloc([128, 8, T], BF16, "hT")
    hTr = [Res(f"hT{i}") for i in range(NT)]
    identb, identb_r = P.alloc([128, 128], BF16, "identb")
    identf, identf_r = P.alloc([128, 128], F32, "identf")
    onesf, onesf_r = P.alloc([128, 128], F32, "onesf")
    SIL, SIL_r = P.alloc([128, 2, 8, 128], BF16, "SIL")
    P.dma("sp", "c_identb", lambda e: e.dma_start(out=identb, in_=I["identb"]), writes=[identb_r])
    P.dma("sp", "c_identf", lambda e: e.dma_start(out=identf, in_=I["identf"]), writes=[identf_r])
    P.dma("sp", "c_onesf", lambda e: e.dma_start(out=onesf, in_=I["onesf"]), writes=[onesf_r])

    def load_cols(dst, dst_r, col0, vec, nchunks, key):
        P.dma("sp", key, lambda e: e.dma_start(
            out=dst[:, col0:col0 + nchunks], in_=vec.rearrange("(c p) -> p c", p=128), allow_slow_non_contiguous=True),
            acc_writes=[dst_r])

    def bcast_rows(vec1d, n):
        return vec1d.rearrange("(o n) -> o n", o=1).broadcast_to([128, n])

    m0 = P.mark()
    ccol, ccol_r = P.alloc([128, 16], F32, "ccol")
    scol, scol_r = P.alloc([128, 16], F32, "scol")
    for g in range(2):
        load_cols(ccol, ccol_r, g * 8, I["cvec"][g], 8, "ccol")
    P.op("act", lambda e: e.activation(out=scol, in_=ccol, func=AF.Silu), reads=[ccol_r], writes=[scol_r])
    for g in range(2):
        for kc in range(8):
            P.op("dve", lambda e, g=g, kc=kc: e.tensor_scalar(
                out=SIL[:, g, kc, :], in0=onesf, scalar1=scol[:, g * 8 + kc:g * 8 + kc + 1], scalar2=None,
                op0=ALU.mult), reads=[scol_r, onesf_r], acc_writes=[SIL_r])
    P.release(m0)

    for tt in range(NTT):
        P.dma("sp", f"xcopy{tt % 4}", lambda e, tt=tt: e.dma_start(
            out=S["X"][tt * 128:(tt + 1) * 128, :], in_=I["x"][tt * 128:(tt + 1) * 128, :]), writes=R["X"][tt])

    def phase_mod(i):
        P.new_phase()
        m = P.mark()
        adab, adab_r = P.alloc([128, 6 * D], F32, "adab")
        mod = [P.alloc([128, 6 * D], F32, f"mod{g}") for g in range(2)]
        gn = [P.alloc([128, D], F32, f"gn{k}") for k in range(2)]
        wch = [P.alloc([128, 8, 512], BF16, f"adw{k}") for k in range(2)]
        P.dma("sp", "adab", lambda e: e.dma_start(out=adab, in_=bcast_rows(I["ada_b"][i], 6 * D)), writes=[adab_r])
        P.dma("sp", "gn0", lambda e: e.dma_start(out=gn[0][0], in_=bcast_rows(I["norm1_g"][i], D)), writes=[gn[0][1]])
        P.dma("sp", "gn1", lambda e: e.dma_start(out=gn[1][0], in_=bcast_rows(I["norm2_g"][i], D)), writes=[gn[1][1]])
        for n in range(12):
            w, wr = wch[n % 2]
            P.dma("pool", f"adw{n % 2}", lambda e, w=w, n=n: e.dma_start(
                out=w, in_=I["ada_w"][i][:, n * 512:(n + 1) * 512].rearrange("(kc p) n -> p kc n", p=128)), writes=[wr])
            for g in range(2):
                b = (n * 2 + g) % 8
                for kc in range(8):
                    P.op("pe", lambda e, w=w, g=g, kc=kc, b=b: e.matmul(
                        bank(b), lhsT=SIL[:, g, kc, :], rhs=w[:, kc, :], start=(kc == 0), stop=(kc == 7)),
                        reads=[SIL_r, wr], writes=[psr[b]] if kc == 0 else (), acc_writes=[psr[b]] if kc else ())
                P.op("dve", lambda e, g=g, n=n, b=b: e.tensor_tensor(
                    out=mod[g][0][:, n * 512:(n + 1) * 512], in0=bank(b), in1=adab[:, n * 512:(n + 1) * 512], op=ALU.add),
                    reads=[psr[b], adab_r], acc_writes=[mod[g][1]])
        for g in range(2):
            for k, off in ((0, D), (1, 4 * D)):
                P.op("dve", lambda e, g=g, k=k, off=off: e.scalar_tensor_tensor(
                    out=mod[g][0][:, off:off + D], in0=mod[g][0][:, off:off + D], scalar=1.0, in1=gn[k][0],
                    op0=ALU.add, op1=ALU.mult), reads=[mod[g][1], gn[k][1]], acc_writes=[mod[g][1]])
            P.dma("sp", f"mods{g}", lambda e, g=g: e.dma_start(out=S["MODS"][g], in_=mod[g][0]),
                  reads=[mod[g][1]], writes=[R["MODS"][g]])
        P.release(m)

    def load_mod(off, key):
        out = []
        for g in range(2):
            t, r = P.alloc([128, D], F32, f"{key}{g}")
            P.dma("sp", f"ldm_{key}{g}", lambda e, t=t, g=g: e.dma_start(out=t, in_=S["MODS"][g][:, off:off + D]),
                  reads=[R["MODS"][g]], writes=[r])
            out.append((t, r))
        return out

    def rstd_ops(ss, ss_r, rs, rs_r, n, eps):
        P.op("act", lambda e: e.activation(out=rs, in_=ss, func=AF.Ln, scale=1.0 / n, bias=float(eps)),
             reads=[ss_r], writes=[rs_r])
        P.op("act", lambda e: e.activation(out=rs, in_=rs, func=AF.Exp, scale=-0.5),
             reads=[rs_r], writes=[rs_r])

    def phase_norm(offA, offB, final=False):
        P.new_phase()
        m = P.mark()
        if final:
            gt, gr = P.alloc([128, D], F32, "fing")
            P.dma("sp", "fing", lambda e: e.dma_start(out=gt, in_=bcast_rows(I["final_g"], D)), writes=[gr])
            A = [(gt, gr), (gt, gr)]
            B = None
        else:
            A = load_mod(offA, "nA")
            B = load_mod(offB, "nB")
        xt = [P.alloc([128, D], F32, f"nx{k}") for k in range(3)]
        x2 = [P.alloc([128, D], F32, f"nx2{k}") for k in range(2)]
        hb = [P.alloc([128, D], BF16, f"nhb{k}") for k in range(2)]
        junk, junk_r = P.alloc([128, D], BF16, "njunk")
        ss = [P.alloc([128, 1], F32, f"nss{k}") for k in range(2)]
        rs = [P.alloc([128, 1], F32, f"nrs{k}") for k in range(2)]
        pend_copy = []

        def flush_copy():
            b, t_ = pend_copy.pop(0)
            P.op("act", lambda e: e.copy(out=hT[:, :, t_ * 128:(t_ + 1) * 128],
                                         in_=bankb(b).rearrange("p (k t) -> p k t", t=128)),
                 reads=[psr[b]], acc_writes=[hTr[t_ // 4]])
        for tt in range(NTT):
            g = 0 if tt < 32 else 1
            x, xr = xt[tt % 3]
            y, yr = x2[tt % 2]
            h, hr = hb[tt % 2]
            s_, s_r = ss[tt % 2]
            r_, r_r = rs[tt % 2]
            P.dma("sp", f"nx{tt % 3}", lambda e, x=x, tt=tt: e.dma_start(out=x, in_=S["X"][tt * 128:(tt + 1) * 128, :]),
                  reads=R["X"][tt], writes=[xr])
            P.op("act", lambda e, x=x, s_=s_: e.activation(out=junk, in_=x, func=AF.Square, accum_out=s_),
                 reads=[xr], writes=[junk_r, s_r])
            rstd_ops(s_, s_r, r_, r_r, D, EPS)
            if pend_copy:
                flush_copy()
            if final:
                P.op("dve", lambda e, x=x, y=y, r_=r_, g=g: e.scalar_tensor_tensor(
                    out=y, in0=x, scalar=r_, in1=A[g][0], op0=ALU.mult, op1=ALU.mult),
                    reads=[xr, r_r, A[g][1]], writes=[yr])
                P.dma("sp", f"fo{tt % 2}", lambda e, y=y, tt=tt: e.dma_start(out=O["y"][tt * 128:(tt + 1) * 128, :], in_=y),
                      reads=[yr], acc_writes=[R["OUT"]])
                continue
            P.op("dve", lambda e, x=x, y=y, r_=r_, g=g: e.scalar_tensor_tensor(
                out=y, in0=x, scalar=r_, in1=A[g][0], op0=ALU.mult, op1=ALU.mult),
                reads=[xr, r_r, A[g][1]], writes=[yr])
            P.op("pool", lambda e, y=y, h=h, g=g: e.tensor_tensor(out=h, in0=y, in1=B[g][0], op=ALU.add),
                 reads=[yr, B[g][1]], writes=[hr])
            b = tt % 2
            for kc in range(8):
                P.op("pe", lambda e, h=h, kc=kc, b=b: e.transpose(bankb(b)[:, kc * 128:(kc + 1) * 128], h[:, kc * 128:(kc + 1) * 128], identb),
                     reads=[hr, identb_r], writes=[psr[b]] if kc == 0 else (), acc_writes=[psr[b]] if kc else ())
            pend_copy.append((b, tt))
        while pend_copy:
            flush_copy()
        P.release(m)

    def stream_w(dst, dst_r, key, src_ap):
        P.dma("pool", key, lambda e: e.dma_start(out=dst, in_=src_ap.rearrange("(kc p) n -> p kc n", p=128)), writes=[dst_r])

    def mm_wstat(b, w, wr, tile, ncols=512):
        for kc in range(8):
            P.op("pe", lambda e, kc=kc: e.matmul(bank(b, ncols), lhsT=w[:, kc, :], rhs=hT[:, kc, tile * 512:tile * 512 + ncols],
                                                 start=(kc == 0), stop=(kc == 7)),
                 reads=[wr, hTr[tile]], writes=[psr[b]] if kc == 0 else (), acc_writes=[psr[b]] if kc else ())

    def mm_tstat(b, w, wr, tt):
        for kc in range(8):
            P.op("pe", lambda e, kc=kc: e.matmul(bank(b), lhsT=hT[:, kc, tt * 128:(tt + 1) * 128], rhs=w[:, kc, :],
                                                 start=(kc == 0), stop=(kc == 7)),
                 reads=[wr, hTr[tt // 4]], writes=[psr[b]] if kc == 0 else (), acc_writes=[psr[b]] if kc else ())

    def phase_qkv(j):
        P.new_phase()
        m = P.mark()
        rcos, rcos_r = P.alloc([128, TS], F32, "rcos")
        rsin, rsin_r = P.alloc([128, TS], F32, "rsin")
        perm, perm_r = P.alloc([128, 128], BF16, "perm")
        P.dma("sp", "rcos", lambda e: e.dma_start(out=rcos, in_=I["rcos"]), writes=[rcos_r])
        P.dma("sp", "rsin", lambda e: e.dma_start(out=rsin, in_=I["rsin"]), writes=[rsin_r])
        P.dma("sp", "perm", lambda e: e.dma_start(out=perm, in_=I["perm"]), writes=[perm_r])
        ckt = [P.alloc([128, D], BF16, f"ckt{k}") for k in range(2)]
        kst = [P.alloc([128, 8, 128], BF16, f"kst{k}") for k in range(2)]
        cvt = [P.alloc([128, D], BF16, f"cvt{k}") for k in range(2)]
        for tk in range(4):
            c_, c_r = ckt[tk % 2]
            k_, k_r = kst[tk % 2]
            v_, v_r = cvt[tk % 2]
            P.dma("pool", f"ckt{tk % 2}", lambda e, c_=c_, tk=tk: e.dma_start(out=c_, in_=I["ck"][j, tk * 128:(tk + 1) * 128, :]), writes=[c_r])
            b = tk % 2
            for h in range(8):
                P.op("pe", lambda e, c_=c_, h=h, b=b: e.transpose(bankb(b)[:, h * 128:(h + 1) * 128], c_[:, h * 128:(h + 1) * 128], identb),
                     reads=[c_r, identb_r], writes=[psr[b]] if h == 0 else (), acc_writes=[psr[b]] if h else ())
            P.op("act", lambda e, k_=k_, b=b: e.copy(out=k_, in_=bankb(b).rearrange("p (k t) -> p k t", t=128)), reads=[psr[b]], writes=[k_r])
            P.dma("sp", f"kst{tk % 2}", lambda e, k_=k_, tk=tk: e.dma_start(
                out=S["KT"][:, :, tk * 128:(tk + 1) * 128].rearrange("h p c -> p h c"), in_=k_), reads=[k_r], acc_writes=[R["KT"]])
            P.dma("pool", f"cvt{tk % 2}", lambda e, v_=v_, tk=tk: e.dma_start(out=v_, in_=I["cv"][j, tk * 128:(tk + 1) * 128, :]), writes=[v_r])
            P.dma("sp", f"cvo{tk % 2}", lambda e, v_=v_, tk=tk: e.dma_start(out=S["VS"][tk * 128:(tk + 1) * 128, :], in_=v_),
                  reads=[v_r], acc_writes=[R["VS"]])
        if stop_after == f"qkv{2 * j}a":
            raise Stop()
        wq = [P.alloc([128, 8, 128], BF16, f"wq{k}") for k in range(3)]
        qb = [P.alloc([128, 512], BF16, f"qb{k}") for k in range(3)]
        t1 = [P.alloc([128, 512], F32, f"t1{k}") for k in range(3)]
        t2 = [P.alloc([128, 512], F32, f"t2{k}") for k in range(3)]
        qr = [P.alloc([128, 512], BF16, f"qr{k}") for k in range(3)]
        pend_tail = []

        def store(q_, q_r, isk, h, tile, key):
            if isk:
                P.dma("sp", key, lambda e: e.dma_start(
                    out=S["KT"][h, :, 512 + tile * 512: 512 + (tile + 1) * 512], in_=q_), reads=[q_r], acc_writes=[R["KT"]])
            else:
                P.dma("sp", key, lambda e: e.dma_start(
                    out=S["QT"][h, :, tile * 512:(tile + 1) * 512], in_=q_), reads=[q_r], acc_writes=[R["QT"]])
        it = 0
        for ch in range(16):
            isk, h = ch // 8, ch % 8
            w, wr = wq[ch % 3]
            stream_w(w, wr, f"wq{ch % 3}", I["attn_w_qkv"][j][:, isk * D + h * 128: isk * D + (h + 1) * 128])
            for tile in range(NT):
                bA = (it % 3) * 2
                bB = bA + 1
                q_, q_r = qr[it % 3]
                mm_wstat(bA, w, wr, tile)
                while pend_tail:
                    pend_tail.pop(0)()
                if tile < 8 and not os.environ.get('NOROPE'):
                    qq, qq_r = qb[it % 3]
                    a1, a1r = t1[it % 3]
                    a2, a2r = t2[it % 3]
                    P.op("act", lambda e, qq=qq, bA=bA: e.copy(out=qq, in_=bank(bA)), reads=[psr[bA]], writes=[qq_r])
                    P.op("dve", lambda e, a1=a1, bA=bA, tile=tile: e.tensor_tensor(
                        out=a1, in0=bank(bA), in1=rcos[:, tile * 512:(tile + 1) * 512], op=ALU.mult),
                        reads=[psr[bA], rcos_r], writes=[a1r])

                    def tail(qq=qq, qq_r=qq_r, bB=bB, a1=a1, a1r=a1r, a2=a2, a2r=a2r, q_=q_, q_r=q_r, tile=tile, isk=isk, h=h, key=f"qro{it % 3}"):
                        P.op("pe", lambda e: e.matmul(bank(bB), lhsT=perm, rhs=qq, start=True, stop=True),
                             reads=[perm_r, qq_r], writes=[psr[bB]])
                        P.op("dve", lambda e: e.tensor_tensor(
                            out=a2, in0=bank(bB), in1=rsin[:, tile * 512:(tile + 1) * 512], op=ALU.mult),
                            reads=[psr[bB], rsin_r], writes=[a2r])
                        P.op("pool", lambda e: e.tensor_tensor(out=q_, in0=a1, in1=a2, op=ALU.add),
                             reads=[a1r, a2r], writes=[q_r])
                        store(q_, q_r, isk, h, tile, key)
                    pend_tail.append(tail)
                else:
                    P.op("act", lambda e, q_=q_, bA=bA: e.copy(out=q_, in_=bank(bA)), reads=[psr[bA]], writes=[q_r])
                    pend_tail.append(lambda q_=q_, q_r=q_r, isk=isk, h=h, tile=tile, key=f"qro{it % 3}": store(q_, q_r, isk, h, tile, key))
                it += 1
        while pend_tail:
            pend_tail.pop(0)()
        if stop_after == f"qkv{2 * j}b":
            raise Stop()
        wv = [P.alloc([128, 8, 512], BF16, f"wv{k}") for k in range(2)]
        vb = [P.alloc([128, 512], BF16, f"vb{k}") for k in range(3)]
        vf = [P.alloc([128, 512], F32, f"vf{k}") for k in range(2)]
        it = 0
        for kind in ("v", "k"):
            for half in range(2):
                w, wr = wv[(it // 64) % 2]
                col0 = (2 * D if kind == "v" else D) + half * 512
                w, wr = wv[half]
                stream_w(w, wr, f"wv{half}{kind}", I["attn_w_qkv"][j][:, col0:col0 + 512])
                tts = range(NTT) if kind == "v" else range(32, 36)
                for tt in tts:
                    b = 6 + it % 2
                    mm_tstat(b, w, wr, tt)
                    if kind == "v":
                        v_, v_r = vb[it % 3]
                        P.op("act", lambda e, v_=v_, b=b: e.copy(out=v_, in_=bank(b)), reads=[psr[b]], writes=[v_r])
                        P.dma("sp", f"vbo{it % 3}", lambda e, v_=v_, tt=tt, half=half: e.dma_start(
                            out=S["VS"][512 + tt * 128: 512 + (tt + 1) * 128, half * 512:(half + 1) * 512], in_=v_),
                            reads=[v_r], acc_writes=[R["VS"]])
                    if tt >= 32:
                        f_, f_r = vf[it % 2]
                        P.op("dve", lambda e, f_=f_, b=b: e.tensor_copy(out=f_, in_=bank(b)), reads=[psr[b]], writes=[f_r])
                        s, r0 = (tt - 32) // 2, ((tt - 32) % 2) * 128
                        dst = O["nv"] if kind == "v" else O["nk"]
                        P.dma("sp", f"vfo{it % 2}", lambda e, f_=f_, s=s, r0=r0, half=half, dst=dst: e.dma_start(
                            out=dst[s, j, r0:r0 + 128, half * 512:(half + 1) * 512], in_=f_), reads=[f_r], acc_writes=[R["OUT"]])
                    it += 1
        P.release(m)

    def phase_attn(j, i):
        P.new_phase()
        m = P.mark()
        lam_init = 0.8 - 0.6 * math.exp(-0.3 * i)
        lp, lp_r = P.alloc([128, 4, 64], F32, "lp")
        lpp, lpp_r = P.alloc([128, 2, 64], F32, "lpp")
        lsum, lsum_r = P.alloc([128, 2], F32, "lsum")
        lexp, lexp_r = P.alloc([128, 2], F32, "lexp")
        nlam, nlam_r = P.alloc([128, 1], F32, "nlam")
        gsub, gsub_r = P.alloc([128, 128], F32, "gsub")
        P.dma("sp", "lp", lambda e: e.dma_start(out=lp, in_=I["attn_lambda"][j].rearrange("(o a) b -> o a b", o=1).broadcast_to([128, 4, 64])), writes=[lp_r])
        P.dma("sp", "gsub", lambda e: e.dma_start(out=gsub, in_=bcast_rows(I["attn_subln_g"][j], 128)), writes=[gsub_r])
        P.op("dve", lambda e: e.tensor_scalar(out=gsub, in0=gsub, scalar1=1.0 - lam_init, scalar2=None, op0=ALU.mult),
             reads=[gsub_r], writes=[gsub_r])
        for k in range(2):
            P.op("dve", lambda e, k=k: e.tensor_tensor(out=lpp[:, k, :], in0=lp[:, 2 * k, :], in1=lp[:, 2 * k + 1, :], op=ALU.mult),
                 reads=[lp_r], acc_writes=[lpp_r])
        P.op("dve", lambda e: e.tensor_reduce(out=lsum, in_=lpp, axis=AX.X, op=ALU.add), reads=[lpp_r], writes=[lsum_r])
        P.op("act", lambda e: e.activation(out=lexp, in_=lsum, func=AF.Exp), reads=[lsum_r], writes=[lexp_r])
        P.op("dve", lambda e: e.tensor_tensor(out=nlam, in0=lexp[:, 1:2], in1=lexp[:, 0:1], op=ALU.subtract), reads=[lexp_r], writes=[nlam_r])
        P.op("dve", lambda e: e.tensor_scalar(out=nlam, in0=nlam, scalar1=-lam_init, scalar2=None, op0=ALU.add), reads=[nlam_r], writes=[nlam_r])

        kTb = [P.alloc([128, 4608], BF16, f"kTh{k}") for k in range(2)]
        vhb = [P.alloc([128, 36, 132], BF16, f"vh{k}") for k in range(2)]
        for k in range(2):
            P.op("pool", lambda e, k=k: e.memset(vhb[k][0], 1.0), writes=[vhb[k][1]])
        qtb = [P.alloc([128, 2, 256], BF16, f"qt{k}") for k in range(3)]
        for k in range(3):
            P.op("pool", lambda e, k=k: e.memset(qtb[k][0], 0.0), writes=[qtb[k][1]])
        Pb = [P.alloc([128, 2, 256], BF16, f"P{k}") for k in range(3)]
        osb = [P.alloc([128, 128], F32, f"o{k}") for k in range(2)]
        obb = [P.alloc([128, 128], BF16, f"ob{k}") for k in range(2)]
        rrb = [P.alloc([128, 2], F32, f"rr{k}") for k in range(2)]
        ssb = [P.alloc([128, 1], F32, f"ss{k}") for k in range(2)]
        rsb = [P.alloc([128, 1], F32, f"rs{k}") for k in range(2)]
        junk, junk_r = P.alloc([128, 128], BF16, "ajunk")
        accs = [P.alloc([128, 4, 132], F32, f"accs{k}") for k in range(2)]
        steps = []
        hcount = 0
        for G in GROUPS:
            for s in range(G["nseq"]):
                L = G["L"]
                nk = G["ncache"] + L
                nkc = nk // 128
                kcol0 = 0 if G["name"] == "s" else 4608 + s * 256
                qtok0 = G["tok0"] + s * L
                for h in range(8):
                    for qb_ in range(L // 256):
                        for c in range(nkc):
                            steps.append(dict(h=h, hid=hcount, q0=qtok0 + qb_ * 256, c=c, nkc=nkc, nk=nk, kcol0=kcol0,
                                              newh=(qb_ == 0 and c == 0), newq=(c == 0)))
                    hcount += 1
        qcount = [0]
        ecount = [0]
        cur = {}

        heads = {st["hid"]: st for st in steps if st["newh"]}
        loaded = set()

        def load_head(hs_):
            kT, kT_r = kTb[hs_["hid"] % 2]
            vh, vh_r = vhb[hs_["hid"] % 2]
            nk, nkc, kcol0, hh = hs_["nk"], hs_["nkc"], hs_["kcol0"], hs_["h"]
            P.dma("sp", f"kTh{hs_['hid'] % 2}", lambda e: e.dma_start(
                out=kT[:, 0:nk], in_=S["KT"][hh, :, kcol0:kcol0 + nk]), reads=[R["KT"]], writes=[kT_r])
            P.dma("sp", f"vh{hs_['hid'] % 2}", lambda e: e.dma_start(
                out=vh[:, 0:nkc, 0:128], in_=S["VS"][kcol0:kcol0 + nk, hh * 128:(hh + 1) * 128].rearrange("(c p) e -> p c e", p=128)),
                reads=[R["VS"]], acc_writes=[vh_r])

        def stage_a(i, st):
            h = st["h"]
            if st["newh"]:
                if st["hid"] not in loaded:
                    loaded.add(st["hid"])
                    load_head(st)
                nxt = heads.get(st["hid"] + 1)
                if nxt is not None and st["nkc"] >= 12:
                    def pf(nxt=nxt):
                        if nxt["hid"] not in loaded:
                            loaded.add(nxt["hid"])
                            load_head(nxt)
                    deferred.append((i + LA, pf))
                cur["kT"], cur["vh"] = kTb[st["hid"] % 2], vhb[st["hid"] % 2]
            if st["newq"]:
                qt, qt_r = qtb[qcount[0] % 3]
                q0 = st["q0"]
                for mp in range(2):
                    P.dma("sp", f"qt{qcount[0] % 3}_{mp}", lambda e, mp=mp: e.dma_start(
                        out=qt[mp * 64:(mp + 1) * 64, mp, :], in_=S["QT"][h, mp * 64:(mp + 1) * 64, q0:q0 + 256]),
                        reads=[R["QT"]], acc_writes=[qt_r])
                qcount[0] += 1
                cur["qt"] = (qt, qt_r)
            kT, kT_r = cur["kT"]
            qt, qt_r = cur["qt"]
            st["vh"] = cur["vh"]
            c = st["c"]
            sb_ = i % 3
            Pt, Pt_r = Pb[i % 3]
            st["P"] = (Pt, Pt_r)
            for mp in range(2):
                P.op("pe", lambda e, mp=mp: e.matmul(
                    bank(sb_)[:, mp * 256:(mp + 1) * 256], lhsT=kT[:, c * 128:(c + 1) * 128],
                    rhs=qt[:, mp, :], start=True, stop=True),
                    reads=[kT_r, qt_r], writes=[psr[sb_]] if mp == 0 else (), acc_writes=[psr[sb_]] if mp else ())
            P.op("act", lambda e: e.activation(
                out=Pt, in_=bank(sb_).rearrange("p (m q) -> p m q", q=256), func=AF.Exp, scale=0.125),
                reads=[psr[sb_]], writes=[Pt_r])

        def stage_b(st):
            Pt, Pt_r = st["P"]
            vh, vh_r = st["vh"]
            c, nkc, h = st["c"], st["nkc"], st["h"]
            for qs in range(2):
                for mp in range(2):
                    ab = 4 + qs * 2 + mp
                    P.op("pe", lambda e, qs=qs, mp=mp, ab=ab: e.matmul(
                        bank(ab, 129), lhsT=Pt[:, mp, qs * 128:(qs + 1) * 128], rhs=vh[:, c, 0:129],
                        start=(c == 0), stop=(c == nkc - 1)),
                        reads=[Pt_r, vh_r], writes=[psr[ab]] if c == 0 else (), acc_writes=[psr[ab]] if c else ())
            if c != nkc - 1:
                return
            if nkc < 12:
                run_deferred(10 ** 9)
            ac, ac_r = accs[ecount[0] % 2]
            P.op("act", lambda e: e.copy(out=ac[:, :, 0:129], in_=PS[:, 4 * 512:8 * 512].rearrange("p (b c) -> p b c", c=512)[:, :, 0:129]),
                 reads=[psr[4], psr[5], psr[6], psr[7]], writes=[ac_r])
            for qs in range(2):
                a0, a1 = qs * 2, qs * 2 + 1
                o_, o_r = osb[ecount[0] % 2]
                ob, ob_r = obb[ecount[0] % 2]
                rr, rr_r = rrb[ecount[0] % 2]
                s_, s_r = ssb[ecount[0] % 2]
                r_, r_r = rsb[ecount[0] % 2]
                ecount[0] += 1
                tok = st["q0"] + qs * 128
                P.op("dve", lambda e, rr=rr, a0=a0: e.reciprocal(out=rr, in_=ac[:, a0:a0 + 2, 128]),
                     reads=[ac_r], writes=[rr_r])
                P.op("dve", lambda e, rr=rr: e.tensor_tensor(out=rr[:, 1:2], in0=rr[:, 1:2], in1=nlam, op=ALU.mult),
                     reads=[rr_r, nlam_r], writes=[rr_r])
                P.op("dve", lambda e, o_=o_, rr=rr, a0=a0: e.tensor_scalar(
                    out=o_, in0=ac[:, a0, 0:128], scalar1=rr[:, 0:1], scalar2=None, op0=ALU.mult),
                    reads=[ac_r, rr_r], writes=[o_r])
                P.op("dve", lambda e, o_=o_, rr=rr, a1=a1: e.scalar_tensor_tensor(
                    out=o_, in0=ac[:, a1, 0:128], scalar=rr[:, 1:2], in1=o_, op0=ALU.mult, op1=ALU.add),
                    reads=[ac_r, rr_r, o_r], writes=[o_r])
                P.op("dve", lambda e, o_=o_, s_=s_: e.scalar_tensor_tensor(
                    out=junk, in0=o_, scalar=1.0, in1=o_, op0=ALU.mult, op1=ALU.mult, accum_out=s_),
                    reads=[o_r], writes=[junk_r, s_r])

                def e2(o_=o_, o_r=o_r, ob=ob, ob_r=ob_r, s_=s_, s_r=s_r, r_=r_, r_r=r_r):
                    rstd_ops(s_, s_r, r_, r_r, 128, SUBLN_EPS)
                    P.op("dve", lambda e: e.scalar_tensor_tensor(
                        out=ob, in0=o_, scalar=r_, in1=gsub, op0=ALU.mult, op1=ALU.mult),
                        reads=[o_r, r_r, gsub_r], writes=[ob_r])

                def e4(ob=ob, ob_r=ob_r):
                    P.op("pe", lambda e: e.transpose(bankb(3)[:, 0:128], ob, identb), reads=[ob_r, identb_r], writes=[psr[3]])

                def e5(tok=tok, h=h):
                    P.op("act", lambda e: e.copy(out=hT[:, h, tok:tok + 128], in_=bankb(3)[:, 0:128]),
                         reads=[psr[3]], acc_writes=[hTr[tok // 512]])
                if nkc >= 12:
                    base = st["idx"] + LA
                    deferred.append((base + 2 + qs, e2))
                    deferred.append((base + 4 + 3 * qs, e4))
                    deferred.append((base + 6 + 3 * qs, e5))
                else:
                    e2()
                    e4()
                    e5()

        LA = 2
        deferred = []

        def run_deferred(now):
            keep = []
            for due, fn in deferred:
                if due <= now:
                    fn()
                else:
                    keep.append((due, fn))
            deferred[:] = keep
        for i, st in enumerate(steps):
            st["idx"] = i
            stage_a(i, st)
            if i >= LA:
                stage_b(steps[i - LA])
            run_deferred(i)
        for st in steps[-LA:]:
            stage_b(st)
        run_deferred(10 ** 9)
        P.release(m)

    def phase_outproj(wsrc, bias_vec, offG):
        P.new_phase()
        m = P.mark()
        G_ = load_mod(offG, "oG")
        if bias_vec is not None:
            brep, brep_r = P.alloc([128, D], F32, "obias")
            P.dma("sp", "obias", lambda e: e.dma_start(out=brep, in_=bcast_rows(bias_vec, D)), writes=[brep_r])
        wo = [P.alloc([128, 8, 512], BF16, f"wo{k}") for k in range(2)]
        xt = [P.alloc([128, 512], F32, f"ox{k}") for k in range(3)]
        tb = [P.alloc([128, 512], F32, f"ot{k}") for k in range(2)]
        it = 0
        for half in range(2):
            w, wr = wo[half]
            stream_w(w, wr, f"wo{half}", wsrc[:, half * 512:(half + 1) * 512])
            for tt in range(NTT):
                g = 0 if tt < 32 else 1
                b = it % 4
                x, xr = xt[it % 3]
                t_, t_r = tb[it % 2]
                P.dma("sp", f"ox{it % 3}", lambda e, x=x, tt=tt, half=half: e.dma_start(
                    out=x, in_=S["X"][tt * 128:(tt + 1) * 128, half * 512:(half + 1) * 512]), reads=[R["X"][tt][half]], writes=[xr])
                mm_tstat(b, w, wr, tt)
                src = bank(b)
                if bias_vec is not None:
                    P.op("dve", lambda e, t_=t_, b=b, half=half: e.tensor_tensor(
                        out=t_, in0=bank(b), in1=brep[:, half * 512:(half + 1) * 512], op=ALU.add),
                        reads=[psr[b], brep_r], writes=[t_r])
                    P.op("dve", lambda e, t_=t_, g=g, half=half: e.tensor_tensor(
                        out=t_, in0=t_, in1=G_[g][0][:, half * 512:(half + 1) * 512], op=ALU.mult),
                        reads=[t_r, G_[g][1]], writes=[t_r])
                else:
                    P.op("dve", lambda e, t_=t_, b=b, g=g, half=half: e.tensor_tensor(
                        out=t_, in0=bank(b), in1=G_[g][0][:, half * 512:(half + 1) * 512], op=ALU.mult),
                        reads=[psr[b], G_[g][1]], writes=[t_r])
                P.op("pool", lambda e, x=x, t_=t_: e.tensor_tensor(out=x, in0=x, in1=t_, op=ALU.add), reads=[xr, t_r], writes=[xr])
                P.dma("sp", f"oxo{it % 3}", lambda e, x=x, tt=tt, half=half: e.dma_start(
                    out=S["X"][tt * 128:(tt + 1) * 128, half * 512:(half + 1) * 512], in_=x), reads=[xr], writes=[R["X"][tt][half]])
                it += 1
        P.release(m)

    def phase_ffn(i):
        P.new_phase()
        m0 = P.mark()
        wd, wd_r = P.alloc([128, NCH_FF, D], BF16, "wd")
        P.dma("pool", "wd", lambda e: e.dma_start(out=wd, in_=I["ffn_w_down"][i].rearrange("(c p) n -> p c n", p=128)), writes=[wd_r])
        m = P.mark()
        wg = [P.alloc([128, 2, 8, 128], BF16, f"wg{k}") for k in range(3)]
        sg = [P.alloc([128, 512], F32, f"sg{k}") for k in range(2)]
        md = [P.alloc([128, 512], BF16, f"md{k}") for k in range(3)]
        it = 0
        for c in range(NCH_FF):
            w, wr = wg[c % 3]
            for u in range(2):
                P.dma("pool", f"wg{c % 3}_{u}", lambda e, w=w, c=c, u=u: e.dma_start(
                    out=w[:, u], in_=I["ffn_w_gu"][i][:, u * DFF + c * 128: u * DFF + (c + 1) * 128].rearrange("(kc p) n -> p kc n", p=128)),
                    writes=[wr] if u == 0 else (), acc_writes=[wr] if u else ())
            for tile in range(NT):
                bG = (it % 4) * 2
                bU = bG + 1
                s_, s_r = sg[it % 2]
                m_, m_r = md[it % 3]
                mm_wstat(bG, w[:, 0], wr, tile)
                mm_wstat(bU, w[:, 1], wr, tile)
                P.op("act", lambda e, s_=s_, bG=bG: e.activation(out=s_, in_=bank(bG), func=AF.Silu), reads=[psr[bG]], writes=[s_r])
                P.op("dve", lambda e, m_=m_, s_=s_, bU=bU: e.tensor_tensor(out=m_, in0=bank(bU), in1=s_, op=ALU.mult),
                     reads=[psr[bU], s_r], writes=[m_r])
                P.dma("sp", f"mdo{it % 3}", lambda e, m_=m_, c=c, tile=tile: e.dma_start(
                    out=S["MID"][c, :, tile * 512:(tile + 1) * 512], in_=m_), reads=[m_r], acc_writes=[R["MID"][tile]])
                it += 1
        P.release(m)
        P.new_phase()
        m = P.mark()
        G_ = load_mod(5 * D, "fG")
        mt = [P.alloc([128, NCH_FF, 512], BF16, f"mt{k}") for k in range(2)]
        xt = [P.alloc([128, 512], F32, f"fx{k}") for k in range(3)]
        tb = [P.alloc([128, 512], F32, f"ft{k}") for k in range(2)]
        it = 0
        for tile in range(NT):
            g = 0 if tile < 8 else 1
            mt_, mt_r = mt[tile % 2]
            P.dma("sp", f"mt{tile % 2}", lambda e, mt_=mt_, tile=tile: e.dma_start(
                out=mt_, in_=S["MID"][:, :, tile * 512:(tile + 1) * 512].rearrange("c p t -> p c t")), reads=[R["MID"][tile]], writes=[mt_r])
            for ts in range(4):
                tt = tile * 4 + ts
                for half in range(2):
                    b = it % 4
                    x, xr = xt[it % 3]
                    t_, t_r = tb[it % 2]
                    P.dma("sp", f"fx{it % 3}", lambda e, x=x, tt=tt, half=half: e.dma_start(
                        out=x, in_=S["X"][tt * 128:(tt + 1) * 128, half * 512:(half + 1) * 512]), reads=[R["X"][tt][half]], writes=[xr])
                    for c in range(NCH_FF):
                        P.op("pe", lambda e, mt_=mt_, c=c, ts=ts, half=half, b=b: e.matmul(
                            bank(b), lhsT=mt_[:, c, ts * 128:(ts + 1) * 128], rhs=wd[:, c, half * 512:(half + 1) * 512],
                            start=(c == 0), stop=(c == NCH_FF - 1)),
                            reads=[mt_r, wd_r], writes=[psr[b]] if c == 0 else (), acc_writes=[psr[b]] if c else ())
                    P.op("dve", lambda e, t_=t_, b=b, g=g, half=half: e.tensor_tensor(
                        out=t_, in0=bank(b), in1=G_[g][0][:, half * 512:(half + 1) * 512], op=ALU.mult),
                        reads=[psr[b], G_[g][1]], writes=[t_r])
                    P.op("pool", lambda e, x=x, t_=t_: e.tensor_tensor(out=x, in0=x, in1=t_, op=ALU.add), reads=[xr, t_r], writes=[xr])
                    P.dma("sp", f"fxo{it % 3}", lambda e, x=x, tt=tt, half=half: e.dma_start(
                        out=S["X"][tt * 128:(tt + 1) * 128, half * 512:(half + 1) * 512], in_=x), reads=[xr], writes=[R["X"][tt][half]])
                    it += 1
        P.release(m)
        P.release(m0)

    def phase_hy_in(j):
        P.new_phase()
        m = P.mark()
        CV, CV_r = P.alloc([128, 120], F32, "CV")
        load_cols(CV, CV_r, 0, I["hy_b_in"][j], 24, "cvl")
        for k in range(3):
            load_cols(CV, CV_r, 24 + k * 24, I["hy_conv_w"][j, k], 24, "cvl")
        load_cols(CV, CV_r, 96, I["hy_conv_b"][j], 24, "cvl")
        ubS = [P.alloc([128, TS + 2], F32, f"ubS{k}") for k in range(2)]
        ubP = [P.alloc([128, 2, 258], F32, f"ubP{k}") for k in range(2)]
        for k in range(2):
            P.op("pool", lambda e, k=k: e.memset(ubS[k][0], 0.0), writes=[ubS[k][1]])
            P.op("pool", lambda e, k=k: e.memset(ubP[k][0], 0.0), writes=[ubP[k][1]])
        cb1, cb1_r = P.alloc([128, T], F32, "cb1")
        cb2 = [P.alloc([128, T], F32, f"cb2{k}") for k in range(2)]
        vst, vst_r = P.alloc([128, NTT, 128], BF16, "vst")
        wi = [P.alloc([128, 8, 128], BF16, f"wi{k}") for k in range(3)]
        pend_tr = []

        def do_transposes(dst, dst_r, jd):
            for t4 in range(NTT // 4):
                b = 4 + t4 % 4
                for q in range(4):
                    tt = t4 * 4 + q
                    P.op("pe", lambda e, tt=tt, q=q, b=b: e.transpose(
                        bank(b)[:, q * 128:(q + 1) * 128], dst[:, tt * 128:(tt + 1) * 128], identf),
                        reads=[dst_r, identf_r], writes=[psr[b]] if q == 0 else (), acc_writes=[psr[b]] if q else ())
                P.op("act", lambda e, t4=t4, b=b: e.copy(out=vst[:, t4 * 4:(t4 + 1) * 4, :], in_=bank(b).rearrange("p (q d) -> p q d", d=128)),
                     reads=[psr[b]], acc_writes=[vst_r])
            P.dma("sp", "vsto", lambda e: e.dma_start(
                out=S["VVTOK"][:, jd * 128:(jd + 1) * 128].rearrange("(c p) d -> p c d", p=128), in_=vst),
                reads=[vst_r], acc_writes=[R["VVTOK"]])
        it = 0
        uc = 0
        c2 = 0
        for jd in range(8):
            for part, fc in (("x1", 8 + jd), ("v", 16 + jd), ("x0", jd)):
                w, wr = wi[it % 3]
                it += 1
                stream_w(w, wr, f"wi{it % 3}", I["hy_w_in"][j][:, fc * 128:(fc + 1) * 128])
                uS, uS_r = ubS[uc % 2]
                uP, uP_r = ubP[uc % 2]
                uc += 1
                for tile in range(NT):
                    b = tile % 4
                    mm_wstat(b, w, wr, tile)
                    if tile < 8:
                        P.op("act", lambda e, uS=uS, b=b, tile=tile, fc=fc: e.activation(
                            out=uS[:, 1 + tile * 512: 1 + (tile + 1) * 512], in_=bank(b), func=AF.Identity, bias=CV[:, fc:fc + 1]),
                            reads=[psr[b], CV_r], acc_writes=[uS_r])
                    else:
                        P.op("act", lambda e, uP=uP, b=b, fc=fc: e.activation(
                            out=uP[:, :, 1:257], in_=bank(b).rearrange("p (s t) -> p s t", t=256), func=AF.Identity, bias=CV[:, fc:fc + 1]),
                            reads=[psr[b], CV_r], acc_writes=[uP_r])
                if part == "x1":
                    dst, dst_r = cb1, cb1_r
                else:
                    dst, dst_r = cb2[c2 % 2]
                    c2 += 1
                w0, w1, w2, cbc = (CV[:, 24 + fc:25 + fc], CV[:, 48 + fc:49 + fc], CV[:, 72 + fc:73 + fc], CV[:, 96 + fc:97 + fc])
                dS = dst[:, 0:TS]
                dP = dst[:, TS:T].rearrange("p (s t) -> p s t", t=256)
                for (dd, uu, ur, n) in ((dS, uS, uS_r, TS), (dP, uP, uP_r, 256)):
                    def sl(o, uu=uu, n=n):
                        return uu[:, o:o + n] if len(uu.shape) == 2 else uu[:, :, o:o + n]
                    P.op("dve", lambda e, dd=dd, sl=sl, w0=w0, cbc=cbc: e.tensor_scalar(out=dd, in0=sl(0), scalar1=w0, scalar2=cbc, op0=ALU.mult, op1=ALU.add),
                         reads=[ur, CV_r], acc_writes=[dst_r])
                    P.op("dve", lambda e, dd=dd, sl=sl, w1=w1: e.scalar_tensor_tensor(out=dd, in0=sl(1), scalar=w1, in1=dd, op0=ALU.mult, op1=ALU.add),
                         reads=[ur, CV_r, dst_r], acc_writes=[dst_r])
                    P.op("dve", lambda e, dd=dd, sl=sl, w2=w2: e.scalar_tensor_tensor(out=dd, in0=sl(2), scalar=w2, in1=dd, op0=ALU.mult, op1=ALU.add),
                         reads=[ur, CV_r, dst_r], acc_writes=[dst_r])
                if part == "v":
                    P.op("pool", lambda e, dst=dst: e.tensor_tensor(out=dst, in0=dst, in1=cb1, op=ALU.mult), reads=[dst_r, cb1_r], writes=[dst_r])
                    P.dma("sp", f"vvt{c2 % 2}", lambda e, dst=dst, jd=jd: e.dma_start(out=S["VVT"][jd], in_=dst), reads=[dst_r], acc_writes=[R["VVT"]])
                    pend_tr.append((dst, dst_r, jd))
                if part != "v" and pend_tr:
                    do_transposes(*pend_tr.pop(0))
                if False:
                    for t4 in range(NTT // 4):
                        b = 4 + t4 % 4
                        for q in range(4):
                            tt = t4 * 4 + q
                            P.op("pe", lambda e, dst=dst, tt=tt, q=q, b=b: e.transpose(
                                bank(b)[:, q * 128:(q + 1) * 128], dst[:, tt * 128:(tt + 1) * 128], identf),
                                reads=[dst_r, identf_r], writes=[psr[b]] if q == 0 else (), acc_writes=[psr[b]] if q else ())
                        P.op("act", lambda e, t4=t4, b=b: e.copy(out=vst[:, t4 * 4:(t4 + 1) * 4, :], in_=bank(b).rearrange("p (q d) -> p q d", d=128)),
                             reads=[psr[b]], acc_writes=[vst_r])
                    P.dma("sp", "vsto", lambda e, jd=jd: e.dma_start(
                        out=S["VVTOK"][:, jd * 128:(jd + 1) * 128].rearrange("(c p) d -> p c d", p=128), in_=vst),
                        reads=[vst_r], acc_writes=[R["VVTOK"]])
                    vst_r.w = dict(vst_r.w)
                if part == "x0":
                    P.dma("sp", f"x0t{c2 % 2}", lambda e, dst=dst, jd=jd: e.dma_start(out=S["X0T"][jd], in_=dst), reads=[dst_r], acc_writes=[R["X0T"]])
        P.release(m)

    def phase_hy_filter(j, G):
        L = G["L"]
        nm = G["name"]
        ntc = L // 128
        HS, HD = S["HS" + nm], S["HD" + nm]
        P.new_phase()
        rn, rn_r = P.alloc([128, D], F32, "rn")
        m = P.mark()
        zT, zT_r = P.alloc([33, L], F32, "zT")
        w1, w1_r = P.alloc([33, 64], F32, "fw1")
        w2, w2_r = P.alloc([64, 64], F32, "fw2")
        w3, w3_r = P.alloc([64, 2 * D], F32, "fw3")
        b3, b3_r = P.alloc([128, 2 * D], F32, "fb3")
        fcol, fcol_r = P.alloc([64, 4], F32, "fcol")
        negt, negt_r = P.alloc([128, ntc], F32, "negt")
        drep, drep_r = P.alloc([128, D], F32, "drep")
        mask0, mask0_r = P.alloc([128, 1], F32, "mask0")
        h1, h1_r = P.alloc([64, L], F32, "h1")
        h2, h2_r = P.alloc([64, L], F32, "h2")
        tmp = [P.alloc([64, 512], F32, f"ftmp{k}") for k in range(2)]
        P.dma("sp", "zT", lambda e: e.dma_start(out=zT, in_=I["z" + nm]), writes=[zT_r])
        P.dma("sp", "fw1", lambda e: e.dma_start(out=w1, in_=I["filt_w1"][j]), writes=[w1_r])
        P.dma("sp", "fw2", lambda e: e.dma_start(out=w2, in_=I["filt_w2"][j]), writes=[w2_r])
        P.dma("sp", "fw3", lambda e: e.dma_start(out=w3, in_=I["filt_w3"][j]), writes=[w3_r])
        P.dma("sp", "fb3", lambda e: e.dma_start(out=b3, in_=bcast_rows(I["filt_b3"][j], 2 * D)), writes=[b3_r])
        P.dma("sp", "negt", lambda e: e.dma_start(out=negt, in_=I["negt" + nm]), writes=[negt_r])
        P.dma("sp", "drep", lambda e: e.dma_start(out=drep, in_=bcast_rows(I["delta"], D)), writes=[drep_r])
        P.dma("sp", "mask0", lambda e: e.dma_start(out=mask0, in_=I["mask0"]), writes=[mask0_r])
        for k, v in enumerate((I["filt_b1"][j], I["filt_b2"][j], I["filt_freq"][j])):
            P.dma("sp", "fcol", lambda e, k=k, v=v: e.dma_start(out=fcol[:, k:k + 1], in_=v.rearrange("(p o) -> p o", o=1)), acc_writes=[fcol_r])
        TWO_PI = 2.0 * math.pi
        SC = TWO_PI * (1.0 - 2e-6)
        wr_, wr_r = P.alloc([64, 512], F32, "fwrap")

        def sin_layer(lhsT, lhsT_r, src, src_r, dstb, dstb_r, bcol):
            n = min(512, L)
            for ti in range(L // n):
                b = ti % 2
                t_, t_r = tmp[ti % 2]
                P.op("pe", lambda e, ti=ti, b=b: e.matmul(bank(b)[0:64, 0:n], lhsT=lhsT, rhs=src[:, ti * n:(ti + 1) * n], start=True, stop=True),
                     reads=[lhsT_r, src_r], writes=[psr[b]])
                P.op("dve", lambda e, t_=t_, b=b: e.tensor_scalar(
                    out=t_[:, 0:n], in0=bank(b)[0:64, 0:n], scalar1=fcol[:, bcol:bcol + 1], scalar2=fcol[:, 2:3], op0=ALU.add, op1=ALU.mult),
                    reads=[psr[b], fcol_r], writes=[t_r])
                for rnd in range(2):
                    for (cmp_, thr, sgn) in ((ALU.is_lt, -math.pi, ALU.add), (ALU.is_gt, math.pi, ALU.subtract)):
                        P.op("dve", lambda e, t_=t_, cmp_=cmp_, thr=thr: e.tensor_scalar(
                            out=wr_[:, 0:n], in0=t_[:, 0:n], scalar1=thr, scalar2=TWO_PI, op0=cmp_, op1=ALU.mult),
                            reads=[t_r], writes=[wr_r])
                        P.op("dve", lambda e, t_=t_, sgn=sgn: e.tensor_tensor(out=t_[:, 0:n], in0=t_[:, 0:n], in1=wr_[:, 0:n], op=sgn),
                             reads=[t_r, wr_r], writes=[t_r])
                P.op("act", lambda e, t_=t_, ti=ti: e.activation(out=dstb[:, ti * n:(ti + 1) * n], in_=t_[:, 0:n], func=AF.Sin, scale=1.0 - 2e-6),
                     reads=[t_r], acc_writes=[dstb_r])
        sin_layer(w1, w1_r, zT, zT_r, h1, h1_r, 0)
        sin_layer(w2, w2_r, h1, h1_r, h2, h2_r, 1)
        dec = [P.alloc([128, D], F32, "dec0")] * 2
        hf = [P.alloc([128, D], F32, f"hf{k}") for k in range(2)]
        hb_ = [P.alloc([128, D], F32, f"hbk{k}") for k in range(2)]
        ab = [P.alloc([128, 2 * D], F32, "ab0")] * 2
        hs = [P.alloc([128, D], BF16, f"hs{k}") for k in range(2)]
        hd = [P.alloc([128, D], BF16, f"hd{k}") for k in range(2)]
        pend_ones = []

        def flush_ones():
            tc_ = pend_ones.pop(0)
            k_ = tc_ % 2
            for q in range(4):
                bq = 4 + q % 2
                first = (tc_ == 0 and q < 2)
                last = (tc_ == ntc - 1 and q >= 2)
                P.op("pe", lambda e, q=q, bq=bq, first=first, last=last: e.matmul(
                    bank(bq), lhsT=onesf, rhs=ab[k_][0][:, q * 512:(q + 1) * 512], start=first, stop=last),
                    reads=[onesf_r, ab[k_][1]], writes=[psr[bq]] if first else (), acc_writes=() if first else [psr[bq]])
        for tc in range(ntc):
            k = tc % 2
            for cb in range(4):
                P.op("pe", lambda e, tc=tc, cb=cb: e.matmul(bank(cb), lhsT=h2[:, tc * 128:(tc + 1) * 128], rhs=w3[:, cb * 512:(cb + 1) * 512], start=True, stop=True),
                     reads=[h2_r, w3_r], writes=[psr[cb]])
            if pend_ones:
                flush_ones()
            P.op("act", lambda e, k=k, tc=tc: e.activation(out=dec[k][0], in_=drep, func=AF.Exp, scale=negt[:, tc:tc + 1]),
                 reads=[drep_r, negt_r], writes=[dec[k][1]])
            for (dst, lo) in ((hf[k], 0), (hb_[k], D)):
                P.op("dve", lambda e, dst=dst, lo=lo: e.tensor_tensor(out=dst[0], in0=PS[:, lo:lo + D], in1=b3[:, lo:lo + D], op=ALU.add),
                     reads=[psr[lo // 512], psr[lo // 512 + 1], b3_r], writes=[dst[1]])
                P.op("dve", lambda e, dst=dst, k=k: e.tensor_tensor(out=dst[0], in0=dst[0], in1=dec[k][0], op=ALU.mult),
                     reads=[dst[1], dec[k][1]], writes=[dst[1]])
                P.op("act", lambda e, dst=dst, lo=lo, k=k: e.activation(out=ab[k][0][:, lo:lo + D], in_=dst[0], func=AF.Abs),
                     reads=[dst[1]], acc_writes=[ab[k][1]])
            pend_ones.append(tc)
            if tc == 0:
                P.op("dve", lambda e, k=k: e.tensor_scalar(out=hb_[k][0], in0=hb_[k][0], scalar1=mask0[:, 0:1], scalar2=None, op0=ALU.mult),
                     reads=[hb_[k][1], mask0_r], writes=[hb_[k][1]])
            P.op("pool", lambda e, k=k: e.tensor_tensor(out=hs[k][0], in0=hf[k][0], in1=hb_[k][0], op=ALU.add),
                 reads=[hf[k][1], hb_[k][1]], writes=[hs[k][1]])
            P.op("pool", lambda e, k=k: e.tensor_tensor(out=hd[k][0], in0=hb_[k][0], in1=hf[k][0], op=ALU.subtract),
                 reads=[hf[k][1], hb_[k][1]], writes=[hd[k][1]])
            P.dma("sp", f"hso{k}", lambda e, k=k, tc=tc: e.dma_start(out=HS[tc * 128:(tc + 1) * 128, :], in_=hs[k][0]), reads=[hs[k][1]], acc_writes=[R["HS"]])
            P.dma("sp", f"hdo{k}", lambda e, k=k, tc=tc: e.dma_start(out=HD[tc * 128:(tc + 1) * 128, :], in_=hd[k][0]), reads=[hd[k][1]], acc_writes=[R["HS"]])
        while pend_ones:
            flush_ones()
        P.op("dve", lambda e: e.tensor_scalar(out=rn, in0=PS[:, 4 * 512:6 * 512], scalar1=1e-6, scalar2=None, op0=ALU.add),
             reads=[psr[4], psr[5]], writes=[rn_r])
        P.op("dve", lambda e: e.reciprocal(out=rn, in_=rn), reads=[rn_r], writes=[rn_r])
        P.release(m)
        return rn, rn_r

    def phase_hy_conv(j, G, s, rn, rn_r):
        L = G["L"]
        nm = G["name"]
        ntc = L // 128
        nfc = 33 if nm == "s" else 3
        SQ = nfc * 128
        Qt, Rt, WFd = I["q" + nm], I["r" + nm], I["wf" + nm]
        HS, HD = S["HS" + nm], S["HD" + nm]
        tok0 = G["tok0"] + s * L
        P.new_phase()
        m = P.mark()
        wf, wf_r = P.alloc([128, nfc], F32, "wf")
        P.dma("sp", "wf", lambda e: e.dma_start(out=wf, in_=WFd), writes=[wf_r])
        vvhs, vvhs_r = P.alloc([128, ntc, 512], BF16, "vvhs")
        vvhd, vvhd_r = P.alloc([128, ntc, 512], BF16, "vvhd")
        qch = [P.alloc([128, ntc, 128], BF16, f"qch{k}") for k in range(2)]
        rch = [P.alloc([128, ntc, 128], BF16, f"rch{k}") for k in range(2)]
        kcs = [P.alloc([128, 256], F32, f"kcs{k}") for k in range(2)]
        kss = [P.alloc([128, 256], F32, f"kss{k}") for k in range(2)]
        ta = [P.alloc([128, 256], F32, f"ta{k}") for k in range(2)]
        tb = [P.alloc([128, 256], F32, f"tb{k}") for k in range(2)]
        yst = [P.alloc([128, 2, 256], BF16, f"yst{k}") for k in range(2)]
        def load_tab(k):
            fc_ = k % nfc
            qc_, qc_r_ = qch[k % 2]
            rc_, rc_r_ = rch[k % 2]
            if nm == "s":
                qsrc, rsrc = I["qfs"][fc_], I["rfs"][fc_]
            else:
                qsrc = Qt[0:ntc, :, fc_ * 128:(fc_ + 1) * 128].rearrange("c p f -> p c f")
                rsrc = Rt[0:ntc, :, fc_ * 128:(fc_ + 1) * 128].rearrange("c p f -> p c f")
            P.dma("sp", f"qch{k % 2}", lambda e: e.dma_start(out=qc_, in_=qsrc), writes=[qc_r_])
            P.dma("sp", f"rch{k % 2}", lambda e: e.dma_start(out=rc_, in_=rsrc), writes=[rc_r_])
        it = 0
        for dq in range(4):
            dh = dq // 2
            vsrc = S["VVTOK"][tok0:tok0 + L, dq * 256:(dq + 1) * 256].rearrange("(c p) d -> p c d", p=128)
            P.dma("sp", "vva", lambda e, vsrc=vsrc: e.dma_start(out=vvhs[:, :, 0:256], in_=vsrc), reads=[R["VVTOK"]], writes=[vvhs_r])
            P.dma("sp", "vvb", lambda e, vsrc=vsrc: e.dma_start(out=vvhd[:, :, 0:256], in_=vsrc), reads=[R["VVTOK"]], writes=[vvhd_r])
            P.dma("sp", "hsh", lambda e, dq=dq: e.dma_start(
                out=vvhs[:, :, 256:512], in_=HS[:, dq * 256:(dq + 1) * 256].rearrange("(c p) d -> p c d", p=128)), reads=[R["HS"]], acc_writes=[vvhs_r])
            P.dma("sp", "hdh", lambda e, dq=dq: e.dma_start(
                out=vvhd[:, :, 256:512], in_=HD[:, dq * 256:(dq + 1) * 256].rearrange("(c p) d -> p c d", p=128)), reads=[R["HS"]], acc_writes=[vvhd_r])
            for fc in range(nfc):
                qc, qc_r = qch[it % 2]
                rc, rc_r = rch[it % 2]
                if it == 0:
                    load_tab(0)
                if it + 1 < 4 * nfc:
                    load_tab(it + 1)
                b0 = (it % 4) * 2
                bC, bS = b0, b0 + 1
                for tc in range(ntc):
                    st, sp_ = (tc == 0), (tc == ntc - 1)
                    for (bb, tab, tab_r, mov, mov_r) in ((bC, qc, qc_r, vvhs, vvhs_r), (bS, rc, rc_r, vvhd, vvhd_r)):
                        P.op("pe", lambda e, bb=bb, tab=tab, mov=mov, tc=tc, st=st, sp_=sp_: e.matmul(
                            bank(bb), lhsT=tab[:, tc, :], rhs=mov[:, tc, :], start=st, stop=sp_),
                            reads=[tab_r, mov_r], writes=[psr[bb]] if st else (), acc_writes=() if st else [psr[bb]])
                bVc = bKc = bC
                bVs = bKs = bS
                Vc_, Kc_ = bank(bC)[:, 0:256], bank(bC)[:, 256:512]
                Vs_, Ks_ = bank(bS)[:, 0:256], bank(bS)[:, 256:512]
                k = it % 2
                wcol = wf[:, fc:fc + 1]
                rsl = rn[:, dq * 256:(dq + 1) * 256]
                P.op("dve", lambda e, k=k, Kc_=Kc_, wcol=wcol, rsl=rsl: e.scalar_tensor_tensor(
                    out=kcs[k][0], in0=Kc_, scalar=wcol, in1=rsl, op0=ALU.mult, op1=ALU.mult),
                    reads=[psr[bKc], wf_r, rn_r], writes=[kcs[k][1]])
                P.op("dve", lambda e, k=k, Ks_=Ks_, wcol=wcol, rsl=rsl: e.scalar_tensor_tensor(
                    out=kss[k][0], in0=Ks_, scalar=wcol, in1=rsl, op0=ALU.mult, op1=ALU.mult),
                    reads=[psr[bKs], wf_r, rn_r], writes=[kss[k][1]])
                P.op("dve", lambda e, k=k, Vc_=Vc_: e.tensor_tensor(out=ta[k][0], in0=Vc_, in1=kcs[k][0], op=ALU.mult),
                     reads=[psr[bVc], kcs[k][1]], writes=[ta[k][1]])
                P.op("dve", lambda e, k=k, Vs_=Vs_: e.tensor_tensor(out=tb[k][0], in0=Vs_, in1=kss[k][0], op=ALU.mult),
                     reads=[psr[bVs], kss[k][1]], writes=[tb[k][1]])
                P.op("pool", lambda e, k=k: e.tensor_tensor(out=yst[k][0][:, 0, :], in0=ta[k][0], in1=tb[k][0], op=ALU.add),
                     reads=[ta[k][1], tb[k][1]], writes=[yst[k][1]])
                P.op("dve", lambda e, k=k, Vs_=Vs_: e.tensor_tensor(out=ta[k][0], in0=Vs_, in1=kcs[k][0], op=ALU.mult),
                     reads=[psr[bVs], kcs[k][1]], writes=[ta[k][1]])
                P.op("dve", lambda e, k=k, Vc_=Vc_: e.tensor_tensor(out=tb[k][0], in0=Vc_, in1=kss[k][0], op=ALU.mult),
                     reads=[psr[bVc], kss[k][1]], writes=[tb[k][1]])
                P.op("pool", lambda e, k=k: e.tensor_tensor(out=yst[k][0][:, 1, :], in0=ta[k][0], in1=tb[k][0], op=ALU.subtract),
                     reads=[ta[k][1], tb[k][1]], acc_writes=[yst[k][1]])
                P.dma("sp", f"yfo{k}", lambda e, k=k, dh=dh, dq=dq, fc=fc: e.dma_start(out=S["YF"][dh, fc][:, :, (dq % 2) * 256:(dq % 2 + 1) * 256], in_=yst[k][0]),
                      reads=[yst[k][1]], acc_writes=[R["YF"]])
                it += 1
        P.release(m)
        P.new_phase()
        m = P.mark()
        skc, skc_r = P.alloc([128, 8], F32, "skc")
        load_cols(skc, skc_r, 0, I["hy_skip"][j], 8, "skc")
        Yg, Yg_r = P.alloc([128, nfc, 2, 512], BF16, "Yg")
        n = min(512, L)
        tq = [P.alloc([128, 2, n], BF16, f"tq{k}") for k in range(6)]
        vvt = [P.alloc([128, n], F32, f"vvt{k}") for k in range(3)]
        x0t = [P.alloc([128, n], F32, f"x0t{k}") for k in range(3)]
        it = 0
        ie = 0
        for dh in range(2):
            P.dma("sp", "Yg", lambda e, dh=dh: e.dma_start(out=Yg, in_=S["YF"][dh, 0:nfc].rearrange("c p s d -> p c s d")),
                  reads=[R["YF"]], writes=[Yg_r])
            for tt in range(L // n):
                b0 = ((dh * (L // n) + tt) % 2) * 4
                for fc in range(nfc):
                    t_, t_r = tq[it % 6]
                    P.dma("sp", f"tq{it % 6}a", lambda e, t_=t_, fc=fc, tt=tt: e.dma_start(out=t_[:, 0, :], in_=Qt[fc, :, tt * n:(tt + 1) * n]), writes=[t_r])
                    P.dma("sp", f"tq{it % 6}b", lambda e, t_=t_, fc=fc, tt=tt: e.dma_start(out=t_[:, 1, :], in_=Rt[fc, :, tt * n:(tt + 1) * n]), acc_writes=[t_r])
                    it += 1
                    for dcl in range(4):
                        for cs in range(2):
                            st = (fc == 0 and cs == 0)
                            sp_ = (fc == nfc - 1 and cs == 1)
                            P.op("pe", lambda e, t_=t_, fc=fc, dcl=dcl, cs=cs, st=st, sp_=sp_, b0=b0: e.matmul(
                                bank(b0 + dcl, n), lhsT=Yg[:, fc, cs, dcl * 128:(dcl + 1) * 128], rhs=t_[:, cs, :], start=st, stop=sp_),
                                reads=[Yg_r, t_r], writes=[psr[b0 + dcl]] if st else (), acc_writes=() if st else [psr[b0 + dcl]])
                for dcl in range(4):
                    dc = dh * 4 + dcl
                    v_, v_r = vvt[ie % 3]
                    x_, x_r = x0t[ie % 3]
                    ie += 1
                    c0 = tok0 + tt * n
                    P.dma("sp", f"vvt{ie % 3}", lambda e, v_=v_, dc=dc, c0=c0: e.dma_start(out=v_, in_=S["VVT"][dc, :, c0:c0 + n]), reads=[R["VVT"]], writes=[v_r])
                    P.dma("sp", f"x0t{ie % 3}", lambda e, x_=x_, dc=dc, c0=c0: e.dma_start(out=x_, in_=S["X0T"][dc, :, c0:c0 + n]), reads=[R["X0T"]], writes=[x_r])
                    P.op("dve", lambda e, v_=v_, dc=dc, dcl=dcl, b0=b0: e.scalar_tensor_tensor(
                        out=v_, in0=v_, scalar=skc[:, dc:dc + 1], in1=bank(b0 + dcl, n), op0=ALU.mult, op1=ALU.add),
                        reads=[v_r, skc_r, psr[b0 + dcl]], writes=[v_r])
                    P.op("pool", lambda e, v_=v_, x_=x_, dc=dc, c0=c0: e.tensor_tensor(out=hT[:, dc, c0:c0 + n], in0=v_, in1=x_, op=ALU.mult),
                         reads=[v_r, x_r], acc_writes=[hTr[c0 // 512]])
        P.release(m)

    class Stop(Exception):
        pass

    def chk(tag):
        if stop_after == tag:
            raise Stop()

    def dump_hT():
        for t in range(NT):
            P.dma("sp", "htd", lambda e, t=t: e.dma_start(out=S["HTD"][:, :, t * 512:(t + 1) * 512], in_=hT[:, :, t * 512:(t + 1) * 512]),
                  reads=[hTr[t]], acc_writes=[R["HTD"]])

    try:
        for i in range(4):
            j = i // 2
            phase_mod(i)
            chk(f"mod{i}")
            phase_norm(D, 0)
            chk(f"norm1_{i}")
            if i % 2 == 0:
                phase_qkv(j)
                chk(f"qkv{i}")
                phase_attn(j, i)
                chk(f"attn{i}")
                phase_outproj(I["attn_w_o"][j], None, 2 * D)
            else:
                phase_hy_in(j)
                chk(f"hyin{i}")
                for G in GROUPS:
                    mk = P.mark()
                    rn, rn_r = phase_hy_filter(j, G)
                    chk(f"hyfilt{i}{G['name']}")
                    for s in range(G["nseq"]):
                        phase_hy_conv(j, G, s, rn, rn_r)
                    P.release(mk)
                chk(f"hyconv{i}")
                phase_outproj(I["hy_w_out"][j], I["hy_b_out"][j], 2 * D)
            chk(f"mix{i}")
            phase_norm(4 * D, 3 * D)
            phase_ffn(i)
            chk(f"ffn{i}")
        phase_norm(0, 0, final=True)
    except Stop:
        if "HTD" in debug_outs:
            dump_hT()

    final_keys = [k for k in P.dma_cnt]
    print("n dma sems", len(final_keys), {e: len(P.q[e]) for e in ENGS})
    P.emit(final_keys)
    return nc, P


_CONSTS = None


def _core_inputs(b, inp, consts):
    m = {}
    m["x"] = np.ascontiguousarray(np.concatenate(
        [inp["x_sample"][b], inp["x_prompt"][2 * b], inp["x_prompt"][2 * b + 1]], axis=0).astype(np.float32))
    m["ck"] = np.ascontiguousarray(inp["cache_k"][b].reshape(2, 512, D).astype(np.float32))
    m["cv"] = np.ascontiguousarray(inp["cache_v"][b].reshape(2, 512, D).astype(np.float32))
    m["cvec"] = np.ascontiguousarray(np.stack([inp["c"][b], inp["c_ctx"]], axis=0).astype(np.float32))
    for nm, shp in W_SPECS:
        m[nm] = np.ascontiguousarray(np.asarray(inp[nm], dtype=np.float32).reshape(shp))
    for nm, shp, dt in CONST_SPECS:
        m[nm] = consts[nm]
    return m


def kernel(**inputs):
    global _CONSTS
    if _CONSTS is None:
        _CONSTS = _host_consts()
    inp = {k: np.asarray(v) for k, v in inputs.items()}
    nc, _ = build_program()
    in_maps = [_core_inputs(b, inp, _CONSTS) for b in range(8)]
    res = run_bass_kernel_spmd(nc, in_maps, core_ids=list(range(8)))
    y_prompt = np.zeros((16, 256, D), np.float32)
    y_sample = np.zeros((8, TS, D), np.float32)
    nk = np.zeros((16, 2, 256, 8, 2, 64), np.float32)
    nv = np.zeros((16, 2, 256, 8, 128), np.float32)
    for b in range(8):
        r = res.results[b]
        y = np.asarray(r["y"], dtype=np.float32)
        y_sample[b] = y[:TS]
        y_prompt[2 * b] = y[TS:TS + 256]
        y_prompt[2 * b + 1] = y[TS + 256:]
        k_ = np.asarray(r["nk"], dtype=np.float32).reshape(2, 2, 256, 8, 2, 64)
        v_ = np.asarray(r["nv"], dtype=np.float32).reshape(2, 2, 256, 8, 128)
        nk[2 * b], nk[2 * b + 1] = k_[0], k_[1]
        nv[2 * b], nv[2 * b + 1] = v_[0], v_[1]
    return (y_prompt, y_sample, nk, nv)
```

```python
import contextlib
import os
import math
import numpy as np
import ml_dtypes
import concourse.bass as bass
import concourse.mybir as mybir
from concourse.bass_utils import run_bass_kernel_spmd

F32 = mybir.dt.float32
BF16 = mybir.dt.bfloat16
AF = mybir.ActivationFunctionType
ALU = mybir.AluOpType
AX = mybir.AxisListType

ENGS = ("pe", "act", "dve", "pool", "sp")
D = 1024
TS, TP, T = 4096, 512, 4608
NT = 9
NTT = 36
DFF = 2816
NCH_FF = 22
EPS = 1e-6
SUBLN_EPS = 1e-5
NKEY = 5120


class Res:
    __slots__ = ("w", "r", "name", "excl")

    def __init__(self, name="", excl=False):
        self.w = {}
        self.r = {}
        self.name = name
        self.excl = excl


class Prog:
    def __init__(self, nc, arena_words):
        self.nc = nc
        self.q = {e: [] for e in ENGS}
        self.known = {e: {} for e in ENGS}
        self.dma_cnt = {}
        self.stack = contextlib.ExitStack()
        self.arena = self.stack.enter_context(nc.sbuf_tensor("arena", [128, arena_words], F32))
        self.arena_words = arena_words
        self.top = 0
        self.live = []
        self.retired = []
        self.nsem = 0
        self.keymap = {}
        self.keyres = {}

    def alloc(self, shape, dt, name=""):
        esz = 4 if dt == F32 else 2
        free = 1
        for s in shape[1:]:
            free *= s
        words = (free * esz + 3) // 4
        words = (words + 7) // 8 * 8
        off = self.top
        assert off + words <= self.arena_words, f"SBUF arena overflow {name} {off + words}"
        self.top += words
        v = self.arena[0:shape[0], off:off + (free * esz) // 4]
        if dt != F32:
            v = v.bitcast(dt)
        if len(shape) == 3:
            v = v.rearrange("p (a b) -> p a b", b=shape[2])
        elif len(shape) == 4:
            v = v.rearrange("p (a b c) -> p a b c", b=shape[2], c=shape[3])
        r = Res(name)
        keep = []
        for (a, b, rr) in self.retired:
            if a < off + words and off < b:
                for k, val in rr.w.items():
                    if r.r.get(k, -1) < val:
                        r.r[k] = val
                for k, val in rr.r.items():
                    if r.r.get(k, -1) < val:
                        r.r[k] = val
                if a >= off and b <= off + words:
                    continue
            keep.append((a, b, rr))
        self.retired = keep
        self.live.append((off, off + words, r))
        return v, r

    def mark(self):
        return (self.top, len(self.live))

    def release(self, m):
        top, n = m
        self.retired.extend(self.live[n:])
        del self.live[n:]
        self.top = top

    def _add(self, eng, fn, reads, writes, acc_writes, own):
        deps = {}

        def upd(d):
            for k, v in d.items():
                if deps.get(k, -1) < v:
                    deps[k] = v
        for r in reads:
            upd(r.w)
            if r.excl:
                upd({k: v for k, v in r.r.items() if k != ("c", eng)})
        for w in writes:
            upd(w.w)
            upd(w.r)
        for w in acc_writes:
            upd(w.r)
            upd({k: v for k, v in w.w.items() if k != own})
        q = self.q[eng]
        idx = len(q)
        waits = []
        kn = self.known[eng]
        for k, v in deps.items():
            if k == ("c", eng):
                if eng == "pe":
                    continue
                vv = -1
                for r in reads:
                    x = r.w.get(k, -1)
                    if x > vv:
                        vv = x
                if vv < 0:
                    continue
                v = vv
            if k[0] == "d":
                v = self.dma_cnt[k[1]]
            if kn.get(k, -1) >= v:
                continue
            kn[k] = v
            waits.append((k, v))
            if k[0] == "c":
                self.q[k[1]][v][2] = True
        op = [fn, waits, False, None]
        q.append(op)
        return op, idx

    def op(self, eng, fn, reads=(), writes=(), acc_writes=()):
        op, idx = self._add(eng, fn, reads, writes, acc_writes, ("c", eng))
        k = ("c", eng)
        for r in reads:
            r.r[k] = idx
        for w in writes:
            w.w = {k: idx}
            w.r = {}
        for w in acc_writes:
            w.w[k] = idx
        return op

    def new_phase(self):
        self.keymap = {}

    def dma(self, eng, semkey, fn, reads=(), writes=(), acc_writes=()):
        km = self.keymap
        if semkey not in km:
            km[semkey] = f"g{len(km)}"
        semkey = km[semkey]
        kr = self.keyres.get(semkey)
        if kr is None:
            kr = self.keyres[semkey] = Res(semkey)
        writes = list(writes) + [kr]
        op, idx = self._add(eng, fn, reads, writes, acc_writes, ("d", semkey))
        c = self.dma_cnt.get(semkey, 0) + 1
        self.dma_cnt[semkey] = c
        op[3] = semkey
        k = ("d", semkey)
        for r in reads:
            r.r[k] = c
        for w in writes:
            w.w = {k: c}
            w.r = {}
        for w in acc_writes:
            w.w[k] = c
        return op

    def emit(self, final_keys):
        nc = self.nc
        st = self.stack
        csem = {e: st.enter_context(nc.semaphore(f"c_{e}")) for e in ENGS if e != "sp"}
        dsem = {k: st.enter_context(nc.semaphore(f"d_{i}")) for i, k in enumerate(self.dma_cnt)}
        cum = {}
        for e in ENGS:
            c = 0
            arr = []
            for o in self.q[e]:
                if o[2]:
                    c += 1
                arr.append(c)
            cum[e] = arr
        engobj = {"pe": "tensor", "act": "scalar", "dve": "vector", "pool": "gpsimd", "sp": "sync"}
        with nc.Block() as block:
            for e in ENGS:
                ops = self.q[e]

                def body(eng, e=e, ops=ops):
                    for fn, waits, sig, semkey in ops:
                        for k, v in waits:
                            if k[0] == "c":
                                eng.wait_ge(csem[k[1]], cum[k[1]][v])
                            else:
                                eng.wait_ge(dsem[k[1]], 16 * v)
                        ins = fn(eng)
                        if semkey is not None:
                            ins.then_inc(dsem[semkey], 16)
                        elif sig:
                            ins.then_inc(csem[e], 1)
                    if e == "sp":
                        for k in final_keys:
                            eng.wait_ge(dsem[k], 16 * self.dma_cnt[k])
                getattr(block, engobj[e])(body)


def _bf(a):
    return np.ascontiguousarray(a.astype(ml_dtypes.bfloat16))


def _dft_tables(L):
    N = 2 * L
    nf = L + 1
    nch = (nf + 127) // 128
    S = nch * 128
    a = np.arange(S, dtype=np.int64)
    m = (a[:, None] * a[None, :]) % N
    ang = 2.0 * np.pi * m.astype(np.float64) / N
    valid = (a[:, None] <= L) & (a[None, :] <= L)
    q = np.where(valid, np.cos(ang), 0.0)
    r = np.where(valid, np.sin(ang), 0.0)
    wf = np.where(a <= L, 2.0 / N, 0.0)
    wf[0] = 1.0 / N
    wf[L] = 1.0 / N
    wfc = wf.reshape(nch, 128).T.astype(np.float32)
    ntc = L // 128
    qf = q[:L].reshape(ntc, 128, nch, 128).transpose(2, 1, 0, 3)
    rf = r[:L].reshape(ntc, 128, nch, 128).transpose(2, 1, 0, 3)
    return (_bf(q.reshape(nch, 128, S)), _bf(r.reshape(nch, 128, S)), np.ascontiguousarray(wfc), nch, S, _bf(qf), _bf(rf))


def _filter_consts(L):
    pos = np.arange(L, dtype=np.float32)
    t = pos / np.float32(max(L - 1, 1))
    w = (np.float32(2.0 * math.pi) * pos / np.float32(L)).astype(np.float32)
    bands = np.linspace(1e-4, 15, 16, dtype=np.float32)
    z = np.concatenate([t[:, None], np.cos(w[:, None] * bands), -np.sin(w[:, None] * bands)], axis=-1)
    zT = np.ascontiguousarray(z.T.astype(np.float32))
    negt = np.ascontiguousarray((-t).reshape(L // 128, 128).T.astype(np.float32))
    return zT, negt


def _host_consts():
    c = {}
    tpos = np.arange(TS)
    rowpos = (tpos // 64).astype(np.float32)
    colpos = (tpos % 64).astype(np.float32)
    inv = (10000.0 ** (-np.arange(16, dtype=np.float32) / 16)).astype(np.float32)
    cos = np.zeros((128, TS), np.float32)
    sins = np.zeros((128, TS), np.float32)
    perm = np.zeros((128, 128), np.float32)
    for p in range(2):
        for a in range(2):
            posv = rowpos if a == 0 else colpos
            for hf in range(2):
                for f in range(16):
                    row = p * 64 + a * 32 + hf * 16 + f
                    ang = (posv * inv[f]).astype(np.float32)
                    cos[row] = np.cos(ang)
                    sins[row] = np.sin(ang) * (-1.0 if hf == 0 else 1.0)
                    other = p * 64 + a * 32 + (1 - hf) * 16 + f
                    perm[other, row] = 1.0
    c["rcos"] = cos
    c["rsin"] = sins
    c["perm"] = _bf(perm)
    c["identb"] = _bf(np.eye(128, dtype=np.float32))
    c["identf"] = np.eye(128, dtype=np.float32)
    c["onesf"] = np.ones((128, 128), np.float32)
    for nm, L in (("s", TS), ("p", 256)):
        q, r, wf, nch, S, qf, rf = _dft_tables(L)
        c["q" + nm], c["r" + nm], c["wf" + nm] = q, r, wf
        if nm == "s":
            c["qfs"], c["rfs"] = qf, rf
        zT, negt = _filter_consts(L)
        c["z" + nm], c["negt" + nm] = zT, negt
    deltas = np.abs(np.linspace(math.log(1e-2) / 1.5, math.log(1e-2) / 0.3, D, dtype=np.float32))
    c["delta"] = deltas.astype(np.float32)
    m0 = np.ones((128, 1), np.float32)
    m0[0, 0] = 0.0
    c["mask0"] = m0
    return c


W_SPECS = [
    ("ada_w", [4, D, 6 * D]), ("ada_b", [4, 6 * D]), ("norm1_g", [4, D]), ("norm2_g", [4, D]),
    ("attn_w_qkv", [2, D, 3 * D]), ("attn_lambda", [2, 4, 64]), ("attn_subln_g", [2, 128]),
    ("attn_w_o", [2, D, D]), ("hy_w_in", [2, D, 3 * D]), ("hy_b_in", [2, 3 * D]),
    ("hy_conv_w", [2, 3, 3 * D]), ("hy_conv_b", [2, 3 * D]), ("filt_w1", [2, 33, 64]),
    ("filt_b1", [2, 64]), ("filt_w2", [2, 64, 64]), ("filt_b2", [2, 64]), ("filt_w3", [2, 64, 2 * D]),
    ("filt_b3", [2, 2 * D]), ("filt_freq", [2, 64]), ("hy_skip", [2, D]), ("hy_w_out", [2, D, D]),
    ("hy_b_out", [2, D]), ("ffn_w_gu", [4, D, 2 * DFF]), ("ffn_w_down", [4, DFF, D]), ("final_g", [D]),
]
CONST_SPECS = [
    ("rcos", [128, TS], F32), ("rsin", [128, TS], F32), ("perm", [128, 128], BF16),
    ("identb", [128, 128], BF16), ("identf", [128, 128], F32), ("onesf", [128, 128], F32),
    ("qs", [33, 128, 4224], BF16), ("rs", [33, 128, 4224], BF16), ("wfs", [128, 33], F32),
    ("qfs", [33, 128, 32, 128], BF16), ("rfs", [33, 128, 32, 128], BF16),
    ("zs", [33, TS], F32), ("negts", [128, 32], F32),
    ("qp", [3, 128, 384], BF16), ("rp", [3, 128, 384], BF16), ("wfp", [128, 3], F32),
    ("zp", [33, 256], F32), ("negtp", [128, 2], F32),
    ("delta", [D], F32), ("mask0", [128, 1], F32),
]

GROUPS = [
    dict(name="s", tok0=0, T=TS, nseq=1, L=TS, rope=True, ncache=512, tiles=list(range(0, 8)), tt=list(range(0, 32))),
    dict(name="p", tok0=TS, T=TP, nseq=2, L=256, rope=False, ncache=0, tiles=[8], tt=list(range(32, 36))),
]


def build_program(stop_after=None, debug_outs=()):
    nc = bass.Bass("TRN2", target_bir_lowering=False)
    I = {}
    I["x"] = nc.dram_tensor("x", [T, D], F32, kind="ExternalInput").ap()
    I["ck"] = nc.dram_tensor("ck", [2, 512, D], F32, kind="ExternalInput").ap()
    I["cv"] = nc.dram_tensor("cv", [2, 512, D], F32, kind="ExternalInput").ap()
    I["cvec"] = nc.dram_tensor("cvec", [2, D], F32, kind="ExternalInput").ap()
    for nm, shp in W_SPECS:
        I[nm] = nc.dram_tensor(nm, shp, F32, kind="ExternalInput").ap()
    for nm, shp, dt in CONST_SPECS:
        I[nm] = nc.dram_tensor(nm, shp, dt, kind="ExternalInput").ap()
    O = {}
    O["y"] = nc.dram_tensor("y", [T, D], F32, kind="ExternalOutput").ap()
    O["nk"] = nc.dram_tensor("nk", [2, 2, 256, D], F32, kind="ExternalOutput").ap()
    O["nv"] = nc.dram_tensor("nv", [2, 2, 256, D], F32, kind="ExternalOutput").ap()

    def scratch(nm, shp, dt):
        kind = "ExternalOutput" if nm in debug_outs else "Internal"
        return nc.dram_tensor(nm, shp, dt, kind=kind).ap()
    S = {}
    S["X"] = scratch("X", [T, D], F32)
    S["MODS"] = scratch("MODS", [2, 128, 6 * D], F32)
    S["KT"] = scratch("KT", [8, 128, NKEY], BF16)
    S["VS"] = scratch("VS", [NKEY, D], BF16)
    S["QT"] = scratch("QT", [8, 128, T], BF16)
    S["MID"] = scratch("MID", [NCH_FF, 128, T], BF16)
    S["VVT"] = scratch("VVT", [8, 128, T], F32)
    S["X0T"] = scratch("X0T", [8, 128, T], F32)
    S["VVTOK"] = scratch("VVTOK", [T, D], BF16)
    S["HSs"] = scratch("HSs", [TS, D], BF16)
    S["HDs"] = scratch("HDs", [TS, D], BF16)
    S["HSp"] = scratch("HSp", [256, D], BF16)
    S["HDp"] = scratch("HDp", [256, D], BF16)
    S["YF"] = scratch("YF", [2, 33, 128, 2, 512], BF16)
    S["HTD"] = scratch("HTD", [128, 8, T], BF16)

    ARENA_WORDS = 49152
    P = Prog(nc, ARENA_WORDS)
    PS = P.stack.enter_context(nc.psum_tensor("ps", [128, 4096], F32))
    psr = [Res(f"bank{b}", excl=True) for b in range(8)]

    def bank(b, n=512):
        return PS[:, b * 512:b * 512 + n]

    def bankb(b):
        return PS[:, b * 512:(b + 1) * 512].bitcast(BF16)

    R = {}
    R["X"] = [[Res(f"X{i}a"), Res(f"X{i}b")] for i in range(NTT)]
    R["MODS"] = [Res(), Res()]
    R["KT"] = Res()
    R["VS"] = Res()
    R["QT"] = Res()
    R["MID"] = [Res() for _ in range(NT)]
    R["VVT"] = Res()
    R["X0T"] = Res()
    R["VVTOK"] = Res()
    R["HS"] = Res()
    R["YF"] = Res()
    R["OUT"] = Res()
    R["HTD"] = Res()
    semctr = [0]

    def sk(prefix):
        semctr[0] += 1
        return f"{prefix}{semctr[0]}"

    hT, _ = P.alloc([128, 8, T], BF16, "hT")
    hTr = [Res(f"hT{i}") for i in range(NT)]
    identb, identb_r = P.alloc([128, 128], BF16, "identb")
    identf, identf_r = P.alloc([128, 128], F32, "identf")
    onesf, onesf_r = P.alloc([128, 128], F32, "onesf")
    SIL, SIL_r = P.alloc([128, 2, 8, 128], BF16, "SIL")
    P.dma("sp", "c_identb", lambda e: e.dma_start(out=identb, in_=I["identb"]), writes=[identb_r])
    P.dma("sp", "c_identf", lambda e: e.dma_start(out=identf, in_=I["identf"]), writes=[identf_r])
    P.dma("sp", "c_onesf", lambda e: e.dma_start(out=onesf, in_=I["onesf"]), writes=[onesf_r])

    def load_cols(dst, dst_r, col0, vec, nchunks, key):
        P.dma("sp", key, lambda e: e.dma_start(
            out=dst[:, col0:col0 + nchunks], in_=vec.rearrange("(c p) -> p c", p=128), allow_slow_non_contiguous=True),
            acc_writes=[dst_r])

    def bcast_rows(vec1d, n):
        return vec1d.rearrange("(o n) -> o n", o=1).broadcast_to([128, n])

    m0 = P.mark()
    ccol, ccol_r = P.alloc([128, 16], F32, "ccol")
    scol, scol_r = P.alloc([128, 16], F32, "scol")
    for g in range(2):
        load_cols(ccol, ccol_r, g * 8, I["cvec"][g], 8, "ccol")
    P.op("act", lambda e: e.activation(out=scol, in_=ccol, func=AF.Silu), reads=[ccol_r], writes=[scol_r])
    for g in range(2):
        for kc in range(8):
            P.op("dve", lambda e, g=g, kc=kc: e.tensor_scalar(
                out=SIL[:, g, kc, :], in0=onesf, scalar1=scol[:, g * 8 + kc:g * 8 + kc + 1], scalar2=None,
                op0=ALU.mult), reads=[scol_r, onesf_r], acc_writes=[SIL_r])
    P.release(m0)

    for tt in range(NTT):
        P.dma("sp", f"xcopy{tt % 4}", lambda e, tt=tt: e.dma_start(
            out=S["X"][tt * 128:(tt + 1) * 128, :], in_=I["x"][tt * 128:(tt + 1) * 128, :]), writes=R["X"][tt])

    def phase_mod(i):
        P.new_phase()
        m = P.mark()
        adab, adab_r = P.alloc([128, 6 * D], F32, "adab")
        mod = [P.alloc([128, 6 * D], F32, f"mod{g}") for g in range(2)]
        gn = [P.alloc([128, D], F32, f"gn{k}") for k in range(2)]
        wch = [P.alloc([128, 8, 512], BF16, f"adw{k}") for k in range(2)]
        P.dma("sp", "adab", lambda e: e.dma_start(out=adab, in_=bcast_rows(I["ada_b"][i], 6 * D)), writes=[adab_r])
        P.dma("sp", "gn0", lambda e: e.dma_start(out=gn[0][0], in_=bcast_rows(I["norm1_g"][i], D)), writes=[gn[0][1]])
        P.dma("sp", "gn1", lambda e: e.dma_start(out=gn[1][0], in_=bcast_rows(I["norm2_g"][i], D)), writes=[gn[1][1]])
        for n in range(12):
            w, wr = wch[n % 2]
            P.dma("pool", f"adw{n % 2}", lambda e, w=w, n=n: e.dma_start(
                out=w, in_=I["ada_w"][i][:, n * 512:(n + 1) * 512].rearrange("(kc p) n -> p kc n", p=128)), writes=[wr])
            for g in range(2):
                b = (n * 2 + g) % 8
                for kc in range(8):
                    P.op("pe", lambda e, w=w, g=g, kc=kc, b=b: e.matmul(
                        bank(b), lhsT=SIL[:, g, kc, :], rhs=w[:, kc, :], start=(kc == 0), stop=(kc == 7)),
                        reads=[SIL_r, wr], writes=[psr[b]] if kc == 0 else (), acc_writes=[psr[b]] if kc else ())
                P.op("dve", lambda e, g=g, n=n, b=b: e.tensor_tensor(
                    out=mod[g][0][:, n * 512:(n + 1) * 512], in0=bank(b), in1=adab[:, n * 512:(n + 1) * 512], op=ALU.add),
                    reads=[psr[b], adab_r], acc_writes=[mod[g][1]])
        for g in range(2):
            for k, off in ((0, D), (1, 4 * D)):
                P.op("dve", lambda e, g=g, k=k, off=off: e.scalar_tensor_tensor(
                    out=mod[g][0][:, off:off + D], in0=mod[g][0][:, off:off + D], scalar=1.0, in1=gn[k][0],
                    op0=ALU.add, op1=ALU.mult), reads=[mod[g][1], gn[k][1]], acc_writes=[mod[g][1]])
            P.dma("sp", f"mods{g}", lambda e, g=g: e.dma_start(out=S["MODS"][g], in_=mod[g][0]),
                  reads=[mod[g][1]], writes=[R["MODS"][g]])
        P.release(m)

    def load_mod(off, key):
        out = []
        for g in range(2):
            t, r = P.alloc([128, D], F32, f"{key}{g}")
            P.dma("sp", f"ldm_{key}{g}", lambda e, t=t, g=g: e.dma_start(out=t, in_=S["MODS"][g][:, off:off + D]),
                  reads=[R["MODS"][g]], writes=[r])
            out.append((t, r))
        return out

    def rstd_ops(ss, ss_r, rs, rs_r, n, eps):
        P.op("act", lambda e: e.activation(out=rs, in_=ss, func=AF.Ln, scale=1.0 / n, bias=float(eps)),
             reads=[ss_r], writes=[rs_r])
        P.op("act", lambda e: e.activation(out=rs, in_=rs, func=AF.Exp, scale=-0.5),
             reads=[rs_r], writes=[rs_r])

    def phase_norm(offA, offB, final=False):
        P.new_phase()
        m = P.mark()
        if final:
            gt, gr = P.alloc([128, D], F32, "fing")
            P.dma("sp", "fing", lambda e: e.dma_start(out=gt, in_=bcast_rows(I["final_g"], D)), writes=[gr])
            A = [(gt, gr), (gt, gr)]
            B = None
        else:
            A = load_mod(offA, "nA")
            B = load_mod(offB, "nB")
        xt = [P.alloc([128, D], F32, f"nx{k}") for k in range(3)]
        x2 = [P.alloc([128, D], F32, f"nx2{k}") for k in range(2)]
        hb = [P.alloc([128, D], BF16, f"nhb{k}") for k in range(2)]
        junk, junk_r = P.alloc([128, D], BF16, "njunk")
        ss = [P.alloc([128, 1], F32, f"nss{k}") for k in range(2)]
        rs = [P.alloc([128, 1], F32, f"nrs{k}") for k in range(2)]
        pend_copy = []

        def flush_copy():
            b, t_ = pend_copy.pop(0)
            P.op("act", lambda e: e.copy(out=hT[:, :, t_ * 128:(t_ + 1) * 128],
                                         in_=bankb(b).rearrange("p (k t) -> p k t", t=128)),
                 reads=[psr[b]], acc_writes=[hTr[t_ // 4]])
        for tt in range(NTT):
            g = 0 if tt < 32 else 1
            x, xr = xt[tt % 3]
            y, yr = x2[tt % 2]
            h, hr = hb[tt % 2]
            s_, s_r = ss[tt % 2]
            r_, r_r = rs[tt % 2]
            P.dma("sp", f"nx{tt % 3}", lambda e, x=x, tt=tt: e.dma_start(out=x, in_=S["X"][tt * 128:(tt + 1) * 128, :]),
                  reads=R["X"][tt], writes=[xr])
            P.op("act", lambda e, x=x, s_=s_: e.activation(out=junk, in_=x, func=AF.Square, accum_out=s_),
                 reads=[xr], writes=[junk_r, s_r])
            rstd_ops(s_, s_r, r_, r_r, D, EPS)
            if pend_copy:
                flush_copy()
            if final:
                P.op("dve", lambda e, x=x, y=y, r_=r_, g=g: e.scalar_tensor_tensor(
                    out=y, in0=x, scalar=r_, in1=A[g][0], op0=ALU.mult, op1=ALU.mult),
                    reads=[xr, r_r, A[g][1]], writes=[yr])
                P.dma("sp", f"fo{tt % 2}", lambda e, y=y, tt=tt: e.dma_start(out=O["y"][tt * 128:(tt + 1) * 128, :], in_=y),
                      reads=[yr], acc_writes=[R["OUT"]])
                continue
            P.op("dve", lambda e, x=x, y=y, r_=r_, g=g: e.scalar_tensor_tensor(
                out=y, in0=x, scalar=r_, in1=A[g][0], op0=ALU.mult, op1=ALU.mult),
                reads=[xr, r_r, A[g][1]], writes=[yr])
            P.op("pool", lambda e, y=y, h=h, g=g: e.tensor_tensor(out=h, in0=y, in1=B[g][0], op=ALU.add),
                 reads=[yr, B[g][1]], writes=[hr])
            b = tt % 2
            for kc in range(8):
                P.op("pe", lambda e, h=h, kc=kc, b=b: e.transpose(bankb(b)[:, kc * 128:(kc + 1) * 128], h[:, kc * 128:(kc + 1) * 128], identb),
                     reads=[hr, identb_r], writes=[psr[b]] if kc == 0 else (), acc_writes=[psr[b]] if kc else ())
            pend_copy.append((b, tt))
        while pend_copy:
            flush_copy()
        P.release(m)

    def stream_w(dst, dst_r, key, src_ap):
        P.dma("pool", key, lambda e: e.dma_start(out=dst, in_=src_ap.rearrange("(kc p) n -> p kc n", p=128)), writes=[dst_r])

    def mm_wstat(b, w, wr, tile, ncols=512):
        for kc in range(8):
            P.op("pe", lambda e, kc=kc: e.matmul(bank(b, ncols), lhsT=w[:, kc, :], rhs=hT[:, kc, tile * 512:tile * 512 + ncols],
                                                 start=(kc == 0), stop=(kc == 7)),
                 reads=[wr, hTr[tile]], writes=[psr[b]] if kc == 0 else (), acc_writes=[psr[b]] if kc else ())

    def mm_tstat(b, w, wr, tt):
        for kc in range(8):
            P.op("pe", lambda e, kc=kc: e.matmul(bank(b), lhsT=hT[:, kc, tt * 128:(tt + 1) * 128], rhs=w[:, kc, :],
                                                 start=(kc == 0), stop=(kc == 7)),
                 reads=[wr, hTr[tt // 4]], writes=[psr[b]] if kc == 0 else (), acc_writes=[psr[b]] if kc else ())

    def phase_qkv(j):
        P.new_phase()
        m = P.mark()
        rcos, rcos_r = P.alloc([128, TS], F32, "rcos")
        rsin, rsin_r = P.alloc([128, TS], F32, "rsin")
        perm, perm_r = P.alloc([128, 128], BF16, "perm")
        P.dma("sp", "rcos", lambda e: e.dma_start(out=rcos, in_=I["rcos"]), writes=[rcos_r])
        P.dma("sp", "rsin", lambda e: e.dma_start(out=rsin, in_=I["rsin"]), writes=[rsin_r])
        P.dma("sp", "perm", lambda e: e.dma_start(out=perm, in_=I["perm"]), writes=[perm_r])
        ckt = [P.alloc([128, D], BF16, f"ckt{k}") for k in range(2)]
        kst = [P.alloc([128, 8, 128], BF16, f"kst{k}") for k in range(2)]
        cvt = [P.alloc([128, D], BF16, f"cvt{k}") for k in range(2)]
        for tk in range(4):
            c_, c_r = ckt[tk % 2]
            k_, k_r = kst[tk % 2]
            v_, v_r = cvt[tk % 2]
            P.dma("pool", f"ckt{tk % 2}", lambda e, c_=c_, tk=tk: e.dma_start(out=c_, in_=I["ck"][j, tk * 128:(tk + 1) * 128, :]), writes=[c_r])
            b = tk % 2
            for h in range(8):
                P.op("pe", lambda e, c_=c_, h=h, b=b: e.transpose(bankb(b)[:, h * 128:(h + 1) * 128], c_[:, h * 128:(h + 1) * 128], identb),
                     reads=[c_r, identb_r], writes=[psr[b]] if h == 0 else (), acc_writes=[psr[b]] if h else ())
            P.op("act", lambda e, k_=k_, b=b: e.copy(out=k_, in_=bankb(b).rearrange("p (k t) -> p k t", t=128)), reads=[psr[b]], writes=[k_r])
            P.dma("sp", f"kst{tk % 2}", lambda e, k_=k_, tk=tk: e.dma_start(
                out=S["KT"][:, :, tk * 128:(tk + 1) * 128].rearrange("h p c -> p h c"), in_=k_), reads=[k_r], acc_writes=[R["KT"]])
            P.dma("pool", f"cvt{tk % 2}", lambda e, v_=v_, tk=tk: e.dma_start(out=v_, in_=I["cv"][j, tk * 128:(tk + 1) * 128, :]), writes=[v_r])
            P.dma("sp", f"cvo{tk % 2}", lambda e, v_=v_, tk=tk: e.dma_start(out=S["VS"][tk * 128:(tk + 1) * 128, :], in_=v_),
                  reads=[v_r], acc_writes=[R["VS"]])
        if stop_after == f"qkv{2 * j}a":
            raise Stop()
        wq = [P.alloc([128, 8, 128], BF16, f"wq{k}") for k in range(3)]
        qb = [P.alloc([128, 512], BF16, f"qb{k}") for k in range(3)]
        t1 = [P.alloc([128, 512], F32, f"t1{k}") for k in range(3)]
        t2 = [P.alloc([128, 512], F32, f"t2{k}") for k in range(3)]
        qr = [P.alloc([128, 512], BF16, f"qr{k}") for k in range(3)]
        pend_tail = []

        def store(q_, q_r, isk, h, tile, key):
            if isk:
                P.dma("sp", key, lambda e: e.dma_start(
                    out=S["KT"][h, :, 512 + tile * 512: 512 + (tile + 1) * 512], in_=q_), reads=[q_r], acc_writes=[R["KT"]])
            else:
                P.dma("sp", key, lambda e: e.dma_start(
                    out=S["QT"][h, :, tile * 512:(tile + 1) * 512], in_=q_), reads=[q_r], acc_writes=[R["QT"]])
        it = 0
        for ch in range(16):
            isk, h = ch // 8, ch % 8
            w, wr = wq[ch % 3]
            stream_w(w, wr, f"wq{ch % 3}", I["attn_w_qkv"][j][:, isk * D + h * 128: isk * D + (h + 1) * 128])
            for tile in range(NT):
                bA = (it % 3) * 2
                bB = bA + 1
                q_, q_r = qr[it % 3]
                mm_wstat(bA, w, wr, tile)
                while pend_tail:
                    pend_tail.pop(0)()
                if tile < 8 and not os.environ.get('NOROPE'):
                    qq, qq_r = qb[it % 3]
                    a1, a1r = t1[it % 3]
                    a2, a2r = t2[it % 3]
                    P.op("act", lambda e, qq=qq, bA=bA: e.copy(out=qq, in_=bank(bA)), reads=[psr[bA]], writes=[qq_r])
                    P.op("dve", lambda e, a1=a1, bA=bA, tile=tile: e.tensor_tensor(
                        out=a1, in0=bank(bA), in1=rcos[:, tile * 512:(tile + 1) * 512], op=ALU.mult),
                        reads=[psr[bA], rcos_r], writes=[a1r])

                    def tail(qq=qq, qq_r=qq_r, bB=bB, a1=a1, a1r=a1r, a2=a2, a2r=a2r, q_=q_, q_r=q_r, tile=tile, isk=isk, h=h, key=f"qro{it % 3}"):
                        P.op("pe", lambda e: e.matmul(bank(bB), lhsT=perm, rhs=qq, start=True, stop=True),
                             reads=[perm_r, qq_r], writes=[psr[bB]])
                        P.op("dve", lambda e: e.tensor_tensor(
                            out=a2, in0=bank(bB), in1=rsin[:, tile * 512:(tile + 1) * 512], op=ALU.mult),
                            reads=[psr[bB], rsin_r], writes=[a2r])
                        P.op("pool", lambda e: e.tensor_tensor(out=q_, in0=a1, in1=a2, op=ALU.add),
                             reads=[a1r, a2r], writes=[q_r])
                        store(q_, q_r, isk, h, tile, key)
                    pend_tail.append(tail)
                else:
                    P.op("act", lambda e, q_=q_, bA=bA: e.copy(out=q_, in_=bank(bA)), reads=[psr[bA]], writes=[q_r])
                    pend_tail.append(lambda q_=q_, q_r=q_r, isk=isk, h=h, tile=tile, key=f"qro{it % 3}": store(q_, q_r, isk, h, tile, key))
                it += 1
        while pend_tail:
            pend_tail.pop(0)()
        if stop_after == f"qkv{2 * j}b":
            raise Stop()
        wv = [P.alloc([128, 8, 512], BF16, f"wv{k}") for k in range(2)]
        vb = [P.alloc([128, 512], BF16, f"vb{k}") for k in range(3)]
        vf = [P.alloc([128, 512], F32, f"vf{k}") for k in range(2)]
        it = 0
        for kind in ("v", "k"):
            for half in range(2):
                w, wr = wv[(it // 64) % 2]
                col0 = (2 * D if kind == "v" else D) + half * 512
                w, wr = wv[half]
                stream_w(w, wr, f"wv{half}{kind}", I["attn_w_qkv"][j][:, col0:col0 + 512])
                tts = range(NTT) if kind == "v" else range(32, 36)
                for tt in tts:
                    b = 6 + it % 2
                    mm_tstat(b, w, wr, tt)
                    if kind == "v":
                        v_, v_r = vb[it % 3]
                        P.op("act", lambda e, v_=v_, b=b: e.copy(out=v_, in_=bank(b)), reads=[psr[b]], writes=[v_r])
                        P.dma("sp", f"vbo{it % 3}", lambda e, v_=v_, tt=tt, half=half: e.dma_start(
                            out=S["VS"][512 + tt * 128: 512 + (tt + 1) * 128, half * 512:(half + 1) * 512], in_=v_),
                            reads=[v_r], acc_writes=[R["VS"]])
                    if tt >= 32:
                        f_, f_r = vf[it % 2]
                        P.op("dve", lambda e, f_=f_, b=b: e.tensor_copy(out=f_, in_=bank(b)), reads=[psr[b]], writes=[f_r])
                        s, r0 = (tt - 32) // 2, ((tt - 32) % 2) * 128
                        dst = O["nv"] if kind == "v" else O["nk"]
                        P.dma("sp", f"vfo{it % 2}", lambda e, f_=f_, s=s, r0=r0, half=half, dst=dst: e.dma_start(
                            out=dst[s, j, r0:r0 + 128, half * 512:(half + 1) * 512], in_=f_), reads=[f_r], acc_writes=[R["OUT"]])
                    it += 1
        P.release(m)

    def phase_attn(j, i):
        P.new_phase()
        m = P.mark()
        lam_init = 0.8 - 0.6 * math.exp(-0.3 * i)
        lp, lp_r = P.alloc([128, 4, 64], F32, "lp")
        lpp, lpp_r = P.alloc([128, 2, 64], F32, "lpp")
        lsum, lsum_r = P.alloc([128, 2], F32, "lsum")
        lexp, lexp_r = P.alloc([128, 2], F32, "lexp")
        nlam, nlam_r = P.alloc([128, 1], F32, "nlam")
        gsub, gsub_r = P.alloc([128, 128], F32, "gsub")
        P.dma("sp", "lp", lambda e: e.dma_start(out=lp, in_=I["attn_lambda"][j].rearrange("(o a) b -> o a b", o=1).broadcast_to([128, 4, 64])), writes=[lp_r])
        P.dma("sp", "gsub", lambda e: e.dma_start(out=gsub, in_=bcast_rows(I["attn_subln_g"][j], 128)), writes=[gsub_r])
        P.op("dve", lambda e: e.tensor_scalar(out=gsub, in0=gsub, scalar1=1.0 - lam_init, scalar2=None, op0=ALU.mult),
             reads=[gsub_r], writes=[gsub_r])
        for k in range(2):
            P.op("dve", lambda e, k=k: e.tensor_tensor(out=lpp[:, k, :], in0=lp[:, 2 * k, :], in1=lp[:, 2 * k + 1, :], op=ALU.mult),
                 reads=[lp_r], acc_writes=[lpp_r])
        P.op("dve", lambda e: e.tensor_reduce(out=lsum, in_=lpp, axis=AX.X, op=ALU.add), reads=[lpp_r], writes=[lsum_r])
        P.op("act", lambda e: e.activation(out=lexp, in_=lsum, func=AF.Exp), reads=[lsum_r], writes=[lexp_r])
        P.op("dve", lambda e: e.tensor_tensor(out=nlam, in0=lexp[:, 1:2], in1=lexp[:, 0:1], op=ALU.subtract), reads=[lexp_r], writes=[nlam_r])
        P.op("dve", lambda e: e.tensor_scalar(out=nlam, in0=nlam, scalar1=-lam_init, scalar2=None, op0=ALU.add), reads=[nlam_r], writes=[nlam_r])

        kTb = [P.alloc([128, 4608], BF16, f"kTh{k}") for k in range(2)]
        vhb = [P.alloc([128, 36, 132], BF16, f"vh{k}") for k in range(2)]
        for k in range(2):
            P.op("pool", lambda e, k=k: e.memset(vhb[k][0], 1.0), writes=[vhb[k][1]])
        qtb = [P.alloc([128, 2, 256], BF16, f"qt{k}") for k in range(3)]
        for k in range(3):
            P.op("pool", lambda e, k=k: e.memset(qtb[k][0], 0.0), writes=[qtb[k][1]])
        Pb = [P.alloc([128, 2, 256], BF16, f"P{k}") for k in range(3)]
        osb = [P.alloc([128, 128], F32, f"o{k}") for k in range(2)]
        obb = [P.alloc([128, 128], BF16, f"ob{k}") for k in range(2)]
        rrb = [P.alloc([128, 2], F32, f"rr{k}") for k in range(2)]
        ssb = [P.alloc([128, 1], F32, f"ss{k}") for k in range(2)]
        rsb = [P.alloc([128, 1], F32, f"rs{k}") for k in range(2)]
        junk, junk_r = P.alloc([128, 128], BF16, "ajunk")
        accs = [P.alloc([128, 4, 132], F32, f"accs{k}") for k in range(2)]
        steps = []
        hcount = 0
        for G in GROUPS:
            for s in range(G["nseq"]):
                L = G["L"]
                nk = G["ncache"] + L
                nkc = nk // 128
                kcol0 = 0 if G["name"] == "s" else 4608 + s * 256
                qtok0 = G["tok0"] + s * L
                for h in range(8):
                    for qb_ in range(L // 256):
                        for c in range(nkc):
                            steps.append(dict(h=h, hid=hcount, q0=qtok0 + qb_ * 256, c=c, nkc=nkc, nk=nk, kcol0=kcol0,
                                              newh=(qb_ == 0 and c == 0), newq=(c == 0)))
                    hcount += 1
        qcount = [0]
        ecount = [0]
        cur = {}

        heads = {st["hid"]: st for st in steps if st["newh"]}
        loaded = set()

        def load_head(hs_):
            kT, kT_r = kTb[hs_["hid"] % 2]
            vh, vh_r = vhb[hs_["hid"] % 2]
            nk, nkc, kcol0, hh = hs_["nk"], hs_["nkc"], hs_["kcol0"], hs_["h"]
            P.dma("sp", f"kTh{hs_['hid'] % 2}", lambda e: e.dma_start(
                out=kT[:, 0:nk], in_=S["KT"][hh, :, kcol0:kcol0 + nk]), reads=[R["KT"]], writes=[kT_r])
            P.dma("sp", f"vh{hs_['hid'] % 2}", lambda e: e.dma_start(
                out=vh[:, 0:nkc, 0:128], in_=S["VS"][kcol0:kcol0 + nk, hh * 128:(hh + 1) * 128].rearrange("(c p) e -> p c e", p=128)),
                reads=[R["VS"]], acc_writes=[vh_r])

        def stage_a(i, st):
            h = st["h"]
            if st["newh"]:
                if st["hid"] not in loaded:
                    loaded.add(st["hid"])
                    load_head(st)
                nxt = heads.get(st["hid"] + 1)
                if nxt is not None and st["nkc"] >= 12:
                    def pf(nxt=nxt):
                        if nxt["hid"] not in loaded:
                            loaded.add(nxt["hid"])
                            load_head(nxt)
                    deferred.append((i + LA, pf))
                cur["kT"], cur["vh"] = kTb[st["hid"] % 2], vhb[st["hid"] % 2]
            if st["newq"]:
                qt, qt_r = qtb[qcount[0] % 3]
                q0 = st["q0"]
                for mp in range(2):
                    P.dma("sp", f"qt{qcount[0] % 3}_{mp}", lambda e, mp=mp: e.dma_start(
                        out=qt[mp * 64:(mp + 1) * 64, mp, :], in_=S["QT"][h, mp * 64:(mp + 1) * 64, q0:q0 + 256]),
                        reads=[R["QT"]], acc_writes=[qt_r])
                qcount[0] += 1
                cur["qt"] = (qt, qt_r)
            kT, kT_r = cur["kT"]
            qt, qt_r = cur["qt"]
            st["vh"] = cur["vh"]
            c = st["c"]
            sb_ = i % 3
            Pt, Pt_r = Pb[i % 3]
            st["P"] = (Pt, Pt_r)
            P.op("pe", lambda e: e.matmul(
                bank(sb_), lhsT=kT[:, c * 128:(c + 1) * 128],
                rhs=qt.rearrange("p m q -> p (m q)"), start=True, stop=True),
                reads=[kT_r, qt_r], writes=[psr[sb_]])
            P.op("act", lambda e: e.activation(
                out=Pt, in_=bank(sb_).rearrange("p (m q) -> p m q", q=256), func=AF.Exp, scale=0.125),
                reads=[psr[sb_]], writes=[Pt_r])

        def stage_b(st):
            Pt, Pt_r = st["P"]
            vh, vh_r = st["vh"]
            c, nkc, h = st["c"], st["nkc"], st["h"]
            for qs in range(2):
                for mp in range(2):
                    ab = 4 + qs * 2 + mp
                    P.op("pe", lambda e, qs=qs, mp=mp, ab=ab: e.matmul(
                        bank(ab, 129), lhsT=Pt[:, mp, qs * 128:(qs + 1) * 128], rhs=vh[:, c, 0:129],
                        start=(c == 0), stop=(c == nkc - 1)),
                        reads=[Pt_r, vh_r], writes=[psr[ab]] if c == 0 else (), acc_writes=[psr[ab]] if c else ())
            if c != nkc - 1:
                return
            if nkc < 12:
                run_deferred(10 ** 9)
            ac, ac_r = accs[ecount[0] % 2]
            P.op("dve", lambda e: e.tensor_copy(out=ac[:, :, 0:129], in_=PS[:, 4 * 512:8 * 512].rearrange("p (b c) -> p b c", c=512)[:, :, 0:129]),
                 reads=[psr[4], psr[5], psr[6], psr[7]], writes=[ac_r])
            for qs in range(2):
                a0, a1 = qs * 2, qs * 2 + 1
                o_, o_r = osb[ecount[0] % 2]
                ob, ob_r = obb[ecount[0] % 2]
                rr, rr_r = rrb[ecount[0] % 2]
                s_, s_r = ssb[ecount[0] % 2]
                r_, r_r = rsb[ecount[0] % 2]
                ecount[0] += 1
                tok = st["q0"] + qs * 128
                P.op("dve", lambda e, rr=rr, a0=a0: e.reciprocal(out=rr, in_=ac[:, a0:a0 + 2, 128]),
                     reads=[ac_r], writes=[rr_r])
                P.op("dve", lambda e, rr=rr: e.tensor_tensor(out=rr[:, 1:2], in0=rr[:, 1:2], in1=nlam, op=ALU.mult),
                     reads=[rr_r, nlam_r], writes=[rr_r])
                P.op("dve", lambda e, o_=o_, rr=rr, a0=a0: e.tensor_scalar(
                    out=o_, in0=ac[:, a0, 0:128], scalar1=rr[:, 0:1], scalar2=None, op0=ALU.mult),
                    reads=[ac_r, rr_r], writes=[o_r])
                P.op("dve", lambda e, o_=o_, rr=rr, a1=a1: e.scalar_tensor_tensor(
                    out=o_, in0=ac[:, a1, 0:128], scalar=rr[:, 1:2], in1=o_, op0=ALU.mult, op1=ALU.add),
                    reads=[ac_r, rr_r, o_r], writes=[o_r])
                P.op("dve", lambda e, o_=o_, s_=s_: e.scalar_tensor_tensor(
                    out=junk, in0=o_, scalar=1.0, in1=o_, op0=ALU.mult, op1=ALU.mult, accum_out=s_),
                    reads=[o_r], writes=[junk_r, s_r])

                def e2(o_=o_, o_r=o_r, ob=ob, ob_r=ob_r, s_=s_, s_r=s_r, r_=r_, r_r=r_r):
                    rstd_ops(s_, s_r, r_, r_r, 128, SUBLN_EPS)
                    P.op("dve", lambda e: e.scalar_tensor_tensor(
                        out=ob, in0=o_, scalar=r_, in1=gsub, op0=ALU.mult, op1=ALU.mult),
                        reads=[o_r, r_r, gsub_r], writes=[ob_r])

                def e4(ob=ob, ob_r=ob_r):
                    P.op("pe", lambda e: e.transpose(bankb(3)[:, 0:128], ob, identb), reads=[ob_r, identb_r], writes=[psr[3]])

                def e5(tok=tok, h=h):
                    P.op("dve", lambda e: e.tensor_copy(out=hT[:, h, tok:tok + 128], in_=bankb(3)[:, 0:128]),
                         reads=[psr[3]], acc_writes=[hTr[tok // 512]])
                if nkc >= 12:
                    base = st["idx"] + LA
                    deferred.append((base + 2 + qs, e2))
                    deferred.append((base + 4 + 3 * qs, e4))
                    deferred.append((base + 6 + 3 * qs, e5))
                else:
                    e2()
                    e4()
                    e5()

        LA = 2
        deferred = []

        def run_deferred(now):
            keep = []
            for due, fn in deferred:
                if due <= now:
                    fn()
                else:
                    keep.append((due, fn))
            deferred[:] = keep
        for i, st in enumerate(steps):
            st["idx"] = i
            stage_a(i, st)
            if i >= LA:
                stage_b(steps[i - LA])
            run_deferred(i)
        for st in steps[-LA:]:
            stage_b(st)
        run_deferred(10 ** 9)
        P.release(m)

    def phase_outproj(wsrc, bias_vec, offG):
        P.new_phase()
        m = P.mark()
        G_ = load_mod(offG, "oG")
        if bias_vec is not None:
            brep, brep_r = P.alloc([128, D], F32, "obias")
            P.dma("sp", "obias", lambda e: e.dma_start(out=brep, in_=bcast_rows(bias_vec, D)), writes=[brep_r])
        wo = [P.alloc([128, 8, 512], BF16, f"wo{k}") for k in range(2)]
        xt = [P.alloc([128, 512], F32, f"ox{k}") for k in range(4)]
        tb = [P.alloc([128, 512], F32, f"ot{k}") for k in range(2)]
        for half in range(2):
            stream_w(wo[half][0], wo[half][1], f"wo{half}", wsrc[:, half * 512:(half + 1) * 512])
        iters = [(half, tt) for half in range(2) for tt in range(NTT)]

        def load_x(k):
            half, tt = iters[k]
            x, xr = xt[k % 4]
            P.dma("sp", f"ox{k % 4}", lambda e: e.dma_start(
                out=x, in_=S["X"][tt * 128:(tt + 1) * 128, half * 512:(half + 1) * 512]), reads=[R["X"][tt][half]], writes=[xr])
        load_x(0)
        load_x(1)
        for it, (half, tt) in enumerate(iters):
            if it + 2 < len(iters):
                load_x(it + 2)
            w, wr = wo[half]
            g = 0 if tt < 32 else 1
            b = it % 4
            x, xr = xt[it % 4]
            t_, t_r = tb[it % 2]
            mm_tstat(b, w, wr, tt)
            if bias_vec is not None:
                P.op("dve", lambda e, t_=t_, b=b, half=half: e.tensor_tensor(
                    out=t_, in0=bank(b), in1=brep[:, half * 512:(half + 1) * 512], op=ALU.add),
                    reads=[psr[b], brep_r], writes=[t_r])
                P.op("dve", lambda e, t_=t_, g=g, half=half: e.tensor_tensor(
                    out=t_, in0=t_, in1=G_[g][0][:, half * 512:(half + 1) * 512], op=ALU.mult),
                    reads=[t_r, G_[g][1]], writes=[t_r])
            else:
                P.op("dve", lambda e, t_=t_, b=b, g=g, half=half: e.tensor_tensor(
                    out=t_, in0=bank(b), in1=G_[g][0][:, half * 512:(half + 1) * 512], op=ALU.mult),
                    reads=[psr[b], G_[g][1]], writes=[t_r])
            P.op("pool", lambda e, x=x, t_=t_: e.tensor_tensor(out=x, in0=x, in1=t_, op=ALU.add), reads=[xr, t_r], writes=[xr])
            P.dma("sp", f"oxo{it % 4}", lambda e, x=x, tt=tt, half=half: e.dma_start(
                out=S["X"][tt * 128:(tt + 1) * 128, half * 512:(half + 1) * 512], in_=x), reads=[xr], writes=[R["X"][tt][half]])
        P.release(m)

    def phase_ffn(i):
        P.new_phase()
        m0 = P.mark()
        wd, wd_r = P.alloc([128, NCH_FF, D], BF16, "wd")
        P.dma("pool", "wd", lambda e: e.dma_start(out=wd, in_=I["ffn_w_down"][i].rearrange("(c p) n -> p c n", p=128)), writes=[wd_r])
        m = P.mark()
        wg = [P.alloc([128, 2, 8, 128], BF16, f"wg{k}") for k in range(3)]
        sg = [P.alloc([128, 512], F32, f"sg{k}") for k in range(2)]
        md = [P.alloc([128, 512], BF16, f"md{k}") for k in range(3)]
        it = 0
        for c in range(NCH_FF):
            w, wr = wg[c % 3]
            for u in range(2):
                P.dma("pool", f"wg{c % 3}_{u}", lambda e, w=w, c=c, u=u: e.dma_start(
                    out=w[:, u], in_=I["ffn_w_gu"][i][:, u * DFF + c * 128: u * DFF + (c + 1) * 128].rearrange("(kc p) n -> p kc n", p=128)),
                    writes=[wr] if u == 0 else (), acc_writes=[wr] if u else ())
            for tile in range(NT):
                bG = (it % 4) * 2
                bU = bG + 1
                s_, s_r = sg[it % 2]
                m_, m_r = md[it % 3]
                mm_wstat(bG, w[:, 0], wr, tile)
                mm_wstat(bU, w[:, 1], wr, tile)
                P.op("act", lambda e, s_=s_, bG=bG: e.activation(out=s_, in_=bank(bG), func=AF.Silu), reads=[psr[bG]], writes=[s_r])
                P.op("dve", lambda e, m_=m_, s_=s_, bU=bU: e.tensor_tensor(out=m_, in0=bank(bU), in1=s_, op=ALU.mult),
                     reads=[psr[bU], s_r], writes=[m_r])
                P.dma("sp", f"mdo{it % 3}", lambda e, m_=m_, c=c, tile=tile: e.dma_start(
                    out=S["MID"][c, :, tile * 512:(tile + 1) * 512], in_=m_), reads=[m_r], acc_writes=[R["MID"][tile]])
                it += 1
        P.release(m)
        P.new_phase()
        m = P.mark()
        G_ = load_mod(5 * D, "fG")
        mt = [P.alloc([128, NCH_FF, 512], BF16, f"mt{k}") for k in range(2)]
        xt = [P.alloc([128, 512], F32, f"fx{k}") for k in range(4)]
        tb = [P.alloc([128, 512], F32, f"ft{k}") for k in range(2)]
        iters = [(tile, ts, half) for tile in range(NT) for ts in range(4) for half in range(2)]

        def load_mt(tile):
            mt_, mt_r = mt[tile % 2]
            P.dma("sp", f"mt{tile % 2}", lambda e: e.dma_start(
                out=mt_, in_=S["MID"][:, :, tile * 512:(tile + 1) * 512].rearrange("c p t -> p c t")), reads=[R["MID"][tile]], writes=[mt_r])

        def load_x(k):
            tile, ts, half = iters[k]
            tt = tile * 4 + ts
            x, xr = xt[k % 4]
            P.dma("sp", f"fx{k % 4}", lambda e: e.dma_start(
                out=x, in_=S["X"][tt * 128:(tt + 1) * 128, half * 512:(half + 1) * 512]), reads=[R["X"][tt][half]], writes=[xr])
        load_mt(0)
        load_x(0)
        load_x(1)
        for it, (tile, ts, half) in enumerate(iters):
            if ts == 0 and half == 0 and tile + 1 < NT:
                load_mt(tile + 1)
            if it + 2 < len(iters):
                load_x(it + 2)
            g = 0 if tile < 8 else 1
            mt_, mt_r = mt[tile % 2]
            tt = tile * 4 + ts
            b = it % 4
            x, xr = xt[it % 4]
            t_, t_r = tb[it % 2]
            for c in range(NCH_FF):
                P.op("pe", lambda e, mt_=mt_, c=c, ts=ts, half=half, b=b: e.matmul(
                    bank(b), lhsT=mt_[:, c, ts * 128:(ts + 1) * 128], rhs=wd[:, c, half * 512:(half + 1) * 512],
                    start=(c == 0), stop=(c == NCH_FF - 1)),
                    reads=[mt_r, wd_r], writes=[psr[b]] if c == 0 else (), acc_writes=[psr[b]] if c else ())
            P.op("dve", lambda e, t_=t_, b=b, g=g, half=half: e.tensor_tensor(
                out=t_, in0=bank(b), in1=G_[g][0][:, half * 512:(half + 1) * 512], op=ALU.mult),
                reads=[psr[b], G_[g][1]], writes=[t_r])
            P.op("pool", lambda e, x=x, t_=t_: e.tensor_tensor(out=x, in0=x, in1=t_, op=ALU.add), reads=[xr, t_r], writes=[xr])
            P.dma("sp", f"fxo{it % 4}", lambda e, x=x, tt=tt, half=half: e.dma_start(
                out=S["X"][tt * 128:(tt + 1) * 128, half * 512:(half + 1) * 512], in_=x), reads=[xr], writes=[R["X"][tt][half]])
        P.release(m)
        P.release(m0)

    def phase_hy_in(j):
        P.new_phase()
        m = P.mark()
        CV, CV_r = P.alloc([128, 120], F32, "CV")
        load_cols(CV, CV_r, 0, I["hy_b_in"][j], 24, "cvl")
        for k in range(3):
            load_cols(CV, CV_r, 24 + k * 24, I["hy_conv_w"][j, k], 24, "cvl")
        load_cols(CV, CV_r, 96, I["hy_conv_b"][j], 24, "cvl")
        ubS = [P.alloc([128, TS + 2], F32, f"ubS{k}") for k in range(2)]
        ubP = [P.alloc([128, 2, 258], F32, f"ubP{k}") for k in range(2)]
        for k in range(2):
            P.op("pool", lambda e, k=k: e.memset(ubS[k][0], 0.0), writes=[ubS[k][1]])
            P.op("pool", lambda e, k=k: e.memset(ubP[k][0], 0.0), writes=[ubP[k][1]])
        cb1, cb1_r = P.alloc([128, T], F32, "cb1")
        cb2 = [P.alloc([128, T], F32, f"cb2{k}") for k in range(2)]
        vst, vst_r = P.alloc([128, NTT, 128], BF16, "vst")
        wi = [P.alloc([128, 8, 128], BF16, f"wi{k}") for k in range(3)]
        pend_tr = []

        def do_transposes(dst, dst_r, jd):
            for t4 in range(NTT // 4):
                b = 4 + t4 % 4
                for q in range(4):
                    tt = t4 * 4 + q
                    P.op("pe", lambda e, tt=tt, q=q, b=b: e.transpose(
                        bank(b)[:, q * 128:(q + 1) * 128], dst[:, tt * 128:(tt + 1) * 128], identf),
                        reads=[dst_r, identf_r], writes=[psr[b]] if q == 0 else (), acc_writes=[psr[b]] if q else ())
                P.op("act", lambda e, t4=t4, b=b: e.copy(out=vst[:, t4 * 4:(t4 + 1) * 4, :], in_=bank(b).rearrange("p (q d) -> p q d", d=128)),
                     reads=[psr[b]], acc_writes=[vst_r])
            P.dma("sp", "vsto", lambda e: e.dma_start(
                out=S["VVTOK"][:, jd * 128:(jd + 1) * 128].rearrange("(c p) d -> p c d", p=128), in_=vst),
                reads=[vst_r], acc_writes=[R["VVTOK"]])
        it = 0
        uc = 0
        c2 = 0
        for jd in range(8):
            for part, fc in (("x1", 8 + jd), ("v", 16 + jd), ("x0", jd)):
                w, wr = wi[it % 3]
                it += 1
                stream_w(w, wr, f"wi{it % 3}", I["hy_w_in"][j][:, fc * 128:(fc + 1) * 128])
                uS, uS_r = ubS[uc % 2]
                uP, uP_r = ubP[uc % 2]
                uc += 1
                for tile in range(NT):
                    b = tile % 4
                    mm_wstat(b, w, wr, tile)
                    if tile < 8:
                        P.op("act", lambda e, uS=uS, b=b, tile=tile, fc=fc: e.activation(
                            out=uS[:, 1 + tile * 512: 1 + (tile + 1) * 512], in_=bank(b), func=AF.Identity, bias=CV[:, fc:fc + 1]),
                            reads=[psr[b], CV_r], acc_writes=[uS_r])
                    else:
                        P.op("act", lambda e, uP=uP, b=b, fc=fc: e.activation(
                            out=uP[:, :, 1:257], in_=bank(b).rearrange("p (s t) -> p s t", t=256), func=AF.Identity, bias=CV[:, fc:fc + 1]),
                            reads=[psr[b], CV_r], acc_writes=[uP_r])
                if part == "x1":
                    dst, dst_r = cb1, cb1_r
                else:
                    dst, dst_r = cb2[c2 % 2]
                    c2 += 1
                w0, w1, w2, cbc = (CV[:, 24 + fc:25 + fc], CV[:, 48 + fc:49 + fc], CV[:, 72 + fc:73 + fc], CV[:, 96 + fc:97 + fc])
                dS = dst[:, 0:TS]
                dP = dst[:, TS:T].rearrange("p (s t) -> p s t", t=256)
                for (dd, uu, ur, n) in ((dS, uS, uS_r, TS), (dP, uP, uP_r, 256)):
                    def sl(o, uu=uu, n=n):
                        return uu[:, o:o + n] if len(uu.shape) == 2 else uu[:, :, o:o + n]
                    P.op("dve", lambda e, dd=dd, sl=sl, w0=w0, cbc=cbc: e.tensor_scalar(out=dd, in0=sl(0), scalar1=w0, scalar2=cbc, op0=ALU.mult, op1=ALU.add),
                         reads=[ur, CV_r], acc_writes=[dst_r])
                    P.op("dve", lambda e, dd=dd, sl=sl, w1=w1: e.scalar_tensor_tensor(out=dd, in0=sl(1), scalar=w1, in1=dd, op0=ALU.mult, op1=ALU.add),
                         reads=[ur, CV_r, dst_r], acc_writes=[dst_r])
                    P.op("dve", lambda e, dd=dd, sl=sl, w2=w2: e.scalar_tensor_tensor(out=dd, in0=sl(2), scalar=w2, in1=dd, op0=ALU.mult, op1=ALU.add),
                         reads=[ur, CV_r, dst_r], acc_writes=[dst_r])
                if part == "v":
                    P.op("pool", lambda e, dst=dst: e.tensor_tensor(out=dst, in0=dst, in1=cb1, op=ALU.mult), reads=[dst_r, cb1_r], writes=[dst_r])
                    P.dma("sp", f"vvt{c2 % 2}", lambda e, dst=dst, jd=jd: e.dma_start(out=S["VVT"][jd], in_=dst), reads=[dst_r], acc_writes=[R["VVT"]])
                    pend_tr.append((dst, dst_r, jd))
                if part != "v" and pend_tr:
                    do_transposes(*pend_tr.pop(0))
                if False:
                    for t4 in range(NTT // 4):
                        b = 4 + t4 % 4
                        for q in range(4):
                            tt = t4 * 4 + q
                            P.op("pe", lambda e, dst=dst, tt=tt, q=q, b=b: e.transpose(
                                bank(b)[:, q * 128:(q + 1) * 128], dst[:, tt * 128:(tt + 1) * 128], identf),
                                reads=[dst_r, identf_r], writes=[psr[b]] if q == 0 else (), acc_writes=[psr[b]] if q else ())
                        P.op("act", lambda e, t4=t4, b=b: e.copy(out=vst[:, t4 * 4:(t4 + 1) * 4, :], in_=bank(b).rearrange("p (q d) -> p q d", d=128)),
                             reads=[psr[b]], acc_writes=[vst_r])
                    P.dma("sp", "vsto", lambda e, jd=jd: e.dma_start(
                        out=S["VVTOK"][:, jd * 128:(jd + 1) * 128].rearrange("(c p) d -> p c d", p=128), in_=vst),
                        reads=[vst_r], acc_writes=[R["VVTOK"]])
                    vst_r.w = dict(vst_r.w)
                if part == "x0":
                    P.dma("sp", f"x0t{c2 % 2}", lambda e, dst=dst, jd=jd: e.dma_start(out=S["X0T"][jd], in_=dst), reads=[dst_r], acc_writes=[R["X0T"]])
        P.release(m)

    def phase_hy_filter(j, G):
        L = G["L"]
        nm = G["name"]
        ntc = L // 128
        HS, HD = S["HS" + nm], S["HD" + nm]
        P.new_phase()
        rn, rn_r = P.alloc([128, D], F32, "rn")
        m = P.mark()
        zT, zT_r = P.alloc([33, L], F32, "zT")
        w1, w1_r = P.alloc([33, 64], F32, "fw1")
        w2, w2_r = P.alloc([64, 64], F32, "fw2")
        w3, w3_r = P.alloc([64, 2 * D], F32, "fw3")
        b3, b3_r = P.alloc([128, 2 * D], F32, "fb3")
        fcol, fcol_r = P.alloc([64, 4], F32, "fcol")
        negt, negt_r = P.alloc([128, ntc], F32, "negt")
        drep, drep_r = P.alloc([128, D], F32, "drep")
        mask0, mask0_r = P.alloc([128, 1], F32, "mask0")
        h1, h1_r = P.alloc([64, L], F32, "h1")
        h2, h2_r = P.alloc([64, L], F32, "h2")
        tmp = [P.alloc([64, 512], F32, f"ftmp{k}") for k in range(2)]
        P.dma("sp", "zT", lambda e: e.dma_start(out=zT, in_=I["z" + nm]), writes=[zT_r])
        P.dma("sp", "fw1", lambda e: e.dma_start(out=w1, in_=I["filt_w1"][j]), writes=[w1_r])
        P.dma("sp", "fw2", lambda e: e.dma_start(out=w2, in_=I["filt_w2"][j]), writes=[w2_r])
        P.dma("sp", "fw3", lambda e: e.dma_start(out=w3, in_=I["filt_w3"][j]), writes=[w3_r])
        P.dma("sp", "fb3", lambda e: e.dma_start(out=b3, in_=bcast_rows(I["filt_b3"][j], 2 * D)), writes=[b3_r])
        P.dma("sp", "negt", lambda e: e.dma_start(out=negt, in_=I["negt" + nm]), writes=[negt_r])
        P.dma("sp", "drep", lambda e: e.dma_start(out=drep, in_=bcast_rows(I["delta"], D)), writes=[drep_r])
        P.dma("sp", "mask0", lambda e: e.dma_start(out=mask0, in_=I["mask0"]), writes=[mask0_r])
        for k, v in enumerate((I["filt_b1"][j], I["filt_b2"][j], I["filt_freq"][j])):
            P.dma("sp", "fcol", lambda e, k=k, v=v: e.dma_start(out=fcol[:, k:k + 1], in_=v.rearrange("(p o) -> p o", o=1)), acc_writes=[fcol_r])
        TWO_PI = 2.0 * math.pi
        SC = TWO_PI * (1.0 - 2e-6)
        wr_, wr_r = P.alloc([64, 512], F32, "fwrap")

        def sin_layer(lhsT, lhsT_r, src, src_r, dstb, dstb_r, bcol):
            n = min(512, L)
            for ti in range(L // n):
                b = ti % 2
                t_, t_r = tmp[ti % 2]
                P.op("pe", lambda e, ti=ti, b=b: e.matmul(bank(b)[0:64, 0:n], lhsT=lhsT, rhs=src[:, ti * n:(ti + 1) * n], start=True, stop=True),
                     reads=[lhsT_r, src_r], writes=[psr[b]])
                P.op("dve", lambda e, t_=t_, b=b: e.tensor_scalar(
                    out=t_[:, 0:n], in0=bank(b)[0:64, 0:n], scalar1=fcol[:, bcol:bcol + 1], scalar2=fcol[:, 2:3], op0=ALU.add, op1=ALU.mult),
                    reads=[psr[b], fcol_r], writes=[t_r])
                for rnd in range(2):
                    for (cmp_, thr, sgn) in ((ALU.is_lt, -math.pi, ALU.add), (ALU.is_gt, math.pi, ALU.subtract)):
                        P.op("dve", lambda e, t_=t_, cmp_=cmp_, thr=thr: e.tensor_scalar(
                            out=wr_[:, 0:n], in0=t_[:, 0:n], scalar1=thr, scalar2=TWO_PI, op0=cmp_, op1=ALU.mult),
                            reads=[t_r], writes=[wr_r])
                        P.op("dve", lambda e, t_=t_, sgn=sgn: e.tensor_tensor(out=t_[:, 0:n], in0=t_[:, 0:n], in1=wr_[:, 0:n], op=sgn),
                             reads=[t_r, wr_r], writes=[t_r])
                P.op("act", lambda e, t_=t_, ti=ti: e.activation(out=dstb[:, ti * n:(ti + 1) * n], in_=t_[:, 0:n], func=AF.Sin, scale=1.0 - 2e-6),
                     reads=[t_r], acc_writes=[dstb_r])
        sin_layer(w1, w1_r, zT, zT_r, h1, h1_r, 0)
        sin_layer(w2, w2_r, h1, h1_r, h2, h2_r, 1)
        dec = [P.alloc([128, D], F32, "dec0")] * 2
        hf = [P.alloc([128, D], F32, f"hf{k}") for k in range(2)]
        hb_ = [P.alloc([128, D], F32, f"hbk{k}") for k in range(2)]
        ab = [P.alloc([128, 2 * D], F32, "ab0")] * 2
        hs = [P.alloc([128, D], BF16, f"hs{k}") for k in range(2)]
        hd = [P.alloc([128, D], BF16, f"hd{k}") for k in range(2)]
        pend_ones = []

        def flush_ones():
            tc_ = pend_ones.pop(0)
            k_ = tc_ % 2
            for q in range(4):
                bq = 4 + q % 2
                first = (tc_ == 0 and q < 2)
                last = (tc_ == ntc - 1 and q >= 2)
                P.op("pe", lambda e, q=q, bq=bq, first=first, last=last: e.matmul(
                    bank(bq), lhsT=onesf, rhs=ab[k_][0][:, q * 512:(q + 1) * 512], start=first, stop=last),
                    reads=[onesf_r, ab[k_][1]], writes=[psr[bq]] if first else (), acc_writes=() if first else [psr[bq]])
        for tc in range(ntc):
            k = tc % 2
            for cb in range(4):
                P.op("pe", lambda e, tc=tc, cb=cb: e.matmul(bank(cb), lhsT=h2[:, tc * 128:(tc + 1) * 128], rhs=w3[:, cb * 512:(cb + 1) * 512], start=True, stop=True),
                     reads=[h2_r, w3_r], writes=[psr[cb]])
            if pend_ones:
                flush_ones()
            P.op("act", lambda e, k=k, tc=tc: e.activation(out=dec[k][0], in_=drep, func=AF.Exp, scale=negt[:, tc:tc + 1]),
                 reads=[drep_r, negt_r], writes=[dec[k][1]])
            for (dst, lo) in ((hf[k], 0), (hb_[k], D)):
                P.op("dve", lambda e, dst=dst, lo=lo: e.tensor_tensor(out=dst[0], in0=PS[:, lo:lo + D], in1=b3[:, lo:lo + D], op=ALU.add),
                     reads=[psr[lo // 512], psr[lo // 512 + 1], b3_r], writes=[dst[1]])
                P.op("dve", lambda e, dst=dst, k=k: e.tensor_tensor(out=dst[0], in0=dst[0], in1=dec[k][0], op=ALU.mult),
                     reads=[dst[1], dec[k][1]], writes=[dst[1]])
                P.op("act", lambda e, dst=dst, lo=lo, k=k: e.activation(out=ab[k][0][:, lo:lo + D], in_=dst[0], func=AF.Abs),
                     reads=[dst[1]], acc_writes=[ab[k][1]])
            pend_ones.append(tc)
            if tc == 0:
                P.op("dve", lambda e, k=k: e.tensor_scalar(out=hb_[k][0], in0=hb_[k][0], scalar1=mask0[:, 0:1], scalar2=None, op0=ALU.mult),
                     reads=[hb_[k][1], mask0_r], writes=[hb_[k][1]])
            P.op("pool", lambda e, k=k: e.tensor_tensor(out=hs[k][0], in0=hf[k][0], in1=hb_[k][0], op=ALU.add),
                 reads=[hf[k][1], hb_[k][1]], writes=[hs[k][1]])
            P.op("pool", lambda e, k=k: e.tensor_tensor(out=hd[k][0], in0=hb_[k][0], in1=hf[k][0], op=ALU.subtract),
                 reads=[hf[k][1], hb_[k][1]], writes=[hd[k][1]])
            P.dma("sp", f"hso{k}", lambda e, k=k, tc=tc: e.dma_start(out=HS[tc * 128:(tc + 1) * 128, :], in_=hs[k][0]), reads=[hs[k][1]], acc_writes=[R["HS"]])
            P.dma("sp", f"hdo{k}", lambda e, k=k, tc=tc: e.dma_start(out=HD[tc * 128:(tc + 1) * 128, :], in_=hd[k][0]), reads=[hd[k][1]], acc_writes=[R["HS"]])
        while pend_ones:
            flush_ones()
        P.op("dve", lambda e: e.tensor_scalar(out=rn, in0=PS[:, 4 * 512:6 * 512], scalar1=1e-6, scalar2=None, op0=ALU.add),
             reads=[psr[4], psr[5]], writes=[rn_r])
        P.op("dve", lambda e: e.reciprocal(out=rn, in_=rn), reads=[rn_r], writes=[rn_r])
        P.release(m)
        return rn, rn_r

    def phase_hy_conv(j, G, s, rn, rn_r):
        L = G["L"]
        nm = G["name"]
        ntc = L // 128
        nfc = 33 if nm == "s" else 3
        SQ = nfc * 128
        Qt, Rt, WFd = I["q" + nm], I["r" + nm], I["wf" + nm]
        HS, HD = S["HS" + nm], S["HD" + nm]
        tok0 = G["tok0"] + s * L
        P.new_phase()
        m = P.mark()
        wf, wf_r = P.alloc([128, nfc], F32, "wf")
        P.dma("sp", "wf", lambda e: e.dma_start(out=wf, in_=WFd), writes=[wf_r])
        vvhs, vvhs_r = P.alloc([128, ntc, 512], BF16, "vvhs")
        vvhd, vvhd_r = P.alloc([128, ntc, 512], BF16, "vvhd")
        qch = [P.alloc([128, ntc, 128], BF16, f"qch{k}") for k in range(2)]
        rch = [P.alloc([128, ntc, 128], BF16, f"rch{k}") for k in range(2)]
        kcs = [P.alloc([128, 256], F32, f"kcs{k}") for k in range(2)]
        kss = [P.alloc([128, 256], F32, f"kss{k}") for k in range(2)]
        ta = [P.alloc([128, 256], F32, f"ta{k}") for k in range(2)]
        tb = [P.alloc([128, 256], F32, f"tb{k}") for k in range(2)]
        yst = [P.alloc([128, 2, 256], BF16, f"yst{k}") for k in range(2)]
        def load_tab(k):
            fc_ = k % nfc
            qc_, qc_r_ = qch[k % 2]
            rc_, rc_r_ = rch[k % 2]
            if nm == "s":
                qsrc, rsrc = I["qfs"][fc_], I["rfs"][fc_]
            else:
                qsrc = Qt[0:ntc, :, fc_ * 128:(fc_ + 1) * 128].rearrange("c p f -> p c f")
                rsrc = Rt[0:ntc, :, fc_ * 128:(fc_ + 1) * 128].rearrange("c p f -> p c f")
            P.dma("sp", f"qch{k % 2}", lambda e: e.dma_start(out=qc_, in_=qsrc), writes=[qc_r_])
            P.dma("sp", f"rch{k % 2}", lambda e: e.dma_start(out=rc_, in_=rsrc), writes=[rc_r_])
        it = 0
        for dq in range(4):
            dh = dq // 2
            vsrc = S["VVTOK"][tok0:tok0 + L, dq * 256:(dq + 1) * 256].rearrange("(c p) d -> p c d", p=128)
            P.dma("sp", "vva", lambda e, vsrc=vsrc: e.dma_start(out=vvhs[:, :, 0:256], in_=vsrc), reads=[R["VVTOK"]], writes=[vvhs_r])
            P.dma("sp", "vvb", lambda e, vsrc=vsrc: e.dma_start(out=vvhd[:, :, 0:256], in_=vsrc), reads=[R["VVTOK"]], writes=[vvhd_r])
            P.dma("sp", "hsh", lambda e, dq=dq: e.dma_start(
                out=vvhs[:, :, 256:512], in_=HS[:, dq * 256:(dq + 1) * 256].rearrange("(c p) d -> p c d", p=128)), reads=[R["HS"]], acc_writes=[vvhs_r])
            P.dma("sp", "hdh", lambda e, dq=dq: e.dma_start(
                out=vvhd[:, :, 256:512], in_=HD[:, dq * 256:(dq + 1) * 256].rearrange("(c p) d -> p c d", p=128)), reads=[R["HS"]], acc_writes=[vvhd_r])
            for fc in range(nfc):
                qc, qc_r = qch[it % 2]
                rc, rc_r = rch[it % 2]
                if it == 0:
                    load_tab(0)
                if it + 1 < 4 * nfc:
                    load_tab(it + 1)
                b0 = (it % 4) * 2
                bC, bS = b0, b0 + 1
                for tc in range(ntc):
                    st, sp_ = (tc == 0), (tc == ntc - 1)
                    for (bb, tab, tab_r, mov, mov_r) in ((bC, qc, qc_r, vvhs, vvhs_r), (bS, rc, rc_r, vvhd, vvhd_r)):
                        P.op("pe", lambda e, bb=bb, tab=tab, mov=mov, tc=tc, st=st, sp_=sp_: e.matmul(
                            bank(bb), lhsT=tab[:, tc, :], rhs=mov[:, tc, :], start=st, stop=sp_),
                            reads=[tab_r, mov_r], writes=[psr[bb]] if st else (), acc_writes=() if st else [psr[bb]])
                bVc = bKc = bC
                bVs = bKs = bS
                Vc_, Kc_ = bank(bC)[:, 0:256], bank(bC)[:, 256:512]
                Vs_, Ks_ = bank(bS)[:, 0:256], bank(bS)[:, 256:512]
                k = it % 2
                wcol = wf[:, fc:fc + 1]
                rsl = rn[:, dq * 256:(dq + 1) * 256]
                P.op("dve", lambda e, k=k, Kc_=Kc_, wcol=wcol, rsl=rsl: e.scalar_tensor_tensor(
                    out=kcs[k][0], in0=Kc_, scalar=wcol, in1=rsl, op0=ALU.mult, op1=ALU.mult),
                    reads=[psr[bKc], wf_r, rn_r], writes=[kcs[k][1]])
                P.op("dve", lambda e, k=k, Ks_=Ks_, wcol=wcol, rsl=rsl: e.scalar_tensor_tensor(
                    out=kss[k][0], in0=Ks_, scalar=wcol, in1=rsl, op0=ALU.mult, op1=ALU.mult),
                    reads=[psr[bKs], wf_r, rn_r], writes=[kss[k][1]])
                P.op("dve", lambda e, k=k, Vc_=Vc_: e.tensor_tensor(out=ta[k][0], in0=Vc_, in1=kcs[k][0], op=ALU.mult),
                     reads=[psr[bVc], kcs[k][1]], writes=[ta[k][1]])
                P.op("dve", lambda e, k=k, Vs_=Vs_: e.tensor_tensor(out=tb[k][0], in0=Vs_, in1=kss[k][0], op=ALU.mult),
                     reads=[psr[bVs], kss[k][1]], writes=[tb[k][1]])
                P.op("pool", lambda e, k=k: e.tensor_tensor(out=yst[k][0][:, 0, :], in0=ta[k][0], in1=tb[k][0], op=ALU.add),
                     reads=[ta[k][1], tb[k][1]], writes=[yst[k][1]])
                P.op("dve", lambda e, k=k, Vs_=Vs_: e.tensor_tensor(out=ta[k][0], in0=Vs_, in1=kcs[k][0], op=ALU.mult),
                     reads=[psr[bVs], kcs[k][1]], writes=[ta[k][1]])
                P.op("dve", lambda e, k=k, Vc_=Vc_: e.tensor_tensor(out=tb[k][0], in0=Vc_, in1=kss[k][0], op=ALU.mult),
                     reads=[psr[bVc], kss[k][1]], writes=[tb[k][1]])
                P.op("pool", lambda e, k=k: e.tensor_tensor(out=yst[k][0][:, 1, :], in0=ta[k][0], in1=tb[k][0], op=ALU.subtract),
                     reads=[ta[k][1], tb[k][1]], acc_writes=[yst[k][1]])
                P.dma("sp", f"yfo{k}", lambda e, k=k, dh=dh, dq=dq, fc=fc: e.dma_start(out=S["YF"][dh, fc][:, :, (dq % 2) * 256:(dq % 2 + 1) * 256], in_=yst[k][0]),
                      reads=[yst[k][1]], acc_writes=[R["YF"]])
                it += 1
        P.release(m)
        P.new_phase()
        m = P.mark()
        skc, skc_r = P.alloc([128, 8], F32, "skc")
        load_cols(skc, skc_r, 0, I["hy_skip"][j], 8, "skc")
        Yg, Yg_r = P.alloc([128, nfc, 2, 512], BF16, "Yg")
        n = min(512, L)
        tq = [P.alloc([128, 2, n], BF16, f"tq{k}") for k in range(6)]
        vvt = [P.alloc([128, n], F32, f"vvt{k}") for k in range(3)]
        x0t = [P.alloc([128, n], F32, f"x0t{k}") for k in range(3)]
        it = 0
        ie = 0
        for dh in range(2):
            P.dma("sp", "Yg", lambda e, dh=dh: e.dma_start(out=Yg, in_=S["YF"][dh, 0:nfc].rearrange("c p s d -> p c s d")),
                  reads=[R["YF"]], writes=[Yg_r])
            for tt in range(L // n):
                b0 = ((dh * (L // n) + tt) % 2) * 4
                for fc in range(nfc):
                    t_, t_r = tq[it % 6]
                    P.dma("sp", f"tq{it % 6}a", lambda e, t_=t_, fc=fc, tt=tt: e.dma_start(out=t_[:, 0, :], in_=Qt[fc, :, tt * n:(tt + 1) * n]), writes=[t_r])
                    P.dma("sp", f"tq{it % 6}b", lambda e, t_=t_, fc=fc, tt=tt: e.dma_start(out=t_[:, 1, :], in_=Rt[fc, :, tt * n:(tt + 1) * n]), acc_writes=[t_r])
                    it += 1
                    for dcl in range(4):
                        for cs in range(2):
                            st = (fc == 0 and cs == 0)
                            sp_ = (fc == nfc - 1 and cs == 1)
                            P.op("pe", lambda e, t_=t_, fc=fc, dcl=dcl, cs=cs, st=st, sp_=sp_, b0=b0: e.matmul(
                                bank(b0 + dcl, n), lhsT=Yg[:, fc, cs, dcl * 128:(dcl + 1) * 128], rhs=t_[:, cs, :], start=st, stop=sp_),
                                reads=[Yg_r, t_r], writes=[psr[b0 + dcl]] if st else (), acc_writes=() if st else [psr[b0 + dcl]])
                for dcl in range(4):
                    dc = dh * 4 + dcl
                    v_, v_r = vvt[ie % 3]
                    x_, x_r = x0t[ie % 3]
                    ie += 1
                    c0 = tok0 + tt * n
                    P.dma("sp", f"vvt{ie % 3}", lambda e, v_=v_, dc=dc, c0=c0: e.dma_start(out=v_, in_=S["VVT"][dc, :, c0:c0 + n]), reads=[R["VVT"]], writes=[v_r])
                    P.dma("sp", f"x0t{ie % 3}", lambda e, x_=x_, dc=dc, c0=c0: e.dma_start(out=x_, in_=S["X0T"][dc, :, c0:c0 + n]), reads=[R["X0T"]], writes=[x_r])
                    P.op("dve", lambda e, v_=v_, dc=dc, dcl=dcl, b0=b0: e.scalar_tensor_tensor(
                        out=v_, in0=v_, scalar=skc[:, dc:dc + 1], in1=bank(b0 + dcl, n), op0=ALU.mult, op1=ALU.add),
                        reads=[v_r, skc_r, psr[b0 + dcl]], writes=[v_r])
                    P.op("pool", lambda e, v_=v_, x_=x_, dc=dc, c0=c0: e.tensor_tensor(out=hT[:, dc, c0:c0 + n], in0=v_, in1=x_, op=ALU.mult),
                         reads=[v_r, x_r], acc_writes=[hTr[c0 // 512]])
        P.release(m)

    class Stop(Exception):
        pass

    def chk(tag):
        if stop_after == tag:
            raise Stop()

    def dump_hT():
        for t in range(NT):
            P.dma("sp", "htd", lambda e, t=t: e.dma_start(out=S["HTD"][:, :, t * 512:(t + 1) * 512], in_=hT[:, :, t * 512:(t + 1) * 512]),
                  reads=[hTr[t]], acc_writes=[R["HTD"]])

    try:
        for i in range(4):
            j = i // 2
            phase_mod(i)
            chk(f"mod{i}")
            phase_norm(D, 0)
            chk(f"norm1_{i}")
            if i % 2 == 0:
                phase_qkv(j)
                chk(f"qkv{i}")
                phase_attn(j, i)
                chk(f"attn{i}")
                phase_outproj(I["attn_w_o"][j], None, 2 * D)
            else:
                phase_hy_in(j)
                chk(f"hyin{i}")
                for G in GROUPS:
                    mk = P.mark()
                    rn, rn_r = phase_hy_filter(j, G)
                    chk(f"hyfilt{i}{G['name']}")
                    for s in range(G["nseq"]):
                        phase_hy_conv(j, G, s, rn, rn_r)
                    P.release(mk)
                chk(f"hyconv{i}")
                phase_outproj(I["hy_w_out"][j], I["hy_b_out"][j], 2 * D)
            chk(f"mix{i}")
            phase_norm(4 * D, 3 * D)
            phase_ffn(i)
            chk(f"ffn{i}")
        phase_norm(0, 0, final=True)
    except Stop:
        if "HTD" in debug_outs:
            dump_hT()

    final_keys = [k for k in P.dma_cnt]
    print("n dma sems", len(final_keys), {e: len(P.q[e]) for e in ENGS})
    P.emit(final_keys)
    return nc, P


_CONSTS = None


def _core_inputs(b, inp, consts):
    m = {}
    m["x"] = np.ascontiguousarray(np.concatenate(
        [inp["x_sample"][b], inp["x_prompt"][2 * b], inp["x_prompt"][2 * b + 1]], axis=0).astype(np.float32))
    m["ck"] = np.ascontiguousarray(inp["cache_k"][b].reshape(2, 512, D).astype(np.float32))
    m["cv"] = np.ascontiguousarray(inp["cache_v"][b].reshape(2, 512, D).astype(np.float32))
    m["cvec"] = np.ascontiguousarray(np.stack([inp["c"][b], inp["c_ctx"]], axis=0).astype(np.float32))
    for nm, shp in W_SPECS:
        m[nm] = np.ascontiguousarray(np.asarray(inp[nm], dtype=np.float32).reshape(shp))
    for nm, shp, dt in CONST_SPECS:
        m[nm] = consts[nm]
    return m


def kernel(**inputs):
    global _CONSTS
    if _CONSTS is None:
        _CONSTS = _host_consts()
    inp = {k: np.asarray(v) for k, v in inputs.items()}
    nc, _ = build_program()
    in_maps = [_core_inputs(b, inp, _CONSTS) for b in range(8)]
    res = run_bass_kernel_spmd(nc, in_maps, core_ids=list(range(8)))
    y_prompt = np.zeros((16, 256, D), np.float32)
    y_sample = np.zeros((8, TS, D), np.float32)
    nk = np.zeros((16, 2, 256, 8, 2, 64), np.float32)
    nv = np.zeros((16, 2, 256, 8, 128), np.float32)
    for b in range(8):
        r = res.results[b]
        y = np.asarray(r["y"], dtype=np.float32)
        y_sample[b] = y[:TS]
        y_prompt[2 * b] = y[TS:TS + 256]
        y_prompt[2 * b + 1] = y[TS + 256:]
        k_ = np.asarray(r["nk"], dtype=np.float32).reshape(2, 2, 256, 8, 2, 64)
        v_ = np.asarray(r["nv"], dtype=np.float32).reshape(2, 2, 256, 8, 128)
        nk[2 * b], nk[2 * b + 1] = k_[0], k_[1]
        nv[2 * b], nv[2 * b + 1] = v_[0], v_[1]
    return (y_prompt, y_sample, nk, nv)
```

```python
import contextlib
import os
import math
import numpy as np
import ml_dtypes
import concourse.bass as bass
import concourse.mybir as mybir
from concourse.bass_utils import run_bass_kernel_spmd

F32 = mybir.dt.float32
BF16 = mybir.dt.bfloat16
AF = mybir.ActivationFunctionType
ALU = mybir.AluOpType
AX = mybir.AxisListType

ENGS = ("pe", "act", "dve", "pool", "sp")
D = 1024
TS, TP, T = 4096, 512, 4608
NT = 9
NTT = 36
DFF = 2816
NCH_FF = 22
EPS = 1e-6
SUBLN_EPS = 1e-5
NKEY = 5120


class Res:
    __slots__ = ("w", "r", "name", "excl")

    def __init__(self, name="", excl=False):
        self.w = {}
        self.r = {}
        self.name = name
        self.excl = excl


class Prog:
    def __init__(self, nc, arena_words):
        self.nc = nc
        self.q = {e: [] for e in ENGS}
        self.known = {e: {} for e in ENGS}
        self.dma_cnt = {}
        self.stack = contextlib.ExitStack()
        self.arena = self.stack.enter_context(nc.sbuf_tensor("arena", [128, arena_words], F32))
        self.arena_words = arena_words
        self.top = 0
        self.live = []
        self.retired = []
        self.nsem = 0
        self.keymap = {}
        self.keyres = {}

    def alloc(self, shape, dt, name=""):
        esz = 4 if dt == F32 else 2
        free = 1
        for s in shape[1:]:
            free *= s
        words = (free * esz + 3) // 4
        words = (words + 7) // 8 * 8
        off = self.top
        assert off + words <= self.arena_words, f"SBUF arena overflow {name} {off + words}"
        self.top += words
        v = self.arena[0:shape[0], off:off + (free * esz) // 4]
        if dt != F32:
            v = v.bitcast(dt)
        if len(shape) == 3:
            v = v.rearrange("p (a b) -> p a b", b=shape[2])
        elif len(shape) == 4:
            v = v.rearrange("p (a b c) -> p a b c", b=shape[2], c=shape[3])
        r = Res(name)
        keep = []
        for (a, b, rr) in self.retired:
            if a < off + words and off < b:
                for k, val in rr.w.items():
                    if r.r.get(k, -1) < val:
                        r.r[k] = val
                for k, val in rr.r.items():
                    if r.r.get(k, -1) < val:
                        r.r[k] = val
                if a >= off and b <= off + words:
                    continue
            keep.append((a, b, rr))
        self.retired = keep
        self.live.append((off, off + words, r))
        return v, r

    def mark(self):
        return (self.top, len(self.live))

    def release(self, m):
        top, n = m
        self.retired.extend(self.live[n:])
        del self.live[n:]
        self.top = top

    def _add(self, eng, fn, reads, writes, acc_writes, own):
        deps = {}

        def upd(d):
            for k, v in d.items():
                if deps.get(k, -1) < v:
                    deps[k] = v
        for r in reads:
            upd(r.w)
            if r.excl:
                upd({k: v for k, v in r.r.items() if k != ("c", eng)})
        for w in writes:
            upd(w.w)
            upd(w.r)
        for w in acc_writes:
            upd(w.r)
            upd({k: v for k, v in w.w.items() if k != own})
        q = self.q[eng]
        idx = len(q)
        waits = []
        kn = self.known[eng]
        for k, v in deps.items():
            if k == ("c", eng):
                if eng == "pe":
                    continue
                vv = -1
                for r in reads:
                    x = r.w.get(k, -1)
                    if x > vv:
                        vv = x
                if vv < 0:
                    continue
                v = vv
            if k[0] == "d":
                v = self.dma_cnt[k[1]]
            if kn.get(k, -1) >= v:
                continue
            kn[k] = v
            waits.append((k, v))
            if k[0] == "c":
                self.q[k[1]][v][2] = True
        op = [fn, waits, False, None]
        q.append(op)
        return op, idx

    def op(self, eng, fn, reads=(), writes=(), acc_writes=()):
        op, idx = self._add(eng, fn, reads, writes, acc_writes, ("c", eng))
        k = ("c", eng)
        for r in reads:
            r.r[k] = idx
        for w in writes:
            w.w = {k: idx}
            w.r = {}
        for w in acc_writes:
            w.w[k] = idx
        return op

    def new_phase(self):
        self.keymap = {}

    def dma(self, eng, semkey, fn, reads=(), writes=(), acc_writes=()):
        km = self.keymap
        if semkey not in km:
            km[semkey] = f"g{len(km)}"
        semkey = km[semkey]
        kr = self.keyres.get(semkey)
        if kr is None:
            kr = self.keyres[semkey] = Res(semkey)
        writes = list(writes) + [kr]
        op, idx = self._add(eng, fn, reads, writes, acc_writes, ("d", semkey))
        c = self.dma_cnt.get(semkey, 0) + 1
        self.dma_cnt[semkey] = c
        op[3] = semkey
        k = ("d", semkey)
        for r in reads:
            r.r[k] = c
        for w in writes:
            w.w = {k: c}
            w.r = {}
        for w in acc_writes:
            w.w[k] = c
        return op

    def emit(self, final_keys):
        nc = self.nc
        st = self.stack
        csem = {e: st.enter_context(nc.semaphore(f"c_{e}")) for e in ENGS if e != "sp"}
        dsem = {k: st.enter_context(nc.semaphore(f"d_{i}")) for i, k in enumerate(self.dma_cnt)}
        cum = {}
        for e in ENGS:
            c = 0
            arr = []
            for o in self.q[e]:
                if o[2]:
                    c += 1
                arr.append(c)
            cum[e] = arr
        engobj = {"pe": "tensor", "act": "scalar", "dve": "vector", "pool": "gpsimd", "sp": "sync"}
        with nc.Block() as block:
            for e in ENGS:
                ops = self.q[e]

                def body(eng, e=e, ops=ops):
                    for fn, waits, sig, semkey in ops:
                        for k, v in waits:
                            if k[0] == "c":
                                eng.wait_ge(csem[k[1]], cum[k[1]][v])
                            else:
                                eng.wait_ge(dsem[k[1]], 16 * v)
                        ins = fn(eng)
                        if semkey is not None:
                            ins.then_inc(dsem[semkey], 16)
                        elif sig:
                            ins.then_inc(csem[e], 1)
                    if e == "sp":
                        for k in final_keys:
                            eng.wait_ge(dsem[k], 16 * self.dma_cnt[k])
                getattr(block, engobj[e])(body)


def _bf(a):
    return np.ascontiguousarray(a.astype(ml_dtypes.bfloat16))


def _dft_tables(L):
    N = 2 * L
    nf = L + 1
    nch = (nf + 127) // 128
    S = nch * 128
    a = np.arange(S, dtype=np.int64)
    m = (a[:, None] * a[None, :]) % N
    ang = 2.0 * np.pi * m.astype(np.float64) / N
    valid = (a[:, None] <= L) & (a[None, :] <= L)
    q = np.where(valid, np.cos(ang), 0.0)
    r = np.where(valid, np.sin(ang), 0.0)
    wf = np.where(a <= L, 2.0 / N, 0.0)
    wf[0] = 1.0 / N
    wf[L] = 1.0 / N
    wfc = wf.reshape(nch, 128).T.astype(np.float32)
    ntc = L // 128
    qf = q[:L].reshape(ntc, 128, nch, 128).transpose(2, 1, 0, 3)
    rf = r[:L].reshape(ntc, 128, nch, 128).transpose(2, 1, 0, 3)
    return (_bf(q.reshape(nch, 128, S)), _bf(r.reshape(nch, 128, S)), np.ascontiguousarray(wfc), nch, S, _bf(qf), _bf(rf))


def _filter_consts(L):
    pos = np.arange(L, dtype=np.float32)
    t = pos / np.float32(max(L - 1, 1))
    w = (np.float32(2.0 * math.pi) * pos / np.float32(L)).astype(np.float32)
    bands = np.linspace(1e-4, 15, 16, dtype=np.float32)
    z = np.concatenate([t[:, None], np.cos(w[:, None] * bands), -np.sin(w[:, None] * bands)], axis=-1)
    zT = np.ascontiguousarray(z.T.astype(np.float32))
    negt = np.ascontiguousarray((-t).reshape(L // 128, 128).T.astype(np.float32))
    return zT, negt


def _host_consts():
    c = {}
    tpos = np.arange(TS)
    rowpos = (tpos // 64).astype(np.float32)
    colpos = (tpos % 64).astype(np.float32)
    inv = (10000.0 ** (-np.arange(16, dtype=np.float32) / 16)).astype(np.float32)
    cos = np.zeros((128, TS), np.float32)
    sins = np.zeros((128, TS), np.float32)
    perm = np.zeros((128, 128), np.float32)
    for p in range(2):
        for a in range(2):
            posv = rowpos if a == 0 else colpos
            for hf in range(2):
                for f in range(16):
                    row = p * 64 + a * 32 + hf * 16 + f
                    ang = (posv * inv[f]).astype(np.float32)
                    cos[row] = np.cos(ang)
                    sins[row] = np.sin(ang) * (-1.0 if hf == 0 else 1.0)
                    other = p * 64 + a * 32 + (1 - hf) * 16 + f
                    perm[other, row] = 1.0
    c["rcos"] = cos
    c["rsin"] = sins
    c["perm"] = _bf(perm)
    c["identb"] = _bf(np.eye(128, dtype=np.float32))
    c["identf"] = np.eye(128, dtype=np.float32)
    c["onesf"] = np.ones((128, 128), np.float32)
    for nm, L in (("s", TS), ("p", 256)):
        q, r, wf, nch, S, qf, rf = _dft_tables(L)
        c["q" + nm], c["r" + nm], c["wf" + nm] = q, r, wf
        if nm == "s":
            c["qfs"], c["rfs"] = qf, rf
        zT, negt = _filter_consts(L)
        c["z" + nm], c["negt" + nm] = zT, negt
    deltas = np.abs(np.linspace(math.log(1e-2) / 1.5, math.log(1e-2) / 0.3, D, dtype=np.float32))
    c["delta"] = deltas.astype(np.float32)
    m0 = np.ones((128, 1), np.float32)
    m0[0, 0] = 0.0
    c["mask0"] = m0
    return c


W_SPECS = [
    ("ada_w", [4, D, 6 * D]), ("ada_b", [4, 6 * D]), ("norm1_g", [4, D]), ("norm2_g", [4, D]),
    ("attn_w_qkv", [2, D, 3 * D]), ("attn_lambda", [2, 4, 64]), ("attn_subln_g", [2, 128]),
    ("attn_w_o", [2, D, D]), ("hy_w_in", [2, D, 3 * D]), ("hy_b_in", [2, 3 * D]),
    ("hy_conv_w", [2, 3, 3 * D]), ("hy_conv_b", [2, 3 * D]), ("filt_w1", [2, 33, 64]),
    ("filt_b1", [2, 64]), ("filt_w2", [2, 64, 64]), ("filt_b2", [2, 64]), ("filt_w3", [2, 64, 2 * D]),
    ("filt_b3", [2, 2 * D]), ("filt_freq", [2, 64]), ("hy_skip", [2, D]), ("hy_w_out", [2, D, D]),
    ("hy_b_out", [2, D]), ("ffn_w_gu", [4, D, 2 * DFF]), ("ffn_w_down", [4, DFF, D]), ("final_g", [D]),
]
CONST_SPECS = [
    ("rcos", [128, TS], F32), ("rsin", [128, TS], F32), ("perm", [128, 128], BF16),
    ("identb", [128, 128], BF16), ("identf", [128, 128], F32), ("onesf", [128, 128], F32),
    ("qs", [33, 128, 4224], BF16), ("rs", [33, 128, 4224], BF16), ("wfs", [128, 33], F32),
    ("qfs", [33, 128, 32, 128], BF16), ("rfs", [33, 128, 32, 128], BF16),
    ("zs", [33, TS], F32), ("negts", [128, 32], F32),
    ("qp", [3, 128, 384], BF16), ("rp", [3, 128, 384], BF16), ("wfp", [128, 3], F32),
    ("zp", [33, 256], F32), ("negtp", [128, 2], F32),
    ("delta", [D], F32), ("mask0", [128, 1], F32),
]

GROUPS = [
    dict(name="s", tok0=0, T=TS, nseq=1, L=TS, rope=True, ncache=512, tiles=list(range(0, 8)), tt=list(range(0, 32))),
    dict(name="p", tok0=TS, T=TP, nseq=2, L=256, rope=False, ncache=0, tiles=[8], tt=list(range(32, 36))),
]


def build_program(stop_after=None, debug_outs=()):
    nc = bass.Bass("TRN2", target_bir_lowering=False)
    I = {}
    I["x"] = nc.dram_tensor("x", [T, D], F32, kind="ExternalInput").ap()
    I["ck"] = nc.dram_tensor("ck", [2, 512, D], F32, kind="ExternalInput").ap()
    I["cv"] = nc.dram_tensor("cv", [2, 512, D], F32, kind="ExternalInput").ap()
    I["cvec"] = nc.dram_tensor("cvec", [2, D], F32, kind="ExternalInput").ap()
    for nm, shp in W_SPECS:
        I[nm] = nc.dram_tensor(nm, shp, F32, kind="ExternalInput").ap()
    for nm, shp, dt in CONST_SPECS:
        I[nm] = nc.dram_tensor(nm, shp, dt, kind="ExternalInput").ap()
    O = {}
    O["y"] = nc.dram_tensor("y", [T, D], F32, kind="ExternalOutput").ap()
    O["nk"] = nc.dram_tensor("nk", [2, 2, 256, D], F32, kind="ExternalOutput").ap()
    O["nv"] = nc.dram_tensor("nv", [2, 2, 256, D], F32, kind="ExternalOutput").ap()

    def scratch(nm, shp, dt):
        kind = "ExternalOutput" if nm in debug_outs else "Internal"
        return nc.dram_tensor(nm, shp, dt, kind=kind).ap()
    S = {}
    S["X"] = scratch("X", [T, D], F32)
    S["MODS"] = scratch("MODS", [2, 128, 6 * D], F32)
    S["KT"] = scratch("KT", [8, 128, NKEY], BF16)
    S["VS"] = scratch("VS", [NKEY, D], BF16)
    S["QT"] = scratch("QT", [8, 128, T], BF16)
    S["MID"] = scratch("MID", [NCH_FF, 128, T], BF16)
    S["VVT"] = scratch("VVT", [8, 128, T], F32)
    S["X0T"] = scratch("X0T", [8, 128, T], F32)
    S["VVTOK"] = scratch("VVTOK", [T, D], BF16)
    S["HSs"] = scratch("HSs", [TS, D], BF16)
    S["HDs"] = scratch("HDs", [TS, D], BF16)
    S["HSp"] = scratch("HSp", [256, D], BF16)
    S["HDp"] = scratch("HDp", [256, D], BF16)
    S["YF"] = scratch("YF", [2, 33, 128, 2, 512], BF16)
    S["HTD"] = scratch("HTD", [128, 8, T], BF16)

    ARENA_WORDS = 49152
    P = Prog(nc, ARENA_WORDS)
    PS = P.stack.enter_context(nc.psum_tensor("ps", [128, 4096], F32))
    psr = [Res(f"bank{b}", excl=True) for b in range(8)]

    def bank(b, n=512):
        return PS[:, b * 512:b * 512 + n]

    def bankb(b):
        return PS[:, b * 512:(b + 1) * 512].bitcast(BF16)

    R = {}
    R["X"] = [[Res(f"X{i}a"), Res(f"X{i}b")] for i in range(NTT)]
    R["MODS"] = [Res(), Res()]
    R["KT"] = Res()
    R["VS"] = Res()
    R["QT"] = Res()
    R["MID"] = [Res() for _ in range(NT)]
    R["VVT"] = Res()
    R["X0T"] = Res()
    R["VVTOK"] = Res()
    R["HS"] = Res()
    R["YF"] = Res()
    R["OUT"] = Res()
    R["HTD"] = Res()
    semctr = [0]

    def sk(prefix):
        semctr[0] += 1
        return f"{prefix}{semctr[0]}"

    hT, _ = P.alloc([128, 8, T], BF16, "hT")
    hTr = [Res(f"hT{i}") for i in range(NT)]
    identb, identb_r = P.alloc([128, 128], BF16, "identb")
    identf, identf_r = P.alloc([128, 128], F32, "identf")
    onesf, onesf_r = P.alloc([128, 128], F32, "onesf")
    SIL, SIL_r = P.alloc([128, 2, 8, 128], BF16, "SIL")
    P.dma("sp", "c_identb", lambda e: e.dma_start(out=identb, in_=I["identb"]), writes=[identb_r])
    P.dma("sp", "c_identf", lambda e: e.dma_start(out=identf, in_=I["identf"]), writes=[identf_r])
    P.dma("sp", "c_onesf", lambda e: e.dma_start(out=onesf, in_=I["onesf"]), writes=[onesf_r])

    def load_cols(dst, dst_r, col0, vec, nchunks, key):
        P.dma("sp", key, lambda e: e.dma_start(
            out=dst[:, col0:col0 + nchunks], in_=vec.rearrange("(c p) -> p c", p=128), allow_slow_non_contiguous=True),
            acc_writes=[dst_r])

    def bcast_rows(vec1d, n):
        return vec1d.rearrange("(o n) -> o n", o=1).broadcast_to([128, n])

    m0 = P.mark()
    ccol, ccol_r = P.alloc([128, 16], F32, "ccol")
    scol, scol_r = P.alloc([128, 16], F32, "scol")
    for g in range(2):
        load_cols(ccol, ccol_r, g * 8, I["cvec"][g], 8, "ccol")
    P.op("act", lambda e: e.activation(out=scol, in_=ccol, func=AF.Silu), reads=[ccol_r], writes=[scol_r])
    for g in range(2):
        for kc in range(8):
            P.op("dve", lambda e, g=g, kc=kc: e.tensor_scalar(
                out=SIL[:, g, kc, :], in0=onesf, scalar1=scol[:, g * 8 + kc:g * 8 + kc + 1], scalar2=None,
                op0=ALU.mult), reads=[scol_r, onesf_r], acc_writes=[SIL_r])
    P.release(m0)

    def phase_mod(i):
        P.new_phase()
        m = P.mark()
        adab, adab_r = P.alloc([128, 6 * D], F32, "adab")
        mod = [P.alloc([128, 6 * D], F32, f"mod{g}") for g in range(2)]
        gn = [P.alloc([128, D], F32, f"gn{k}") for k in range(2)]
        wch = [P.alloc([128, 8, 1024], BF16, f"adw{k}") for k in range(2)]
        P.dma("sp", "adab", lambda e: e.dma_start(out=adab, in_=bcast_rows(I["ada_b"][i], 6 * D)), writes=[adab_r])
        P.dma("sp", "gn0", lambda e: e.dma_start(out=gn[0][0], in_=bcast_rows(I["norm1_g"][i], D)), writes=[gn[0][1]])
        P.dma("sp", "gn1", lambda e: e.dma_start(out=gn[1][0], in_=bcast_rows(I["norm2_g"][i], D)), writes=[gn[1][1]])
        for n in range(6):
            w, wr = wch[n % 2]
            P.dma("pool", f"adw{n % 2}", lambda e, w=w, n=n: e.dma_start(
                out=w, in_=I["ada_w"][i][:, n * 1024:(n + 1) * 1024].rearrange("(kc p) n -> p kc n", p=128)), writes=[wr])
            for g in range(2):
                for sub in range(2):
                    b = (n * 4 + g * 2 + sub) % 8
                    c0 = n * 1024 + sub * 512
                    for kc in range(8):
                        P.op("pe", lambda e, w=w, g=g, kc=kc, b=b, sub=sub: e.matmul(
                            bank(b), lhsT=SIL[:, g, kc, :], rhs=w[:, kc, sub * 512:(sub + 1) * 512], start=(kc == 0), stop=(kc == 7)),
                            reads=[SIL_r, wr], writes=[psr[b]] if kc == 0 else (), acc_writes=[psr[b]] if kc else ())
                    P.op("dve", lambda e, g=g, c0=c0, b=b: e.tensor_tensor(
                        out=mod[g][0][:, c0:c0 + 512], in0=bank(b), in1=adab[:, c0:c0 + 512], op=ALU.add),
                        reads=[psr[b], adab_r], acc_writes=[mod[g][1]])
        for g in range(2):
            for k, off in ((0, D), (1, 4 * D)):
                P.op("dve", lambda e, g=g, k=k, off=off: e.scalar_tensor_tensor(
                    out=mod[g][0][:, off:off + D], in0=mod[g][0][:, off:off + D], scalar=1.0, in1=gn[k][0],
                    op0=ALU.add, op1=ALU.mult), reads=[mod[g][1], gn[k][1]], acc_writes=[mod[g][1]])
            P.dma("sp", f"mods{g}", lambda e, g=g: e.dma_start(out=S["MODS"][g], in_=mod[g][0]),
                  reads=[mod[g][1]], writes=[R["MODS"][g]])
        P.release(m)

    def load_mod(off, key):
        out = []
        for g in range(2):
            t, r = P.alloc([128, D], F32, f"{key}{g}")
            P.dma("sp", f"ldm_{key}{g}", lambda e, t=t, g=g: e.dma_start(out=t, in_=S["MODS"][g][:, off:off + D]),
                  reads=[R["MODS"][g]], writes=[r])
            out.append((t, r))
        return out

    def rstd_ops(ss, ss_r, rs, rs_r, n, eps):
        P.op("act", lambda e: e.activation(out=rs, in_=ss, func=AF.Ln, scale=1.0 / n, bias=float(eps)),
             reads=[ss_r], writes=[rs_r])
        P.op("act", lambda e: e.activation(out=rs, in_=rs, func=AF.Exp, scale=-0.5),
             reads=[rs_r], writes=[rs_r])

    def phase_norm(offA, offB, final=False, xsrc=None):
        P.new_phase()
        m = P.mark()
        if final:
            gt, gr = P.alloc([128, D], F32, "fing")
            P.dma("sp", "fing", lambda e: e.dma_start(out=gt, in_=bcast_rows(I["final_g"], D)), writes=[gr])
            A = [(gt, gr), (gt, gr)]
            B = None
        else:
            A = load_mod(offA, "nA")
            B = load_mod(offB, "nB")
        xt = [P.alloc([128, D], F32, f"nx{k}") for k in range(3)]
        x2 = [P.alloc([128, D], F32, f"nx2{k}") for k in range(2)]
        hb = [P.alloc([128, D], BF16, f"nhb{k}") for k in range(2)]
        junk, junk_r = P.alloc([128, D], BF16, "njunk")
        ss = [P.alloc([128, 1], F32, f"nss{k}") for k in range(2)]
        rs = [P.alloc([128, 1], F32, f"nrs{k}") for k in range(2)]
        pend_copy = []

        def flush_copy():
            b, t_ = pend_copy.pop(0)
            P.op("act", lambda e: e.copy(out=hT[:, :, t_ * 128:(t_ + 1) * 128],
                                         in_=bankb(b).rearrange("p (k t) -> p k t", t=128)),
                 reads=[psr[b]], acc_writes=[hTr[t_ // 4]])
        for tt in range(NTT):
            g = 0 if tt < 32 else 1
            x, xr = xt[tt % 3]
            y, yr = x2[tt % 2]
            h, hr = hb[tt % 2]
            s_, s_r = ss[tt % 2]
            r_, r_r = rs[tt % 2]
            xs_ = S["X"] if xsrc is None else xsrc
            P.dma("sp", f"nx{tt % 3}", lambda e, x=x, tt=tt, xs_=xs_: e.dma_start(out=x, in_=xs_[tt * 128:(tt + 1) * 128, :]),
                  reads=R["X"][tt] if xsrc is None else (), writes=[xr])
            P.op("act", lambda e, x=x, s_=s_: e.activation(out=junk, in_=x, func=AF.Square, accum_out=s_),
                 reads=[xr], writes=[junk_r, s_r])
            rstd_ops(s_, s_r, r_, r_r, D, EPS)
            if pend_copy:
                flush_copy()
            if final:
                P.op("dve", lambda e, x=x, y=y, r_=r_, g=g: e.scalar_tensor_tensor(
                    out=y, in0=x, scalar=r_, in1=A[g][0], op0=ALU.mult, op1=ALU.mult),
                    reads=[xr, r_r, A[g][1]], writes=[yr])
                P.dma("sp", f"fo{tt % 2}", lambda e, y=y, tt=tt: e.dma_start(out=O["y"][tt * 128:(tt + 1) * 128, :], in_=y),
                      reads=[yr], acc_writes=[R["OUT"]])
                continue
            P.op("dve", lambda e, x=x, y=y, r_=r_, g=g: e.scalar_tensor_tensor(
                out=y, in0=x, scalar=r_, in1=A[g][0], op0=ALU.mult, op1=ALU.mult),
                reads=[xr, r_r, A[g][1]], writes=[yr])
            P.op("pool", lambda e, y=y, h=h, g=g: e.tensor_tensor(out=h, in0=y, in1=B[g][0], op=ALU.add),
                 reads=[yr, B[g][1]], writes=[hr])
            b = tt % 2
            for kc in range(8):
                P.op("pe", lambda e, h=h, kc=kc, b=b: e.transpose(bankb(b)[:, kc * 128:(kc + 1) * 128], h[:, kc * 128:(kc + 1) * 128], identb),
                     reads=[hr, identb_r], writes=[psr[b]] if kc == 0 else (), acc_writes=[psr[b]] if kc else ())
            pend_copy.append((b, tt))
        while pend_copy:
            flush_copy()
        P.release(m)

    def stream_w(dst, dst_r, key, src_ap):
        P.dma("pool", key, lambda e: e.dma_start(out=dst, in_=src_ap.rearrange("(kc p) n -> p kc n", p=128)), writes=[dst_r])

    def mm_wstat(b, w, wr, tile, ncols=512):
        for kc in range(8):
            P.op("pe", lambda e, kc=kc: e.matmul(bank(b, ncols), lhsT=w[:, kc, :], rhs=hT[:, kc, tile * 512:tile * 512 + ncols],
                                                 start=(kc == 0), stop=(kc == 7)),
                 reads=[wr, hTr[tile]], writes=[psr[b]] if kc == 0 else (), acc_writes=[psr[b]] if kc else ())

    def mm_tstat(b, w, wr, tt):
        for kc in range(8):
            P.op("pe", lambda e, kc=kc: e.matmul(bank(b), lhsT=hT[:, kc, tt * 128:(tt + 1) * 128], rhs=w[:, kc, :],
                                                 start=(kc == 0), stop=(kc == 7)),
                 reads=[wr, hTr[tt // 4]], writes=[psr[b]] if kc == 0 else (), acc_writes=[psr[b]] if kc else ())

    def phase_qkv(j):
        P.new_phase()
        m = P.mark()
        rcos, rcos_r = P.alloc([128, TS], F32, "rcos")
        rsin, rsin_r = P.alloc([128, TS], F32, "rsin")
        perm, perm_r = P.alloc([128, 128], BF16, "perm")
        P.dma("sp", "rcos", lambda e: e.dma_start(out=rcos, in_=I["rcos"]), writes=[rcos_r])
        P.dma("sp", "rsin", lambda e: e.dma_start(out=rsin, in_=I["rsin"]), writes=[rsin_r])
        P.dma("sp", "perm", lambda e: e.dma_start(out=perm, in_=I["perm"]), writes=[perm_r])
        ckt = [P.alloc([128, D], BF16, f"ckt{k}") for k in range(2)]
        kst = [P.alloc([128, 8, 128], BF16, f"kst{k}") for k in range(2)]
        cvt = [P.alloc([128, D], BF16, f"cvt{k}") for k in range(2)]
        for tk in range(4):
            c_, c_r = ckt[tk % 2]
            k_, k_r = kst[tk % 2]
            v_, v_r = cvt[tk % 2]
            P.dma("pool", f"ckt{tk % 2}", lambda e, c_=c_, tk=tk: e.dma_start(out=c_, in_=I["ck"][j, tk * 128:(tk + 1) * 128, :]), writes=[c_r])
            b = tk % 2
            for h in range(8):
                P.op("pe", lambda e, c_=c_, h=h, b=b: e.transpose(bankb(b)[:, h * 128:(h + 1) * 128], c_[:, h * 128:(h + 1) * 128], identb),
                     reads=[c_r, identb_r], writes=[psr[b]] if h == 0 else (), acc_writes=[psr[b]] if h else ())
            P.op("act", lambda e, k_=k_, b=b: e.copy(out=k_, in_=bankb(b).rearrange("p (k t) -> p k t", t=128)), reads=[psr[b]], writes=[k_r])
            P.dma("sp", f"kst{tk % 2}", lambda e, k_=k_, tk=tk: e.dma_start(
                out=S["KT"][:, :, tk * 128:(tk + 1) * 128].rearrange("h p c -> p h c"), in_=k_), reads=[k_r], acc_writes=[R["KT"]])
            P.dma("pool", f"cvt{tk % 2}", lambda e, v_=v_, tk=tk: e.dma_start(out=v_, in_=I["cv"][j, tk * 128:(tk + 1) * 128, :]), writes=[v_r])
            P.dma("sp", f"cvo{tk % 2}", lambda e, v_=v_, tk=tk: e.dma_start(out=S["VS"][tk * 128:(tk + 1) * 128, :], in_=v_),
                  reads=[v_r], acc_writes=[R["VS"]])
        if stop_after == f"qkv{2 * j}a":
            raise Stop()
        wq = [P.alloc([128, 8, 128], BF16, f"wq{k}") for k in range(3)]
        qb = [P.alloc([128, 512], BF16, f"qb{k}") for k in range(3)]
        t1 = [P.alloc([128, 512], F32, f"t1{k}") for k in range(3)]
        t2 = [P.alloc([128, 512], F32, f"t2{k}") for k in range(3)]
        qr = [P.alloc([128, 512], BF16, f"qr{k}") for k in range(3)]
        pend_tail = []

        def store(q_, q_r, isk, h, tile, key):
            if isk:
                P.dma("sp", key, lambda e: e.dma_start(
                    out=S["KT"][h, :, 512 + tile * 512: 512 + (tile + 1) * 512], in_=q_), reads=[q_r], acc_writes=[R["KT"]])
            else:
                P.dma("sp", key, lambda e: e.dma_start(
                    out=S["QT"][h, :, tile * 512:(tile + 1) * 512], in_=q_), reads=[q_r], acc_writes=[R["QT"]])
        it = 0
        for ch in range(16):
            isk, h = ch // 8, ch % 8
            w, wr = wq[ch % 3]
            stream_w(w, wr, f"wq{ch % 3}", I["attn_w_qkv"][j][:, isk * D + h * 128: isk * D + (h + 1) * 128])
            for tile in range(NT):
                bA = (it % 3) * 2
                bB = bA + 1
                q_, q_r = qr[it % 3]
                mm_wstat(bA, w, wr, tile)
                while pend_tail:
                    pend_tail.pop(0)()
                if tile < 8 and not os.environ.get('NOROPE'):
                    qq, qq_r = qb[it % 3]
                    a1, a1r = t1[it % 3]
                    a2, a2r = t2[it % 3]
                    P.op("act", lambda e, qq=qq, bA=bA: e.copy(out=qq, in_=bank(bA)), reads=[psr[bA]], writes=[qq_r])
                    P.op("dve", lambda e, a1=a1, bA=bA, tile=tile: e.tensor_tensor(
                        out=a1, in0=bank(bA), in1=rcos[:, tile * 512:(tile + 1) * 512], op=ALU.mult),
                        reads=[psr[bA], rcos_r], writes=[a1r])

                    def tail(qq=qq, qq_r=qq_r, bB=bB, a1=a1, a1r=a1r, a2=a2, a2r=a2r, q_=q_, q_r=q_r, tile=tile, isk=isk, h=h, key=f"qro{it % 3}"):
                        P.op("pe", lambda e: e.matmul(bank(bB), lhsT=perm, rhs=qq, start=True, stop=True),
                             reads=[perm_r, qq_r], writes=[psr[bB]])
                        P.op("dve", lambda e: e.tensor_tensor(
                            out=a2, in0=bank(bB), in1=rsin[:, tile * 512:(tile + 1) * 512], op=ALU.mult),
                            reads=[psr[bB], rsin_r], writes=[a2r])
                        P.op("pool", lambda e: e.tensor_tensor(out=q_, in0=a1, in1=a2, op=ALU.add),
                             reads=[a1r, a2r], writes=[q_r])
                        store(q_, q_r, isk, h, tile, key)
                    pend_tail.append(tail)
                else:
                    P.op("act", lambda e, q_=q_, bA=bA: e.copy(out=q_, in_=bank(bA)), reads=[psr[bA]], writes=[q_r])
                    pend_tail.append(lambda q_=q_, q_r=q_r, isk=isk, h=h, tile=tile, key=f"qro{it % 3}": store(q_, q_r, isk, h, tile, key))
                it += 1
        while pend_tail:
            pend_tail.pop(0)()
        if stop_after == f"qkv{2 * j}b":
            raise Stop()
        wv = [P.alloc([128, 8, 512], BF16, f"wv{k}") for k in range(2)]
        vb = [P.alloc([128, 512], BF16, f"vb{k}") for k in range(3)]
        vf = [P.alloc([128, 512], F32, f"vf{k}") for k in range(2)]
        it = 0
        for kind in ("v", "k"):
            for half in range(2):
                w, wr = wv[(it // 64) % 2]
                col0 = (2 * D if kind == "v" else D) + half * 512
                w, wr = wv[half]
                stream_w(w, wr, f"wv{half}{kind}", I["attn_w_qkv"][j][:, col0:col0 + 512])
                tts = range(NTT) if kind == "v" else range(32, 36)
                for tt in tts:
                    b = 6 + it % 2
                    mm_tstat(b, w, wr, tt)
                    if kind == "v":
                        v_, v_r = vb[it % 3]
                        P.op("act", lambda e, v_=v_, b=b: e.copy(out=v_, in_=bank(b)), reads=[psr[b]], writes=[v_r])
                        P.dma("sp", f"vbo{it % 3}", lambda e, v_=v_, tt=tt, half=half: e.dma_start(
                            out=S["VS"][512 + tt * 128: 512 + (tt + 1) * 128, half * 512:(half + 1) * 512], in_=v_),
                            reads=[v_r], acc_writes=[R["VS"]])
                    if tt >= 32:
                        f_, f_r = vf[it % 2]
                        P.op("dve", lambda e, f_=f_, b=b: e.tensor_copy(out=f_, in_=bank(b)), reads=[psr[b]], writes=[f_r])
                        s, r0 = (tt - 32) // 2, ((tt - 32) % 2) * 128
                        dst = O["nv"] if kind == "v" else O["nk"]
                        P.dma("sp", f"vfo{it % 2}", lambda e, f_=f_, s=s, r0=r0, half=half, dst=dst: e.dma_start(
                            out=dst[s, j, r0:r0 + 128, half * 512:(half + 1) * 512], in_=f_), reads=[f_r], acc_writes=[R["OUT"]])
                    it += 1
        P.release(m)

    def phase_attn(j, i):
        P.new_phase()
        m = P.mark()
        lam_init = 0.8 - 0.6 * math.exp(-0.3 * i)
        lp, lp_r = P.alloc([128, 4, 64], F32, "lp")
        lpp, lpp_r = P.alloc([128, 2, 64], F32, "lpp")
        lsum, lsum_r = P.alloc([128, 2], F32, "lsum")
        lexp, lexp_r = P.alloc([128, 2], F32, "lexp")
        nlam, nlam_r = P.alloc([128, 1], F32, "nlam")
        gsub, gsub_r = P.alloc([128, 128], F32, "gsub")
        P.dma("sp", "lp", lambda e: e.dma_start(out=lp, in_=I["attn_lambda"][j].rearrange("(o a) b -> o a b", o=1).broadcast_to([128, 4, 64])), writes=[lp_r])
        P.dma("sp", "gsub", lambda e: e.dma_start(out=gsub, in_=bcast_rows(I["attn_subln_g"][j], 128)), writes=[gsub_r])
        P.op("dve", lambda e: e.tensor_scalar(out=gsub, in0=gsub, scalar1=1.0 - lam_init, scalar2=None, op0=ALU.mult),
             reads=[gsub_r], writes=[gsub_r])
        for k in range(2):
            P.op("dve", lambda e, k=k: e.tensor_tensor(out=lpp[:, k, :], in0=lp[:, 2 * k, :], in1=lp[:, 2 * k + 1, :], op=ALU.mult),
                 reads=[lp_r], acc_writes=[lpp_r])
        P.op("dve", lambda e: e.tensor_reduce(out=lsum, in_=lpp, axis=AX.X, op=ALU.add), reads=[lpp_r], writes=[lsum_r])
        P.op("act", lambda e: e.activation(out=lexp, in_=lsum, func=AF.Exp), reads=[lsum_r], writes=[lexp_r])
        P.op("dve", lambda e: e.tensor_tensor(out=nlam, in0=lexp[:, 1:2], in1=lexp[:, 0:1], op=ALU.subtract), reads=[lexp_r], writes=[nlam_r])
        P.op("dve", lambda e: e.tensor_scalar(out=nlam, in0=nlam, scalar1=-lam_init, scalar2=None, op0=ALU.add), reads=[nlam_r], writes=[nlam_r])

        kTb = [P.alloc([128, 4608], BF16, f"kTh{k}") for k in range(2)]
        vhb = [P.alloc([128, 36, 132], BF16, f"vh{k}") for k in range(2)]
        for k in range(2):
            P.op("pool", lambda e, k=k: e.memset(vhb[k][0], 1.0), writes=[vhb[k][1]])
        qtb = [P.alloc([128, 2, 256], BF16, f"qt{k}") for k in range(3)]
        for k in range(3):
            P.op("pool", lambda e, k=k: e.memset(qtb[k][0], 0.0), writes=[qtb[k][1]])
        Pb = [P.alloc([128, 2, 256], BF16, f"P{k}") for k in range(3)]
        osb = [P.alloc([128, 128], F32, f"o{k}") for k in range(2)]
        obb = [P.alloc([128, 128], BF16, f"ob{k}") for k in range(2)]
        rrb = [P.alloc([128, 2], F32, f"rr{k}") for k in range(2)]
        ssb = [P.alloc([128, 1], F32, f"ss{k}") for k in range(2)]
        rsb = [P.alloc([128, 1], F32, f"rs{k}") for k in range(2)]
        junk, junk_r = P.alloc([128, 128], BF16, "ajunk")
        accs = [P.alloc([128, 4, 132], F32, f"accs{k}") for k in range(2)]
        steps = []
        hcount = 0
        for G in GROUPS:
            for s in range(G["nseq"]):
                L = G["L"]
                nk = G["ncache"] + L
                nkc = nk // 128
                kcol0 = 0 if G["name"] == "s" else 4608 + s * 256
                qtok0 = G["tok0"] + s * L
                for h in range(8):
                    for qb_ in range(L // 256):
                        for c in range(nkc):
                            steps.append(dict(h=h, hid=hcount, q0=qtok0 + qb_ * 256, c=c, nkc=nkc, nk=nk, kcol0=kcol0,
                                              newh=(qb_ == 0 and c == 0), newq=(c == 0)))
                    hcount += 1
        qcount = [0]
        ecount = [0]
        cur = {}

        heads = {st["hid"]: st for st in steps if st["newh"]}
        loaded = set()

        def load_head(hs_):
            kT, kT_r = kTb[hs_["hid"] % 2]
            vh, vh_r = vhb[hs_["hid"] % 2]
            nk, nkc, kcol0, hh = hs_["nk"], hs_["nkc"], hs_["kcol0"], hs_["h"]
            P.dma("sp", f"kTh{hs_['hid'] % 2}", lambda e: e.dma_start(
                out=kT[:, 0:nk], in_=S["KT"][hh, :, kcol0:kcol0 + nk]), reads=[R["KT"]], writes=[kT_r])
            P.dma("sp", f"vh{hs_['hid'] % 2}", lambda e: e.dma_start(
                out=vh[:, 0:nkc, 0:128], in_=S["VS"][kcol0:kcol0 + nk, hh * 128:(hh + 1) * 128].rearrange("(c p) e -> p c e", p=128)),
                reads=[R["VS"]], acc_writes=[vh_r])

        def stage_a(i, st):
            h = st["h"]
            if st["newh"]:
                if st["hid"] not in loaded:
                    loaded.add(st["hid"])
                    load_head(st)
                nxt = heads.get(st["hid"] + 1)
                if nxt is not None and st["nkc"] >= 12:
                    def pf(nxt=nxt):
                        if nxt["hid"] not in loaded:
                            loaded.add(nxt["hid"])
                            load_head(nxt)
                    deferred.append((i + LA, pf))
                cur["kT"], cur["vh"] = kTb[st["hid"] % 2], vhb[st["hid"] % 2]
            if st["newq"]:
                qt, qt_r = qtb[qcount[0] % 3]
                q0 = st["q0"]
                for mp in range(2):
                    P.dma("sp", f"qt{qcount[0] % 3}_{mp}", lambda e, mp=mp: e.dma_start(
                        out=qt[mp * 64:(mp + 1) * 64, mp, :], in_=S["QT"][h, mp * 64:(mp + 1) * 64, q0:q0 + 256]),
                        reads=[R["QT"]], acc_writes=[qt_r])
                qcount[0] += 1
                cur["qt"] = (qt, qt_r)
            kT, kT_r = cur["kT"]
            qt, qt_r = cur["qt"]
            st["vh"] = cur["vh"]
            c = st["c"]
            sb_ = i % 3
            Pt, Pt_r = Pb[i % 3]
            st["P"] = (Pt, Pt_r)
            P.op("pe", lambda e: e.matmul(
                bank(sb_), lhsT=kT[:, c * 128:(c + 1) * 128],
                rhs=qt.rearrange("p m q -> p (m q)"), start=True, stop=True),
                reads=[kT_r, qt_r], writes=[psr[sb_]])
            P.op("act", lambda e: e.activation(
                out=Pt, in_=bank(sb_).rearrange("p (m q) -> p m q", q=256), func=AF.Exp, scale=0.125),
                reads=[psr[sb_]], writes=[Pt_r])

        def stage_b(st):
            Pt, Pt_r = st["P"]
            vh, vh_r = st["vh"]
            c, nkc, h = st["c"], st["nkc"], st["h"]
            for qs in range(2):
                for mp in range(2):
                    ab = 4 + qs * 2 + mp
                    P.op("pe", lambda e, qs=qs, mp=mp, ab=ab: e.matmul(
                        bank(ab, 129), lhsT=Pt[:, mp, qs * 128:(qs + 1) * 128], rhs=vh[:, c, 0:129],
                        start=(c == 0), stop=(c == nkc - 1)),
                        reads=[Pt_r, vh_r], writes=[psr[ab]] if c == 0 else (), acc_writes=[psr[ab]] if c else ())
            if c != nkc - 1:
                return
            if nkc < 12:
                run_deferred(10 ** 9)
            ac, ac_r = accs[ecount[0] % 2]
            P.op("dve", lambda e: e.tensor_copy(out=ac[:, :, 0:129], in_=PS[:, 4 * 512:8 * 512].rearrange("p (b c) -> p b c", c=512)[:, :, 0:129]),
                 reads=[psr[4], psr[5], psr[6], psr[7]], writes=[ac_r])
            for qs in range(2):
                a0, a1 = qs * 2, qs * 2 + 1
                o_, o_r = osb[ecount[0] % 2]
                ob, ob_r = obb[ecount[0] % 2]
                rr, rr_r = rrb[ecount[0] % 2]
                s_, s_r = ssb[ecount[0] % 2]
                r_, r_r = rsb[ecount[0] % 2]
                ecount[0] += 1
                tok = st["q0"] + qs * 128
                P.op("dve", lambda e, rr=rr, a0=a0: e.reciprocal(out=rr, in_=ac[:, a0:a0 + 2, 128]),
                     reads=[ac_r], writes=[rr_r])
                P.op("dve", lambda e, rr=rr: e.tensor_tensor(out=rr[:, 1:2], in0=rr[:, 1:2], in1=nlam, op=ALU.mult),
                     reads=[rr_r, nlam_r], writes=[rr_r])
                P.op("dve", lambda e, o_=o_, rr=rr, a0=a0: e.tensor_scalar(
                    out=o_, in0=ac[:, a0, 0:128], scalar1=rr[:, 0:1], scalar2=None, op0=ALU.mult),
                    reads=[ac_r, rr_r], writes=[o_r])
                P.op("dve", lambda e, o_=o_, rr=rr, a1=a1: e.scalar_tensor_tensor(
                    out=o_, in0=ac[:, a1, 0:128], scalar=rr[:, 1:2], in1=o_, op0=ALU.mult, op1=ALU.add),
                    reads=[ac_r, rr_r, o_r], writes=[o_r])
                P.op("dve", lambda e, o_=o_, s_=s_: e.scalar_tensor_tensor(
                    out=junk, in0=o_, scalar=1.0, in1=o_, op0=ALU.mult, op1=ALU.mult, accum_out=s_),
                    reads=[o_r], writes=[junk_r, s_r])

                def e2(o_=o_, o_r=o_r, ob=ob, ob_r=ob_r, s_=s_, s_r=s_r, r_=r_, r_r=r_r):
                    rstd_ops(s_, s_r, r_, r_r, 128, SUBLN_EPS)
                    P.op("dve", lambda e: e.scalar_tensor_tensor(
                        out=ob, in0=o_, scalar=r_, in1=gsub, op0=ALU.mult, op1=ALU.mult),
                        reads=[o_r, r_r, gsub_r], writes=[ob_r])

                def e4(ob=ob, ob_r=ob_r):
                    P.op("pe", lambda e: e.transpose(bankb(3)[:, 0:128], ob, identb), reads=[ob_r, identb_r], writes=[psr[3]])

                def e5(tok=tok, h=h):
                    P.op("dve", lambda e: e.tensor_copy(out=hT[:, h, tok:tok + 128], in_=bankb(3)[:, 0:128]),
                         reads=[psr[3]], acc_writes=[hTr[tok // 512]])
                if nkc >= 12:
                    base = st["idx"] + LA
                    deferred.append((base + 2 + qs, e2))
                    deferred.append((base + 4 + 3 * qs, e4))
                    deferred.append((base + 6 + 3 * qs, e5))
                else:
                    e2()
                    e4()
                    e5()

        LA = 2
        deferred = []

        def run_deferred(now):
            keep = []
            for due, fn in deferred:
                if due <= now:
                    fn()
                else:
                    keep.append((due, fn))
            deferred[:] = keep
        for i, st in enumerate(steps):
            st["idx"] = i
            stage_a(i, st)
            if i >= LA:
                stage_b(steps[i - LA])
            run_deferred(i)
        for st in steps[-LA:]:
            stage_b(st)
        run_deferred(10 ** 9)
        P.release(m)

    def phase_outproj(wsrc, bias_vec, offG, xsrc=None):
        P.new_phase()
        m = P.mark()
        G_ = load_mod(offG, "oG")
        if bias_vec is not None:
            brep, brep_r = P.alloc([128, D], F32, "obias")
            P.dma("sp", "obias", lambda e: e.dma_start(out=brep, in_=bcast_rows(bias_vec, D)), writes=[brep_r])
        wo = [P.alloc([128, 8, 512], BF16, f"wo{k}") for k in range(2)]
        xt = [P.alloc([128, 512], F32, f"ox{k}") for k in range(4)]
        tb = [P.alloc([128, 512], F32, f"ot{k}") for k in range(2)]
        for half in range(2):
            stream_w(wo[half][0], wo[half][1], f"wo{half}", wsrc[:, half * 512:(half + 1) * 512])
        iters = [(half, tt) for half in range(2) for tt in range(NTT)]

        def load_x(k):
            half, tt = iters[k]
            x, xr = xt[k % 4]
            xs_ = S["X"] if xsrc is None else xsrc
            P.dma("sp", f"ox{k % 4}", lambda e: e.dma_start(
                out=x, in_=xs_[tt * 128:(tt + 1) * 128, half * 512:(half + 1) * 512]),
                reads=[R["X"][tt][half]] if xsrc is None else (), writes=[xr])
        load_x(0)
        load_x(1)
        for it, (half, tt) in enumerate(iters):
            if it + 2 < len(iters):
                load_x(it + 2)
            w, wr = wo[half]
            g = 0 if tt < 32 else 1
            b = it % 4
            x, xr = xt[it % 4]
            t_, t_r = tb[it % 2]
            mm_tstat(b, w, wr, tt)
            if bias_vec is not None:
                P.op("dve", lambda e, t_=t_, b=b, half=half: e.tensor_tensor(
                    out=t_, in0=bank(b), in1=brep[:, half * 512:(half + 1) * 512], op=ALU.add),
                    reads=[psr[b], brep_r], writes=[t_r])
                P.op("dve", lambda e, t_=t_, g=g, half=half: e.tensor_tensor(
                    out=t_, in0=t_, in1=G_[g][0][:, half * 512:(half + 1) * 512], op=ALU.mult),
                    reads=[t_r, G_[g][1]], writes=[t_r])
            else:
                P.op("dve", lambda e, t_=t_, b=b, g=g, half=half: e.tensor_tensor(
                    out=t_, in0=bank(b), in1=G_[g][0][:, half * 512:(half + 1) * 512], op=ALU.mult),
                    reads=[psr[b], G_[g][1]], writes=[t_r])
            P.op("pool", lambda e, x=x, t_=t_: e.tensor_tensor(out=x, in0=x, in1=t_, op=ALU.add), reads=[xr, t_r], writes=[xr])
            P.dma("sp", f"oxo{it % 4}", lambda e, x=x, tt=tt, half=half: e.dma_start(
                out=S["X"][tt * 128:(tt + 1) * 128, half * 512:(half + 1) * 512], in_=x), reads=[xr], writes=[R["X"][tt][half]])
        P.release(m)

    def phase_ffn(i):
        P.new_phase()
        m0 = P.mark()
        wd, wd_r = P.alloc([128, NCH_FF, D], BF16, "wd")
        P.dma("pool", "wd", lambda e: e.dma_start(out=wd, in_=I["ffn_w_down"][i].rearrange("(c p) n -> p c n", p=128)), writes=[wd_r])
        m = P.mark()
        wg = [P.alloc([128, 2, 8, 128], BF16, f"wg{k}") for k in range(3)]
        sg = [P.alloc([128, 512], F32, f"sg{k}") for k in range(2)]
        md = [P.alloc([128, 512], BF16, f"md{k}") for k in range(3)]
        it = 0
        for c in range(NCH_FF):
            w, wr = wg[c % 3]
            for u in range(2):
                P.dma("pool", f"wg{c % 3}_{u}", lambda e, w=w, c=c, u=u: e.dma_start(
                    out=w[:, u], in_=I["ffn_w_gu"][i][:, u * DFF + c * 128: u * DFF + (c + 1) * 128].rearrange("(kc p) n -> p kc n", p=128)),
                    writes=[wr] if u == 0 else (), acc_writes=[wr] if u else ())
            for tile in range(NT):
                bG = (it % 4) * 2
                bU = bG + 1
                s_, s_r = sg[it % 2]
                m_, m_r = md[it % 3]
                mm_wstat(bG, w[:, 0], wr, tile)
                mm_wstat(bU, w[:, 1], wr, tile)
                P.op("act", lambda e, s_=s_, bG=bG: e.activation(out=s_, in_=bank(bG), func=AF.Silu), reads=[psr[bG]], writes=[s_r])
                P.op("dve", lambda e, m_=m_, s_=s_, bU=bU: e.tensor_tensor(out=m_, in0=bank(bU), in1=s_, op=ALU.mult),
                     reads=[psr[bU], s_r], writes=[m_r])
                P.dma("sp", f"mdo{it % 3}", lambda e, m_=m_, c=c, tile=tile: e.dma_start(
                    out=S["MID"][c, :, tile * 512:(tile + 1) * 512], in_=m_), reads=[m_r], acc_writes=[R["MID"][tile]])
                it += 1
        P.release(m)
        P.new_phase()
        m = P.mark()
        G_ = load_mod(5 * D, "fG")
        mt = [P.alloc([128, NCH_FF, 512], BF16, f"mt{k}") for k in range(2)]
        xt = [P.alloc([128, 512], F32, f"fx{k}") for k in range(4)]
        tb = [P.alloc([128, 512], F32, f"ft{k}") for k in range(2)]
        iters = [(tile, ts, half) for tile in range(NT) for ts in range(4) for half in range(2)]

        def load_mt(tile):
            mt_, mt_r = mt[tile % 2]
            P.dma("sp", f"mt{tile % 2}", lambda e: e.dma_start(
                out=mt_, in_=S["MID"][:, :, tile * 512:(tile + 1) * 512].rearrange("c p t -> p c t")), reads=[R["MID"][tile]], writes=[mt_r])

        def load_x(k):
            tile, ts, half = iters[k]
            tt = tile * 4 + ts
            x, xr = xt[k % 4]
            P.dma("sp", f"fx{k % 4}", lambda e: e.dma_start(
                out=x, in_=S["X"][tt * 128:(tt + 1) * 128, half * 512:(half + 1) * 512]), reads=[R["X"][tt][half]], writes=[xr])
        load_mt(0)
        load_x(0)
        load_x(1)
        for it, (tile, ts, half) in enumerate(iters):
            if ts == 0 and half == 0 and tile + 1 < NT:
                load_mt(tile + 1)
            if it + 2 < len(iters):
                load_x(it + 2)
            g = 0 if tile < 8 else 1
            mt_, mt_r = mt[tile % 2]
            tt = tile * 4 + ts
            b = it % 4
            x, xr = xt[it % 4]
            t_, t_r = tb[it % 2]
            for c in range(NCH_FF):
                P.op("pe", lambda e, mt_=mt_, c=c, ts=ts, half=half, b=b: e.matmul(
                    bank(b), lhsT=mt_[:, c, ts * 128:(ts + 1) * 128], rhs=wd[:, c, half * 512:(half + 1) * 512],
                    start=(c == 0), stop=(c == NCH_FF - 1)),
                    reads=[mt_r, wd_r], writes=[psr[b]] if c == 0 else (), acc_writes=[psr[b]] if c else ())
            P.op("dve", lambda e, t_=t_, b=b, g=g, half=half: e.tensor_tensor(
                out=t_, in0=bank(b), in1=G_[g][0][:, half * 512:(half + 1) * 512], op=ALU.mult),
                reads=[psr[b], G_[g][1]], writes=[t_r])
            P.op("pool", lambda e, x=x, t_=t_: e.tensor_tensor(out=x, in0=x, in1=t_, op=ALU.add), reads=[xr, t_r], writes=[xr])
            P.dma("sp", f"fxo{it % 4}", lambda e, x=x, tt=tt, half=half: e.dma_start(
                out=S["X"][tt * 128:(tt + 1) * 128, half * 512:(half + 1) * 512], in_=x), reads=[xr], writes=[R["X"][tt][half]])
        P.release(m)
        P.release(m0)

    def phase_hy_in(j):
        P.new_phase()
        m = P.mark()
        CV, CV_r = P.alloc([128, 120], F32, "CV")
        load_cols(CV, CV_r, 0, I["hy_b_in"][j], 24, "cvl")
        for k in range(3):
            load_cols(CV, CV_r, 24 + k * 24, I["hy_conv_w"][j, k], 24, "cvl")
        load_cols(CV, CV_r, 96, I["hy_conv_b"][j], 24, "cvl")
        ubS = [P.alloc([128, TS + 2], F32, f"ubS{k}") for k in range(2)]
        ubP = [P.alloc([128, 2, 258], F32, f"ubP{k}") for k in range(2)]
        for k in range(2):
            P.op("pool", lambda e, k=k: e.memset(ubS[k][0], 0.0), writes=[ubS[k][1]])
            P.op("pool", lambda e, k=k: e.memset(ubP[k][0], 0.0), writes=[ubP[k][1]])
        cb1, cb1_r = P.alloc([128, T], F32, "cb1")
        cb2 = [P.alloc([128, T], F32, f"cb2{k}") for k in range(2)]
        vst, vst_r = P.alloc([128, NTT, 128], BF16, "vst")
        wi = [P.alloc([128, 8, 128], BF16, f"wi{k}") for k in range(3)]
        pend_tr = []

        def do_transposes(dst, dst_r, jd):
            for t4 in range(NTT // 4):
                b = 4 + t4 % 4
                for q in range(4):
                    tt = t4 * 4 + q
                    P.op("pe", lambda e, tt=tt, q=q, b=b: e.transpose(
                        bank(b)[:, q * 128:(q + 1) * 128], dst[:, tt * 128:(tt + 1) * 128], identf),
                        reads=[dst_r, identf_r], writes=[psr[b]] if q == 0 else (), acc_writes=[psr[b]] if q else ())
                P.op("act", lambda e, t4=t4, b=b: e.copy(out=vst[:, t4 * 4:(t4 + 1) * 4, :], in_=bank(b).rearrange("p (q d) -> p q d", d=128)),
                     reads=[psr[b]], acc_writes=[vst_r])
            P.dma("sp", "vsto", lambda e: e.dma_start(
                out=S["VVTOK"][:, jd * 128:(jd + 1) * 128].rearrange("(c p) d -> p c d", p=128), in_=vst),
                reads=[vst_r], acc_writes=[R["VVTOK"]])
        it = 0
        uc = 0
        c2 = 0
        for jd in range(8):
            for part, fc in (("x1", 8 + jd), ("v", 16 + jd), ("x0", jd)):
                w, wr = wi[it % 3]
                it += 1
                stream_w(w, wr, f"wi{it % 3}", I["hy_w_in"][j][:, fc * 128:(fc + 1) * 128])
                uS, uS_r = ubS[uc % 2]
                uP, uP_r = ubP[uc % 2]
                uc += 1
                for tile in range(NT):
                    b = tile % 4
                    mm_wstat(b, w, wr, tile)
                    if tile < 8:
                        P.op("act", lambda e, uS=uS, b=b, tile=tile, fc=fc: e.activation(
                            out=uS[:, 1 + tile * 512: 1 + (tile + 1) * 512], in_=bank(b), func=AF.Identity, bias=CV[:, fc:fc + 1]),
                            reads=[psr[b], CV_r], acc_writes=[uS_r])
                    else:
                        P.op("act", lambda e, uP=uP, b=b, fc=fc: e.activation(
                            out=uP[:, :, 1:257], in_=bank(b).rearrange("p (s t) -> p s t", t=256), func=AF.Identity, bias=CV[:, fc:fc + 1]),
                            reads=[psr[b], CV_r], acc_writes=[uP_r])
                if part == "x1":
                    dst, dst_r = cb1, cb1_r
                else:
                    dst, dst_r = cb2[c2 % 2]
                    c2 += 1
                w0, w1, w2, cbc = (CV[:, 24 + fc:25 + fc], CV[:, 48 + fc:49 + fc], CV[:, 72 + fc:73 + fc], CV[:, 96 + fc:97 + fc])
                dS = dst[:, 0:TS]
                dP = dst[:, TS:T].rearrange("p (s t) -> p s t", t=256)
                for (dd, uu, ur, n) in ((dS, uS, uS_r, TS), (dP, uP, uP_r, 256)):
                    def sl(o, uu=uu, n=n):
                        return uu[:, o:o + n] if len(uu.shape) == 2 else uu[:, :, o:o + n]
                    P.op("dve", lambda e, dd=dd, sl=sl, w0=w0, cbc=cbc: e.tensor_scalar(out=dd, in0=sl(0), scalar1=w0, scalar2=cbc, op0=ALU.mult, op1=ALU.add),
                         reads=[ur, CV_r], acc_writes=[dst_r])
                    P.op("dve", lambda e, dd=dd, sl=sl, w1=w1: e.scalar_tensor_tensor(out=dd, in0=sl(1), scalar=w1, in1=dd, op0=ALU.mult, op1=ALU.add),
                         reads=[ur, CV_r, dst_r], acc_writes=[dst_r])
                    P.op("dve", lambda e, dd=dd, sl=sl, w2=w2: e.scalar_tensor_tensor(out=dd, in0=sl(2), scalar=w2, in1=dd, op0=ALU.mult, op1=ALU.add),
                         reads=[ur, CV_r, dst_r], acc_writes=[dst_r])
                if part == "v":
                    P.op("pool", lambda e, dst=dst: e.tensor_tensor(out=dst, in0=dst, in1=cb1, op=ALU.mult), reads=[dst_r, cb1_r], writes=[dst_r])
                    P.dma("sp", f"vvt{c2 % 2}", lambda e, dst=dst, jd=jd: e.dma_start(out=S["VVT"][jd], in_=dst), reads=[dst_r], acc_writes=[R["VVT"]])
                    pend_tr.append((dst, dst_r, jd))
                if part != "v" and pend_tr:
                    do_transposes(*pend_tr.pop(0))
                if False:
                    for t4 in range(NTT // 4):
                        b = 4 + t4 % 4
                        for q in range(4):
                            tt = t4 * 4 + q
                            P.op("pe", lambda e, dst=dst, tt=tt, q=q, b=b: e.transpose(
                                bank(b)[:, q * 128:(q + 1) * 128], dst[:, tt * 128:(tt + 1) * 128], identf),
                                reads=[dst_r, identf_r], writes=[psr[b]] if q == 0 else (), acc_writes=[psr[b]] if q else ())
                        P.op("act", lambda e, t4=t4, b=b: e.copy(out=vst[:, t4 * 4:(t4 + 1) * 4, :], in_=bank(b).rearrange("p (q d) -> p q d", d=128)),
                             reads=[psr[b]], acc_writes=[vst_r])
                    P.dma("sp", "vsto", lambda e, jd=jd: e.dma_start(
                        out=S["VVTOK"][:, jd * 128:(jd + 1) * 128].rearrange("(c p) d -> p c d", p=128), in_=vst),
                        reads=[vst_r], acc_writes=[R["VVTOK"]])
                    vst_r.w = dict(vst_r.w)
                if part == "x0":
                    P.dma("sp", f"x0t{c2 % 2}", lambda e, dst=dst, jd=jd: e.dma_start(out=S["X0T"][jd], in_=dst), reads=[dst_r], acc_writes=[R["X0T"]])
        P.release(m)

    def phase_hy_filter(j, G):
        L = G["L"]
        nm = G["name"]
        ntc = L // 128
        HS, HD = S["HS" + nm], S["HD" + nm]
        P.new_phase()
        rn, rn_r = P.alloc([128, D], F32, "rn")
        m = P.mark()
        zT, zT_r = P.alloc([33, L], F32, "zT")
        w1, w1_r = P.alloc([33, 64], F32, "fw1")
        w2, w2_r = P.alloc([64, 64], F32, "fw2")
        w3, w3_r = P.alloc([64, 2 * D], F32, "fw3")
        b3, b3_r = P.alloc([128, 2 * D], F32, "fb3")
        fcol, fcol_r = P.alloc([64, 4], F32, "fcol")
        negt, negt_r = P.alloc([128, ntc], F32, "negt")
        drep, drep_r = P.alloc([128, D], F32, "drep")
        mask0, mask0_r = P.alloc([128, 1], F32, "mask0")
        h1, h1_r = P.alloc([64, L], F32, "h1")
        h2, h2_r = P.alloc([64, L], F32, "h2")
        tmp = [P.alloc([64, 512], F32, f"ftmp{k}") for k in range(2)]
        P.dma("sp", "zT", lambda e: e.dma_start(out=zT, in_=I["z" + nm]), writes=[zT_r])
        P.dma("sp", "fw1", lambda e: e.dma_start(out=w1, in_=I["filt_w1"][j]), writes=[w1_r])
        P.dma("sp", "fw2", lambda e: e.dma_start(out=w2, in_=I["filt_w2"][j]), writes=[w2_r])
        P.dma("sp", "fw3", lambda e: e.dma_start(out=w3, in_=I["filt_w3"][j]), writes=[w3_r])
        P.dma("sp", "fb3", lambda e: e.dma_start(out=b3, in_=bcast_rows(I["filt_b3"][j], 2 * D)), writes=[b3_r])
        P.dma("sp", "negt", lambda e: e.dma_start(out=negt, in_=I["negt" + nm]), writes=[negt_r])
        P.dma("sp", "drep", lambda e: e.dma_start(out=drep, in_=bcast_rows(I["delta"], D)), writes=[drep_r])
        P.dma("sp", "mask0", lambda e: e.dma_start(out=mask0, in_=I["mask0"]), writes=[mask0_r])
        for k, v in enumerate((I["filt_b1"][j], I["filt_b2"][j], I["filt_freq"][j])):
            P.dma("sp", "fcol", lambda e, k=k, v=v: e.dma_start(out=fcol[:, k:k + 1], in_=v.rearrange("(p o) -> p o", o=1)), acc_writes=[fcol_r])
        TWO_PI = 2.0 * math.pi
        SC = TWO_PI * (1.0 - 2e-6)
        wr_, wr_r = P.alloc([64, 512], F32, "fwrap")

        def sin_layer(lhsT, lhsT_r, src, src_r, dstb, dstb_r, bcol):
            n = min(512, L)
            for ti in range(L // n):
                b = ti % 2
                t_, t_r = tmp[ti % 2]
                P.op("pe", lambda e, ti=ti, b=b: e.matmul(bank(b)[0:64, 0:n], lhsT=lhsT, rhs=src[:, ti * n:(ti + 1) * n], start=True, stop=True),
                     reads=[lhsT_r, src_r], writes=[psr[b]])
                P.op("dve", lambda e, t_=t_, b=b: e.tensor_scalar(
                    out=t_[:, 0:n], in0=bank(b)[0:64, 0:n], scalar1=fcol[:, bcol:bcol + 1], scalar2=fcol[:, 2:3], op0=ALU.add, op1=ALU.mult),
                    reads=[psr[b], fcol_r], writes=[t_r])
                for rnd in range(2):
                    for (cmp_, thr, sgn) in ((ALU.is_lt, -math.pi, ALU.add), (ALU.is_gt, math.pi, ALU.subtract)):
                        P.op("dve", lambda e, t_=t_, cmp_=cmp_, thr=thr: e.tensor_scalar(
                            out=wr_[:, 0:n], in0=t_[:, 0:n], scalar1=thr, scalar2=TWO_PI, op0=cmp_, op1=ALU.mult),
                            reads=[t_r], writes=[wr_r])
                        P.op("dve", lambda e, t_=t_, sgn=sgn: e.tensor_tensor(out=t_[:, 0:n], in0=t_[:, 0:n], in1=wr_[:, 0:n], op=sgn),
                             reads=[t_r, wr_r], writes=[t_r])
                P.op("act", lambda e, t_=t_, ti=ti: e.activation(out=dstb[:, ti * n:(ti + 1) * n], in_=t_[:, 0:n], func=AF.Sin, scale=1.0 - 2e-6),
                     reads=[t_r], acc_writes=[dstb_r])
        sin_layer(w1, w1_r, zT, zT_r, h1, h1_r, 0)
        sin_layer(w2, w2_r, h1, h1_r, h2, h2_r, 1)
        dec = [P.alloc([128, D], F32, "dec0")] * 2
        hf = [P.alloc([128, D], F32, f"hf{k}") for k in range(2)]
        hb_ = [P.alloc([128, D], F32, f"hbk{k}") for k in range(2)]
        ab = [P.alloc([128, 2 * D], F32, "ab0")] * 2
        hs = [P.alloc([128, D], BF16, f"hs{k}") for k in range(2)]
        hd = [P.alloc([128, D], BF16, f"hd{k}") for k in range(2)]
        pend_ones = []

        def flush_ones():
            tc_ = pend_ones.pop(0)
            k_ = tc_ % 2
            for q in range(4):
                bq = 4 + q % 2
                first = (tc_ == 0 and q < 2)
                last = (tc_ == ntc - 1 and q >= 2)
                P.op("pe", lambda e, q=q, bq=bq, first=first, last=last: e.matmul(
                    bank(bq), lhsT=onesf, rhs=ab[k_][0][:, q * 512:(q + 1) * 512], start=first, stop=last),
                    reads=[onesf_r, ab[k_][1]], writes=[psr[bq]] if first else (), acc_writes=() if first else [psr[bq]])
        for tc in range(ntc):
            k = tc % 2
            for cb in range(4):
                P.op("pe", lambda e, tc=tc, cb=cb: e.matmul(bank(cb), lhsT=h2[:, tc * 128:(tc + 1) * 128], rhs=w3[:, cb * 512:(cb + 1) * 512], start=True, stop=True),
                     reads=[h2_r, w3_r], writes=[psr[cb]])
            if pend_ones:
                flush_ones()
            P.op("act", lambda e, k=k, tc=tc: e.activation(out=dec[k][0], in_=drep, func=AF.Exp, scale=negt[:, tc:tc + 1]),
                 reads=[drep_r, negt_r], writes=[dec[k][1]])
            for (dst, lo) in ((hf[k], 0), (hb_[k], D)):
                P.op("dve", lambda e, dst=dst, lo=lo: e.tensor_tensor(out=dst[0], in0=PS[:, lo:lo + D], in1=b3[:, lo:lo + D], op=ALU.add),
                     reads=[psr[lo // 512], psr[lo // 512 + 1], b3_r], writes=[dst[1]])
                P.op("dve", lambda e, dst=dst, k=k: e.tensor_tensor(out=dst[0], in0=dst[0], in1=dec[k][0], op=ALU.mult),
                     reads=[dst[1], dec[k][1]], writes=[dst[1]])
                P.op("act", lambda e, dst=dst, lo=lo, k=k: e.activation(out=ab[k][0][:, lo:lo + D], in_=dst[0], func=AF.Abs),
                     reads=[dst[1]], acc_writes=[ab[k][1]])
            pend_ones.append(tc)
            if tc == 0:
                P.op("dve", lambda e, k=k: e.tensor_scalar(out=hb_[k][0], in0=hb_[k][0], scalar1=mask0[:, 0:1], scalar2=None, op0=ALU.mult),
                     reads=[hb_[k][1], mask0_r], writes=[hb_[k][1]])
            P.op("pool", lambda e, k=k: e.tensor_tensor(out=hs[k][0], in0=hf[k][0], in1=hb_[k][0], op=ALU.add),
                 reads=[hf[k][1], hb_[k][1]], writes=[hs[k][1]])
            P.op("pool", lambda e, k=k: e.tensor_tensor(out=hd[k][0], in0=hb_[k][0], in1=hf[k][0], op=ALU.subtract),
                 reads=[hf[k][1], hb_[k][1]], writes=[hd[k][1]])
            P.dma("sp", f"hso{k}", lambda e, k=k, tc=tc: e.dma_start(out=HS[tc * 128:(tc + 1) * 128, :], in_=hs[k][0]), reads=[hs[k][1]], acc_writes=[R["HS"]])
            P.dma("sp", f"hdo{k}", lambda e, k=k, tc=tc: e.dma_start(out=HD[tc * 128:(tc + 1) * 128, :], in_=hd[k][0]), reads=[hd[k][1]], acc_writes=[R["HS"]])
        while pend_ones:
            flush_ones()
        P.op("dve", lambda e: e.tensor_scalar(out=rn, in0=PS[:, 4 * 512:6 * 512], scalar1=1e-6, scalar2=None, op0=ALU.add),
             reads=[psr[4], psr[5]], writes=[rn_r])
        P.op("dve", lambda e: e.reciprocal(out=rn, in_=rn), reads=[rn_r], writes=[rn_r])
        P.release(m)
        return rn, rn_r

    def phase_hy_conv(j, G, s, rn, rn_r):
        L = G["L"]
        nm = G["name"]
        ntc = L // 128
        nfc = 33 if nm == "s" else 3
        SQ = nfc * 128
        Qt, Rt, WFd = I["q" + nm], I["r" + nm], I["wf" + nm]
        HS, HD = S["HS" + nm], S["HD" + nm]
        tok0 = G["tok0"] + s * L
        P.new_phase()
        m = P.mark()
        wf, wf_r = P.alloc([128, nfc], F32, "wf")
        P.dma("sp", "wf", lambda e: e.dma_start(out=wf, in_=WFd), writes=[wf_r])
        vvhs, vvhs_r = P.alloc([128, ntc, 512], BF16, "vvhs")
        vvhd, vvhd_r = P.alloc([128, ntc, 512], BF16, "vvhd")
        qch = [P.alloc([128, ntc, 128], BF16, f"qch{k}") for k in range(2)]
        rch = [P.alloc([128, ntc, 128], BF16, f"rch{k}") for k in range(2)]
        kcs = [P.alloc([128, 256], F32, f"kcs{k}") for k in range(2)]
        kss = [P.alloc([128, 256], F32, f"kss{k}") for k in range(2)]
        ta = [P.alloc([128, 256], F32, f"ta{k}") for k in range(2)]
        tb = [P.alloc([128, 256], F32, f"tb{k}") for k in range(2)]
        yst = [P.alloc([128, 2, 256], BF16, f"yst{k}") for k in range(2)]
        def load_tab(k):
            fc_ = k % nfc
            qc_, qc_r_ = qch[k % 2]
            rc_, rc_r_ = rch[k % 2]
            if nm == "s":
                qsrc, rsrc = I["qfs"][fc_], I["rfs"][fc_]
            else:
                qsrc = Qt[0:ntc, :, fc_ * 128:(fc_ + 1) * 128].rearrange("c p f -> p c f")
                rsrc = Rt[0:ntc, :, fc_ * 128:(fc_ + 1) * 128].rearrange("c p f -> p c f")
            P.dma("sp", f"qch{k % 2}", lambda e: e.dma_start(out=qc_, in_=qsrc), writes=[qc_r_])
            P.dma("sp", f"rch{k % 2}", lambda e: e.dma_start(out=rc_, in_=rsrc), writes=[rc_r_])
        it = 0
        for dq in range(4):
            dh = dq // 2
            vsrc = S["VVTOK"][tok0:tok0 + L, dq * 256:(dq + 1) * 256].rearrange("(c p) d -> p c d", p=128)
            P.dma("sp", "vva", lambda e, vsrc=vsrc: e.dma_start(out=vvhs[:, :, 0:256], in_=vsrc), reads=[R["VVTOK"]], writes=[vvhs_r])
            P.dma("sp", "vvb", lambda e, vsrc=vsrc: e.dma_start(out=vvhd[:, :, 0:256], in_=vsrc), reads=[R["VVTOK"]], writes=[vvhd_r])
            P.dma("sp", "hsh", lambda e, dq=dq: e.dma_start(
                out=vvhs[:, :, 256:512], in_=HS[:, dq * 256:(dq + 1) * 256].rearrange("(c p) d -> p c d", p=128)), reads=[R["HS"]], acc_writes=[vvhs_r])
            P.dma("sp", "hdh", lambda e, dq=dq: e.dma_start(
                out=vvhd[:, :, 256:512], in_=HD[:, dq * 256:(dq + 1) * 256].rearrange("(c p) d -> p c d", p=128)), reads=[R["HS"]], acc_writes=[vvhd_r])
            for fc in range(nfc):
                qc, qc_r = qch[it % 2]
                rc, rc_r = rch[it % 2]
                if it == 0:
                    load_tab(0)
                if it + 1 < 4 * nfc:
                    load_tab(it + 1)
                b0 = (it % 4) * 2
                bC, bS = b0, b0 + 1
                for tc in range(ntc):
                    st, sp_ = (tc == 0), (tc == ntc - 1)
                    for (bb, tab, tab_r, mov, mov_r) in ((bC, qc, qc_r, vvhs, vvhs_r), (bS, rc, rc_r, vvhd, vvhd_r)):
                        P.op("pe", lambda e, bb=bb, tab=tab, mov=mov, tc=tc, st=st, sp_=sp_: e.matmul(
                            bank(bb), lhsT=tab[:, tc, :], rhs=mov[:, tc, :], start=st, stop=sp_),
                            reads=[tab_r, mov_r], writes=[psr[bb]] if st else (), acc_writes=() if st else [psr[bb]])
                bVc = bKc = bC
                bVs = bKs = bS
                Vc_, Kc_ = bank(bC)[:, 0:256], bank(bC)[:, 256:512]
                Vs_, Ks_ = bank(bS)[:, 0:256], bank(bS)[:, 256:512]
                k = it % 2
                wcol = wf[:, fc:fc + 1]
                rsl = rn[:, dq * 256:(dq + 1) * 256]
                P.op("dve", lambda e, k=k, Kc_=Kc_, wcol=wcol, rsl=rsl: e.scalar_tensor_tensor(
                    out=kcs[k][0], in0=Kc_, scalar=wcol, in1=rsl, op0=ALU.mult, op1=ALU.mult),
                    reads=[psr[bKc], wf_r, rn_r], writes=[kcs[k][1]])
                P.op("dve", lambda e, k=k, Ks_=Ks_, wcol=wcol, rsl=rsl: e.scalar_tensor_tensor(
                    out=kss[k][0], in0=Ks_, scalar=wcol, in1=rsl, op0=ALU.mult, op1=ALU.mult),
                    reads=[psr[bKs], wf_r, rn_r], writes=[kss[k][1]])
                P.op("dve", lambda e, k=k, Vc_=Vc_: e.tensor_tensor(out=ta[k][0], in0=Vc_, in1=kcs[k][0], op=ALU.mult),
                     reads=[psr[bVc], kcs[k][1]], writes=[ta[k][1]])
                P.op("dve", lambda e, k=k, Vs_=Vs_: e.tensor_tensor(out=tb[k][0], in0=Vs_, in1=kss[k][0], op=ALU.mult),
                     reads=[psr[bVs], kss[k][1]], writes=[tb[k][1]])
                P.op("pool", lambda e, k=k: e.tensor_tensor(out=yst[k][0][:, 0, :], in0=ta[k][0], in1=tb[k][0], op=ALU.add),
                     reads=[ta[k][1], tb[k][1]], writes=[yst[k][1]])
                P.op("dve", lambda e, k=k, Vs_=Vs_: e.tensor_tensor(out=ta[k][0], in0=Vs_, in1=kcs[k][0], op=ALU.mult),
                     reads=[psr[bVs], kcs[k][1]], writes=[ta[k][1]])
                P.op("dve", lambda e, k=k, Vc_=Vc_: e.tensor_tensor(out=tb[k][0], in0=Vc_, in1=kss[k][0], op=ALU.mult),
                     reads=[psr[bVc], kss[k][1]], writes=[tb[k][1]])
                P.op("pool", lambda e, k=k: e.tensor_tensor(out=yst[k][0][:, 1, :], in0=ta[k][0], in1=tb[k][0], op=ALU.subtract),
                     reads=[ta[k][1], tb[k][1]], acc_writes=[yst[k][1]])
                P.dma("sp", f"yfo{k}", lambda e, k=k, dh=dh, dq=dq, fc=fc: e.dma_start(out=S["YF"][dh, fc][:, :, (dq % 2) * 256:(dq % 2 + 1) * 256], in_=yst[k][0]),
                      reads=[yst[k][1]], acc_writes=[R["YF"]])
                it += 1
        P.release(m)
        P.new_phase()
        m = P.mark()
        skc, skc_r = P.alloc([128, 8], F32, "skc")
        load_cols(skc, skc_r, 0, I["hy_skip"][j], 8, "skc")
        Yg, Yg_r = P.alloc([128, nfc, 2, 512], BF16, "Yg")
        n = min(512, L)
        tq = [P.alloc([128, 2, n], BF16, f"tq{k}") for k in range(6)]
        vvt = [P.alloc([128, n], F32, f"vvt{k}") for k in range(3)]
        x0t = [P.alloc([128, n], F32, f"x0t{k}") for k in range(3)]
        it = 0
        ie = 0
        for dh in range(2):
            P.dma("sp", "Yg", lambda e, dh=dh: e.dma_start(out=Yg, in_=S["YF"][dh, 0:nfc].rearrange("c p s d -> p c s d")),
                  reads=[R["YF"]], writes=[Yg_r])
            for tt in range(L // n):
                b0 = ((dh * (L // n) + tt) % 2) * 4
                for fc in range(nfc):
                    t_, t_r = tq[it % 6]
                    P.dma("sp", f"tq{it % 6}a", lambda e, t_=t_, fc=fc, tt=tt: e.dma_start(out=t_[:, 0, :], in_=Qt[fc, :, tt * n:(tt + 1) * n]), writes=[t_r])
                    P.dma("sp", f"tq{it % 6}b", lambda e, t_=t_, fc=fc, tt=tt: e.dma_start(out=t_[:, 1, :], in_=Rt[fc, :, tt * n:(tt + 1) * n]), acc_writes=[t_r])
                    it += 1
                    for dcl in range(4):
                        for cs in range(2):
                            st = (fc == 0 and cs == 0)
                            sp_ = (fc == nfc - 1 and cs == 1)
                            P.op("pe", lambda e, t_=t_, fc=fc, dcl=dcl, cs=cs, st=st, sp_=sp_, b0=b0: e.matmul(
                                bank(b0 + dcl, n), lhsT=Yg[:, fc, cs, dcl * 128:(dcl + 1) * 128], rhs=t_[:, cs, :], start=st, stop=sp_),
                                reads=[Yg_r, t_r], writes=[psr[b0 + dcl]] if st else (), acc_writes=() if st else [psr[b0 + dcl]])
                for dcl in range(4):
                    dc = dh * 4 + dcl
                    v_, v_r = vvt[ie % 3]
                    x_, x_r = x0t[ie % 3]
                    ie += 1
                    c0 = tok0 + tt * n
                    P.dma("sp", f"vvt{ie % 3}", lambda e, v_=v_, dc=dc, c0=c0: e.dma_start(out=v_, in_=S["VVT"][dc, :, c0:c0 + n]), reads=[R["VVT"]], writes=[v_r])
                    P.dma("sp", f"x0t{ie % 3}", lambda e, x_=x_, dc=dc, c0=c0: e.dma_start(out=x_, in_=S["X0T"][dc, :, c0:c0 + n]), reads=[R["X0T"]], writes=[x_r])
                    P.op("dve", lambda e, v_=v_, dc=dc, dcl=dcl, b0=b0: e.scalar_tensor_tensor(
                        out=v_, in0=v_, scalar=skc[:, dc:dc + 1], in1=bank(b0 + dcl, n), op0=ALU.mult, op1=ALU.add),
                        reads=[v_r, skc_r, psr[b0 + dcl]], writes=[v_r])
                    P.op("pool", lambda e, v_=v_, x_=x_, dc=dc, c0=c0: e.tensor_tensor(out=hT[:, dc, c0:c0 + n], in0=v_, in1=x_, op=ALU.mult),
                         reads=[v_r, x_r], acc_writes=[hTr[c0 // 512]])
        P.release(m)

    class Stop(Exception):
        pass

    def chk(tag):
        if stop_after == tag:
            raise Stop()

    def dump_hT():
        for t in range(NT):
            P.dma("sp", "htd", lambda e, t=t: e.dma_start(out=S["HTD"][:, :, t * 512:(t + 1) * 512], in_=hT[:, :, t * 512:(t + 1) * 512]),
                  reads=[hTr[t]], acc_writes=[R["HTD"]])

    try:
        for i in range(4):
            j = i // 2
            phase_mod(i)
            chk(f"mod{i}")
            phase_norm(D, 0, xsrc=I["x"] if i == 0 else None)
            chk(f"norm1_{i}")
            if i % 2 == 0:
                phase_qkv(j)
                chk(f"qkv{i}")
                phase_attn(j, i)
                chk(f"attn{i}")
                phase_outproj(I["attn_w_o"][j], None, 2 * D, xsrc=I["x"] if i == 0 else None)
            else:
                phase_hy_in(j)
                chk(f"hyin{i}")
                for G in GROUPS:
                    mk = P.mark()
                    rn, rn_r = phase_hy_filter(j, G)
                    chk(f"hyfilt{i}{G['name']}")
                    for s in range(G["nseq"]):
                        phase_hy_conv(j, G, s, rn, rn_r)
                    P.release(mk)
                chk(f"hyconv{i}")
                phase_outproj(I["hy_w_out"][j], I["hy_b_out"][j], 2 * D)
            chk(f"mix{i}")
            phase_norm(4 * D, 3 * D)
            phase_ffn(i)
            chk(f"ffn{i}")
        phase_norm(0, 0, final=True)
    except Stop:
        if "HTD" in debug_outs:
            dump_hT()

    final_keys = [k for k in P.dma_cnt]
    print("n dma sems", len(final_keys), {e: len(P.q[e]) for e in ENGS})
    P.emit(final_keys)
    return nc, P


_CONSTS = None


def _core_inputs(b, inp, consts):
    m = {}
    m["x"] = np.ascontiguousarray(np.concatenate(
        [inp["x_sample"][b], inp["x_prompt"][2 * b], inp["x_prompt"][2 * b + 1]], axis=0).astype(np.float32))
    m["ck"] = np.ascontiguousarray(inp["cache_k"][b].reshape(2, 512, D).astype(np.float32))
    m["cv"] = np.ascontiguousarray(inp["cache_v"][b].reshape(2, 512, D).astype(np.float32))
    m["cvec"] = np.ascontiguousarray(np.stack([inp["c"][b], inp["c_ctx"]], axis=0).astype(np.float32))
    for nm, shp in W_SPECS:
        m[nm] = np.ascontiguousarray(np.asarray(inp[nm], dtype=np.float32).reshape(shp))
    for nm, shp, dt in CONST_SPECS:
        m[nm] = consts[nm]
    return m


def kernel(**inputs):
    global _CONSTS
    if _CONSTS is None:
        _CONSTS = _host_consts()
    inp = {k: np.asarray(v) for k, v in inputs.items()}
    nc, _ = build_program()
    in_maps = [_core_inputs(b, inp, _CONSTS) for b in range(8)]
    res = run_bass_kernel_spmd(nc, in_maps, core_ids=list(range(8)))
    y_prompt = np.zeros((16, 256, D), np.float32)
    y_sample = np.zeros((8, TS, D), np.float32)
    nk = np.zeros((16, 2, 256, 8, 2, 64), np.float32)
    nv = np.zeros((16, 2, 256, 8, 128), np.float32)
    for b in range(8):
        r = res.results[b]
        y = np.asarray(r["y"], dtype=np.float32)
        y_sample[b] = y[:TS]
        y_prompt[2 * b] = y[TS:TS + 256]
        y_prompt[2 * b + 1] = y[TS + 256:]
        k_ = np.asarray(r["nk"], dtype=np.float32).reshape(2, 2, 256, 8, 2, 64)
        v_ = np.asarray(r["nv"], dtype=np.float32).reshape(2, 2, 256, 8, 128)
        nk[2 * b], nk[2 * b + 1] = k_[0], k_[1]
        nv[2 * b], nv[2 * b + 1] = v_[0], v_[1]
    return (y_prompt, y_sample, nk, nv)
```
